# Optimizing a Trainium2 kernel written in Bass

```python
import math
import jax, jax.numpy as jnp
from jax import lax
import numpy as np


D_MODEL = 1024
BATCH = 1
SEQ = 16384
DEPTH = 2
DEC_BATCH = 8
DEC_SEQ = 4096
PAST_LEN = 128

GRID_W = 64
GROUP_DIM = 16
N_GROUPS = D_MODEL // GROUP_DIM
STATE_DIM = 64
N_DIR = 2
N_HEADS = 16
HEAD_DIM = D_MODEL // N_HEADS
WIN_H = 8
WIN_W = 16
D_FF = ((8 * D_MODEL // 3 + 127) // 128) * 128
CONV_W = 3
N_MIXERS = 2
N_S5 = (DEPTH + 1) // 2
N_NA = DEPTH // 2
ALPHA = (2 * DEPTH) ** 0.25
BETA = (8 * DEPTH) ** -0.25
LN_EPS = 1e-5
DT_MIN = 1e-3
DT_MAX = 1e-1

kernel_name = 'hybrid_s5_natten_deepnorm_encoder'


def layer_norm(x, g, b):
    xf = x.astype(jnp.float32)
    mu = jnp.mean(xf, axis=-1, keepdims=True)
    var = jnp.mean(jnp.square(xf - mu), axis=-1, keepdims=True)
    y = (xf - mu) * lax.rsqrt(var + LN_EPS)
    return (y * g.astype(jnp.float32) + b.astype(jnp.float32)).astype(x.dtype)


def _cmul(ar, ai, br, bi):
    return ar * br - ai * bi, ar * bi + ai * br


def _ssm_combine(e1, e2):
    a1r, a1i, b1r, b1i = e1
    a2r, a2i, b2r, b2i = e2
    ar, ai = _cmul(a2r, a2i, a1r, a1i)
    br, bi = _cmul(a2r, a2i, b1r, b1i)
    return ar, ai, br + b2r, bi + b2i


def s5_direction(u, lam_re, lam_im, log_dt, b_re, b_im, c_re, c_im, reverse):
    dt = jnp.exp(log_dt)[:, None]
    z_re, z_im = lam_re * dt, lam_im * dt
    mag = jnp.exp(z_re)
    ab_re, ab_im = mag * jnp.cos(z_im), mag * jnp.sin(z_im)
    den = lam_re * lam_re + lam_im * lam_im
    nr, ni = ab_re - 1.0, ab_im
    f_re = (nr * lam_re + ni * lam_im) / den
    f_im = (ni * lam_re - nr * lam_im) / den
    bb_re = f_re[..., None] * b_re - f_im[..., None] * b_im
    bb_im = f_re[..., None] * b_im + f_im[..., None] * b_re
    bu_re = jnp.einsum('bsgh,gph->bsgp', u, bb_re)
    bu_im = jnp.einsum('bsgh,gph->bsgp', u, bb_im)
    a_re = jnp.broadcast_to(ab_re, bu_re.shape)
    a_im = jnp.broadcast_to(ab_im, bu_re.shape)
    _, _, s_re, s_im = lax.associative_scan(_ssm_combine, (a_re, a_im, bu_re, bu_im), axis=1, reverse=reverse)
    return jnp.einsum('bsgp,ghp->bsgh', s_re, c_re) - jnp.einsum('bsgp,ghp->bsgh', s_im, c_im)


def s5_mixer(h, lam_re, lam_im, log_dt, b_re, b_im, c_re, c_im, d_skip, w_glu, b_glu):
    bsz, slen, _ = h.shape
    f32 = jnp.float32
    u = h.astype(f32).reshape(bsz, slen, N_GROUPS, GROUP_DIM)
    y = d_skip.astype(f32) * u
    for d in range(N_DIR):
        y = y + s5_direction(u, lam_re[d].astype(f32), lam_im[d].astype(f32), log_dt[d].astype(f32),
                             b_re[d].astype(f32), b_im[d].astype(f32), c_re[d].astype(f32), c_im[d].astype(f32),
                             reverse=(d == 1))
    y = jax.nn.gelu(y.reshape(bsz, slen, D_MODEL)).astype(h.dtype)
    ag = y @ w_glu + b_glu
    a, g = jnp.split(ag, 2, axis=-1)
    return a * jax.nn.sigmoid(g)


def neighborhood_attention(h, w_qkv, b_qkv, rpb, w_o, b_o):
    bsz, slen, _ = h.shape
    rows = slen // GRID_W
    kh = min(WIN_H, rows)
    qkv = h @ w_qkv + b_qkv
    q, k, v = jnp.split(qkv, 3, axis=-1)
    grid = (bsz, rows, GRID_W, N_HEADS, HEAD_DIM)
    q = q.reshape(grid) * (HEAD_DIM ** -0.5)
    k = k.reshape(grid)
    v = v.reshape(grid)
    col = jnp.arange(GRID_W)
    col_start = jnp.clip(col - WIN_W // 2, 0, GRID_W - WIN_W)
    col_idx = col_start[:, None] + jnp.arange(WIN_W)[None, :]
    col_off = col_idx - col[:, None] + (WIN_W - 1)

    def one_row(r):
        rs = jnp.clip(r - kh // 2, 0, rows - kh)
        k_rows = lax.dynamic_slice_in_dim(k, rs, kh, axis=1)
        v_rows = lax.dynamic_slice_in_dim(v, rs, kh, axis=1)
        k_win = k_rows[:, :, col_idx]
        v_win = v_rows[:, :, col_idx]
        q_r = lax.dynamic_index_in_dim(q, r, axis=1, keepdims=False)
        s = jnp.einsum('bqhd,bkqwhd->bhqkw', q_r, k_win).astype(jnp.float32)
        row_off = rs + jnp.arange(kh) - r + (WIN_H - 1)
        bias = rpb.astype(jnp.float32)[:, row_off][:, :, col_off]
        s = s + jnp.transpose(bias, (0, 2, 1, 3))[None]
        p = jax.nn.softmax(s.reshape(bsz, N_HEADS, GRID_W, kh * WIN_W), axis=-1)
        p = p.reshape(bsz, N_HEADS, GRID_W, kh, WIN_W).astype(v.dtype)
        return jnp.einsum('bhqkw,bkqwhd->bqhd', p, v_win)

    o = lax.map(one_row, jnp.arange(rows))
    o = jnp.moveaxis(o, 0, 1).reshape(bsz, slen, D_MODEL)
    return o @ w_o + b_o


def conv_ffn(h, w_in, b_in, conv_w, conv_b, w_out, b_out):
    ug = h @ w_in + b_in
    u, g = jnp.split(ug, 2, axis=-1)
    up = jnp.pad(u, ((0, 0), (1, 1), (0, 0)))
    u = up[:, :-2] * conv_w[0] + up[:, 1:-1] * conv_w[1] + up[:, 2:] * conv_w[2] + conv_b
    return (jax.nn.gelu(u) * g) @ w_out + b_out


def encoder_trunk(x, c, w_ada, b_ada, ln_g, ln_b, s5_p, na_p, ffn_p):
    cond = jax.nn.silu(c)
    for i in range(DEPTH):
        mod = (cond @ w_ada[i] + b_ada[i])[:, None, :]
        sh_m, sc_m, g_m, sh_f, sc_f, g_f = jnp.split(mod, 6, axis=-1)
        h = x * (1.0 + sc_m) + sh_m
        j = i // N_MIXERS
        if i % N_MIXERS == 0:
            y = s5_mixer(h, *[p[j] for p in s5_p])
        else:
            y = neighborhood_attention(h, *[p[j] for p in na_p])
        x = layer_norm(ALPHA * x + g_m * y, ln_g[i, 0], ln_b[i, 0])
        h = x * (1.0 + sc_f) + sh_f
        y = conv_ffn(h, *[p[i] for p in ffn_p])
        x = layer_norm(ALPHA * x + g_f * y, ln_g[i, 1], ln_b[i, 1])
    return x


def setup_inputs(seed: int = 0) -> dict:
    key = jax.random.key(seed)
    ks = jax.random.split(key, 32)
    f32 = jnp.float32

    def nrm(k, shape, s):
        return s * jax.random.normal(k, shape, f32)

    D = D_MODEL
    n_idx = jnp.arange(STATE_DIM, dtype=f32)
    s5_shape = (N_S5, N_DIR, N_GROUPS, STATE_DIM)
    qkv_scale = jnp.concatenate([jnp.ones((2 * D,), f32), jnp.full((D,), BETA, f32)])
    return {
        'x_prompt': nrm(ks[0], (BATCH, SEQ, D), 1.0),
        'x_sample': nrm(ks[1], (DEC_BATCH, DEC_SEQ, D), 1.0),
        'c_prompt': nrm(ks[2], (BATCH, D), 1.0),
        'c_sample': nrm(ks[3], (DEC_BATCH, D), 1.0),
        'w_ada': nrm(ks[4], (DEPTH, D, 6 * D), 0.5 * D ** -0.5),
        'b_ada': nrm(ks[5], (DEPTH, 6 * D), 0.02),
        'ln_g': 1.0 + nrm(ks[6], (DEPTH, 2, D), 0.02),
        'ln_b': nrm(ks[7], (DEPTH, 2, D), 0.02),
        's5_lam_re': -0.5 + nrm(ks[8], s5_shape, 0.01),
        's5_lam_im': math.pi * n_idx + nrm(ks[9], s5_shape, 0.01),
        's5_log_dt': jax.random.uniform(ks[10], (N_S5, N_DIR, N_GROUPS), f32, math.log(DT_MIN), math.log(DT_MAX)),
        's5_b_re': nrm(ks[11], (N_S5, N_DIR, N_GROUPS, STATE_DIM, GROUP_DIM), (2 * GROUP_DIM) ** -0.5),
        's5_b_im': nrm(ks[12], (N_S5, N_DIR, N_GROUPS, STATE_DIM, GROUP_DIM), (2 * GROUP_DIM) ** -0.5),
        's5_c_re': nrm(ks[13], (N_S5, N_DIR, N_GROUPS, GROUP_DIM, STATE_DIM), (2 * STATE_DIM) ** -0.5),
        's5_c_im': nrm(ks[14], (N_S5, N_DIR, N_GROUPS, GROUP_DIM, STATE_DIM), (2 * STATE_DIM) ** -0.5),
        's5_d': nrm(ks[15], (N_S5, N_GROUPS, GROUP_DIM), 1.0),
        's5_w_glu': nrm(ks[16], (N_S5, D, 2 * D), BETA * D ** -0.5),
        's5_b_glu': nrm(ks[17], (N_S5, 2 * D), 0.02),
        'na_w_qkv': nrm(ks[18], (N_NA, D, 3 * D), D ** -0.5) * qkv_scale,
        'na_b_qkv': nrm(ks[19], (N_NA, 3 * D), 0.02),
        'na_rpb': nrm(ks[20], (N_NA, N_HEADS, 2 * WIN_H - 1, 2 * WIN_W - 1), 0.1),
        'na_w_o': nrm(ks[21], (N_NA, D, D), BETA * D ** -0.5),
        'na_b_o': nrm(ks[22], (N_NA, D), 0.02),
        'ffn_w_in': nrm(ks[23], (DEPTH, D, 2 * D_FF), D ** -0.5),
        'ffn_b_in': nrm(ks[24], (DEPTH, 2 * D_FF), 0.02),
        'ffn_conv_w': nrm(ks[25], (DEPTH, CONV_W, D_FF), CONV_W ** -0.5),
        'ffn_conv_b': nrm(ks[26], (DEPTH, D_FF), 0.02),
        'ffn_w_out': nrm(ks[27], (DEPTH, D_FF, D), BETA * D_FF ** -0.5),
        'ffn_b_out': nrm(ks[28], (DEPTH, D), 0.02),
    }


def reference(x_prompt, x_sample, c_prompt, c_sample, w_ada, b_ada, ln_g, ln_b,
              s5_lam_re, s5_lam_im, s5_log_dt, s5_b_re, s5_b_im, s5_c_re, s5_c_im, s5_d, s5_w_glu, s5_b_glu,
              na_w_qkv, na_b_qkv, na_rpb, na_w_o, na_b_o,
              ffn_w_in, ffn_b_in, ffn_conv_w, ffn_conv_b, ffn_w_out, ffn_b_out):
    s5_p = (s5_lam_re, s5_lam_im, s5_log_dt, s5_b_re, s5_b_im, s5_c_re, s5_c_im, s5_d, s5_w_glu, s5_b_glu)
    na_p = (na_w_qkv, na_b_qkv, na_rpb, na_w_o, na_b_o)
    ffn_p = (ffn_w_in, ffn_b_in, ffn_conv_w, ffn_conv_b, ffn_w_out, ffn_b_out)
    y_prompt = encoder_trunk(x_prompt, c_prompt, w_ada, b_ada, ln_g, ln_b, s5_p, na_p, ffn_p)
    y_sample = encoder_trunk(x_sample, c_sample, w_ada, b_ada, ln_g, ln_b, s5_p, na_p, ffn_p)
    return (y_prompt, y_sample)
```

```python
import math
import itertools
import types
import numpy as np
from contextlib import ExitStack
import concourse.bass as bass
import concourse.mybir as mybir
from concourse.bass_utils import run_bass_kernel_spmd

F32 = mybir.dt.float32
BF16 = mybir.dt.bfloat16
AF = mybir.ActivationFunctionType
ALU = mybir.AluOpType

ENGS = ("pe", "dve", "act", "pool", "sp")
D = 1024
DFF = 2816
ALPHA = 4.0 ** 0.25
LN_EPS = 1e-5
NS_FULL = 4096
NP_FULL = 3072


def _freeze(fn):
    if fn is None or fn.__closure__ is None:
        return fn
    cells = tuple(types.CellType(c.cell_contents) for c in fn.__closure__)
    return types.FunctionType(fn.__code__, fn.__globals__, fn.__name__, fn.__defaults__, cells)


class Op:
    __slots__ = ("eng", "fn", "deps", "is_dma", "idx", "needed", "semval", "dsem", "dval", "waits", "prewait")

    def __init__(self, eng, fn, is_dma):
        self.eng = eng
        self.fn = fn
        self.is_dma = is_dma
        self.deps = set()
        self.needed = False
        self.semval = None
        self.dsem = None
        self.dval = None
        self.waits = []
        self.prewait = None


class Prog:
    def __init__(self, nc, es, n_dma_sems=14):
        self.nc = nc
        self.es = es
        self.ops = {e: [] for e in ENGS}
        self.last_w = {}
        self.readers = {}
        self.K = n_dma_sems
        self.out_dmas = []
        self.last_real = {e: None for e in ENGS}
        self.dma_hist = {e: [] for e in ENGS}

    def _track(self, o, reads, writes):
        deps = o.deps
        for k in reads:
            w = self.last_w.get(k)
            if w is not None:
                deps.add(w)
        for k in writes:
            w = self.last_w.get(k)
            if w is not None:
                deps.add(w)
            for r in self.readers.get(k, ()):
                deps.add(r)
        deps.discard(o)
        for k in writes:
            self.last_w[k] = o
            self.readers[k] = []
        for k in reads:
            self.readers.setdefault(k, []).append(o)

    dead = False
    paranoid = False

    def op(self, eng, fn, reads=(), writes=()):
        if self.dead:
            return None
        o = Op(eng, _freeze(fn), False)
        o.idx = len(self.ops[eng])
        self._track(o, reads, writes)
        if self.paranoid:
            for e2 in ENGS:
                if self.last_real[e2] is not None:
                    o.deps.add(self.last_real[e2])
                for d in self.dma_hist[e2][-self.K:]:
                    o.deps.add(d)
        self.ops[eng].append(o)
        self.last_real[eng] = o
        return o

    def dma(self, out, in_, reads=(), writes=(), q="sp", is_output=False, **kw):
        if self.dead:
            return None

        def fn(e):
            return e.dma_start(out=out, in_=in_, **kw)
        o = Op(q, fn, True)
        o.idx = len(self.ops[q])
        self._track(o, reads, writes)
        if self.paranoid:
            for e2 in ENGS:
                if self.last_real[e2] is not None:
                    o.deps.add(self.last_real[e2])
                for d in self.dma_hist[e2][-self.K:]:
                    o.deps.add(d)
        self.ops[q].append(o)
        self.dma_hist[q].append(o)
        if is_output:
            self.out_dmas.append(o)
        return o

    def barrier(self):
        if self.dead:
            return
        b = Op("sp", lambda e: e.nop(), False)
        b.idx = len(self.ops["sp"])
        for e in ENGS:
            if self.last_real[e] is not None:
                b.deps.add(self.last_real[e])
            for d in self.dma_hist[e][-self.K:]:
                b.deps.add(d)
        self.ops["sp"].append(b)
        self.last_real["sp"] = b
        for e in ENGS:
            if e == "sp":
                continue
            o = Op(e, None, False)
            o.idx = len(self.ops[e])
            o.deps.add(b)
            self.ops[e].append(o)
        self.last_w.clear()
        self.readers.clear()

    def emit(self):
        nc = self.nc
        es = self.es
        esem = {e: es.enter_context(nc.semaphore(f"s_{e}")) for e in ENGS}
        dsems = {e: [es.enter_context(nc.semaphore(f"d_{e}{i}")) for i in range(self.K)] for e in ("sp", "act", "pool")}
        for e in ENGS:
            n = 0
            for o in self.ops[e]:
                if o.is_dma:
                    o.dsem = dsems[e][n % self.K]
                    o.dval = 16 * (n // self.K + 1)
                    if n >= self.K:
                        o.prewait = (o.dsem, o.dval - 16)
                    n += 1
        fin = Op("sp", None, False)
        fin.idx = len(self.ops["sp"])
        fin.deps = set(self.out_dmas)
        self.ops["sp"].append(fin)
        for f in ENGS:
            waited = {e: -1 for e in ENGS}
            dwaited = set()
            for o in self.ops[f]:
                best = {}
                for d in o.deps:
                    if d.is_dma:
                        if id(d) in dwaited:
                            continue
                        dwaited.add(id(d))
                        o.waits.append(("dma", d))
                    else:
                        if d.eng == "pe" and f == "pe":
                            continue
                        if d.idx <= waited[d.eng]:
                            continue
                        if d.eng not in best or best[d.eng].idx < d.idx:
                            best[d.eng] = d
                for en, d in best.items():
                    waited[en] = d.idx
                    d.needed = True
                    o.waits.append(("eng", d))
        for e in ENGS:
            c = 0
            for o in self.ops[e]:
                if o.needed and not o.is_dma:
                    c += 1
                    o.semval = c
        self.stats = {e: len(self.ops[e]) for e in ENGS}

        def run(ename, eng):
            for o in self.ops[ename]:
                if o.prewait is not None:
                    eng.wait_ge(o.prewait[0], o.prewait[1])
                for kind, d in o.waits:
                    if kind == "dma":
                        eng.wait_ge(d.dsem, d.dval)
                    else:
                        eng.wait_ge(esem[d.eng], d.semval)
                if o.fn is None:
                    continue
                ins = o.fn(eng)
                if o.is_dma:
                    ins.then_inc(o.dsem, 16)
                elif o.needed:
                    ins.then_inc(esem[ename], 1)

        with nc.Block() as block:
            @block.sync
            def _(e):
                run("sp", e)

            @block.tensor
            def _(e):
                run("pe", e)

            @block.vector
            def _(e):
                run("dve", e)

            @block.scalar
            def _(e):
                run("act", e)

            @block.gpsimd
            def _(e):
                run("pool", e)


class Arena:
    def __init__(self, nc, es, words):
        self.t = es.enter_context(nc.sbuf_tensor("arena", [128, words], F32))[:, :]
        self.words = words
        self.off = 0
        self.uid = 0

    def mark(self):
        return self.off

    def release(self, m):
        self.off = m

    def f32(self, n, parts=128):
        assert self.off + n <= self.words, ("arena overflow", self.off, n, self.words)
        v = self.t[0:parts, self.off:self.off + n]
        self.off += n
        return v

    def bf16(self, n, parts=128):
        w = (n + 1) // 2
        v = self.f32(w, parts).bitcast(BF16)
        return v[:, 0:n]

    def key(self, base):
        self.uid += 1
        return f"{base}#{self.uid}"


class Ctx:
    pass


def build(cfg):
    ns, npw = cfg["ns"], cfg["np"]
    phases = cfg.get("phases", ("setup", "s5", "ffn0", "na", "ffn1"))
    cfg = dict(cfg)
    cfg["phases"] = phases
    nc = bass.Bass("TRN2", target_bir_lowering=False)
    C = Ctx()
    C.nc = nc
    C.cfg = cfg

    def din(name, shape, dt=F32):
        return nc.dram_tensor(name, list(shape), dt, kind="ExternalInput").ap()

    def dout(name, shape, dt=F32):
        return nc.dram_tensor(name, list(shape), dt, kind="ExternalOutput").ap()

    def dscr(name, shape, dt=F32):
        kind = "ExternalOutput" if cfg.get("debug") and name in cfg.get("taps", ()) else "Internal"
        return nc.dram_tensor(name, list(shape), dt, kind=kind).ap()

    seqs = []
    for nm, n in (("s", ns), ("p", npw)):
        if n == 0:
            continue
        s = Ctx()
        s.name = nm
        s.n = n
        s.ci = 0 if nm == "s" else 1
        s.x0 = din(f"x_{nm}", [n, D])
        s.y = dout(f"y_{nm}", [n, D])
        s.x1 = dscr(f"x1_{nm}", [n, D])
        s.x2 = dscr(f"x2_{nm}", [n, D])
        s.x3 = dscr(f"x3_{nm}", [n, D])
        s.qT = dscr(f"qT_{nm}", [128, 8, n], BF16)
        s.kT = dscr(f"kT_{nm}", [128, 8, n], BF16)
        s.vS = dscr(f"vS_{nm}", [n, 1040], BF16)
        s.U = dscr(f"U_{nm}", [n // 1024, 128, 64, 128], BF16)
        s.Z = dscr(f"Z_{nm}", [n // 512, 128, 64, 64], BF16)
        seqs.append(s)
    C.seqs = seqs
    C.cs = din("cs", [2, D])
    C.w_ada = din("w_ada", [2, D, 6 * D])
    C.b_ada = din("b_ada", [2, 6 * D])
    C.ln_g = din("ln_g", [2, 2, D])
    C.ln_b = din("ln_b", [2, 2, D])
    C.s5_lam_re = din("s5_lam_re", [2, 64, 64])
    C.s5_lam_im = din("s5_lam_im", [2, 64, 64])
    C.s5_log_dt = din("s5_log_dt", [2, 64])
    C.s5_b_re = din("s5_b_re", [2, 64, 64, 16])
    C.s5_b_im = din("s5_b_im", [2, 64, 64, 16])
    C.s5_c_re = din("s5_c_re", [2, 64, 16, 64])
    C.s5_c_im = din("s5_c_im", [2, 64, 16, 64])
    C.s5_d = din("s5_d", [64, 16])
    C.s5_w_glu = din("s5_w_glu", [D, 2 * D])
    C.s5_b_glu = din("s5_b_glu", [2 * D])
    C.na_w_qkv = din("na_w_qkv", [D, 3 * D])
    C.na_b_qkv = din("na_b_qkv", [3 * D])
    C.na_tab = din("na_tab", [128, 16, 1088])
    C.s5_masks = din("s5_masks", [2, 128, 128])
    C.na_w_o = din("na_w_o", [D, D])
    C.na_b_o = din("na_b_o", [D])
    C.ffn_w_in = din("ffn_w_in", [2, D, 2 * DFF])
    C.ffn_b_in = din("ffn_b_in", [2, 2 * DFF])
    C.ffn_conv_w = din("ffn_conv_w", [2, 3, DFF])
    C.ffn_conv_b = din("ffn_conv_b", [2, DFF])
    C.ffn_w_out = din("ffn_w_out", [2, DFF, D])
    C.ffn_b_out = din("ffn_b_out", [2, D])
    C.modd = dscr("modd", [2, 2, 6 * D])
    C.wglu_s = dscr("wglu_s", [128, 8, 2 * D], BF16)
    C.wqkv_s = dscr("wqkv_s", [128, 8, 3 * D], BF16)
    C.wo_s = dscr("wo_s", [64, 16, D], BF16)
    C.win_s = dscr("win_s", [2, 11, 128, 8, 512], BF16)
    C.wout_s = dscr("wout_s", [2, 128, 22, D], BF16)
    C.bo2 = dscr("bo2", [D])
    C.s5w = dscr("s5w", [64, 128, 1024], BF16)
    C.s5t = dscr("s5t", [64, 128, 2, 128])
    C.s5tp = dscr("s5tp", [2, 64, 128, 128])

    with ExitStack() as es:
        P = Prog(nc, es, n_dma_sems=cfg.get('ksem', 14))
        P.paranoid = bool(cfg.get('paranoid'))
        C.P = P
        A = Arena(nc, es, cfg.get("arena_words", 53184))
        C.A = A
        C.psum = [es.enter_context(nc.psum_tensor(f"ps{i}", [128, 512], F32))[:, :] for i in range(8)]
        C.ident = A.f32(128)
        C.identb = A.bf16(128)
        C.eps = A.f32(1)
        P.op("pool", lambda e: e.memset(C.ident, 0.0), writes=["ident"])
        P.op("pool", lambda e: e.affine_select(out=C.ident, in_=C.ident, pattern=[[-1, 128]], compare_op=ALU.not_equal,
                                               fill=1.0, base=0, channel_multiplier=1), reads=["ident"], writes=["ident"])
        P.op("pool", lambda e: e.tensor_copy(out=C.identb, in_=C.ident), reads=["ident"], writes=["identb"])
        P.op("pool", lambda e: e.memset(C.eps, LN_EPS), writes=["eps"])
        P.barrier()
        if "setup" in phases:
            setup_phase(C)
        if "s5" in phases:
            if len(phases) == 2:
                for s_ in seqs:
                    s_.x1 = s_.y
            s5_phase(C)
        if "ffn0" in phases:
            ffn_phase(C, 0, [(s, (s.x1 if "s5" in phases else s.x0), (s.x2 if ("na" in phases or "ffn1" in phases) else s.y)) for s in seqs])
        if "na" in phases:
            na_phase(C, [(s, (s.x2 if "ffn0" in phases else s.x0), (s.x3 if "ffn1" in phases else s.y)) for s in seqs])
        if "ffn1" in phases:
            ffn_phase(C, 1, [(s, (s.x3 if "na" in phases else s.x0), s.y) for s in seqs])
        P.emit()
        C.stats = P.stats
    return nc, C


def cut(C, label):
    if C.cfg.get("cut") == label:
        C.P.barrier()
        C.P.dead = True


def load_pp(C, src, dst, key):
    P, A = C.P, C.A
    R, W = src.shape[0], src.shape[1]
    stg = A.f32(W, parts=R)
    sk = A.key("ppstg")
    P.dma(stg, src, writes=[sk])
    ps = C.psum[7][0:W, 0:R]
    P.op("pe", lambda e: e.transpose(ps, stg, C.ident[0:R, 0:R]), reads=[sk, "ident"], writes=["bank7"])
    P.op("dve", lambda e: e.tensor_copy(out=dst, in_=ps), reads=["bank7"], writes=[key])


def bc_load(C, dst, src_row, key, q="sp"):
    n = dst.shape[1]
    parts = dst.shape[0]
    C.P.dma(dst, src_row.rearrange("(o n) -> o n", o=1).partition_broadcast(parts), writes=[key], q=q)


def setup_phase(C):
    P, A, nc = C.P, C.A, C.nc
    m0 = A.mark()
    cT = A.f32(16).rearrange("p (k b) -> p k b", b=2)
    crow = A.f32(D, parts=2)
    P.dma(crow, C.cs, writes=["crow"])
    P.op("act", lambda e: e.activation(out=crow, in_=crow, func=AF.Silu), reads=["crow"], writes=["crow"])
    pct = C.psum[2][:, 0:16]
    for k in range(8):
        P.op("pe", lambda e, k=k: e.transpose(pct[:, 2 * k:2 * k + 2], crow[0:2, k * 128:(k + 1) * 128], C.ident[0:2, 0:2]),
             reads=["crow", "ident"], writes=["pct"])
    P.op("dve", lambda e: e.tensor_copy(out=cT, in_=pct.rearrange("p (k b) -> p k b", b=2)), reads=["pct"], writes=["cT"])
    wa = [A.f32(8 * 512).rearrange("p (k c) -> p k c", k=8) for _ in range(2)]
    bad = A.f32(6 * D, parts=2)
    modsb = A.f32(6 * D, parts=2)
    it = 0
    for l in range(2):
        P.dma(bad, C.b_ada[l].rearrange("(o n) -> o n", o=1).partition_broadcast(2), writes=["bad"])
        for cb in range(12):
            w = wa[it % 2]
            wk = f"wa{it % 2}"
            P.dma(w, C.w_ada[l][:, cb * 512:(cb + 1) * 512].rearrange("(k p) c -> p k c", p=128), writes=[wk],
                  q=("sp" if it % 2 == 0 else "act"))
            ps = C.psum[it % 2][0:2, :]
            pk = f"ps{it % 2}"
            for k in range(8):
                P.op("pe", lambda e, ps=ps, w=w, k=k: e.matmul(ps, lhsT=cT[:, k, :], rhs=w[:, k, :], start=(k == 0), stop=(k == 7)),
                     reads=[wk, "cT"], writes=[pk])
            P.op("dve", lambda e, ps=ps, cb=cb: e.tensor_tensor(out=modsb[:, cb * 512:(cb + 1) * 512], in0=ps,
                                                                in1=bad[:, cb * 512:(cb + 1) * 512], op=ALU.add),
                 reads=[pk, "bad"], writes=["modsb"])
            it += 1
        P.dma(C.modd[l], modsb, reads=["modsb"], writes=["modd"])
    A.release(m0)
    P.barrier()
    m0c = A.mark()
    CW = 5632
    stg = [A.f32(CW)] * 2
    stb = [A.bf16(CW)] * 2
    cnt = [0]
    cast_eng = ("act", "pool", "dve")

    def cast_rows(src, ncols, stores):
        i = cnt[0]
        cnt[0] += 1
        s_f, s_b = stg[i % 2], stb[i % 2]
        kf, kb = "stg0", "stb0"
        P.dma(s_f[:, 0:ncols], src, writes=[kf], q=("sp" if i % 2 == 0 else "act"))
        ce = cast_eng[i % 3]
        if ce == "act":
            P.op("act", lambda e: e.copy(out=s_b[:, 0:ncols], in_=s_f[:, 0:ncols]), reads=[kf], writes=[kb])
        else:
            P.op(ce, lambda e: e.tensor_copy(out=s_b[:, 0:ncols], in_=s_f[:, 0:ncols]), reads=[kf], writes=[kb])
        for dst, (c0, c1), (p0, p1) in stores:
            P.dma(dst, s_b[p0:p1, c0:c1], reads=[kb], writes=["wscr"], q="sp")

    ph = C.cfg.get("phases")

    def casts():
        if "s5" in ph:
            for k in range(8):
                cast_rows(C.s5_w_glu[k * 128:(k + 1) * 128, :], 2048, [(C.wglu_s[:, k, :], (0, 2048), (0, 128))])
                yield
        if "na" in ph:
            for k in range(8):
                cast_rows(C.na_w_qkv[k * 128:(k + 1) * 128, :], 3072, [(C.wqkv_s[:, k, :], (0, 3072), (0, 128))])
                yield
            for k in range(8):
                cast_rows(C.na_w_o[k * 128:(k + 1) * 128, :], 1024,
                          [(C.wo_s[:, 2 * k, :], (0, 1024), (0, 64)), (C.wo_s[:, 2 * k + 1, :], (0, 1024), (64, 128))])
                yield
        for l in range(2):
            if f"ffn{l}" not in ph and f"ffn{l}_castonly" not in ph:
                continue
            for k in range(8):
                dst_u = C.win_s[l][:, :, k, 0:256].rearrange("m p c -> p m c")
                dst_g = C.win_s[l][:, :, k, 256:512].rearrange("m p c -> p m c")
                i = cnt[0]
                cast_rows(C.ffn_w_in[l][k * 128:(k + 1) * 128, :], 5632, [])
                s_b = stb[0]
                kb = "stb0"
                P.dma(dst_u, s_b[:, 0:2816].rearrange("p (m c) -> p m c", c=256), reads=[kb], writes=["wscr"], q="sp")
                P.dma(dst_g, s_b[:, 2816:5632].rearrange("p (m c) -> p m c", c=256), reads=[kb], writes=["wscr"], q="sp")
                yield
            for k in range(22):
                cast_rows(C.ffn_w_out[l][k * 128:(k + 1) * 128, :], 1024, [(C.wout_s[l][:, k, :], (0, 1024), (0, 128))])
                yield

    bg = casts()
    C.bg = bg
    if "s5" in ph:
        s5_setup(C)
    for _ in bg:
        pass
    A.release(m0c)
    P.barrier()


def bg_step(C, n=1):
    bg = getattr(C, "bg", None)
    if bg is None:
        return
    for _ in range(n):
        next(bg, None)


def ffn_phase(C, l, seq_io):
    P, A, nc = C.P, C.A, C.nc
    m0 = A.mark()
    PS = C.psum
    wout = A.bf16(22 * D).rearrange("p (k c) -> p k c", k=22)
    P.dma(wout, C.wout_s[l], writes=["wout"], q="act")
    lnG = A.f32(D)
    lnB = A.f32(D)
    bc_load(C, lnG, C.ln_g[l, 1], "lnG")
    bc_load(C, lnB, C.ln_b[l, 1], "lnB")
    bout = A.f32(D)
    bc_load(C, bout, C.ffn_b_out[l], "bout")
    binp = A.f32(44)
    cw = A.f32(66).rearrange("p (j m) -> p j m", j=3)
    cb = A.f32(22)
    mst = A.mark()
    load_pp(C, C.ffn_b_in[l].rearrange("(m p) -> m p", p=128), binp, "binp")
    load_pp(C, C.ffn_conv_w[l].rearrange("j (m p) -> (j m) p", p=128), cw.rearrange("p j m -> p (j m)"), "cw")
    load_pp(C, C.ffn_conv_b[l].rearrange("(m p) -> m p", p=128), cb, "cb")
    sc1 = A.f32(D)
    sh = A.f32(D)
    gf = A.f32(D)
    gfb = A.f32(D)
    xt_ring = [A.f32(4 * D).rearrange("p (b c) -> p b c", b=4) for _ in range(2)]
    xh_r = [A.f32(D, parts=2) for _ in range(2)]
    tmp = A.f32(D)
    hb_r = [A.bf16(4 * D).rearrange("p (b c) -> p b c", b=4) for _ in range(2)]
    hhb_r = [A.bf16(D, parts=2) for _ in range(2)]
    hfm_r = [A.bf16(8 * 516).rearrange("p (k t) -> p k t", k=8) for _ in range(2)]
    wring = [A.bf16(8 * 512).rearrange("p (k c) -> p k c", k=8) for _ in range(2)]
    hid = A.bf16(22 * 512).rearrange("p (m t) -> p m t", m=22)
    u_sb = [A.f32(514) for _ in range(2)]
    vv = [A.f32(512) for _ in range(2)]
    g1 = [A.f32(512) for _ in range(2)]
    rt = vv
    st = A.f32(12).rearrange("p (a b) -> p a b", a=2)
    mv = A.f32(2)
    rs = A.f32(1)
    nmr = A.f32(1)
    if C.cfg.get("verbose"):
        print("ffn arena high-water", A.off, "of", A.words)
    pT = PS[0][:, 0:256].bitcast(BF16)
    pu = [PS[1], PS[2]]
    pg = [PS[3], PS[4]]
    phl = PS[5][:, 0:44]
    pTh = PS[5][:, 64:72].bitcast(BF16)
    po = [PS[6], PS[7]]
    gi = [0]
    cut(C, "A")
    for (sq, xin, xout) in seq_io:
        n = sq.n
        ci = sq.ci
        bc_load(C, sh, C.modd[l, ci, 3 * D:4 * D], "sh")
        bc_load(C, sc1, C.modd[l, ci, 4 * D:5 * D], "sc1")
        bc_load(C, gf, C.modd[l, ci, 5 * D:6 * D], "gf")
        P.op("pool", lambda e: e.tensor_scalar(out=sc1, in0=sc1, scalar1=1.0, scalar2=None, op0=ALU.add), reads=["sc1"], writes=["sc1"])
        P.op("pool", lambda e: e.tensor_tensor(out=gfb, in0=gf, in1=bout, op=ALU.mult), reads=["gf", "bout"], writes=["gfb"])
        nt = n // 512
        it0 = gi[0]
        gi[0] += nt

        def prologue(T):
            t0 = T * 512
            it = it0 + T
            i2 = it % 2
            xt, xh, hb, hhb, hfm = xt_ring[i2], xh_r[i2], hb_r[i2], hhb_r[i2], hfm_r[i2]
            xk = f"xt{i2}"
            has_l = t0 > 0
            has_r = t0 + 512 < n
            P.dma(xt, xin[t0:t0 + 512, :].rearrange("(b p) c -> p b c", p=128), writes=[xk], q="sp")
            if (has_l or has_r) and not (has_l and has_r):
                P.op("pool", lambda e: e.memset(xh, 0.0), writes=[f"xh0{i2}", f"xh1{i2}"])
            if has_l:
                P.dma(xh[0:1, :], xin[t0 - 1:t0, :], writes=[f"xh0{i2}"], q="sp")
            if has_r:
                P.dma(xh[1:2, :], xin[t0 + 512:t0 + 513, :], writes=[f"xh1{i2}"], q="sp")
            for b in range(4):
                P.op("dve", lambda e, b=b: e.tensor_tensor(out=tmp, in0=xt[:, b, :], in1=sc1, op=ALU.mult), reads=[xk, "sc1"], writes=["tmp"])
                P.op("dve", lambda e, b=b: e.tensor_tensor(out=hb[:, b, :], in0=tmp, in1=sh, op=ALU.add), reads=["tmp", "sh"], writes=[f"hb{i2}{b}"])
            if has_l or has_r:
                P.op("dve", lambda e: e.tensor_tensor(out=tmp[0:2, :], in0=xh, in1=sc1[0:2, :], op=ALU.mult), reads=[f"xh0{i2}", f"xh1{i2}", "sc1"], writes=["tmp"])
                P.op("dve", lambda e: e.tensor_tensor(out=hhb, in0=tmp[0:2, :], in1=sh[0:2, :], op=ALU.add), reads=["tmp", "sh"], writes=[f"hhb{i2}"])
            for b in range(4):
                P.op("dve", lambda e, b=b: e.scalar_tensor_tensor(out=xt[:, b, :], in0=xt[:, b, :], scalar=ALPHA, in1=gfb, op0=ALU.mult, op1=ALU.add),
                     reads=[xk, f"hb{i2}{b}", "gfb"], writes=[xk])

        def prologue_b(T):
            t0 = T * 512
            it = it0 + T
            i2 = it % 2
            xt, xh, hb, hhb, hfm = xt_ring[i2], xh_r[i2], hb_r[i2], hhb_r[i2], hfm_r[i2]
            has_l = t0 > 0
            has_r = t0 + 512 < n
            for k in range(8):
                for b in range(4):
                    P.op("pe", lambda e, b=b, k=k: e.transpose(pT[:, b * 128:(b + 1) * 128], hb[:, b, k * 128:(k + 1) * 128], C.identb),
                         reads=[f"hb{i2}{b}", "identb"], writes=["bank0"])
                if k % 2 == 0:
                    P.op("act", lambda e, k=k: e.copy(out=hfm[:, k, 2:514], in_=pT), reads=["bank0"], writes=[f"hfm{i2}{k}"])
                else:
                    P.op("dve", lambda e, k=k: e.tensor_copy(out=hfm[:, k, 2:514], in_=pT), reads=["bank0"], writes=[f"hfm{i2}{k}"])
            if has_l or has_r:
                for k in range(8):
                    P.op("pe", lambda e, k=k: e.transpose(pTh[:, 2 * k:2 * k + 2], hhb[0:2, k * 128:(k + 1) * 128], C.identb[0:2, 0:2]),
                         reads=[f"hhb{i2}", "identb"], writes=["bank5"])
                P.op("dve", lambda e: e.tensor_copy(out=hfm[:, :, 1], in_=pTh[:, 0:16:2]), reads=["bank5"], writes=[f"hfmh{i2}"])
                P.op("dve", lambda e: e.tensor_copy(out=hfm[:, :, 514], in_=pTh[:, 1:16:2]), reads=["bank5"], writes=[f"hfmh{i2}"])

        prologue(0)
        prologue_b(0)
        preloaded = set()
        for T in range(nt):
            t0 = T * 512
            it = it0 + T
            i2 = it % 2
            xt, hfm = xt_ring[i2], hfm_r[i2]
            xk = f"xt{i2}"
            has_l = t0 > 0
            has_r = t0 + 512 < n
            cut(C, "D")
            hf_keys = [f"hfm{i2}{k}" for k in range(8)]
            lvl = C.cfg.get("ffn_level", 3)
            if lvl == 1:
                P.op("dve", lambda e: e.tensor_copy(out=xt[:, 0, 0:514], in_=hfm[:, 0, 1:515]), reads=hf_keys + [f"hfmh{i2}", xk], writes=[xk])
                P.dma(xout[t0:t0 + 512, :].rearrange("(b p) c -> p b c", p=128), xt, reads=[xk], writes=["xout"], q="sp", is_output=(xout is sq.y))
                continue
            for m2 in range(11):
                wi = (it * 11 + m2) % 2
                w = wring[wi]
                wk = f"wr{wi}"
                if (it, m2) not in preloaded:
                    P.dma(w, C.win_s[l, m2], writes=[wk], q="sp")
                if m2 == 1 and T + 1 < nt:
                    prologue(T + 1)
                if m2 == 7 and T + 1 < nt:
                    prologue_b(T + 1)
                for mi in range(2):
                    m = m2 * 2 + mi
                    j = m % 2
                    puk, pgk = f"bank{1 + j}", f"bank{3 + j}"
                    for k in range(8):
                        P.op("pe", lambda e, j=j, k=k, mi=mi, w=w: e.matmul(pu[j], lhsT=w[:, k, mi * 128:(mi + 1) * 128], rhs=hfm[:, k, 2:514],
                                                                            start=(k == 0), stop=(k == 7)),
                             reads=[wk, hf_keys[k]], writes=[puk])
                    for k in range(8):
                        P.op("pe", lambda e, j=j, k=k, mi=mi, w=w: e.matmul(pg[j], lhsT=w[:, k, 256 + mi * 128:256 + (mi + 1) * 128], rhs=hfm[:, k, 2:514],
                                                                            start=(k == 0), stop=(k == 7)),
                             reads=[wk, hf_keys[k]], writes=[pgk])
                    if has_l or has_r:
                        for k in range(8):
                            P.op("pe", lambda e, k=k, mi=mi, m=m, w=w: e.matmul(phl[:, 2 * m:2 * m + 2], lhsT=w[:, k, mi * 128:(mi + 1) * 128],
                                                                                rhs=hfm[:, k, 1:515:513], start=(k == 0), stop=(k == 7)),
                                 reads=[wk, f"hfmh{i2}"], writes=["bank5"])
                    u = u_sb[j]
                    uk = f"u{j}"
                    P.op("act", lambda e, u=u, j=j, m=m: e.activation(out=u[:, 1:513], in_=pu[j], func=AF.Identity, bias=binp[:, m:m + 1], scale=1.0),
                         reads=[puk, "binp"], writes=[uk + "m"])
                    if has_l:
                        P.op("act", lambda e, u=u, m=m: e.activation(out=u[:, 0:1], in_=phl[:, 2 * m:2 * m + 1], func=AF.Identity, bias=binp[:, m:m + 1], scale=1.0),
                             reads=["bank5", "binp"], writes=[uk + "l"])
                    else:
                        P.op("pool", lambda e, u=u: e.memset(u[:, 0:1], 0.0), writes=[uk + "l"])
                    if has_r:
                        P.op("act", lambda e, u=u, m=m: e.activation(out=u[:, 513:514], in_=phl[:, 2 * m + 1:2 * m + 2], func=AF.Identity, bias=binp[:, m:m + 1], scale=1.0),
                             reads=["bank5", "binp"], writes=[uk + "r"])
                    else:
                        P.op("pool", lambda e, u=u: e.memset(u[:, 513:514], 0.0), writes=[uk + "r"])
                    v = vv[j]
                    vk = f"v{j}"
                    P.op("dve", lambda e, u=u, v=v, m=m: e.tensor_scalar(out=v, in0=u[:, 1:513], scalar1=cw[:, 1, m:m + 1], scalar2=cb[:, m:m + 1],
                                                                           op0=ALU.mult, op1=ALU.add),
                         reads=[uk + "m", "cw", "cb"], writes=[vk])
                    P.op("dve", lambda e, u=u, v=v, m=m: e.scalar_tensor_tensor(out=v, in0=u[:, 0:512], scalar=cw[:, 0, m:m + 1], in1=v, op0=ALU.mult, op1=ALU.add),
                         reads=[uk + "m", uk + "l", vk, "cw"], writes=[vk])
                    P.op("dve", lambda e, u=u, v=v, m=m: e.scalar_tensor_tensor(out=v, in0=u[:, 2:514], scalar=cw[:, 2, m:m + 1], in1=v, op0=ALU.mult, op1=ALU.add),
                         reads=[uk + "m", uk + "r", vk, "cw"], writes=[vk])
                    gg = g1[j]
                    gk = f"g1{j}"
                    P.op("act", lambda e, v=v, gg=gg: e.activation(out=gg, in_=v, func=AF.Gelu_apprx_tanh), reads=[vk], writes=[gk])
                    P.op("dve", lambda e, gg=gg, j=j, m=m: e.scalar_tensor_tensor(out=hid[:, m, :], in0=pg[j], scalar=binp[:, 22 + m:23 + m], in1=gg,
                                                                                   op0=ALU.add, op1=ALU.mult),
                         reads=[pgk, gk, "binp"], writes=[f"hid{m}"])
            hid_keys = [f"hid{m}" for m in range(22)]
            if T + 1 < nt:
                wi_n = ((it + 1) * 11) % 2
                P.dma(wring[wi_n], C.win_s[l, 0], writes=[f"wr{wi_n}"], q="sp")
                preloaded.add((it + 1, 0))
            if lvl == 2:
                P.op("dve", lambda e: e.tensor_copy(out=xt[:, 0, 0:512], in_=hid[:, 5, :]), reads=hid_keys + [xk], writes=[xk])
                P.dma(xout[t0:t0 + 512, :].rearrange("(b p) c -> p b c", p=128), xt, reads=[xk], writes=["xout"], q="sp", is_output=(xout is sq.y))
                continue
            for b in range(4):
                for half in range(2):
                    j = (b * 2 + half) % 2
                    pok = f"bank{6 + j}"
                    for k in range(22):
                        P.op("pe", lambda e, j=j, k=k, b=b, half=half: e.matmul(po[j], lhsT=hid[:, k, b * 128:(b + 1) * 128],
                                                                                  rhs=wout[:, k, half * 512:(half + 1) * 512], start=(k == 0), stop=(k == 21)),
                             reads=[hid_keys[k], "wout"], writes=[pok])
                    r = rt[j]
                    rk = f"v{j}"
                    xs = xt[:, b, half * 512:(half + 1) * 512]
                    P.op("dve", lambda e, j=j, r=r, half=half: e.tensor_tensor(out=r, in0=po[j], in1=gf[:, half * 512:(half + 1) * 512], op=ALU.mult),
                         reads=[pok, "gf"], writes=[rk])
                    P.op("dve", lambda e, r=r, xs=xs: e.tensor_tensor(out=xs, in0=xs, in1=r, op=ALU.add), reads=[rk, xk], writes=[xk])
                layer_norm_block(C, xt[:, b, :], xk, lnG, lnB, st, mv, rs, nmr)
            P.dma(xout[t0:t0 + 512, :].rearrange("(b p) c -> p b c", p=128), xt, reads=[xk], writes=["xout"], q="sp",
                  is_output=(xout is sq.y))
    A.release(m0)
    P.barrier()


def layer_norm_block(C, xb, xk, lnG, lnB, st, mv, rs, nmr):
    P = C.P
    P.op("dve", lambda e: e.bn_stats(out=st[:, 0, :], in_=xb[:, 0:512]), reads=[xk], writes=["st"])
    P.op("dve", lambda e: e.bn_stats(out=st[:, 1, :], in_=xb[:, 512:1024]), reads=[xk, "st"], writes=["st"])
    P.op("dve", lambda e: e.bn_aggr(out=mv, in_=st), reads=["st"], writes=["mv"])
    P.op("act", lambda e: e.activation(out=rs, in_=mv[:, 1:2], func=AF.Sqrt, bias=C.eps, scale=1.0), reads=["mv", "eps"], writes=["rs"])
    P.op("dve", lambda e: e.reciprocal(out=rs, in_=rs), reads=["rs"], writes=["rs"])
    P.op("dve", lambda e: e.tensor_scalar(out=nmr, in0=mv[:, 0:1], scalar1=rs, scalar2=-1.0, op0=ALU.mult, op1=ALU.mult),
         reads=["mv", "rs"], writes=["nmr"])
    P.op("act", lambda e: e.activation(out=xb, in_=xb, func=AF.Identity, bias=nmr, scale=rs), reads=[xk, "rs", "nmr"], writes=[xk])
    P.op("dve", lambda e: e.tensor_tensor(out=xb, in0=xb, in1=lnG, op=ALU.mult), reads=[xk, "lnG"], writes=[xk])
    P.op("dve", lambda e: e.tensor_tensor(out=xb, in0=xb, in1=lnB, op=ALU.add), reads=[xk, "lnB"], writes=[xk])


TWO_PI = 2.0 * math.pi
MAGIC = 12582912.0
CW1 = 6.28125
CW2 = TWO_PI - 6.28125


def sincos(C, ang, n, sin_out, cos_out, tag):
    P, A = C.P, C.A
    k = A.f32(n)
    r = A.f32(n)
    ka, kk, kr = tag + "a", tag + "k", tag + "r"
    for (shift, outp) in ((0.0, sin_out), (0.5 * math.pi, cos_out)):
        if outp is None:
            continue
        P.op("dve", lambda e, shift=shift: e.tensor_scalar(out=r, in0=ang, scalar1=shift, scalar2=None, op0=ALU.add), reads=[ka], writes=[kr])
        P.op("dve", lambda e: e.tensor_scalar(out=k, in0=r, scalar1=1.0 / TWO_PI, scalar2=MAGIC, op0=ALU.mult, op1=ALU.add), reads=[kr], writes=[kk])
        P.op("dve", lambda e: e.tensor_scalar(out=k, in0=k, scalar1=MAGIC, scalar2=None, op0=ALU.subtract), reads=[kk], writes=[kk])
        P.op("dve", lambda e: e.scalar_tensor_tensor(out=r, in0=k, scalar=-CW1, in1=r, op0=ALU.mult, op1=ALU.add), reads=[kk, kr], writes=[kr])
        P.op("dve", lambda e: e.scalar_tensor_tensor(out=r, in0=k, scalar=-CW2, in1=r, op0=ALU.mult, op1=ALU.add), reads=[kk, kr], writes=[kr])
        P.op("dve", lambda e: e.tensor_scalar(out=r, in0=r, scalar1=math.pi, scalar2=-math.pi, op0=ALU.min, op1=ALU.max), reads=[kr], writes=[kr])
        P.op("act", lambda e, outp=outp: e.activation(out=outp, in_=r, func=AF.Sin), reads=[kr], writes=[tag + "o"])


def dup_transpose(C, src, dst, key):
    P, A = C.P, C.A
    stg = A.f32(128)
    sk = A.key("dtstg")
    P.dma(stg[:, 0:64], src, writes=[sk])
    P.dma(stg[:, 64:128], src, writes=[sk + "b"])
    ps = C.psum[7][:, 0:128]
    P.op("pe", lambda e: e.transpose(ps, stg, C.ident), reads=[sk, sk + "b"], writes=["bank7"])
    P.op("dve", lambda e: e.tensor_copy(out=dst, in_=ps), reads=["bank7"], writes=[key])


def s5_setup(C):
    P, A, nc = C.P, C.A, C.nc
    PS = C.psum
    m0 = A.mark()
    NDG = 128
    LR = A.f32(NDG)
    LI = A.f32(NDG)
    DT = A.f32(NDG)
    dup_transpose(C, C.s5_lam_re.rearrange("d g p -> (d g) p"), LR, "LR")
    dup_transpose(C, C.s5_lam_im.rearrange("d g p -> (d g) p"), LI, "LI")
    ldt = A.f32(1)
    P.dma(ldt, C.s5_log_dt.rearrange("d (g o) -> (d g) o", o=1), writes=["ldt"])
    ldtb = A.f32(128)
    P.op("dve", lambda e: e.tensor_copy(out=ldtb, in_=ldt.to_broadcast([128, 128])), reads=["ldt"], writes=["ldtb"])
    P.op("pe", lambda e: e.transpose(PS[7][:, 0:128], ldtb, C.ident), reads=["ldtb"], writes=["bank7"])
    P.op("act", lambda e: e.activation(out=DT, in_=PS[7][:, 0:128], func=AF.Exp), reads=["bank7"], writes=["DT"])
    ZR = A.f32(NDG)
    ZI = A.f32(NDG)
    P.op("dve", lambda e: e.tensor_tensor(out=ZR, in0=LR, in1=DT, op=ALU.mult), reads=["LR", "DT"], writes=["ZR"])
    P.op("dve", lambda e: e.tensor_tensor(out=ZI, in0=LI, in1=DT, op=ALU.mult), reads=["LI", "DT"], writes=["ZI"])
    NP_ = 17
    PWr = A.f32(NP_ * NDG).rearrange("p (i g) -> p i g", i=NP_)
    PWi = A.f32(NP_ * NDG).rearrange("p (i g) -> p i g", i=NP_)
    Fr = A.f32(NDG)
    Fi = A.f32(NDG)
    RHO = A.f32(NDG)
    rho16 = A.f32(NDG)
    mS1 = A.mark()
    nvec = A.f32(NP_)
    for i in range(NP_):
        P.op("pool", lambda e, i=i: e.memset(nvec[:, i:i + 1], float(i - 8)), writes=["nvec"])
    ang = A.f32(NP_ * NDG).rearrange("p (i g) -> p i g", i=NP_)
    mag = A.f32(NP_ * NDG).rearrange("p (i g) -> p i g", i=NP_)
    nb3 = nvec.unsqueeze(2).to_broadcast([128, NP_, NDG])
    P.op("dve", lambda e: e.tensor_tensor(out=ang, in0=ZI.unsqueeze(1).to_broadcast([128, NP_, NDG]), in1=nb3, op=ALU.mult), reads=["ZI", "nvec"], writes=["anga"])
    P.op("dve", lambda e: e.tensor_tensor(out=mag, in0=ZR.unsqueeze(1).to_broadcast([128, NP_, NDG]), in1=nb3, op=ALU.mult), reads=["ZR", "nvec"], writes=["mag"])
    P.op("act", lambda e: e.activation(out=mag, in_=mag, func=AF.Exp), reads=["mag"], writes=["mag"])
    sn = A.f32(NP_ * NDG).rearrange("p (i g) -> p i g", i=NP_)
    cs = A.f32(NP_ * NDG).rearrange("p (i g) -> p i g", i=NP_)
    f2 = lambda t: t.rearrange("p i g -> p (i g)")
    m1 = A.mark()
    sincos(C, f2(ang), NP_ * NDG, f2(sn), f2(cs), "ang")
    P.op("dve", lambda e: e.tensor_tensor(out=PWr, in0=mag, in1=cs, op=ALU.mult), reads=["mag", "ango"], writes=["PWr"])
    P.op("dve", lambda e: e.tensor_tensor(out=PWi, in0=mag, in1=sn, op=ALU.mult), reads=["mag", "ango"], writes=["PWi"])
    den = A.f32(NDG)
    t0_ = A.f32(NDG)
    a_re, a_im = PWr[:, 9, :], PWi[:, 9, :]
    P.op("dve", lambda e: e.tensor_tensor(out=den, in0=LR, in1=LR, op=ALU.mult), reads=["LR"], writes=["den"])
    P.op("dve", lambda e: e.tensor_tensor(out=t0_, in0=LI, in1=LI, op=ALU.mult), reads=["LI"], writes=["t0_"])
    P.op("dve", lambda e: e.tensor_tensor(out=den, in0=den, in1=t0_, op=ALU.add), reads=["den", "t0_"], writes=["den"])
    P.op("dve", lambda e: e.reciprocal(out=den, in_=den), reads=["den"], writes=["den"])
    nr = A.f32(NDG)
    P.op("dve", lambda e: e.tensor_scalar(out=nr, in0=a_re, scalar1=-1.0, scalar2=None, op0=ALU.add), reads=["PWr"], writes=["nr"])
    P.op("dve", lambda e: e.tensor_tensor(out=Fr, in0=nr, in1=LR, op=ALU.mult), reads=["nr", "LR"], writes=["Fr"])
    P.op("dve", lambda e: e.tensor_tensor(out=t0_, in0=a_im, in1=LI, op=ALU.mult), reads=["PWi", "LI", "den"], writes=["t0_"])
    P.op("dve", lambda e: e.tensor_tensor(out=Fr, in0=Fr, in1=t0_, op=ALU.add), reads=["Fr", "t0_"], writes=["Fr"])
    P.op("dve", lambda e: e.tensor_tensor(out=Fr, in0=Fr, in1=den, op=ALU.mult), reads=["Fr", "den"], writes=["Fr"])
    P.op("dve", lambda e: e.tensor_tensor(out=Fi, in0=a_im, in1=LR, op=ALU.mult), reads=["PWi", "LR"], writes=["Fi"])
    P.op("dve", lambda e: e.tensor_tensor(out=t0_, in0=nr, in1=LI, op=ALU.mult), reads=["nr", "LI", "Fr"], writes=["t0_"])
    P.op("dve", lambda e: e.tensor_tensor(out=Fi, in0=Fi, in1=t0_, op=ALU.subtract), reads=["Fi", "t0_"], writes=["Fi"])
    P.op("dve", lambda e: e.tensor_tensor(out=Fi, in0=Fi, in1=den, op=ALU.mult), reads=["Fi", "den"], writes=["Fi"])
    P.op("dve", lambda e: e.tensor_copy(out=RHO, in_=mag[:, 16, :]), reads=["mag"], writes=["RHO"])
    P.op("act", lambda e: e.activation(out=rho16, in_=ZR, func=AF.Exp, scale=128.0), reads=["ZR"], writes=["rho16"])
    P.barrier()
    A.release(mS1)
    ZI2 = A.f32(NDG)
    P.op("dve", lambda e: e.tensor_tensor(out=ZI2, in0=LI, in1=DT, op=ALU.mult), writes=["ZI2"])
    mS2 = A.mark()
    jv = A.f32(48)
    for j in range(16):
        P.op("pool", lambda e, j=j: e.memset(jv[:, j:j + 1], 8.0 * j), writes=["jv"])
    for s_ in range(32):
        P.op("pool", lambda e, s_=s_: e.memset(jv[:, 16 + s_:17 + s_], 128.0 * s_), writes=["jv"])
    TT = A.f32(64 * 128).rearrange("p (g c) -> p g c", c=128)
    a48 = A.f32(64 * 48).rearrange("p (g j) -> p g j", j=48)
    s48 = A.f32(64 * 48).rearrange("p (g j) -> p g j", j=48)
    c48 = A.f32(64 * 48).rearrange("p (g j) -> p g j", j=48)
    g2 = lambda t: t.rearrange("p g j -> p (g j)")
    mS3 = A.mark()
    for d in range(2):
        ds_ = slice(d * 64, (d + 1) * 64)
        P.op("pool", lambda e: e.memset(TT, 0.0), reads=["TT"], writes=["TT"])
        P.op("dve", lambda e, ds_=ds_: e.tensor_tensor(out=a48, in0=ZI2[:, ds_].unsqueeze(2).to_broadcast([128, 64, 48]), in1=jv.unsqueeze(1).to_broadcast([128, 64, 48]), op=ALU.mult),
             reads=["ZI2", "jv", "a48o"], writes=["a48a"])
        sincos(C, g2(a48), 64 * 48, g2(s48), g2(c48), "a48")
        A.release(mS3)
        if d == 0:
            P.op("dve", lambda e: e.tensor_copy(out=TT[:, :, 0:16], in_=c48[:, :, 0:16]), reads=["a48o", "TT"], writes=["TT"])
            P.op("dve", lambda e: e.tensor_copy(out=TT[:, :, 16:32], in_=s48[:, :, 0:16]), reads=["a48o", "TT"], writes=["TT"])
        else:
            P.op("dve", lambda e: e.tensor_copy(out=TT[:, :, 0:16], in_=c48[:, :, 15::-1]), reads=["a48o", "TT"], writes=["TT"])
            P.op("dve", lambda e: e.tensor_copy(out=TT[:, :, 16:32], in_=s48[:, :, 15::-1]), reads=["a48o", "TT"], writes=["TT"])
        P.op("dve", lambda e: e.tensor_copy(out=TT[:, :, 48:80], in_=c48[:, :, 16:48]), reads=["a48o", "TT"], writes=["TT"])
        P.op("dve", lambda e: e.tensor_copy(out=TT[:, :, 80:112], in_=s48[:, :, 16:48]), reads=["a48o", "TT"], writes=["TT"])
        P.op("dve", lambda e, ds_=ds_: e.tensor_copy(out=TT[:, :, 33:48], in_=RHO[:, ds_].unsqueeze(2).to_broadcast([128, 64, 15])), reads=["RHO", "TT"], writes=["TT"])
        P.op("dve", lambda e, ds_=ds_: e.tensor_copy(out=TT[:, :, 112], in_=RHO[:, ds_]), reads=["RHO", "TT"], writes=["TT"])
        P.op("dve", lambda e, ds_=ds_: e.tensor_copy(out=TT[:, :, 113], in_=rho16[:, ds_]), reads=["rho16", "TT"], writes=["TT"])
        P.dma(C.s5t[:, :, d, :].rearrange("g p c -> p g c"), TT, reads=["TT"], writes=["s5t"], q=("sp" if d == 0 else "act"))
    P.barrier()
    A.release(mS2)
    CTr = A.f32(NDG * 16).rearrange("p (g h) -> p g h", h=16)
    CTi = A.f32(NDG * 16).rearrange("p (g h) -> p g h", h=16)
    BTr = A.f32(NDG * 16).rearrange("p (g h) -> p g h", h=16)
    BTi = A.f32(NDG * 16).rearrange("p (g h) -> p g h", h=16)
    for (src, dst, key) in ((C.s5_c_re, CTr, "CTr"), (C.s5_c_im, CTi, "CTi")):
        rows = src.rearrange("d g h p -> (d g h) p")
        for blk in range(16):
            stg = A.f32(128)
            sk = A.key("cstg")
            P.dma(stg[:, 0:64], rows[blk * 128:(blk + 1) * 128, :], writes=[sk], q=("sp" if blk % 2 == 0 else "act"))
            P.dma(stg[:, 64:128], rows[blk * 128:(blk + 1) * 128, :], writes=[sk + "b"], q=("act" if blk % 2 == 0 else "sp"))
            bk = f"bank{6 + blk % 2}"
            ps = PS[6 + blk % 2][:, 0:128]
            P.op("pe", lambda e, ps=ps, stg=stg: e.transpose(ps, stg, C.ident), reads=[sk, sk + "b"], writes=[bk])
            P.op("dve", lambda e, ps=ps, dst=dst, blk=blk: e.tensor_copy(out=dst[:, blk * 8:(blk + 1) * 8, :], in_=ps.rearrange("p (g h) -> p g h", h=16)),
                 reads=[bk], writes=[key])
    for (src, dst, key) in ((C.s5_b_re, BTr, "BTr"), (C.s5_b_im, BTi, "BTi")):
        v = src.rearrange("d g p h -> p (d g) h")
        for half in range(2):
            for q4 in range(4):
                P.dma(dst[half * 64:(half + 1) * 64, q4 * 32:(q4 + 1) * 32, :], v[:, q4 * 32:(q4 + 1) * 32, :], writes=[key + f"{half}{q4}"],
                      q=("sp" if (half + q4) % 2 == 0 else "act"))
    bt_keys = lambda k: [k + f"{h}{q}" for h in range(2) for q in range(4)]
    BBr = A.f32(NDG * 16).rearrange("p (g h) -> p g h", h=16)
    BBi = A.f32(NDG * 16).rearrange("p (g h) -> p g h", h=16)
    tq = A.f32(NDG * 16).rearrange("p (g h) -> p g h", h=16)
    frb = Fr.unsqueeze(2).to_broadcast([128, NDG, 16])
    fib = Fi.unsqueeze(2).to_broadcast([128, NDG, 16])
    P.op("dve", lambda e: e.tensor_tensor(out=BBr, in0=BTr, in1=frb, op=ALU.mult), reads=bt_keys("BTr") + ["Fr"], writes=["BBr"])
    P.op("dve", lambda e: e.tensor_tensor(out=tq, in0=BTi, in1=fib, op=ALU.mult), reads=bt_keys("BTi") + ["Fi"], writes=["tq"])
    P.op("dve", lambda e: e.tensor_tensor(out=BBr, in0=BBr, in1=tq, op=ALU.subtract), reads=["BBr", "tq"], writes=["BBr"])
    P.op("dve", lambda e: e.tensor_tensor(out=BBi, in0=BTi, in1=frb, op=ALU.mult), reads=bt_keys("BTi") + ["Fr"], writes=["BBi"])
    P.op("dve", lambda e: e.tensor_tensor(out=tq, in0=BTr, in1=fib, op=ALU.mult), reads=bt_keys("BTr") + ["Fi", "BBr"], writes=["tq"])
    P.op("dve", lambda e: e.tensor_tensor(out=BBi, in0=BBi, in1=tq, op=ALU.add), reads=["BBi", "tq"], writes=["BBi"])
    dst_ = A.f32(128, parts=64)
    for t in range(8):
        P.dma(dst_[:, t * 16:(t + 1) * 16], C.s5_d, writes=[f"dst{t}"], q=("sp" if t % 2 == 0 else "act"))
    dcol = A.f32(64)
    P.op("pe", lambda e: e.transpose(PS[7][:, 0:64], dst_, C.ident[0:64, 0:64]), reads=[f"dst{t}" for t in range(8)], writes=["bank7"])
    P.op("dve", lambda e: e.tensor_copy(out=dcol, in_=PS[7][:, 0:64]), reads=["bank7"], writes=["dcol"])
    mf = A.f32(128)
    mb = A.f32(128)
    P.dma(mf, C.s5_masks[0], writes=["mf"])
    P.dma(mb, C.s5_masks[1], writes=["mb"])
    GC = 8
    al = [A.f32(NDG * 8).rearrange("p (g t) -> p g t", t=8) for _ in range(8)]
    def gather(out_t, PW, idx_f, idx_b, sgn_top, sgn_bot, key, rk):
        for (g0, idx) in ((0, idx_f), (64, idx_b)):
            for t in range(8):
                for (p0, sg) in ((0, sgn_top), (64, sgn_bot)):
                    src_t = PW[0][p0:p0 + 64, idx[t] + 8, g0:g0 + 64] if sg[1] == "r" else PW[1][p0:p0 + 64, idx[t] + 8, g0:g0 + 64]
                    P.op("pool", lambda e, out_t=out_t, src_t=src_t, p0=p0, g0=g0, t=t, sg=sg: e.tensor_scalar(
                        out=out_t[p0:p0 + 64, g0:g0 + 64, t], in0=src_t, scalar1=float(sg[0]), scalar2=None, op0=ALU.mult), reads=rk, writes=[key])
    PW = (PWr, PWi)
    pk = ["PWr", "PWi"]
    tf = [t + 1 for t in range(8)]
    tb_ = [8 - t for t in range(8)]
    gather(al[0], PW, tf, tb_, (1, "r"), (-1, "i"), "al0", pk)
    gather(al[1], PW, tf, tb_, (-1, "i"), (-1, "r"), "al1", pk)
    gather(al[2], PW, tf, tb_, (-1, "i"), (-1, "r"), "al2", pk)
    gather(al[3], PW, tf, tb_, (-1, "r"), (1, "i"), "al3", pk)
    xin_f = [7 - t for t in range(8)]
    xin_b = [t for t in range(8)]
    gather(al[4], PW, xin_f, xin_b, (1, "r"), (1, "i"), "al4", pk)
    gather(al[5], PW, xin_f, xin_b, (-1, "i"), (1, "r"), "al5", pk)
    xtp_f = [-1 - t for t in range(8)]
    xtp_b = [t - 8 for t in range(8)]
    gather(al[6], PW, xtp_f, xtp_b, (1, "r"), (1, "i"), "al6", pk)
    gather(al[7], PW, xtp_f, xtp_b, (-1, "i"), (1, "r"), "al7", pk)
    gen = [A.bf16(GC * 128).rearrange("p (g t h) -> p g t h", g=GC, t=8) for _ in range(4)]
    tm1 = A.f32(GC * 128).rearrange("p (g t h) -> p g t h", g=GC, t=8)
    tm2 = A.f32(GC * 128).rearrange("p (g t h) -> p g t h", g=GC, t=8)
    wstage = [A.bf16(1024) for _ in range(2)]
    tpf = A.f32(128)
    tpb = A.f32(128)
    specs = ((CTr, CTi, 0, 1, ["CTr", "CTi"]), (CTr, CTi, 2, 3, ["CTr", "CTi"]), (BBr, BBi, 4, 5, ["BBr", "BBi"]), (BBr, BBi, 6, 7, ["BBr", "BBi"]))
    GEN = {}
    for d in range(2):
        for gc in range(64 // GC):
            bg_step(C, 3)
            dg0 = d * 64 + gc * GC
            for gi_, (Sr, Si, ia, ib, rk) in enumerate(specs):
                eng = "dve" if gi_ % 2 == 0 else "pool"
                srb = Sr[:, dg0:dg0 + GC, :].unsqueeze(2).to_broadcast([128, GC, 8, 16])
                sib = Si[:, dg0:dg0 + GC, :].unsqueeze(2).to_broadcast([128, GC, 8, 16])
                aa = al[ia][:, dg0:dg0 + GC, :].unsqueeze(3).to_broadcast([128, GC, 8, 16])
                bb = al[ib][:, dg0:dg0 + GC, :].unsqueeze(3).to_broadcast([128, GC, 8, 16])
                gk = f"gen{gi_}"
                P.op(eng, lambda e, srb=srb, aa=aa: e.tensor_tensor(out=tm1, in0=srb, in1=aa, op=ALU.mult), reads=rk + [f"al{ia}"], writes=["tm1"])
                P.op(eng, lambda e, sib=sib, bb=bb: e.tensor_tensor(out=tm2, in0=sib, in1=bb, op=ALU.mult), reads=rk + [f"al{ib}"], writes=["tm2"])
                P.op(eng, lambda e, gi_=gi_: e.tensor_tensor(out=gen[gi_], in0=tm1, in1=tm2, op=ALU.add), reads=["tm1", "tm2"], writes=[gk])
            for gl in range(GC):
                g = gc * GC + gl
                base = 128 + d * 448
                W1g = gen[0][:, gl].rearrange("p t h -> p (t h)")
                W2g = gen[1][:, gl].rearrange("p t h -> p (t h)")
                Xin = gen[2][:, gl].rearrange("p t h -> p (t h)")
                Xtp = gen[3][:, gl].rearrange("p t h -> p (t h)")
                ws = wstage[(d * 64 + g) % 2]
                wk = f"ws{(d * 64 + g) % 2}"
                pst = PS[0][:, 0:64].bitcast(BF16)
                P.op("pe", lambda e, pst=pst, Xin=Xin: e.transpose(pst, Xin, C.identb), reads=["gen2"], writes=["bank0"])
                P.op("act", lambda e, ws=ws, pst=pst: e.copy(out=ws[:, 0:128], in_=pst), reads=["bank0"], writes=[wk + "a"])
                P.op("act", lambda e, ws=ws, pst=pst: e.activation(out=ws[:, 128:192], in_=pst[:, 0:64], func=AF.Identity, scale=-1.0), reads=["bank0"], writes=[wk + "b"])
                P.op("dve", lambda e, ws=ws, W1g=W1g: e.tensor_copy(out=ws[:, 192:320], in_=W1g), reads=["gen0"], writes=[wk + "c"])
                P.op("dve", lambda e, ws=ws, W2g=W2g: e.tensor_copy(out=ws[:, 320:448], in_=W2g), reads=["gen1"], writes=[wk + "d"])
                P.dma(C.s5w[g, :, base:base + 448], ws[:, 0:448], reads=[wk + "a", wk + "b", wk + "c", wk + "d"], writes=["s5w"], q=("sp" if g % 2 == 0 else "act"))
                ptp = PS[1 + d][:, 0:128]
                P.op("pe", lambda e, ptp=ptp, Xtp=Xtp, W1g=W1g: e.matmul(ptp, lhsT=Xtp, rhs=W1g, start=True, stop=True), reads=["gen3", "gen0"], writes=[f"bank{1 + d}"])
                P.op("dve", lambda e, ptp=ptp, d=d: e.tensor_tensor(out=(tpf if d == 0 else tpb), in0=ptp, in1=(mf if d == 0 else mb), op=ALU.mult),
                     reads=[f"bank{1 + d}", "mf", "mb"], writes=["tpx"])
                P.dma(C.s5tp[d, g], (tpf if d == 0 else tpb), reads=["tpx"], writes=["s5tp"], q="sp")
    ta = [A.f32(128) for _ in range(2)]
    tb2 = [A.f32(128) for _ in range(2)]
    tob = [A.bf16(128) for _ in range(2)]
    for g in range(64):
        i = g % 2
        if g % 2 == 0:
            bg_step(C, 1)
        P.dma(ta[i], C.s5tp[0, g], reads=["s5tp"], writes=[f"ta{i}"], q="sp")
        P.dma(tb2[i], C.s5tp[1, g], reads=["s5tp"], writes=[f"tb{i}"], q="act")
        P.op("dve", lambda e, i=i: e.tensor_tensor(out=ta[i], in0=ta[i], in1=tb2[i], op=ALU.add), reads=[f"ta{i}", f"tb{i}"], writes=[f"ta{i}"])
        P.op("dve", lambda e, i=i, g=g: e.scalar_tensor_tensor(out=tob[i], in0=C.ident, scalar=dcol[:, g:g + 1], in1=ta[i], op0=ALU.mult, op1=ALU.add),
             reads=[f"ta{i}", "dcol"], writes=[f"tob{i}"])
        P.dma(C.s5w[g, :, 0:128], tob[i], reads=[f"tob{i}"], writes=["s5w"], q="sp")
    A.release(m0)
    P.barrier()


def s5_phase(C):
    P, A, nc = C.P, C.A, C.nc
    PS = C.psum
    l = 0
    seqs = C.seqs
    m0 = A.mark()
    sc1 = A.f32(D)
    sh = A.f32(D)
    xc = A.f32(8 * D).rearrange("p (t c) -> p t c", t=8)
    tmp = A.f32(D)
    hperm = A.bf16(8 * D).rearrange("p (g t h) -> p g t h", g=64, t=8)
    ust = [A.bf16(64 * 128).rearrange("p (g c) -> p g c", g=64) for _ in range(2)]
    cnt = 0
    for sq in seqs:
        ci = sq.ci
        bc_load(C, sh, C.modd[l, ci, 0:D], "sh")
        bc_load(C, sc1, C.modd[l, ci, D:2 * D], "sc1")
        P.op("pool", lambda e: e.tensor_scalar(out=sc1, in0=sc1, scalar1=1.0, scalar2=None, op0=ALU.add), reads=["sc1"], writes=["sc1"])
        for cb in range(sq.n // 1024):
            P.dma(xc, sq.x0[cb * 1024:(cb + 1) * 1024, :].rearrange("(c t) d -> c t d", t=8), writes=["xc"], q="sp")
            for t in range(8):
                P.op("pool", lambda e, t=t: e.tensor_tensor(out=tmp, in0=xc[:, t, :], in1=sc1, op=ALU.mult), reads=["xc", "sc1"], writes=["tmp"])
                P.op("pool", lambda e, t=t: e.tensor_tensor(out=hperm[:, :, t, :], in0=tmp.rearrange("p (g h) -> p g h", h=16),
                                                             in1=sh.rearrange("p (g h) -> p g h", h=16), op=ALU.add), reads=["tmp", "sh"], writes=["hperm"])
            us = ust[cnt % 2]
            uk = f"ust{cnt % 2}"
            cnt += 1
            for g4 in range(16):
                pT = PS[g4 % 2][:, 0:256].bitcast(BF16)
                bk = f"bank{g4 % 2}"
                for gi_ in range(4):
                    g = g4 * 4 + gi_
                    P.op("pe", lambda e, pT=pT, g=g, gi_=gi_: e.transpose(pT[:, gi_ * 128:(gi_ + 1) * 128], hperm[:, g].rearrange("p t h -> p (t h)"), C.identb),
                         reads=["hperm"], writes=[bk])
                if g4 % 2 == 0:
                    P.op("act", lambda e, pT=pT, us=us, g4=g4: e.copy(out=us[:, g4 * 4:(g4 + 1) * 4, :], in_=pT.rearrange("p (g c) -> p g c", g=4)), reads=[bk], writes=[uk + "a"])
                else:
                    P.op("dve", lambda e, pT=pT, us=us, g4=g4: e.tensor_copy(out=us[:, g4 * 4:(g4 + 1) * 4, :], in_=pT.rearrange("p (g c) -> p g c", g=4)), reads=[bk], writes=[uk + "d"])
            P.dma(sq.U[cb], us, reads=[uk + "a", uk + "d"], writes=["Uscr"], q="sp")
    A.release(m0)
    P.barrier()
    m0 = A.mark()
    Jm = A.f32(128)
    P.op("dve", lambda e: e.tensor_scalar(out=Jm[:, 0:64], in0=C.ident[:, 64:128], scalar1=-1.0, scalar2=None, op0=ALU.mult), writes=["Jm"])
    P.op("dve", lambda e: e.tensor_copy(out=Jm[:, 64:128], in_=C.ident[:, 0:64]), reads=["Jm"], writes=["Jm"])
    NCM = 512
    wg_r = [A.bf16(1024) for _ in range(2)]
    tg_r = [A.f32(256).rearrange("p (d c) -> p d c", d=2) for _ in range(2)]
    ug_r = [A.bf16(NCM) for _ in range(2)]
    t1_r = [A.f32(NCM) for _ in range(2)]
    t2_r = [A.f32(NCM) for _ in range(2)]
    bp_r = [A.f32(NCM) for _ in range(2)]
    mf_r = [A.f32(NCM) for _ in range(2)]
    ql_r = [A.f32(NCM) for _ in range(2)]
    qq_r = [A.f32(NCM) for _ in range(2)]
    cq_r = [A.bf16(NCM + 2) for _ in range(2)]
    sq_r = [A.bf16(NCM + 2) for _ in range(2)]
    zg_r = [A.bf16(NCM) for _ in range(2)]
    ec_r = [A.f32(32) for _ in range(2)]
    e1_r = [A.f32(32) for _ in range(2)]
    e2_r = [A.f32(32) for _ in range(2)]
    hin_r = [A.f32(34) for _ in range(2)]
    r16_r = [A.f32(32) for _ in range(2)]
    gg_r = [A.f32(32) for _ in range(2)]
    it = 0
    for sq in seqs:
        n_c = sq.n // 8
        n_b = n_c // 16
        ncb = sq.n // 1024
        for g in range(64):
            wg = wg_r[g % 2]
            tg = tg_r[g % 2]
            ug = ug_r[g % 2]
            wk, tk, uk = f"wg{g % 2}", f"tg{g % 2}", f"ug{g % 2}"
            P.dma(wg, C.s5w[g], writes=[wk], q="sp")
            P.dma(tg, C.s5t[g], writes=[tk], q="sp")
            P.dma(ug[:, 0:n_c].rearrange("p (b c) -> p b c", c=128), sq.U[:, :, g, :].rearrange("b p c -> p b c"), writes=[uk], q="sp")
            py = PS[4 + g % 2][:, 0:n_c]
            pyk = f"bank{4 + g % 2}"
            P.op("pe", lambda e, py=py, wg=wg, ug=ug, n_c=n_c: e.matmul(py, lhsT=wg[:, 0:128], rhs=ug[:, 0:n_c], start=True, stop=False), reads=[wk, uk], writes=[pyk])
            def chain(d, i2):
                base = 128 + d * 448
                pX = PS[0 + 2 * i2][:, 0:n_c]
                pXt = PS[1 + 2 * i2][:, 0:n_c]
                kX, kXt = f"bank{0 + 2 * i2}", f"bank{1 + 2 * i2}"
                P.op("pe", lambda e, pX=pX, wg=wg, ug=ug, base=base, n_c=n_c: e.matmul(pX, lhsT=wg[:, base:base + 128], rhs=ug[:, 0:n_c], start=True, stop=True),
                     reads=[wk, uk], writes=[kX])
                yield
                P.op("pe", lambda e, pXt=pXt, wg=wg, ug=ug, base=base, n_c=n_c: e.matmul(pXt, lhsT=wg[:, base + 64:base + 192], rhs=ug[:, 0:n_c], start=True, stop=True),
                     reads=[wk, uk], writes=[kXt])
                yield
                t1, t2, bp, mfu, ql, qq = t1_r[i2], t2_r[i2], bp_r[i2], mf_r[i2], ql_r[i2], qq_r[i2]
                ec, e1, e2, hin, r16, gg_ = ec_r[i2], e1_r[i2], e2_r[i2], hin_r[i2], r16_r[i2], gg_r[i2]
                kec, ke1, ke2, khin, kr16, kgg = f"ec{i2}", f"e1{i2}", f"e2{i2}", f"hin{i2}", f"r16{i2}", f"gg{i2}"
                cq, sqq = cq_r[i2], sq_r[i2]
                k1, k2, kb, km, kl_, kq, kc, ks = f"t1{i2}", f"t2{i2}", f"bp{i2}", f"mf{i2}", f"ql{i2}", f"qq{i2}", f"cq{i2}", f"sq{i2}"
                v3 = lambda t_, n_c=n_c: t_[:, 0:n_c].rearrange("p (b j) -> p b j", j=16)
                cosb = tg[:, d, 0:16].unsqueeze(1).to_broadcast([128, n_b, 16])
                sinb = tg[:, d, 16:32].unsqueeze(1).to_broadcast([128, n_b, 16])
                mrow = tg[:, d, 32:48].unsqueeze(1).to_broadcast([128, n_b, 16])
                cos2 = tg[:, d, 48:48 + n_b]
                sin2 = tg[:, d, 80:80 + n_b]
                rho = tg[:, d, 112:113]
                rho16 = tg[:, d, 113:114]
                P.op("dve", lambda e, t1=t1, pX=pX, cosb=cosb, v3=v3: e.tensor_tensor(out=v3(t1), in0=pX.rearrange("p (b j) -> p b j", j=16), in1=cosb, op=ALU.mult),
                     reads=[kX, tk], writes=[k1])
                yield
                P.op("dve", lambda e, t2=t2, pXt=pXt, sinb=sinb, v3=v3: e.tensor_tensor(out=v3(t2), in0=pXt.rearrange("p (b j) -> p b j", j=16), in1=sinb, op=ALU.mult),
                     reads=[kXt, tk], writes=[k2])
                yield
                P.op("pool", lambda e, bp=bp, t1=t1, t2=t2, n_c=n_c: e.tensor_tensor(out=bp[:, 0:n_c], in0=t1[:, 0:n_c], in1=t2[:, 0:n_c], op=ALU.add), reads=[k1, k2], writes=[kb])
                yield
                P.op("act", lambda e, mfu=mfu, mrow=mrow, v3=v3: e.copy(out=v3(mfu), in_=mrow), reads=[tk], writes=[km])
                yield
                P.op("pool", lambda e, rho16=rho16, n_b=n_b: e.tensor_copy(out=r16[:, 0:n_b], in_=rho16.to_broadcast([128, n_b])), reads=[tk], writes=[kr16])
                yield
                rv = (lambda a_, n_c=n_c: a_[:, 0:n_c]) if d == 0 else (lambda a_, n_c=n_c: a_[:, n_c - 1::-1])
                P.op("dve", lambda e, ql=ql, mfu=mfu, bp=bp, rv=rv, n_c=n_c: e.tensor_tensor_scan(out=rv(ql), data0=mfu[:, 0:n_c], data1=rv(bp), initial=0.0, op0=ALU.mult, op1=ALU.add),
                     reads=[km, kb], writes=[kl_])
                yield
                if d == 0:
                    e_src = ql[:, 15:n_c:16]
                    b_inj = bp[:, 0:n_c:16]
                else:
                    e_src = ql[:, n_c - 16::-16]
                    b_inj = bp[:, n_c - 1::-16]
                P.op("dve", lambda e, e_src=e_src, n_b=n_b: e.tensor_copy(out=ec[:, 0:n_b], in_=e_src), reads=[kl_], writes=[kec])
                yield
                P.op("pe", lambda e, n_b=n_b: e.matmul(PS[6][:, d * 64:d * 64 + n_b], lhsT=Jm, rhs=ec[:, 0:n_b], start=True, stop=True), reads=["Jm", kec], writes=["bank6"])
                yield
                P.op("dve", lambda e, cos2=cos2, n_b=n_b: e.tensor_tensor(out=e1[:, 0:n_b], in0=ec[:, 0:n_b], in1=cos2, op=ALU.mult), reads=[kec, tk], writes=[ke1])
                yield
                P.op("dve", lambda e, sin2=sin2, n_b=n_b: e.tensor_tensor(out=e2[:, 0:n_b], in0=PS[6][:, d * 64:d * 64 + n_b], in1=sin2, op=ALU.mult), reads=["bank6", tk], writes=[ke2])
                yield
                P.op("dve", lambda e, n_b=n_b: e.tensor_tensor(out=e1[:, 0:n_b], in0=e1[:, 0:n_b], in1=e2[:, 0:n_b], op=ALU.subtract), reads=[ke1, ke2], writes=[ke1])
                yield
                P.op("dve", lambda e: e.memset(hin[:, 0:1], 0.0), writes=[khin])
                yield
                P.op("dve", lambda e, n_b=n_b: e.tensor_tensor_scan(out=hin[:, 1:n_b + 1], data0=r16[:, 0:n_b], data1=e1[:, 0:n_b], initial=0.0, op0=ALU.mult, op1=ALU.add),
                     reads=[kr16, ke1, khin], writes=[khin])
                yield
                P.op("pe", lambda e, n_b=n_b: e.matmul(PS[7][:, d * 64:d * 64 + n_b], lhsT=Jm, rhs=hin[:, 0:n_b], start=True, stop=True), reads=["Jm", khin], writes=["bank7"])
                yield
                P.op("dve", lambda e, cos2=cos2, n_b=n_b: e.tensor_tensor(out=gg_[:, 0:n_b], in0=hin[:, 0:n_b], in1=cos2, op=ALU.mult), reads=[khin, tk], writes=[kgg])
                yield
                P.op("dve", lambda e, sin2=sin2, n_b=n_b: e.tensor_tensor(out=e2[:, 0:n_b], in0=PS[7][:, d * 64:d * 64 + n_b], in1=sin2, op=ALU.mult), reads=["bank7", tk, ke1], writes=[ke2])
                yield
                P.op("dve", lambda e, n_b=n_b: e.tensor_tensor(out=gg_[:, 0:n_b], in0=gg_[:, 0:n_b], in1=e2[:, 0:n_b], op=ALU.add), reads=[kgg, ke2], writes=[kgg])
                yield
                P.op("dve", lambda e, b_inj=b_inj, rho=rho, n_b=n_b: e.scalar_tensor_tensor(out=b_inj, in0=gg_[:, 0:n_b], scalar=rho, in1=b_inj, op0=ALU.mult, op1=ALU.add),
                     reads=[kgg, tk, kb, kl_], writes=[kb])
                yield
                P.op("dve", lambda e, qq=qq, mfu=mfu, bp=bp, rv=rv, n_c=n_c: e.tensor_tensor_scan(out=rv(qq), data0=mfu[:, 0:n_c], data1=rv(bp), initial=0.0, op0=ALU.mult, op1=ALU.add),
                     reads=[km, kb], writes=[kq])
                yield
                o = 1 if d == 0 else 0
                zc = 0 if d == 0 else n_c
                P.op("pool", lambda e, cq=cq, zc=zc: e.memset(cq[:, zc:zc + 1], 0.0), reads=[kc], writes=[kc])
                yield
                P.op("pool", lambda e, sqq=sqq, zc=zc: e.memset(sqq[:, zc:zc + 1], 0.0), reads=[ks], writes=[ks])
                yield
                P.op("pool", lambda e, cq=cq, qq=qq, cosb=cosb, o=o, n_c=n_c: e.tensor_tensor(out=cq[:, o:o + n_c].rearrange("p (b j) -> p b j", j=16),
                                                                                              in0=qq[:, 0:n_c].rearrange("p (b j) -> p b j", j=16), in1=cosb, op=ALU.mult),
                     reads=[kq, tk, kc], writes=[kc])
                yield
                P.op("dve", lambda e, sqq=sqq, qq=qq, sinb=sinb, o=o, n_c=n_c: e.tensor_tensor(out=sqq[:, o:o + n_c].rearrange("p (b j) -> p b j", j=16),
                                                                                                in0=qq[:, 0:n_c].rearrange("p (b j) -> p b j", j=16), in1=sinb, op=ALU.mult),
                     reads=[kq, tk, ks], writes=[ks])
                yield
                shf = 0 if d == 0 else 1
                P.op("pe", lambda e, py=py, wg=wg, cq=cq, base=base, shf=shf, n_c=n_c: e.matmul(py, lhsT=wg[:, base + 192:base + 320], rhs=cq[:, shf:shf + n_c], start=False, stop=False),
                     reads=[wk, kc], writes=[pyk])
                yield
                P.op("pe", lambda e, py=py, wg=wg, sqq=sqq, base=base, shf=shf, n_c=n_c, d=d: e.matmul(py, lhsT=wg[:, base + 320:base + 448], rhs=sqq[:, shf:shf + n_c], start=False, stop=(d == 1)),
                     reads=[wk, ks], writes=[pyk])
                yield
            for _ in itertools.zip_longest(chain(0, 0), chain(1, 1)):
                pass
            zg = zg_r[g % 2]
            zk = f"zg{g % 2}"
            P.op("act", lambda e, zg=zg, py=py, n_c=n_c: e.activation(out=zg[:, 0:n_c], in_=py, func=AF.Gelu_apprx_tanh), reads=[pyk], writes=[zk])
            P.dma(sq.Z[:, :, g, :].rearrange("t p c -> p t c"), zg[:, 0:n_c].rearrange("p (t c) -> p t c", c=64), reads=[zk], writes=["Zscr"], q="act")
    A.release(m0)
    P.barrier()
    m0 = A.mark()
    wglu = A.bf16(8 * 2048).rearrange("p (k c) -> p k c", k=8)
    P.dma(wglu, C.wglu_s, writes=["wglu"], q="act")
    bgl_f = A.f32(2048, parts=1)
    P.dma(bgl_f, C.s5_b_glu.rearrange("(o n) -> o n", o=1), writes=["bgl_f"])
    bgl = A.bf16(2048, parts=1)
    P.op("dve", lambda e: e.tensor_copy(out=bgl, in_=bgl_f), reads=["bgl_f"], writes=["bgl"])
    ones1 = A.bf16(128, parts=1)
    P.op("pool", lambda e: e.memset(ones1, 1.0), writes=["ones1"])
    lnG = A.f32(D)
    lnB = A.f32(D)
    bc_load(C, lnG, C.ln_g[l, 0], "lnG")
    bc_load(C, lnB, C.ln_b[l, 0], "lnB")
    gm = A.f32(D)
    xt = A.f32(4 * D).rearrange("p (b c) -> p b c", b=4)
    zt = A.bf16(64 * 64).rearrange("p (g c) -> p g c", g=64)
    zc_ = A.bf16(8 * D, parts=64).rearrange("p (t c) -> p t c", t=8)
    zf = A.bf16(8 * 512).rearrange("p (k t) -> p k t", k=8)
    ga = [A.f32(512) for _ in range(2)]
    aa_ = [A.f32(512) for _ in range(2)]
    st = A.f32(12).rearrange("p (a b) -> p a b", a=2)
    mv = A.f32(2)
    rs = A.f32(1)
    nmr = A.f32(1)
    for sq in seqs:
        ci = sq.ci
        bc_load(C, gm, C.modd[l, ci, 2 * D:3 * D], "gm")
        for T in range(sq.n // 512):
            t0 = T * 512
            P.dma(zt, sq.Z[T], writes=["zt"], q="sp")
            P.dma(xt, sq.x0[t0:t0 + 512, :].rearrange("(b p) c -> p b c", p=128), writes=["xt"], q="sp")
            for b in range(4):
                P.op("act", lambda e, b=b: e.activation(out=xt[:, b, :], in_=xt[:, b, :], func=AF.Copy, scale=ALPHA), reads=["xt"], writes=["xt"])
            for g8 in range(8):
                p1 = PS[g8 % 2][0:64, :].bitcast(BF16)
                bk = f"bank{g8 % 2}"
                for gi_ in range(8):
                    g = g8 * 8 + gi_
                    P.op("pe", lambda e, p1=p1, g=g, gi_=gi_: e.transpose(p1[:, gi_ * 128:(gi_ + 1) * 128], zt[:, g, :], C.identb), reads=["zt"], writes=[bk])
                eng = "act" if g8 % 2 == 0 else "dve"
                outv = zc_[:, :, g8 * 128:(g8 + 1) * 128].rearrange("p t (g h) -> p g t h", h=16)
                inv = p1.rearrange("p (g t h) -> p g t h", g=8, t=8)
                if eng == "act":
                    P.op("act", lambda e, outv=outv, inv=inv: e.copy(out=outv, in_=inv), reads=[bk], writes=[f"zc{g8}"])
                else:
                    P.op("dve", lambda e, outv=outv, inv=inv: e.tensor_copy(out=outv, in_=inv), reads=[bk], writes=[f"zc{g8}"])
            for k in range(8):
                p2 = PS[2 + k % 2][:, 0:256].bitcast(BF16)
                bk = f"bank{2 + k % 2}"
                for t in range(8):
                    P.op("pe", lambda e, p2=p2, t=t, k=k: e.transpose(p2[:, t * 64:(t + 1) * 64], zc_[:, t, k * 128:(k + 1) * 128], C.identb[0:64, 0:64]), reads=[f"zc{k}"], writes=[bk])
                outv = zf[:, k, :].rearrange("p (c t) -> p t c", t=8)
                inv = p2.rearrange("p (t c) -> p t c", t=8)
                if k % 2 == 0:
                    P.op("act", lambda e, outv=outv, inv=inv: e.copy(out=outv, in_=inv), reads=[bk], writes=[f"zf{k}"])
                else:
                    P.op("dve", lambda e, outv=outv, inv=inv: e.tensor_copy(out=outv, in_=inv), reads=[bk], writes=[f"zf{k}"])
            zf_keys = [f"zf{k}" for k in range(8)]
            for b in range(4):
                for hh in range(2):
                    pa = PS[4 + hh]
                    pg_ = PS[6 + hh]
                    for k in range(8):
                        P.op("pe", lambda e, pa=pa, k=k, b=b, hh=hh: e.matmul(pa, lhsT=zf[:, k, b * 128:(b + 1) * 128], rhs=wglu[:, k, hh * 512:(hh + 1) * 512], start=(k == 0), stop=False),
                             reads=[zf_keys[k], "wglu"], writes=[f"bank{4 + hh}"])
                    P.op("pe", lambda e, pa=pa, hh=hh: e.matmul(pa, lhsT=ones1, rhs=bgl[:, hh * 512:(hh + 1) * 512], start=False, stop=True), reads=["ones1", "bgl"], writes=[f"bank{4 + hh}"])
                    for k in range(8):
                        P.op("pe", lambda e, pg_=pg_, k=k, b=b, hh=hh: e.matmul(pg_, lhsT=zf[:, k, b * 128:(b + 1) * 128], rhs=wglu[:, k, 1024 + hh * 512:1024 + (hh + 1) * 512], start=(k == 0), stop=False),
                             reads=[zf_keys[k], "wglu"], writes=[f"bank{6 + hh}"])
                    P.op("pe", lambda e, pg_=pg_, hh=hh: e.matmul(pg_, lhsT=ones1, rhs=bgl[:, 1024 + hh * 512:1024 + (hh + 1) * 512], start=False, stop=True), reads=["ones1", "bgl"], writes=[f"bank{6 + hh}"])
                    g_t, a_t = ga[hh], aa_[hh]
                    P.op("act", lambda e, g_t=g_t, pg_=pg_: e.activation(out=g_t, in_=pg_, func=AF.Sigmoid), reads=[f"bank{6 + hh}"], writes=[f"ga{hh}"])
                    P.op("dve", lambda e, a_t=a_t, pa=pa, g_t=g_t: e.tensor_tensor(out=a_t, in0=pa, in1=g_t, op=ALU.mult), reads=[f"bank{4 + hh}", f"ga{hh}"], writes=[f"aa{hh}"])
                    P.op("pool", lambda e, a_t=a_t, hh=hh: e.tensor_tensor(out=a_t, in0=a_t, in1=gm[:, hh * 512:(hh + 1) * 512], op=ALU.mult), reads=[f"aa{hh}", "gm"], writes=[f"aa{hh}"])
                    xs = xt[:, b, hh * 512:(hh + 1) * 512]
                    P.op("pool", lambda e, a_t=a_t, xs=xs: e.tensor_tensor(out=xs, in0=xs, in1=a_t, op=ALU.add), reads=[f"aa{hh}", "xt"], writes=["xt"])
                layer_norm_block(C, xt[:, b, :], "xt", lnG, lnB, st, mv, rs, nmr)
            P.dma(sq.x1[t0:t0 + 512, :].rearrange("(b p) c -> p b c", p=128), xt, reads=["xt"], writes=["x1"], q="sp", is_output=(sq.x1 is sq.y))
    A.release(m0)
    P.barrier()


def na_phase(C, seq_io):
    P, A, nc = C.P, C.A, C.nc
    PS = C.psum
    l = 1
    m0 = A.mark()
    wqkv = A.bf16(8 * 3072).rearrange("p (k c) -> p k c", k=8)
    P.dma(wqkv, C.wqkv_s, writes=["wqkv"], q="act")
    bqk = A.f32(16)
    load_pp(C, C.na_b_qkv[0:2048].rearrange("(m p) -> m p", p=128), bqk, "bqk")
    sc1 = A.f32(D)
    sh = A.f32(D)
    xt_r = [A.f32(4 * D).rearrange("p (b c) -> p b c", b=4) for _ in range(2)]
    tmp = A.f32(D)
    hb_r = [A.bf16(4 * D).rearrange("p (b c) -> p b c", b=4) for _ in range(2)]
    hfm_r = [A.bf16(8 * 512).rearrange("p (k t) -> p k t", k=8) for _ in range(2)]
    qk_st = [A.bf16(512) for _ in range(3)]
    v_st = [A.bf16(1040).rearrange("p (h d) -> p h d", h=16) for _ in range(2)]
    for i in range(2):
        P.op("pool", lambda e, i=i: e.memset(v_st[i][:, :, 64:65], 1.0), writes=[f"vst{i}"])
    pT = PS[0][:, 0:256].bitcast(BF16)
    cnt = 0
    gtile = 0
    for (sq, xin, xout) in seq_io:
        n = sq.n
        ci = sq.ci
        bc_load(C, sh, C.modd[l, ci, 0:D], "sh")
        bc_load(C, sc1, C.modd[l, ci, D:2 * D], "sc1")
        P.op("pool", lambda e: e.tensor_scalar(out=sc1, in0=sc1, scalar1=1.0, scalar2=None, op0=ALU.add), reads=["sc1"], writes=["sc1"])
        g0 = gtile
        gtile += n // 512

        def prologue(T):
            t0 = T * 512
            i2 = (g0 + T) % 2
            xt, hb, hfm = xt_r[i2], hb_r[i2], hfm_r[i2]
            P.dma(xt, xin[t0:t0 + 512, :].rearrange("(b p) c -> p b c", p=128), writes=[f"xt{i2}"], q="sp")
            for b in range(4):
                P.op("pool", lambda e, b=b: e.tensor_tensor(out=tmp, in0=xt[:, b, :], in1=sc1, op=ALU.mult), reads=[f"xt{i2}", "sc1"], writes=["tmp"])
                P.op("pool", lambda e, b=b: e.tensor_tensor(out=hb[:, b, :], in0=tmp, in1=sh, op=ALU.add), reads=["tmp", "sh"], writes=[f"hb{i2}{b}"])
            for k in range(8):
                for b in range(4):
                    P.op("pe", lambda e, b=b, k=k: e.transpose(pT[:, b * 128:(b + 1) * 128], hb[:, b, k * 128:(k + 1) * 128], C.identb),
                         reads=[f"hb{i2}{b}"], writes=["bank0"])
                if k % 2 == 0:
                    P.op("act", lambda e, k=k: e.copy(out=hfm[:, k, :], in_=pT), reads=["bank0"], writes=[f"hfm{i2}{k}"])
                else:
                    P.op("dve", lambda e, k=k: e.tensor_copy(out=hfm[:, k, :], in_=pT), reads=["bank0"], writes=[f"hfm{i2}{k}"])

        prologue(0)
        for T in range(n // 512):
            t0 = T * 512
            i2 = (g0 + T) % 2
            hfm = hfm_r[i2]
            if T + 1 < n // 512:
                prologue(T + 1)
            hf_keys = [f"hfm{i2}{k}" for k in range(8)]
            for mt in range(16):
                j = cnt % 3
                cnt += 1
                pb = PS[1 + j]
                pk = f"bank{1 + j}"
                for k in range(8):
                    P.op("pe", lambda e, pb=pb, k=k, mt=mt: e.matmul(pb, lhsT=wqkv[:, k, mt * 128:(mt + 1) * 128], rhs=hfm[:, k, :], start=(k == 0), stop=(k == 7)),
                         reads=["wqkv", hf_keys[k]], writes=[pk])
                stt = qk_st[j]
                sk = f"qkst{j}"
                P.op("act", lambda e, stt=stt, pb=pb, mt=mt: e.activation(out=stt, in_=pb, func=AF.Identity, bias=bqk[:, mt:mt + 1], scale=1.0),
                     reads=[pk, "bqk"], writes=[sk])
                dst = (sq.qT if mt < 8 else sq.kT)[:, mt % 8, t0:t0 + 512]
                P.dma(dst, stt, reads=[sk], writes=["qkscr"], q="sp")
            for b in range(4):
                vs = v_st[b % 2]
                vk = f"vst{b % 2}"
                for half in range(2):
                    j = cnt % 3
                    cnt += 1
                    pb = PS[1 + j]
                    pk = f"bank{1 + j}"
                    for k in range(8):
                        P.op("pe", lambda e, pb=pb, k=k, b=b, half=half: e.matmul(pb, lhsT=hfm[:, k, b * 128:(b + 1) * 128],
                                                                                   rhs=wqkv[:, k, 2048 + half * 512:2048 + (half + 1) * 512], start=(k == 0), stop=(k == 7)),
                             reads=["wqkv", hf_keys[k]], writes=[pk])
                    P.op("dve", lambda e, vs=vs, pb=pb, half=half: e.tensor_copy(out=vs[:, half * 8:(half + 1) * 8, 0:64], in_=pb.rearrange("p (h d) -> p h d", h=8)),
                         reads=[pk], writes=[vk])
                P.dma(sq.vS[t0 + b * 128:t0 + (b + 1) * 128, :], vs.rearrange("p h d -> p (h d)"), reads=[vk], writes=["vscr"], q="sp")
    A.release(m0)
    P.barrier()
    m0 = A.mark()
    wo = A.bf16(16 * D, parts=64).rearrange("p (h c) -> p h c", h=16)
    P.dma(wo, C.wo_s, writes=["wo"], q="act")
    TAB = A.bf16(16 * 17 * 64).rearrange("p (h x) -> p h x", h=16)
    m1 = A.mark()
    tstg = [A.f32(2176) for _ in range(2)]
    for hh in range(8):
        ts_ = tstg[hh % 2]
        tk = f"tstg{hh % 2}"
        P.dma(ts_, C.na_tab[:, 2 * hh:2 * hh + 2, :].rearrange("p h x -> p (h x)"), writes=[tk], q="sp")
        P.op("act", lambda e, ts_=ts_, hh=hh: e.activation(out=TAB[:, 2 * hh:2 * hh + 2, :].rearrange("p h x -> p (h x)"), in_=ts_, func=AF.Exp),
             reads=[tk], writes=["TAB"])
    P.barrier()
    A.release(m1)
    lnG = A.f32(D)
    lnB = A.f32(D)
    bc_load(C, lnG, C.ln_g[l, 0], "lnG")
    bc_load(C, lnB, C.ln_b[l, 0], "lnB")
    bvT = A.f32(16, parts=64)
    load_pp(C, C.na_b_qkv[2048:3072].rearrange("(h d) -> h d", d=64), bvT, "bvT")
    bvTb = A.bf16(16, parts=64)
    P.op("dve", lambda e: e.tensor_copy(out=bvTb, in_=bvT), reads=["bvT"], writes=["bvTb"])
    borow = A.f32(D, parts=1)
    P.dma(borow, C.na_b_o.rearrange("(o n) -> o n", o=1), writes=["borow"])
    for half in range(2):
        pb = PS[6 + half][0:1, :]
        for h in range(16):
            P.op("pe", lambda e, pb=pb, h=h, half=half: e.matmul(pb, lhsT=bvTb[:, h:h + 1], rhs=wo[:, h, half * 512:(half + 1) * 512], start=(h == 0), stop=(h == 15)),
                 reads=["bvTb", "wo"], writes=[f"bank{6 + half}"])
        P.op("dve", lambda e, pb=pb, half=half: e.tensor_tensor(out=borow[:, half * 512:(half + 1) * 512], in0=pb, in1=borow[:, half * 512:(half + 1) * 512], op=ALU.add),
             reads=[f"bank{6 + half}", "borow"], writes=["borow"])
    P.dma(C.bo2.rearrange("(o n) -> o n", o=1), borow, reads=["borow"], writes=["bo2"])
    bo = A.f32(D)
    bc_load_dep(C, bo, C.bo2, "bo", ["bo2"])
    gm = A.f32(D)
    gmb = A.f32(D)
    ones_r = A.f32(64, parts=65)
    P.op("pool", lambda e: e.memset(ones_r[64:65, :], 1.0), writes=["ones_r"])
    xt = A.f32(4 * D).rearrange("p (b c) -> p b c", b=4)
    kw = A.bf16(8 * 1536).rearrange("p (k t) -> p k t", k=8)
    qw = A.bf16(8 * 512).rearrange("p (k t) -> p k t", k=8)
    vw = A.bf16(12 * 1040).rearrange("p (b h d) -> p b h d", b=12, h=16)
    es_ring = [A.f32(512) for _ in range(3)]
    pt_ring = [A.bf16(512) for _ in range(4)]
    es4 = [A.f32(128) for _ in range(3)]
    pt4 = [A.bf16(128) for _ in range(3)]
    oT = A.bf16(16 * 512, parts=64).rearrange("p (h t) -> p h t", h=16)
    sums = A.f32(512, parts=65)
    rcp = A.f32(512, parts=64)
    rt = [A.f32(512) for _ in range(2)]
    st = A.f32(12).rearrange("p (a b) -> p a b", a=2)
    mv = A.f32(2)
    rs = A.f32(1)
    nmr = A.f32(1)
    c_es = c_pt = c_s4 = c_h = 0
    for (sq, xin, xout) in seq_io:
        n = sq.n
        ci = sq.ci
        R = n // 64
        nt = n // 512
        bc_load(C, gm, C.modd[l, ci, 2 * D:3 * D], "gm")
        P.op("pool", lambda e: e.tensor_tensor(out=gmb, in0=gm, in1=bo, op=ALU.mult), reads=["gm", "bo"], writes=["gmb"])
        for T in range(nt):
            t0 = T * 512
            wt0 = t0 - 512
            lo = max(wt0, 0)
            hi = min(t0 + 1024, n)
            P.dma(kw[:, :, lo - wt0:hi - wt0], sq.kT[:, :, lo:hi], writes=["kw"], q="sp")
            P.dma(qw, sq.qT[:, :, t0:t0 + 512], writes=["qw"], q="sp")
            P.dma(vw[:, (lo - wt0) // 128:(hi - wt0) // 128, :, :].rearrange("p b h d -> p b (h d)"),
                  sq.vS[lo:hi, :].rearrange("(b p) c -> p b c", p=128), writes=["vw"], q="sp")
            P.dma(xt, xin[t0:t0 + 512, :].rearrange("(b p) c -> p b c", p=128), writes=["xt"], q="sp")
            for b in range(4):
                P.op("dve", lambda e, b=b: e.scalar_tensor_tensor(out=xt[:, b, :], in0=xt[:, b, :], scalar=ALPHA, in1=gmb, op0=ALU.mult, op1=ALU.add), reads=["xt", "gmb"], writes=["xt"])
            wr0 = 8 * T - 8
            LA = 2
            units = [(h, qb) for h in range(16) for qb in range(4)]
            pend = {}

            def valid(kr_, r_):
                rs_ = min(max(r_ - 4, 0), R - 8)
                return kr_ < R and rs_ <= kr_ < rs_ + 8

            def issue_scores(ui):
                nonlocal c_es, c_s4
                h, qb = units[ui]
                hp, pbase = h // 2, (h % 2) * 64
                r = 8 * T + 2 * qb
                kr0 = min(max(r - 4, 0), R - 10)
                sb_i = c_es % 3
                ps_s = PS[sb_i]
                psk = f"bank{sb_i}"
                esi = c_es % 3
                c_es += 1
                for j in range(4):
                    kc0 = (kr0 + 2 * j - wr0) * 64
                    P.op("pe", lambda e, ps_s=ps_s, j=j, kc0=kc0, qb=qb, hp=hp, pbase=pbase: e.matmul(
                        ps_s[:, (3 - j) * 128:(4 - j) * 128], lhsT=kw[pbase:pbase + 64, hp, kc0:kc0 + 128],
                        rhs=qw[pbase:pbase + 64, hp, qb * 128:(qb + 1) * 128], start=True, stop=True),
                        reads=["kw", "qw"], writes=[psk])
                pend[ui] = [sb_i, esi, None]

            def issue_j4(ui):
                nonlocal c_s4
                h, qb = units[ui]
                hp, pbase = h // 2, (h % 2) * 64
                r = 8 * T + 2 * qb
                kr0 = min(max(r - 4, 0), R - 10)
                kr4 = kr0 + 8
                nv4 = sum(1 for kl in range(2) for rl in range(2) if valid(kr4 + kl, r + rl))
                if nv4:
                    i4 = c_s4 % 2
                    c_s4 += 1
                    b4 = (3, 6)[i4]
                    ps4 = PS[b4][:, 0:128]
                    kc0 = (kr4 - wr0) * 64
                    P.op("pe", lambda e, ps4=ps4, kc0=kc0, qb=qb, hp=hp, pbase=pbase: e.matmul(
                        ps4, lhsT=kw[pbase:pbase + 64, hp, kc0:kc0 + 128], rhs=qw[pbase:pbase + 64, hp, qb * 128:(qb + 1) * 128],
                        start=True, stop=True), reads=["kw", "qw"], writes=[f"bank{b4}"])
                    pend[ui][2] = i4

            def finish_unit(ui, po, pok):
                nonlocal c_pt
                h, qb = units[ui]
                sb_i, esi, i4 = pend.pop(ui)
                r = 8 * T + 2 * qb
                kr0 = min(max(r - 4, 0), R - 10)
                x03 = 1 + r - kr0
                ps_s = PS[sb_i]
                psk = f"bank{sb_i}"
                es = es_ring[esi]
                esk = f"es{esi}"
                P.op("act", lambda e, es=es, ps_s=ps_s: e.activation(out=es, in_=ps_s, func=AF.Exp, scale=0.125), reads=[psk], writes=[esk])
                pt = pt_ring[c_pt % 4]
                ptk = f"pt{c_pt % 4}"
                c_pt += 1
                P.op("dve", lambda e, pt=pt, es=es, h=h, x03=x03: e.tensor_tensor(out=pt, in0=es, in1=TAB[:, h, x03 * 64:(x03 + 8) * 64], op=ALU.mult),
                     reads=[esk, "TAB"], writes=[ptk])
                tiles = []
                zkeys = []
                for j in range(4):
                    kr = kr0 + 2 * j
                    nv = 0
                    for kl in range(2):
                        for rl in range(2):
                            if valid(kr + kl, r + rl):
                                nv += 1
                            else:
                                zk_ = ptk + f"z{len(zkeys)}"
                                zkeys.append(zk_)
                                P.op("pool", lambda e, pt=pt, j=j, kl=kl, rl=rl: e.memset(
                                    pt[kl * 64:(kl + 1) * 64, (3 - j) * 128 + rl * 64:(3 - j) * 128 + (rl + 1) * 64], 0.0),
                                    reads=[ptk], writes=[zk_])
                    if nv:
                        tiles.append((pt[:, (3 - j) * 128:(4 - j) * 128], [ptk] + zkeys, (kr - wr0) // 2))
                tiles = [(a_, [ptk] + zkeys, c_) for (a_, _, c_) in tiles]
                if i4 is not None:
                    kr = kr0 + 8
                    b4 = (3, 6)[i4]
                    ps4 = PS[b4][:, 0:128]
                    i5 = i4
                    e4, p4 = es4[i5], pt4[i5]
                    P.op("act", lambda e, e4=e4, ps4=ps4: e.activation(out=e4, in_=ps4, func=AF.Exp, scale=0.125), reads=[f"bank{b4}"], writes=[f"es4{i5}"])
                    P.op("dve", lambda e, e4=e4, p4=p4, h=h, x03=x03: e.tensor_tensor(out=p4, in0=e4, in1=TAB[:, h, (x03 - 2) * 64:x03 * 64], op=ALU.mult),
                         reads=[f"es4{i5}", "TAB"], writes=[f"pt4{i5}"])
                    z4 = []
                    for kl in range(2):
                        for rl in range(2):
                            if not valid(kr + kl, r + rl):
                                zk_ = f"pt4{i5}z{len(z4)}"
                                z4.append(zk_)
                                P.op("pool", lambda e, p4=p4, kl=kl, rl=rl: e.memset(p4[kl * 64:(kl + 1) * 64, rl * 64:(rl + 1) * 64], 0.0),
                                     reads=[f"pt4{i5}"], writes=[zk_])
                    tiles.append((p4, [f"pt4{i5}"] + z4, (kr - wr0) // 2))
                for ti, (rhs_t, rk_, blk) in enumerate(tiles):
                    P.op("pe", lambda e, po=po, rhs_t=rhs_t, blk=blk, h=h, qb=qb, ti=ti, nti=len(tiles): e.matmul(
                        po[0:65, qb * 128:(qb + 1) * 128], lhsT=vw[:, blk, h, 0:65], rhs=rhs_t, start=(ti == 0), stop=(ti == nti - 1)),
                        reads=rk_ + ["vw"], writes=[pok])

            def normalise(h, po, pok):
                P.op("act", lambda e, po=po: e.activation(out=sums[64:65, :], in_=po[64:65, :], func=AF.Ln), reads=[pok], writes=["sums"])
                P.op("act", lambda e: e.activation(out=sums[64:65, :], in_=sums[64:65, :], func=AF.Exp, scale=-1.0), reads=["sums"], writes=["sums"])
                P.op("pe", lambda e: e.matmul(PS[7][0:64, :], lhsT=ones_r[64:65, :], rhs=sums[64:65, :], start=True, stop=True),
                     reads=["sums", "ones_r"], writes=["bank7"])
                P.op("act", lambda e: e.copy(out=rcp, in_=PS[7][0:64, :]), reads=["bank7"], writes=["rcp"])
                P.op("dve", lambda e, po=po, h=h: e.tensor_tensor(out=oT[:, h, :], in0=po[0:64, :], in1=rcp, op=ALU.mult), reads=[pok, "rcp"], writes=[f"oT{h}"])

            for ui in range(min(LA, len(units))):
                issue_scores(ui)
            issue_j4(0)
            deferred = None
            for ui, (h, qb) in enumerate(units):
                if qb == 0:
                    po = PS[4 + c_h % 2]
                    pok = f"bank{4 + c_h % 2}"
                    c_h += 1
                if ui + LA < len(units):
                    issue_scores(ui + LA)
                if ui + 1 < len(units):
                    issue_j4(ui + 1)
                finish_unit(ui, po, pok)
                if qb == 1 and deferred is not None:
                    normalise(*deferred)
                    deferred = None
                if qb == 3:
                    deferred = (h, po, pok)
            normalise(*deferred)
            oT_keys = [f"oT{h}" for h in range(16)]
            for b in range(4):
                for half in range(2):
                    j = (b * 2 + half) % 2
                    pw = PS[6 + j]
                    pwk = f"bank{6 + j}"
                    for h in range(16):
                        P.op("pe", lambda e, pw=pw, h=h, b=b, half=half: e.matmul(pw, lhsT=oT[:, h, b * 128:(b + 1) * 128], rhs=wo[:, h, half * 512:(half + 1) * 512],
                                                                                  start=(h == 0), stop=(h == 15)), reads=[oT_keys[h], "wo"], writes=[pwk])
                    r_ = rt[j]
                    rk = f"rt{j}"
                    xs = xt[:, b, half * 512:(half + 1) * 512]
                    P.op("dve", lambda e, pw=pw, r_=r_, half=half: e.tensor_tensor(out=r_, in0=pw, in1=gm[:, half * 512:(half + 1) * 512], op=ALU.mult),
                         reads=[pwk, "gm"], writes=[rk])
                    P.op("pool", lambda e, r_=r_, xs=xs: e.tensor_tensor(out=xs, in0=xs, in1=r_, op=ALU.add), reads=[rk, "xt"], writes=["xt"])
                layer_norm_block(C, xt[:, b, :], "xt", lnG, lnB, st, mv, rs, nmr)
            P.dma(xout[t0:t0 + 512, :].rearrange("(b p) c -> p b c", p=128), xt, reads=["xt"], writes=["xout"], q="sp", is_output=(xout is sq.y))
    A.release(m0)
    P.barrier()


def bc_load_dep(C, dst, src_row, key, reads):
    parts = dst.shape[0]
    C.P.dma(dst, src_row.rearrange("(o n) -> o n", o=1).partition_broadcast(parts), reads=reads, writes=[key])


def rpb_layout(rpb):
    rpb = np.asarray(rpb, np.float32)
    kc = np.arange(64)[:, None]
    qc = np.arange(64)[None, :]
    cs_ = np.clip(qc - 8, 0, 48)
    valid = (kc >= cs_) & (kc < cs_ + 16)
    idx = np.clip(kc - qc + 15, 0, 30)
    out = rpb[:, :, idx]
    out = np.where(valid[None, None], out, np.float32(-30000.0)).astype(np.float32)
    return np.ascontiguousarray(out)


def tab_layout(rpb):
    t = rpb_layout(rpb)
    out = np.full((2, 64, 16, 17, 64), -30000.0, np.float32)
    for kl in range(2):
        for x in range(17):
            dr = kl + 7 - x
            if -7 <= dr <= 7:
                out[kl, :, :, x, :] = np.transpose(t[:, dr + 7, :, :], (1, 0, 2))
    return np.ascontiguousarray(out.reshape(128, 16, 17 * 64))


def s5_masks_const():
    tau = np.arange(128)[:, None] // 16
    t = np.arange(128)[None, :] // 16
    return np.ascontiguousarray(np.stack([(t >= tau), (tau >= t)]).astype(np.float32))


def shared_inputs(inp):
    f = lambda a: np.ascontiguousarray(np.asarray(a, np.float32))
    return {
        "s5_masks": s5_masks_const(),
        "w_ada": f(inp["w_ada"]), "b_ada": f(inp["b_ada"]), "ln_g": f(inp["ln_g"]), "ln_b": f(inp["ln_b"]),
        "s5_lam_re": f(inp["s5_lam_re"][0]), "s5_lam_im": f(inp["s5_lam_im"][0]), "s5_log_dt": f(inp["s5_log_dt"][0]),
        "s5_b_re": f(inp["s5_b_re"][0]), "s5_b_im": f(inp["s5_b_im"][0]), "s5_c_re": f(inp["s5_c_re"][0]), "s5_c_im": f(inp["s5_c_im"][0]),
        "s5_d": f(inp["s5_d"][0]), "s5_w_glu": f(inp["s5_w_glu"][0]), "s5_b_glu": f(inp["s5_b_glu"][0]),
        "na_w_qkv": f(inp["na_w_qkv"][0]), "na_b_qkv": f(inp["na_b_qkv"][0]), "na_tab": tab_layout(inp["na_rpb"][0]),
        "na_w_o": f(inp["na_w_o"][0]), "na_b_o": f(inp["na_b_o"][0]),
        "ffn_w_in": f(inp["ffn_w_in"]), "ffn_b_in": f(inp["ffn_b_in"]), "ffn_conv_w": f(inp["ffn_conv_w"]),
        "ffn_conv_b": f(inp["ffn_conv_b"]), "ffn_w_out": f(inp["ffn_w_out"]), "ffn_b_out": f(inp["ffn_b_out"]),
    }


_CACHE = {}


def prompt_window(i):
    r0 = min(max(32 * i - 8, 0), 256 - 48)
    return r0 * 64


def kernel(**inputs):
    inp = {k: np.asarray(v) for k, v in inputs.items()}
    cfg = dict(ns=NS_FULL, np=NP_FULL)
    if "nc" not in _CACHE:
        _CACHE["nc"] = build(cfg)
    nc, C = _CACHE["nc"]
    shared = shared_inputs(inp)
    xs = np.asarray(inp["x_sample"], np.float32)
    xp = np.asarray(inp["x_prompt"], np.float32)[0]
    in_maps = []
    for i in range(8):
        w0 = prompt_window(i)
        m = dict(shared)
        m["x_s"] = np.ascontiguousarray(xs[i])
        m["x_p"] = np.ascontiguousarray(xp[w0:w0 + NP_FULL])
        m["cs"] = np.ascontiguousarray(np.stack([inp["c_sample"][i], inp["c_prompt"][0]]).astype(np.float32))
        in_maps.append(m)
    res = run_bass_kernel_spmd(nc, in_maps, core_ids=list(range(8)))
    y_s = np.stack([np.asarray(res.results[i]["y_s"], np.float32) for i in range(8)])
    y_p = np.zeros((1, 16384, D), np.float32)
    for i in range(8):
        w0 = prompt_window(i)
        off = 2048 * i - w0
        y_p[0, 2048 * i:2048 * (i + 1)] = np.asarray(res.results[i]["y_p"], np.float32)[off:off + 2048]
    return (y_p, y_s)
```

```python
import math
import itertools
import types
import numpy as np
from contextlib import ExitStack
import concourse.bass as bass
import concourse.mybir as mybir
from concourse.bass_utils import run_bass_kernel_spmd

F32 = mybir.dt.float32
BF16 = mybir.dt.bfloat16
AF = mybir.ActivationFunctionType
ALU = mybir.AluOpType

ENGS = ("pe", "dve", "act", "pool", "sp")
D = 1024
DFF = 2816
ALPHA = 4.0 ** 0.25
LN_EPS = 1e-5
NS_FULL = 4096
NP_FULL = 3072


def _freeze(fn):
    if fn is None or fn.__closure__ is None:
        return fn
    cells = tuple(types.CellType(c.cell_contents) for c in fn.__closure__)
    return types.FunctionType(fn.__code__, fn.__globals__, fn.__name__, fn.__defaults__, cells)


class Op:
    __slots__ = ("eng", "fn", "deps", "is_dma", "idx", "needed", "semval", "dsem", "dval", "waits", "prewait")

    def __init__(self, eng, fn, is_dma):
        self.eng = eng
        self.fn = fn
        self.is_dma = is_dma
        self.deps = set()
        self.needed = False
        self.semval = None
        self.dsem = None
        self.dval = None
        self.waits = []
        self.prewait = None


class Prog:
    def __init__(self, nc, es, n_dma_sems=14):
        self.nc = nc
        self.es = es
        self.ops = {e: [] for e in ENGS}
        self.last_w = {}
        self.readers = {}
        self.K = n_dma_sems
        self.out_dmas = []
        self.last_real = {e: None for e in ENGS}
        self.dma_hist = {e: [] for e in ENGS}

    def _track(self, o, reads, writes):
        deps = o.deps
        for k in reads:
            w = self.last_w.get(k)
            if w is not None:
                deps.add(w)
        for k in writes:
            w = self.last_w.get(k)
            if w is not None:
                deps.add(w)
            for r in self.readers.get(k, ()):
                deps.add(r)
        deps.discard(o)
        for k in writes:
            self.last_w[k] = o
            self.readers[k] = []
        for k in reads:
            self.readers.setdefault(k, []).append(o)

    dead = False
    paranoid = False

    def op(self, eng, fn, reads=(), writes=()):
        if self.dead:
            return None
        o = Op(eng, _freeze(fn), False)
        o.idx = len(self.ops[eng])
        self._track(o, reads, writes)
        if self.paranoid:
            for e2 in ENGS:
                if self.last_real[e2] is not None:
                    o.deps.add(self.last_real[e2])
                for d in self.dma_hist[e2][-self.K:]:
                    o.deps.add(d)
        self.ops[eng].append(o)
        self.last_real[eng] = o
        return o

    def dma(self, out, in_, reads=(), writes=(), q="sp", is_output=False, **kw):
        if self.dead:
            return None

        def fn(e):
            return e.dma_start(out=out, in_=in_, **kw)
        o = Op(q, fn, True)
        o.idx = len(self.ops[q])
        self._track(o, reads, writes)
        if self.paranoid:
            for e2 in ENGS:
                if self.last_real[e2] is not None:
                    o.deps.add(self.last_real[e2])
                for d in self.dma_hist[e2][-self.K:]:
                    o.deps.add(d)
        self.ops[q].append(o)
        self.dma_hist[q].append(o)
        if is_output:
            self.out_dmas.append(o)
        return o

    def barrier(self):
        if self.dead:
            return
        b = Op("sp", lambda e: e.nop(), False)
        b.idx = len(self.ops["sp"])
        for e in ENGS:
            if self.last_real[e] is not None:
                b.deps.add(self.last_real[e])
            for d in self.dma_hist[e][-self.K:]:
                b.deps.add(d)
        self.ops["sp"].append(b)
        self.last_real["sp"] = b
        for e in ENGS:
            if e == "sp":
                continue
            o = Op(e, None, False)
            o.idx = len(self.ops[e])
            o.deps.add(b)
            self.ops[e].append(o)
        self.last_w.clear()
        self.readers.clear()

    def emit(self):
        nc = self.nc
        es = self.es
        esem = {e: es.enter_context(nc.semaphore(f"s_{e}")) for e in ENGS}
        dsems = {e: [es.enter_context(nc.semaphore(f"d_{e}{i}")) for i in range(self.K)] for e in ("sp", "act", "pool")}
        for e in ENGS:
            n = 0
            for o in self.ops[e]:
                if o.is_dma:
                    o.dsem = dsems[e][n % self.K]
                    o.dval = 16 * (n // self.K + 1)
                    if n >= self.K:
                        o.prewait = (o.dsem, o.dval - 16)
                    n += 1
        fin = Op("sp", None, False)
        fin.idx = len(self.ops["sp"])
        fin.deps = set(self.out_dmas)
        self.ops["sp"].append(fin)
        for f in ENGS:
            waited = {e: -1 for e in ENGS}
            dwaited = set()
            for o in self.ops[f]:
                best = {}
                for d in o.deps:
                    if d.is_dma:
                        if id(d) in dwaited:
                            continue
                        dwaited.add(id(d))
                        o.waits.append(("dma", d))
                    else:
                        if d.eng == "pe" and f == "pe":
                            continue
                        if d.idx <= waited[d.eng]:
                            continue
                        if d.eng not in best or best[d.eng].idx < d.idx:
                            best[d.eng] = d
                for en, d in best.items():
                    waited[en] = d.idx
                    d.needed = True
                    o.waits.append(("eng", d))
        for e in ENGS:
            c = 0
            for o in self.ops[e]:
                if o.needed and not o.is_dma:
                    c += 1
                    o.semval = c
        self.stats = {e: len(self.ops[e]) for e in ENGS}

        def run(ename, eng):
            for o in self.ops[ename]:
                if o.prewait is not None:
                    eng.wait_ge(o.prewait[0], o.prewait[1])
                for kind, d in o.waits:
                    if kind == "dma":
                        eng.wait_ge(d.dsem, d.dval)
                    else:
                        eng.wait_ge(esem[d.eng], d.semval)
                if o.fn is None:
                    continue
                ins = o.fn(eng)
                if o.is_dma:
                    ins.then_inc(o.dsem, 16)
                elif o.needed:
                    ins.then_inc(esem[ename], 1)

        with nc.Block() as block:
            @block.sync
            def _(e):
                run("sp", e)

            @block.tensor
            def _(e):
                run("pe", e)

            @block.vector
            def _(e):
                run("dve", e)

            @block.scalar
            def _(e):
                run("act", e)

            @block.gpsimd
            def _(e):
                run("pool", e)


class Arena:
    def __init__(self, nc, es, words):
        self.t = es.enter_context(nc.sbuf_tensor("arena", [128, words], F32))[:, :]
        self.words = words
        self.off = 0
        self.uid = 0

    def mark(self):
        return self.off

    def release(self, m):
        self.off = m

    def f32(self, n, parts=128):
        assert self.off + n <= self.words, ("arena overflow", self.off, n, self.words)
        v = self.t[0:parts, self.off:self.off + n]
        self.off += n
        return v

    def bf16(self, n, parts=128):
        w = (n + 1) // 2
        v = self.f32(w, parts).bitcast(BF16)
        return v[:, 0:n]

    def key(self, base):
        self.uid += 1
        return f"{base}#{self.uid}"


class Ctx:
    pass


def build(cfg):
    ns, npw = cfg["ns"], cfg["np"]
    phases = cfg.get("phases", ("setup", "s5", "ffn0", "na", "ffn1"))
    cfg = dict(cfg)
    cfg["phases"] = phases
    nc = bass.Bass("TRN2", target_bir_lowering=False)
    C = Ctx()
    C.nc = nc
    C.cfg = cfg

    def din(name, shape, dt=F32):
        return nc.dram_tensor(name, list(shape), dt, kind="ExternalInput").ap()

    def dout(name, shape, dt=F32):
        return nc.dram_tensor(name, list(shape), dt, kind="ExternalOutput").ap()

    def dscr(name, shape, dt=F32):
        kind = "ExternalOutput" if cfg.get("debug") and name in cfg.get("taps", ()) else "Internal"
        return nc.dram_tensor(name, list(shape), dt, kind=kind).ap()

    seqs = []
    for nm, n in (("s", ns), ("p", npw)):
        if n == 0:
            continue
        s = Ctx()
        s.name = nm
        s.n = n
        s.ci = 0 if nm == "s" else 1
        s.x0 = din(f"x_{nm}", [n, D])
        s.y = dout(f"y_{nm}", [n, D])
        s.x1 = dscr(f"x1_{nm}", [n, D])
        s.x2 = dscr(f"x2_{nm}", [n, D])
        s.x3 = dscr(f"x3_{nm}", [n, D])
        s.qT = dscr(f"qT_{nm}", [128, 8, n], BF16)
        s.kT = dscr(f"kT_{nm}", [128, 8, n], BF16)
        s.vS = dscr(f"vS_{nm}", [n, 1040], BF16)
        s.U = dscr(f"U_{nm}", [n // 1024, 128, 64, 128], BF16)
        s.Z = dscr(f"Z_{nm}", [n // 512, 128, 64, 64], BF16)
        seqs.append(s)
    C.seqs = seqs
    C.cs = din("cs", [2, D])
    C.w_ada = din("w_ada", [2, D, 6 * D])
    C.b_ada = din("b_ada", [2, 6 * D])
    C.ln_g = din("ln_g", [2, 2, D])
    C.ln_b = din("ln_b", [2, 2, D])
    C.s5_lam_re = din("s5_lam_re", [2, 64, 64])
    C.s5_lam_im = din("s5_lam_im", [2, 64, 64])
    C.s5_log_dt = din("s5_log_dt", [2, 64])
    C.s5_b_re = din("s5_b_re", [2, 64, 64, 16])
    C.s5_b_im = din("s5_b_im", [2, 64, 64, 16])
    C.s5_c_re = din("s5_c_re", [2, 64, 16, 64])
    C.s5_c_im = din("s5_c_im", [2, 64, 16, 64])
    C.s5_d = din("s5_d", [64, 16])
    C.s5_w_glu = din("s5_w_glu", [D, 2 * D])
    C.s5_b_glu = din("s5_b_glu", [2 * D])
    C.na_w_qkv = din("na_w_qkv", [D, 3 * D])
    C.na_b_qkv = din("na_b_qkv", [3 * D])
    C.na_tab = din("na_tab", [128, 16, 1088])
    C.s5_masks = din("s5_masks", [2, 128, 128])
    C.na_w_o = din("na_w_o", [D, D])
    C.na_b_o = din("na_b_o", [D])
    C.ffn_w_in = din("ffn_w_in", [2, D, 2 * DFF])
    C.ffn_b_in = din("ffn_b_in", [2, 2 * DFF])
    C.ffn_conv_w = din("ffn_conv_w", [2, 3, DFF])
    C.ffn_conv_b = din("ffn_conv_b", [2, DFF])
    C.ffn_w_out = din("ffn_w_out", [2, DFF, D])
    C.ffn_b_out = din("ffn_b_out", [2, D])
    C.modd = dscr("modd", [2, 2, 6 * D])
    C.wglu_s = dscr("wglu_s", [128, 8, 2 * D], BF16)
    C.wqkv_s = dscr("wqkv_s", [128, 8, 3 * D], BF16)
    C.wo_s = dscr("wo_s", [64, 16, D], BF16)
    C.win_s = dscr("win_s", [2, 11, 128, 8, 512], BF16)
    C.wout_s = dscr("wout_s", [2, 128, 22, D], BF16)
    C.bo2 = dscr("bo2", [D])
    C.s5w = dscr("s5w", [64, 128, 1024], BF16)
    C.s5t = dscr("s5t", [64, 128, 2, 128])
    C.s5tp = dscr("s5tp", [2, 64, 128, 128])

    with ExitStack() as es:
        P = Prog(nc, es, n_dma_sems=cfg.get('ksem', 14))
        P.paranoid = bool(cfg.get('paranoid'))
        C.P = P
        A = Arena(nc, es, cfg.get("arena_words", 53184))
        C.A = A
        C.psum = [es.enter_context(nc.psum_tensor(f"ps{i}", [128, 512], F32))[:, :] for i in range(8)]
        C.ident = A.f32(128)
        C.identb = A.bf16(128)
        C.eps = A.f32(1)
        P.op("pool", lambda e: e.memset(C.ident, 0.0), writes=["ident"])
        P.op("pool", lambda e: e.affine_select(out=C.ident, in_=C.ident, pattern=[[-1, 128]], compare_op=ALU.not_equal,
                                               fill=1.0, base=0, channel_multiplier=1), reads=["ident"], writes=["ident"])
        P.op("pool", lambda e: e.tensor_copy(out=C.identb, in_=C.ident), reads=["ident"], writes=["identb"])
        P.op("pool", lambda e: e.memset(C.eps, LN_EPS), writes=["eps"])
        P.barrier()
        if "setup" in phases:
            setup_phase(C)
        if "s5" in phases:
            if len(phases) == 2:
                for s_ in seqs:
                    s_.x1 = s_.y
            s5_phase(C)
        if "ffn0" in phases:
            ffn_phase(C, 0, [(s, (s.x1 if "s5" in phases else s.x0), (s.x2 if ("na" in phases or "ffn1" in phases) else s.y)) for s in seqs])
        if "na" in phases:
            na_phase(C, [(s, (s.x2 if "ffn0" in phases else s.x0), (s.x3 if "ffn1" in phases else s.y)) for s in seqs])
        if "ffn1" in phases:
            ffn_phase(C, 1, [(s, (s.x3 if "na" in phases else s.x0), s.y) for s in seqs])
        P.emit()
        C.stats = P.stats
    return nc, C


def cut(C, label):
    if C.cfg.get("cut") == label:
        C.P.barrier()
        C.P.dead = True


def load_pp(C, src, dst, key):
    P, A = C.P, C.A
    R, W = src.shape[0], src.shape[1]
    stg = A.f32(W, parts=R)
    sk = A.key("ppstg")
    P.dma(stg, src, writes=[sk])
    ps = C.psum[7][0:W, 0:R]
    P.op("pe", lambda e: e.transpose(ps, stg, C.ident[0:R, 0:R]), reads=[sk, "ident"], writes=["bank7"])
    P.op("dve", lambda e: e.tensor_copy(out=dst, in_=ps), reads=["bank7"], writes=[key])


def bc_load(C, dst, src_row, key, q="sp"):
    n = dst.shape[1]
    parts = dst.shape[0]
    C.P.dma(dst, src_row.rearrange("(o n) -> o n", o=1).partition_broadcast(parts), writes=[key], q=q)


def setup_phase(C):
    P, A, nc = C.P, C.A, C.nc
    m0 = A.mark()
    cT = A.f32(16).rearrange("p (k b) -> p k b", b=2)
    crow = A.f32(D, parts=2)
    P.dma(crow, C.cs, writes=["crow"])
    P.op("act", lambda e: e.activation(out=crow, in_=crow, func=AF.Silu), reads=["crow"], writes=["crow"])
    pct = C.psum[2][:, 0:16]
    for k in range(8):
        P.op("pe", lambda e, k=k: e.transpose(pct[:, 2 * k:2 * k + 2], crow[0:2, k * 128:(k + 1) * 128], C.ident[0:2, 0:2]),
             reads=["crow", "ident"], writes=["pct"])
    P.op("dve", lambda e: e.tensor_copy(out=cT, in_=pct.rearrange("p (k b) -> p k b", b=2)), reads=["pct"], writes=["cT"])
    wa = [A.f32(8 * 512).rearrange("p (k c) -> p k c", k=8) for _ in range(2)]
    bad = A.f32(6 * D, parts=2)
    modsb = A.f32(6 * D, parts=2)
    it = 0
    for l in range(2):
        P.dma(bad, C.b_ada[l].rearrange("(o n) -> o n", o=1).partition_broadcast(2), writes=["bad"])
        for cb in range(12):
            w = wa[it % 2]
            wk = f"wa{it % 2}"
            P.dma(w, C.w_ada[l][:, cb * 512:(cb + 1) * 512].rearrange("(k p) c -> p k c", p=128), writes=[wk],
                  q=("sp" if it % 2 == 0 else "act"))
            ps = C.psum[it % 2][0:2, :]
            pk = f"ps{it % 2}"
            for k in range(8):
                P.op("pe", lambda e, ps=ps, w=w, k=k: e.matmul(ps, lhsT=cT[:, k, :], rhs=w[:, k, :], start=(k == 0), stop=(k == 7)),
                     reads=[wk, "cT"], writes=[pk])
            P.op("dve", lambda e, ps=ps, cb=cb: e.tensor_tensor(out=modsb[:, cb * 512:(cb + 1) * 512], in0=ps,
                                                                in1=bad[:, cb * 512:(cb + 1) * 512], op=ALU.add),
                 reads=[pk, "bad"], writes=["modsb"])
            it += 1
        P.dma(C.modd[l], modsb, reads=["modsb"], writes=["modd"])
    A.release(m0)
    P.barrier()
    m0c = A.mark()
    CW = 5632
    stg = [A.f32(CW)] * 2
    stb = [A.bf16(CW)] * 2
    cnt = [0]
    cast_eng = ("act", "pool", "dve")

    def cast_rows(src, ncols, stores):
        i = cnt[0]
        cnt[0] += 1
        s_f, s_b = stg[i % 2], stb[i % 2]
        kf, kb = "stg0", "stb0"
        P.dma(s_f[:, 0:ncols], src, writes=[kf], q=("sp" if i % 2 == 0 else "act"))
        ce = cast_eng[i % 3]
        if ce == "act":
            P.op("act", lambda e: e.copy(out=s_b[:, 0:ncols], in_=s_f[:, 0:ncols]), reads=[kf], writes=[kb])
        else:
            P.op(ce, lambda e: e.tensor_copy(out=s_b[:, 0:ncols], in_=s_f[:, 0:ncols]), reads=[kf], writes=[kb])
        for dst, (c0, c1), (p0, p1) in stores:
            P.dma(dst, s_b[p0:p1, c0:c1], reads=[kb], writes=["wscr"], q="sp")

    ph = C.cfg.get("phases")

    def casts():
        if "s5" in ph:
            for k in range(8):
                cast_rows(C.s5_w_glu[k * 128:(k + 1) * 128, :], 2048, [(C.wglu_s[:, k, :], (0, 2048), (0, 128))])
                yield
        if "na" in ph:
            for k in range(8):
                cast_rows(C.na_w_qkv[k * 128:(k + 1) * 128, :], 3072, [(C.wqkv_s[:, k, :], (0, 3072), (0, 128))])
                yield
            for k in range(8):
                cast_rows(C.na_w_o[k * 128:(k + 1) * 128, :], 1024,
                          [(C.wo_s[:, 2 * k, :], (0, 1024), (0, 64)), (C.wo_s[:, 2 * k + 1, :], (0, 1024), (64, 128))])
                yield
        for l in range(2):
            if f"ffn{l}" not in ph and f"ffn{l}_castonly" not in ph:
                continue
            for k in range(8):
                dst_u = C.win_s[l][:, :, k, 0:256].rearrange("m p c -> p m c")
                dst_g = C.win_s[l][:, :, k, 256:512].rearrange("m p c -> p m c")
                i = cnt[0]
                cast_rows(C.ffn_w_in[l][k * 128:(k + 1) * 128, :], 5632, [])
                s_b = stb[0]
                kb = "stb0"
                P.dma(dst_u, s_b[:, 0:2816].rearrange("p (m c) -> p m c", c=256), reads=[kb], writes=["wscr"], q="sp")
                P.dma(dst_g, s_b[:, 2816:5632].rearrange("p (m c) -> p m c", c=256), reads=[kb], writes=["wscr"], q="sp")
                yield
            for k in range(22):
                cast_rows(C.ffn_w_out[l][k * 128:(k + 1) * 128, :], 1024, [(C.wout_s[l][:, k, :], (0, 1024), (0, 128))])
                yield

    bg = casts()
    C.bg = bg
    if "s5" in ph:
        s5_setup(C)
    for _ in bg:
        pass
    A.release(m0c)
    P.barrier()


def bg_step(C, n=1):
    bg = getattr(C, "bg", None)
    if bg is None:
        return
    for _ in range(n):
        next(bg, None)


def ffn_phase(C, l, seq_io):
    P, A, nc = C.P, C.A, C.nc
    m0 = A.mark()
    PS = C.psum
    wout = A.bf16(22 * D).rearrange("p (k c) -> p k c", k=22)
    P.dma(wout, C.wout_s[l], writes=["wout"], q="act")
    lnG = A.f32(D)
    lnB = A.f32(D)
    bc_load(C, lnG, C.ln_g[l, 1], "lnG")
    bc_load(C, lnB, C.ln_b[l, 1], "lnB")
    bout = A.f32(D)
    bc_load(C, bout, C.ffn_b_out[l], "bout")
    binp = A.f32(44)
    cw = A.f32(66).rearrange("p (j m) -> p j m", j=3)
    cb = A.f32(22)
    mst = A.mark()
    load_pp(C, C.ffn_b_in[l].rearrange("(m p) -> m p", p=128), binp, "binp")
    load_pp(C, C.ffn_conv_w[l].rearrange("j (m p) -> (j m) p", p=128), cw.rearrange("p j m -> p (j m)"), "cw")
    load_pp(C, C.ffn_conv_b[l].rearrange("(m p) -> m p", p=128), cb, "cb")
    sc1 = A.f32(D)
    sh = A.f32(D)
    gf = A.f32(D)
    gfb = A.f32(D)
    xt_ring = [A.f32(4 * D).rearrange("p (b c) -> p b c", b=4) for _ in range(2)]
    xh_r = [A.f32(D, parts=2) for _ in range(2)]
    tmp = A.f32(D)
    hb_r = [A.bf16(4 * D).rearrange("p (b c) -> p b c", b=4) for _ in range(2)]
    hhb_r = [A.bf16(D, parts=2) for _ in range(2)]
    hfm_r = [A.bf16(8 * 516).rearrange("p (k t) -> p k t", k=8) for _ in range(2)]
    wring = [A.bf16(8 * 512).rearrange("p (k c) -> p k c", k=8) for _ in range(2)]
    hid = A.bf16(22 * 512).rearrange("p (m t) -> p m t", m=22)
    u_sb = [A.f32(514) for _ in range(2)]
    vv = [A.f32(512) for _ in range(2)]
    g1 = [A.f32(512) for _ in range(2)]
    rt = vv
    st = A.f32(12).rearrange("p (a b) -> p a b", a=2)
    mv = A.f32(2)
    rs = A.f32(1)
    nmr = A.f32(1)
    if C.cfg.get("verbose"):
        print("ffn arena high-water", A.off, "of", A.words)
    pT = PS[0][:, 0:256].bitcast(BF16)
    pu = [PS[1], PS[2]]
    pg = [PS[3], PS[4]]
    phl = PS[5][:, 0:44]
    pTh = PS[5][:, 64:72].bitcast(BF16)
    po = [PS[6], PS[7]]
    gi = [0]
    cut(C, "A")
    for (sq, xin, xout) in seq_io:
        n = sq.n
        ci = sq.ci
        bc_load(C, sh, C.modd[l, ci, 3 * D:4 * D], "sh")
        bc_load(C, sc1, C.modd[l, ci, 4 * D:5 * D], "sc1")
        bc_load(C, gf, C.modd[l, ci, 5 * D:6 * D], "gf")
        P.op("pool", lambda e: e.tensor_scalar(out=sc1, in0=sc1, scalar1=1.0, scalar2=None, op0=ALU.add), reads=["sc1"], writes=["sc1"])
        P.op("pool", lambda e: e.tensor_tensor(out=gfb, in0=gf, in1=bout, op=ALU.mult), reads=["gf", "bout"], writes=["gfb"])
        nt = n // 512
        it0 = gi[0]
        gi[0] += nt

        def prologue(T):
            t0 = T * 512
            it = it0 + T
            i2 = it % 2
            xt, xh, hb, hhb, hfm = xt_ring[i2], xh_r[i2], hb_r[i2], hhb_r[i2], hfm_r[i2]
            xk = f"xt{i2}"
            has_l = t0 > 0
            has_r = t0 + 512 < n
            P.dma(xt, xin[t0:t0 + 512, :].rearrange("(b p) c -> p b c", p=128), writes=[xk], q="sp")
            if (has_l or has_r) and not (has_l and has_r):
                P.op("pool", lambda e: e.memset(xh, 0.0), writes=[f"xh0{i2}", f"xh1{i2}"])
            if has_l:
                P.dma(xh[0:1, :], xin[t0 - 1:t0, :], writes=[f"xh0{i2}"], q="sp")
            if has_r:
                P.dma(xh[1:2, :], xin[t0 + 512:t0 + 513, :], writes=[f"xh1{i2}"], q="sp")
            for b in range(4):
                P.op("dve", lambda e, b=b: e.tensor_tensor(out=tmp, in0=xt[:, b, :], in1=sc1, op=ALU.mult), reads=[xk, "sc1"], writes=["tmp"])
                P.op("dve", lambda e, b=b: e.tensor_tensor(out=hb[:, b, :], in0=tmp, in1=sh, op=ALU.add), reads=["tmp", "sh"], writes=[f"hb{i2}{b}"])
            if has_l or has_r:
                P.op("dve", lambda e: e.tensor_tensor(out=tmp[0:2, :], in0=xh, in1=sc1[0:2, :], op=ALU.mult), reads=[f"xh0{i2}", f"xh1{i2}", "sc1"], writes=["tmp"])
                P.op("dve", lambda e: e.tensor_tensor(out=hhb, in0=tmp[0:2, :], in1=sh[0:2, :], op=ALU.add), reads=["tmp", "sh"], writes=[f"hhb{i2}"])
            for b in range(4):
                P.op("dve", lambda e, b=b: e.scalar_tensor_tensor(out=xt[:, b, :], in0=xt[:, b, :], scalar=ALPHA, in1=gfb, op0=ALU.mult, op1=ALU.add),
                     reads=[xk, f"hb{i2}{b}", "gfb"], writes=[xk])

        def prologue_b(T):
            t0 = T * 512
            it = it0 + T
            i2 = it % 2
            xt, xh, hb, hhb, hfm = xt_ring[i2], xh_r[i2], hb_r[i2], hhb_r[i2], hfm_r[i2]
            has_l = t0 > 0
            has_r = t0 + 512 < n
            for k in range(8):
                for b in range(4):
                    P.op("pe", lambda e, b=b, k=k: e.transpose(pT[:, b * 128:(b + 1) * 128], hb[:, b, k * 128:(k + 1) * 128], C.identb),
                         reads=[f"hb{i2}{b}", "identb"], writes=["bank0"])
                if k % 2 == 0:
                    P.op("act", lambda e, k=k: e.copy(out=hfm[:, k, 2:514], in_=pT), reads=["bank0"], writes=[f"hfm{i2}{k}"])
                else:
                    P.op("dve", lambda e, k=k: e.tensor_copy(out=hfm[:, k, 2:514], in_=pT), reads=["bank0"], writes=[f"hfm{i2}{k}"])
            if has_l or has_r:
                for k in range(8):
                    P.op("pe", lambda e, k=k: e.transpose(pTh[:, 2 * k:2 * k + 2], hhb[0:2, k * 128:(k + 1) * 128], C.identb[0:2, 0:2]),
                         reads=[f"hhb{i2}", "identb"], writes=["bank5"])
                P.op("dve", lambda e: e.tensor_copy(out=hfm[:, :, 1], in_=pTh[:, 0:16:2]), reads=["bank5"], writes=[f"hfmh{i2}"])
                P.op("dve", lambda e: e.tensor_copy(out=hfm[:, :, 514], in_=pTh[:, 1:16:2]), reads=["bank5"], writes=[f"hfmh{i2}"])

        prologue(0)
        prologue_b(0)
        preloaded = set()
        for T in range(nt):
            t0 = T * 512
            it = it0 + T
            i2 = it % 2
            xt, hfm = xt_ring[i2], hfm_r[i2]
            xk = f"xt{i2}"
            has_l = t0 > 0
            has_r = t0 + 512 < n
            cut(C, "D")
            hf_keys = [f"hfm{i2}{k}" for k in range(8)]
            lvl = C.cfg.get("ffn_level", 3)
            if lvl == 1:
                P.op("dve", lambda e: e.tensor_copy(out=xt[:, 0, 0:514], in_=hfm[:, 0, 1:515]), reads=hf_keys + [f"hfmh{i2}", xk], writes=[xk])
                P.dma(xout[t0:t0 + 512, :].rearrange("(b p) c -> p b c", p=128), xt, reads=[xk], writes=["xout"], q="sp", is_output=(xout is sq.y))
                continue
            for m2 in range(11):
                wi = (it * 11 + m2) % 2
                w = wring[wi]
                wk = f"wr{wi}"
                if (it, m2) not in preloaded:
                    P.dma(w, C.win_s[l, m2], writes=[wk], q="sp")
                if m2 == 1 and T + 1 < nt:
                    prologue(T + 1)
                if m2 == 7 and T + 1 < nt:
                    prologue_b(T + 1)
                for mi in range(2):
                    m = m2 * 2 + mi
                    j = m % 2
                    puk, pgk = f"bank{1 + j}", f"bank{3 + j}"
                    for k in range(8):
                        P.op("pe", lambda e, j=j, k=k, mi=mi, w=w: e.matmul(pu[j], lhsT=w[:, k, mi * 128:(mi + 1) * 128], rhs=hfm[:, k, 2:514],
                                                                            start=(k == 0), stop=(k == 7)),
                             reads=[wk, hf_keys[k]], writes=[puk])
                    for k in range(8):
                        P.op("pe", lambda e, j=j, k=k, mi=mi, w=w: e.matmul(pg[j], lhsT=w[:, k, 256 + mi * 128:256 + (mi + 1) * 128], rhs=hfm[:, k, 2:514],
                                                                            start=(k == 0), stop=(k == 7)),
                             reads=[wk, hf_keys[k]], writes=[pgk])
                    if has_l or has_r:
                        for k in range(8):
                            P.op("pe", lambda e, k=k, mi=mi, m=m, w=w: e.matmul(phl[:, 2 * m:2 * m + 2], lhsT=w[:, k, mi * 128:(mi + 1) * 128],
                                                                                rhs=hfm[:, k, 1:515:513], start=(k == 0), stop=(k == 7)),
                                 reads=[wk, f"hfmh{i2}"], writes=["bank5"])
                    u = u_sb[j]
                    uk = f"u{j}"
                    P.op("act", lambda e, u=u, j=j, m=m: e.activation(out=u[:, 1:513], in_=pu[j], func=AF.Identity, bias=binp[:, m:m + 1], scale=1.0),
                         reads=[puk, "binp"], writes=[uk + "m"])
                    if has_l:
                        P.op("act", lambda e, u=u, m=m: e.activation(out=u[:, 0:1], in_=phl[:, 2 * m:2 * m + 1], func=AF.Identity, bias=binp[:, m:m + 1], scale=1.0),
                             reads=["bank5", "binp"], writes=[uk + "l"])
                    else:
                        P.op("pool", lambda e, u=u: e.memset(u[:, 0:1], 0.0), writes=[uk + "l"])
                    if has_r:
                        P.op("act", lambda e, u=u, m=m: e.activation(out=u[:, 513:514], in_=phl[:, 2 * m + 1:2 * m + 2], func=AF.Identity, bias=binp[:, m:m + 1], scale=1.0),
                             reads=["bank5", "binp"], writes=[uk + "r"])
                    else:
                        P.op("pool", lambda e, u=u: e.memset(u[:, 513:514], 0.0), writes=[uk + "r"])
                    v = vv[j]
                    vk = f"v{j}"
                    P.op("dve", lambda e, u=u, v=v, m=m: e.tensor_scalar(out=v, in0=u[:, 1:513], scalar1=cw[:, 1, m:m + 1], scalar2=cb[:, m:m + 1],
                                                                           op0=ALU.mult, op1=ALU.add),
                         reads=[uk + "m", "cw", "cb"], writes=[vk])
                    P.op("dve", lambda e, u=u, v=v, m=m: e.scalar_tensor_tensor(out=v, in0=u[:, 0:512], scalar=cw[:, 0, m:m + 1], in1=v, op0=ALU.mult, op1=ALU.add),
                         reads=[uk + "m", uk + "l", vk, "cw"], writes=[vk])
                    P.op("dve", lambda e, u=u, v=v, m=m: e.scalar_tensor_tensor(out=v, in0=u[:, 2:514], scalar=cw[:, 2, m:m + 1], in1=v, op0=ALU.mult, op1=ALU.add),
                         reads=[uk + "m", uk + "r", vk, "cw"], writes=[vk])
                    gg = g1[j]
                    gk = f"g1{j}"
                    P.op("act", lambda e, v=v, gg=gg: e.activation(out=gg, in_=v, func=AF.Gelu_apprx_tanh), reads=[vk], writes=[gk])
                    P.op("dve", lambda e, gg=gg, j=j, m=m: e.scalar_tensor_tensor(out=hid[:, m, :], in0=pg[j], scalar=binp[:, 22 + m:23 + m], in1=gg,
                                                                                   op0=ALU.add, op1=ALU.mult),
                         reads=[pgk, gk, "binp"], writes=[f"hid{m}"])
            hid_keys = [f"hid{m}" for m in range(22)]
            if T + 1 < nt:
                wi_n = ((it + 1) * 11) % 2
                P.dma(wring[wi_n], C.win_s[l, 0], writes=[f"wr{wi_n}"], q="sp")
                preloaded.add((it + 1, 0))
            if lvl == 2:
                P.op("dve", lambda e: e.tensor_copy(out=xt[:, 0, 0:512], in_=hid[:, 5, :]), reads=hid_keys + [xk], writes=[xk])
                P.dma(xout[t0:t0 + 512, :].rearrange("(b p) c -> p b c", p=128), xt, reads=[xk], writes=["xout"], q="sp", is_output=(xout is sq.y))
                continue
            for b in range(4):
                for half in range(2):
                    j = (b * 2 + half) % 2
                    pok = f"bank{6 + j}"
                    for k in range(22):
                        P.op("pe", lambda e, j=j, k=k, b=b, half=half: e.matmul(po[j], lhsT=hid[:, k, b * 128:(b + 1) * 128],
                                                                                  rhs=wout[:, k, half * 512:(half + 1) * 512], start=(k == 0), stop=(k == 21)),
                             reads=[hid_keys[k], "wout"], writes=[pok])
                    r = rt[j]
                    rk = f"v{j}"
                    xs = xt[:, b, half * 512:(half + 1) * 512]
                    P.op("dve", lambda e, j=j, r=r, half=half: e.tensor_tensor(out=r, in0=po[j], in1=gf[:, half * 512:(half + 1) * 512], op=ALU.mult),
                         reads=[pok, "gf"], writes=[rk])
                    P.op("dve", lambda e, r=r, xs=xs: e.tensor_tensor(out=xs, in0=xs, in1=r, op=ALU.add), reads=[rk, xk], writes=[xk])
                layer_norm_block(C, xt[:, b, :], xk, lnG, lnB, st, mv, rs, nmr)
            P.dma(xout[t0:t0 + 512, :].rearrange("(b p) c -> p b c", p=128), xt, reads=[xk], writes=["xout"], q="sp",
                  is_output=(xout is sq.y))
    A.release(m0)
    P.barrier()


def layer_norm_block(C, xb, xk, lnG, lnB, st, mv, rs, nmr):
    P = C.P
    P.op("dve", lambda e: e.bn_stats(out=st[:, 0, :], in_=xb[:, 0:512]), reads=[xk], writes=["st"])
    P.op("dve", lambda e: e.bn_stats(out=st[:, 1, :], in_=xb[:, 512:1024]), reads=[xk, "st"], writes=["st"])
    P.op("dve", lambda e: e.bn_aggr(out=mv, in_=st), reads=["st"], writes=["mv"])
    P.op("act", lambda e: e.activation(out=rs, in_=mv[:, 1:2], func=AF.Sqrt, bias=C.eps, scale=1.0), reads=["mv", "eps"], writes=["rs"])
    P.op("dve", lambda e: e.reciprocal(out=rs, in_=rs), reads=["rs"], writes=["rs"])
    P.op("dve", lambda e: e.tensor_scalar(out=nmr, in0=mv[:, 0:1], scalar1=rs, scalar2=-1.0, op0=ALU.mult, op1=ALU.mult),
         reads=["mv", "rs"], writes=["nmr"])
    P.op("act", lambda e: e.activation(out=xb, in_=xb, func=AF.Identity, bias=nmr, scale=rs), reads=[xk, "rs", "nmr"], writes=[xk])
    P.op("dve", lambda e: e.tensor_tensor(out=xb, in0=xb, in1=lnG, op=ALU.mult), reads=[xk, "lnG"], writes=[xk])
    P.op("dve", lambda e: e.tensor_tensor(out=xb, in0=xb, in1=lnB, op=ALU.add), reads=[xk, "lnB"], writes=[xk])


TWO_PI = 2.0 * math.pi
MAGIC = 12582912.0
CW1 = 6.28125
CW2 = TWO_PI - 6.28125


def sincos(C, ang, n, sin_out, cos_out, tag):
    P, A = C.P, C.A
    k = A.f32(n)
    r = A.f32(n)
    ka, kk, kr = tag + "a", tag + "k", tag + "r"
    for (shift, outp) in ((0.0, sin_out), (0.5 * math.pi, cos_out)):
        if outp is None:
            continue
        P.op("dve", lambda e, shift=shift: e.tensor_scalar(out=r, in0=ang, scalar1=shift, scalar2=None, op0=ALU.add), reads=[ka], writes=[kr])
        P.op("dve", lambda e: e.tensor_scalar(out=k, in0=r, scalar1=1.0 / TWO_PI, scalar2=MAGIC, op0=ALU.mult, op1=ALU.add), reads=[kr], writes=[kk])
        P.op("dve", lambda e: e.tensor_scalar(out=k, in0=k, scalar1=MAGIC, scalar2=None, op0=ALU.subtract), reads=[kk], writes=[kk])
        P.op("dve", lambda e: e.scalar_tensor_tensor(out=r, in0=k, scalar=-CW1, in1=r, op0=ALU.mult, op1=ALU.add), reads=[kk, kr], writes=[kr])
        P.op("dve", lambda e: e.scalar_tensor_tensor(out=r, in0=k, scalar=-CW2, in1=r, op0=ALU.mult, op1=ALU.add), reads=[kk, kr], writes=[kr])
        P.op("dve", lambda e: e.tensor_scalar(out=r, in0=r, scalar1=math.pi, scalar2=-math.pi, op0=ALU.min, op1=ALU.max), reads=[kr], writes=[kr])
        P.op("act", lambda e, outp=outp: e.activation(out=outp, in_=r, func=AF.Sin), reads=[kr], writes=[tag + "o"])


def dup_transpose(C, src, dst, key):
    P, A = C.P, C.A
    stg = A.f32(128)
    sk = A.key("dtstg")
    P.dma(stg[:, 0:64], src, writes=[sk])
    P.dma(stg[:, 64:128], src, writes=[sk + "b"])
    ps = C.psum[7][:, 0:128]
    P.op("pe", lambda e: e.transpose(ps, stg, C.ident), reads=[sk, sk + "b"], writes=["bank7"])
    P.op("dve", lambda e: e.tensor_copy(out=dst, in_=ps), reads=["bank7"], writes=[key])


def s5_setup(C):
    P, A, nc = C.P, C.A, C.nc
    PS = C.psum
    m0 = A.mark()
    NDG = 128
    LR = A.f32(NDG)
    LI = A.f32(NDG)
    DT = A.f32(NDG)
    dup_transpose(C, C.s5_lam_re.rearrange("d g p -> (d g) p"), LR, "LR")
    dup_transpose(C, C.s5_lam_im.rearrange("d g p -> (d g) p"), LI, "LI")
    ldt = A.f32(1)
    P.dma(ldt, C.s5_log_dt.rearrange("d (g o) -> (d g) o", o=1), writes=["ldt"])
    ldtb = A.f32(128)
    P.op("dve", lambda e: e.tensor_copy(out=ldtb, in_=ldt.to_broadcast([128, 128])), reads=["ldt"], writes=["ldtb"])
    P.op("pe", lambda e: e.transpose(PS[7][:, 0:128], ldtb, C.ident), reads=["ldtb"], writes=["bank7"])
    P.op("act", lambda e: e.activation(out=DT, in_=PS[7][:, 0:128], func=AF.Exp), reads=["bank7"], writes=["DT"])
    ZR = A.f32(NDG)
    ZI = A.f32(NDG)
    P.op("dve", lambda e: e.tensor_tensor(out=ZR, in0=LR, in1=DT, op=ALU.mult), reads=["LR", "DT"], writes=["ZR"])
    P.op("dve", lambda e: e.tensor_tensor(out=ZI, in0=LI, in1=DT, op=ALU.mult), reads=["LI", "DT"], writes=["ZI"])
    NP_ = 17
    PWr = A.f32(NP_ * NDG).rearrange("p (i g) -> p i g", i=NP_)
    PWi = A.f32(NP_ * NDG).rearrange("p (i g) -> p i g", i=NP_)
    Fr = A.f32(NDG)
    Fi = A.f32(NDG)
    RHO = A.f32(NDG)
    rho16 = A.f32(NDG)
    mS1 = A.mark()
    nvec = A.f32(NP_)
    for i in range(NP_):
        P.op("pool", lambda e, i=i: e.memset(nvec[:, i:i + 1], float(i - 8)), writes=["nvec"])
    ang = A.f32(NP_ * NDG).rearrange("p (i g) -> p i g", i=NP_)
    mag = A.f32(NP_ * NDG).rearrange("p (i g) -> p i g", i=NP_)
    nb3 = nvec.unsqueeze(2).to_broadcast([128, NP_, NDG])
    P.op("dve", lambda e: e.tensor_tensor(out=ang, in0=ZI.unsqueeze(1).to_broadcast([128, NP_, NDG]), in1=nb3, op=ALU.mult), reads=["ZI", "nvec"], writes=["anga"])
    P.op("dve", lambda e: e.tensor_tensor(out=mag, in0=ZR.unsqueeze(1).to_broadcast([128, NP_, NDG]), in1=nb3, op=ALU.mult), reads=["ZR", "nvec"], writes=["mag"])
    P.op("act", lambda e: e.activation(out=mag, in_=mag, func=AF.Exp), reads=["mag"], writes=["mag"])
    sn = A.f32(NP_ * NDG).rearrange("p (i g) -> p i g", i=NP_)
    cs = A.f32(NP_ * NDG).rearrange("p (i g) -> p i g", i=NP_)
    f2 = lambda t: t.rearrange("p i g -> p (i g)")
    m1 = A.mark()
    sincos(C, f2(ang), NP_ * NDG, f2(sn), f2(cs), "ang")
    P.op("dve", lambda e: e.tensor_tensor(out=PWr, in0=mag, in1=cs, op=ALU.mult), reads=["mag", "ango"], writes=["PWr"])
    P.op("dve", lambda e: e.tensor_tensor(out=PWi, in0=mag, in1=sn, op=ALU.mult), reads=["mag", "ango"], writes=["PWi"])
    den = A.f32(NDG)
    t0_ = A.f32(NDG)
    a_re, a_im = PWr[:, 9, :], PWi[:, 9, :]
    P.op("dve", lambda e: e.tensor_tensor(out=den, in0=LR, in1=LR, op=ALU.mult), reads=["LR"], writes=["den"])
    P.op("dve", lambda e: e.tensor_tensor(out=t0_, in0=LI, in1=LI, op=ALU.mult), reads=["LI"], writes=["t0_"])
    P.op("dve", lambda e: e.tensor_tensor(out=den, in0=den, in1=t0_, op=ALU.add), reads=["den", "t0_"], writes=["den"])
    P.op("dve", lambda e: e.reciprocal(out=den, in_=den), reads=["den"], writes=["den"])
    nr = A.f32(NDG)
    P.op("dve", lambda e: e.tensor_scalar(out=nr, in0=a_re, scalar1=-1.0, scalar2=None, op0=ALU.add), reads=["PWr"], writes=["nr"])
    P.op("dve", lambda e: e.tensor_tensor(out=Fr, in0=nr, in1=LR, op=ALU.mult), reads=["nr", "LR"], writes=["Fr"])
    P.op("dve", lambda e: e.tensor_tensor(out=t0_, in0=a_im, in1=LI, op=ALU.mult), reads=["PWi", "LI", "den"], writes=["t0_"])
    P.op("dve", lambda e: e.tensor_tensor(out=Fr, in0=Fr, in1=t0_, op=ALU.add), reads=["Fr", "t0_"], writes=["Fr"])
    P.op("dve", lambda e: e.tensor_tensor(out=Fr, in0=Fr, in1=den, op=ALU.mult), reads=["Fr", "den"], writes=["Fr"])
    P.op("dve", lambda e: e.tensor_tensor(out=Fi, in0=a_im, in1=LR, op=ALU.mult), reads=["PWi", "LR"], writes=["Fi"])
    P.op("dve", lambda e: e.tensor_tensor(out=t0_, in0=nr, in1=LI, op=ALU.mult), reads=["nr", "LI", "Fr"], writes=["t0_"])
    P.op("dve", lambda e: e.tensor_tensor(out=Fi, in0=Fi, in1=t0_, op=ALU.subtract), reads=["Fi", "t0_"], writes=["Fi"])
    P.op("dve", lambda e: e.tensor_tensor(out=Fi, in0=Fi, in1=den, op=ALU.mult), reads=["Fi", "den"], writes=["Fi"])
    P.op("dve", lambda e: e.tensor_copy(out=RHO, in_=mag[:, 16, :]), reads=["mag"], writes=["RHO"])
    P.op("act", lambda e: e.activation(out=rho16, in_=ZR, func=AF.Exp, scale=128.0), reads=["ZR"], writes=["rho16"])
    P.barrier()
    A.release(mS1)
    ZI2 = A.f32(NDG)
    P.op("dve", lambda e: e.tensor_tensor(out=ZI2, in0=LI, in1=DT, op=ALU.mult), writes=["ZI2"])
    mS2 = A.mark()
    jv = A.f32(48)
    for j in range(16):
        P.op("pool", lambda e, j=j: e.memset(jv[:, j:j + 1], 8.0 * j), writes=["jv"])
    for s_ in range(32):
        P.op("pool", lambda e, s_=s_: e.memset(jv[:, 16 + s_:17 + s_], 128.0 * s_), writes=["jv"])
    TT = A.f32(64 * 128).rearrange("p (g c) -> p g c", c=128)
    a48 = A.f32(64 * 48).rearrange("p (g j) -> p g j", j=48)
    s48 = A.f32(64 * 48).rearrange("p (g j) -> p g j", j=48)
    c48 = A.f32(64 * 48).rearrange("p (g j) -> p g j", j=48)
    g2 = lambda t: t.rearrange("p g j -> p (g j)")
    mS3 = A.mark()
    for d in range(2):
        ds_ = slice(d * 64, (d + 1) * 64)
        P.op("pool", lambda e: e.memset(TT, 0.0), reads=["TT"], writes=["TT"])
        P.op("dve", lambda e, ds_=ds_: e.tensor_tensor(out=a48, in0=ZI2[:, ds_].unsqueeze(2).to_broadcast([128, 64, 48]), in1=jv.unsqueeze(1).to_broadcast([128, 64, 48]), op=ALU.mult),
             reads=["ZI2", "jv", "a48o"], writes=["a48a"])
        sincos(C, g2(a48), 64 * 48, g2(s48), g2(c48), "a48")
        A.release(mS3)
        if d == 0:
            P.op("dve", lambda e: e.tensor_copy(out=TT[:, :, 0:16], in_=c48[:, :, 0:16]), reads=["a48o", "TT"], writes=["TT"])
            P.op("dve", lambda e: e.tensor_copy(out=TT[:, :, 16:32], in_=s48[:, :, 0:16]), reads=["a48o", "TT"], writes=["TT"])
        else:
            P.op("dve", lambda e: e.tensor_copy(out=TT[:, :, 0:16], in_=c48[:, :, 15::-1]), reads=["a48o", "TT"], writes=["TT"])
            P.op("dve", lambda e: e.tensor_copy(out=TT[:, :, 16:32], in_=s48[:, :, 15::-1]), reads=["a48o", "TT"], writes=["TT"])
        P.op("dve", lambda e: e.tensor_copy(out=TT[:, :, 48:80], in_=c48[:, :, 16:48]), reads=["a48o", "TT"], writes=["TT"])
        P.op("dve", lambda e: e.tensor_copy(out=TT[:, :, 80:112], in_=s48[:, :, 16:48]), reads=["a48o", "TT"], writes=["TT"])
        P.op("dve", lambda e, ds_=ds_: e.tensor_copy(out=TT[:, :, 33:48], in_=RHO[:, ds_].unsqueeze(2).to_broadcast([128, 64, 15])), reads=["RHO", "TT"], writes=["TT"])
        P.op("dve", lambda e, ds_=ds_: e.tensor_copy(out=TT[:, :, 112], in_=RHO[:, ds_]), reads=["RHO", "TT"], writes=["TT"])
        P.op("dve", lambda e, ds_=ds_: e.tensor_copy(out=TT[:, :, 113], in_=rho16[:, ds_]), reads=["rho16", "TT"], writes=["TT"])
        P.dma(C.s5t[:, :, d, :].rearrange("g p c -> p g c"), TT, reads=["TT"], writes=["s5t"], q=("sp" if d == 0 else "act"))
    P.barrier()
    A.release(mS2)
    CTr = A.f32(NDG * 16).rearrange("p (g h) -> p g h", h=16)
    CTi = A.f32(NDG * 16).rearrange("p (g h) -> p g h", h=16)
    BTr = A.f32(NDG * 16).rearrange("p (g h) -> p g h", h=16)
    BTi = A.f32(NDG * 16).rearrange("p (g h) -> p g h", h=16)
    for (src, dst, key) in ((C.s5_c_re, CTr, "CTr"), (C.s5_c_im, CTi, "CTi")):
        rows = src.rearrange("d g h p -> (d g h) p")
        for blk in range(16):
            stg = A.f32(128)
            sk = A.key("cstg")
            P.dma(stg[:, 0:64], rows[blk * 128:(blk + 1) * 128, :], writes=[sk], q=("sp" if blk % 2 == 0 else "act"))
            P.dma(stg[:, 64:128], rows[blk * 128:(blk + 1) * 128, :], writes=[sk + "b"], q=("act" if blk % 2 == 0 else "sp"))
            bk = f"bank{6 + blk % 2}"
            ps = PS[6 + blk % 2][:, 0:128]
            P.op("pe", lambda e, ps=ps, stg=stg: e.transpose(ps, stg, C.ident), reads=[sk, sk + "b"], writes=[bk])
            P.op("dve", lambda e, ps=ps, dst=dst, blk=blk: e.tensor_copy(out=dst[:, blk * 8:(blk + 1) * 8, :], in_=ps.rearrange("p (g h) -> p g h", h=16)),
                 reads=[bk], writes=[key])
    for (src, dst, key) in ((C.s5_b_re, BTr, "BTr"), (C.s5_b_im, BTi, "BTi")):
        v = src.rearrange("d g p h -> p (d g) h")
        for half in range(2):
            for q4 in range(4):
                P.dma(dst[half * 64:(half + 1) * 64, q4 * 32:(q4 + 1) * 32, :], v[:, q4 * 32:(q4 + 1) * 32, :], writes=[key + f"{half}{q4}"],
                      q=("sp" if (half + q4) % 2 == 0 else "act"))
    bt_keys = lambda k: [k + f"{h}{q}" for h in range(2) for q in range(4)]
    BBr = A.f32(NDG * 16).rearrange("p (g h) -> p g h", h=16)
    BBi = A.f32(NDG * 16).rearrange("p (g h) -> p g h", h=16)
    tq = A.f32(NDG * 16).rearrange("p (g h) -> p g h", h=16)
    frb = Fr.unsqueeze(2).to_broadcast([128, NDG, 16])
    fib = Fi.unsqueeze(2).to_broadcast([128, NDG, 16])
    P.op("dve", lambda e: e.tensor_tensor(out=BBr, in0=BTr, in1=frb, op=ALU.mult), reads=bt_keys("BTr") + ["Fr"], writes=["BBr"])
    P.op("dve", lambda e: e.tensor_tensor(out=tq, in0=BTi, in1=fib, op=ALU.mult), reads=bt_keys("BTi") + ["Fi"], writes=["tq"])
    P.op("dve", lambda e: e.tensor_tensor(out=BBr, in0=BBr, in1=tq, op=ALU.subtract), reads=["BBr", "tq"], writes=["BBr"])
    P.op("dve", lambda e: e.tensor_tensor(out=BBi, in0=BTi, in1=frb, op=ALU.mult), reads=bt_keys("BTi") + ["Fr"], writes=["BBi"])
    P.op("dve", lambda e: e.tensor_tensor(out=tq, in0=BTr, in1=fib, op=ALU.mult), reads=bt_keys("BTr") + ["Fi", "BBr"], writes=["tq"])
    P.op("dve", lambda e: e.tensor_tensor(out=BBi, in0=BBi, in1=tq, op=ALU.add), reads=["BBi", "tq"], writes=["BBi"])
    dst_ = A.f32(128, parts=64)
    for t in range(8):
        P.dma(dst_[:, t * 16:(t + 1) * 16], C.s5_d, writes=[f"dst{t}"], q=("sp" if t % 2 == 0 else "act"))
    dcol = A.f32(64)
    P.op("pe", lambda e: e.transpose(PS[7][:, 0:64], dst_, C.ident[0:64, 0:64]), reads=[f"dst{t}" for t in range(8)], writes=["bank7"])
    P.op("dve", lambda e: e.tensor_copy(out=dcol, in_=PS[7][:, 0:64]), reads=["bank7"], writes=["dcol"])
    mf = A.f32(128)
    mb = A.f32(128)
    P.dma(mf, C.s5_masks[0], writes=["mf"])
    P.dma(mb, C.s5_masks[1], writes=["mb"])
    GC = 8
    al = [A.f32(NDG * 8).rearrange("p (g t) -> p g t", t=8) for _ in range(8)]
    def gather(out_t, PW, idx_f, idx_b, sgn_top, sgn_bot, key, rk):
        for (g0, idx) in ((0, idx_f), (64, idx_b)):
            for t in range(8):
                for (p0, sg) in ((0, sgn_top), (64, sgn_bot)):
                    src_t = PW[0][p0:p0 + 64, idx[t] + 8, g0:g0 + 64] if sg[1] == "r" else PW[1][p0:p0 + 64, idx[t] + 8, g0:g0 + 64]
                    P.op("pool", lambda e, out_t=out_t, src_t=src_t, p0=p0, g0=g0, t=t, sg=sg: e.tensor_scalar(
                        out=out_t[p0:p0 + 64, g0:g0 + 64, t], in0=src_t, scalar1=float(sg[0]), scalar2=None, op0=ALU.mult), reads=rk, writes=[key])
    PW = (PWr, PWi)
    pk = ["PWr", "PWi"]
    tf = [t + 1 for t in range(8)]
    tb_ = [8 - t for t in range(8)]
    gather(al[0], PW, tf, tb_, (1, "r"), (-1, "i"), "al0", pk)
    gather(al[1], PW, tf, tb_, (-1, "i"), (-1, "r"), "al1", pk)
    gather(al[2], PW, tf, tb_, (-1, "i"), (-1, "r"), "al2", pk)
    gather(al[3], PW, tf, tb_, (-1, "r"), (1, "i"), "al3", pk)
    xin_f = [7 - t for t in range(8)]
    xin_b = [t for t in range(8)]
    gather(al[4], PW, xin_f, xin_b, (1, "r"), (1, "i"), "al4", pk)
    gather(al[5], PW, xin_f, xin_b, (-1, "i"), (1, "r"), "al5", pk)
    xtp_f = [-1 - t for t in range(8)]
    xtp_b = [t - 8 for t in range(8)]
    gather(al[6], PW, xtp_f, xtp_b, (1, "r"), (1, "i"), "al6", pk)
    gather(al[7], PW, xtp_f, xtp_b, (-1, "i"), (1, "r"), "al7", pk)
    gen = [A.bf16(GC * 128).rearrange("p (g t h) -> p g t h", g=GC, t=8) for _ in range(4)]
    tm1 = A.f32(GC * 128).rearrange("p (g t h) -> p g t h", g=GC, t=8)
    tm2 = A.f32(GC * 128).rearrange("p (g t h) -> p g t h", g=GC, t=8)
    wstage = [A.bf16(1024) for _ in range(2)]
    tpf = A.f32(128)
    tpb = A.f32(128)
    specs = ((CTr, CTi, 0, 1, ["CTr", "CTi"]), (CTr, CTi, 2, 3, ["CTr", "CTi"]), (BBr, BBi, 4, 5, ["BBr", "BBi"]), (BBr, BBi, 6, 7, ["BBr", "BBi"]))
    GEN = {}
    for d in range(2):
        for gc in range(64 // GC):
            bg_step(C, 3)
            dg0 = d * 64 + gc * GC
            for gi_, (Sr, Si, ia, ib, rk) in enumerate(specs):
                eng = "dve" if gi_ % 2 == 0 else "pool"
                srb = Sr[:, dg0:dg0 + GC, :].unsqueeze(2).to_broadcast([128, GC, 8, 16])
                sib = Si[:, dg0:dg0 + GC, :].unsqueeze(2).to_broadcast([128, GC, 8, 16])
                aa = al[ia][:, dg0:dg0 + GC, :].unsqueeze(3).to_broadcast([128, GC, 8, 16])
                bb = al[ib][:, dg0:dg0 + GC, :].unsqueeze(3).to_broadcast([128, GC, 8, 16])
                gk = f"gen{gi_}"
                P.op(eng, lambda e, srb=srb, aa=aa: e.tensor_tensor(out=tm1, in0=srb, in1=aa, op=ALU.mult), reads=rk + [f"al{ia}"], writes=["tm1"])
                P.op(eng, lambda e, sib=sib, bb=bb: e.tensor_tensor(out=tm2, in0=sib, in1=bb, op=ALU.mult), reads=rk + [f"al{ib}"], writes=["tm2"])
                P.op(eng, lambda e, gi_=gi_: e.tensor_tensor(out=gen[gi_], in0=tm1, in1=tm2, op=ALU.add), reads=["tm1", "tm2"], writes=[gk])
            for gl in range(GC):
                g = gc * GC + gl
                base = 128 + d * 448
                W1g = gen[0][:, gl].rearrange("p t h -> p (t h)")
                W2g = gen[1][:, gl].rearrange("p t h -> p (t h)")
                Xin = gen[2][:, gl].rearrange("p t h -> p (t h)")
                Xtp = gen[3][:, gl].rearrange("p t h -> p (t h)")
                ws = wstage[(d * 64 + g) % 2]
                wk = f"ws{(d * 64 + g) % 2}"
                pst = PS[0][:, 0:64].bitcast(BF16)
                P.op("pe", lambda e, pst=pst, Xin=Xin: e.transpose(pst, Xin, C.identb), reads=["gen2"], writes=["bank0"])
                P.op("act", lambda e, ws=ws, pst=pst: e.copy(out=ws[:, 0:128], in_=pst), reads=["bank0"], writes=[wk + "a"])
                P.op("act", lambda e, ws=ws, pst=pst: e.activation(out=ws[:, 128:192], in_=pst[:, 0:64], func=AF.Identity, scale=-1.0), reads=["bank0"], writes=[wk + "b"])
                P.op("dve", lambda e, ws=ws, W1g=W1g: e.tensor_copy(out=ws[:, 192:320], in_=W1g), reads=["gen0"], writes=[wk + "c"])
                P.op("dve", lambda e, ws=ws, W2g=W2g: e.tensor_copy(out=ws[:, 320:448], in_=W2g), reads=["gen1"], writes=[wk + "d"])
                P.dma(C.s5w[g, :, base:base + 448], ws[:, 0:448], reads=[wk + "a", wk + "b", wk + "c", wk + "d"], writes=["s5w"], q=("sp" if g % 2 == 0 else "act"))
                ptp = PS[1 + d][:, 0:128]
                P.op("pe", lambda e, ptp=ptp, Xtp=Xtp, W1g=W1g: e.matmul(ptp, lhsT=Xtp, rhs=W1g, start=True, stop=True), reads=["gen3", "gen0"], writes=[f"bank{1 + d}"])
                P.op("dve", lambda e, ptp=ptp, d=d: e.tensor_tensor(out=(tpf if d == 0 else tpb), in0=ptp, in1=(mf if d == 0 else mb), op=ALU.mult),
                     reads=[f"bank{1 + d}", "mf", "mb"], writes=["tpx"])
                P.dma(C.s5tp[d, g], (tpf if d == 0 else tpb), reads=["tpx"], writes=["s5tp"], q="sp")
    ta = [A.f32(128) for _ in range(2)]
    tb2 = [A.f32(128) for _ in range(2)]
    tob = [A.bf16(128) for _ in range(2)]
    for g in range(64):
        i = g % 2
        if g % 2 == 0:
            bg_step(C, 1)
        P.dma(ta[i], C.s5tp[0, g], reads=["s5tp"], writes=[f"ta{i}"], q="sp")
        P.dma(tb2[i], C.s5tp[1, g], reads=["s5tp"], writes=[f"tb{i}"], q="act")
        P.op("dve", lambda e, i=i: e.tensor_tensor(out=ta[i], in0=ta[i], in1=tb2[i], op=ALU.add), reads=[f"ta{i}", f"tb{i}"], writes=[f"ta{i}"])
        P.op("dve", lambda e, i=i, g=g: e.scalar_tensor_tensor(out=tob[i], in0=C.ident, scalar=dcol[:, g:g + 1], in1=ta[i], op0=ALU.mult, op1=ALU.add),
             reads=[f"ta{i}", "dcol"], writes=[f"tob{i}"])
        P.dma(C.s5w[g, :, 0:128], tob[i], reads=[f"tob{i}"], writes=["s5w"], q="sp")
    A.release(m0)
    P.barrier()


def s5_phase(C):
    P, A, nc = C.P, C.A, C.nc
    PS = C.psum
    l = 0
    seqs = C.seqs
    m0 = A.mark()
    sc1 = A.f32(D)
    sh = A.f32(D)
    xc = A.f32(8 * D).rearrange("p (t c) -> p t c", t=8)
    tmp = A.f32(D)
    hperm = A.bf16(8 * D).rearrange("p (g t h) -> p g t h", g=64, t=8)
    ust = [A.bf16(64 * 128).rearrange("p (g c) -> p g c", g=64) for _ in range(2)]
    cnt = 0
    for sq in seqs:
        ci = sq.ci
        bc_load(C, sh, C.modd[l, ci, 0:D], "sh")
        bc_load(C, sc1, C.modd[l, ci, D:2 * D], "sc1")
        P.op("pool", lambda e: e.tensor_scalar(out=sc1, in0=sc1, scalar1=1.0, scalar2=None, op0=ALU.add), reads=["sc1"], writes=["sc1"])
        for cb in range(sq.n // 1024):
            P.dma(xc, sq.x0[cb * 1024:(cb + 1) * 1024, :].rearrange("(c t) d -> c t d", t=8), writes=["xc"], q="sp")
            for t in range(8):
                P.op("dve", lambda e, t=t: e.tensor_tensor(out=tmp, in0=xc[:, t, :], in1=sc1, op=ALU.mult), reads=["xc", "sc1"], writes=["tmp"])
                P.op("dve", lambda e, t=t: e.tensor_tensor(out=hperm[:, :, t, :], in0=tmp.rearrange("p (g h) -> p g h", h=16),
                                                             in1=sh.rearrange("p (g h) -> p g h", h=16), op=ALU.add), reads=["tmp", "sh"], writes=["hperm"])
            us = ust[cnt % 2]
            uk = f"ust{cnt % 2}"
            cnt += 1
            for g4 in range(16):
                pT = PS[g4 % 2][:, 0:256].bitcast(BF16)
                bk = f"bank{g4 % 2}"
                for gi_ in range(4):
                    g = g4 * 4 + gi_
                    P.op("pe", lambda e, pT=pT, g=g, gi_=gi_: e.transpose(pT[:, gi_ * 128:(gi_ + 1) * 128], hperm[:, g].rearrange("p t h -> p (t h)"), C.identb),
                         reads=["hperm"], writes=[bk])
                if g4 % 2 == 0:
                    P.op("act", lambda e, pT=pT, us=us, g4=g4: e.copy(out=us[:, g4 * 4:(g4 + 1) * 4, :], in_=pT.rearrange("p (g c) -> p g c", g=4)), reads=[bk], writes=[uk + "a"])
                else:
                    P.op("dve", lambda e, pT=pT, us=us, g4=g4: e.tensor_copy(out=us[:, g4 * 4:(g4 + 1) * 4, :], in_=pT.rearrange("p (g c) -> p g c", g=4)), reads=[bk], writes=[uk + "d"])
            P.dma(sq.U[cb], us, reads=[uk + "a", uk + "d"], writes=["Uscr"], q="sp")
    A.release(m0)
    P.barrier()
    m0 = A.mark()
    Jm = A.f32(128)
    P.op("dve", lambda e: e.tensor_scalar(out=Jm[:, 0:64], in0=C.ident[:, 64:128], scalar1=-1.0, scalar2=None, op0=ALU.mult), writes=["Jm"])
    P.op("dve", lambda e: e.tensor_copy(out=Jm[:, 64:128], in_=C.ident[:, 0:64]), reads=["Jm"], writes=["Jm"])
    NCM = 512
    wg_r = [A.bf16(1024) for _ in range(2)]
    tg_r = [A.f32(256).rearrange("p (d c) -> p d c", d=2) for _ in range(2)]
    ug_r = [A.bf16(NCM) for _ in range(2)]
    t1_r = [A.f32(NCM) for _ in range(2)]
    t2_r = [A.f32(NCM) for _ in range(2)]
    bp_r = [A.f32(NCM) for _ in range(2)]
    mf_r = [A.f32(NCM) for _ in range(2)]
    ql_r = [A.f32(NCM) for _ in range(2)]
    qq_r = [A.f32(NCM) for _ in range(2)]
    cq_r = [A.bf16(NCM + 2) for _ in range(2)]
    sq_r = [A.bf16(NCM + 2) for _ in range(2)]
    zg_r = [A.bf16(NCM) for _ in range(2)]
    ec_r = [A.f32(32) for _ in range(2)]
    e1_r = [A.f32(32) for _ in range(2)]
    e2_r = [A.f32(32) for _ in range(2)]
    hin_r = [A.f32(34) for _ in range(2)]
    r16_r = [A.f32(32) for _ in range(2)]
    gg_r = [A.f32(32) for _ in range(2)]
    it = 0
    for sq in seqs:
        n_c = sq.n // 8
        n_b = n_c // 16
        ncb = sq.n // 1024
        for g in range(64):
            wg = wg_r[g % 2]
            tg = tg_r[g % 2]
            ug = ug_r[g % 2]
            wk, tk, uk = f"wg{g % 2}", f"tg{g % 2}", f"ug{g % 2}"
            P.dma(wg, C.s5w[g], writes=[wk], q="sp")
            P.dma(tg, C.s5t[g], writes=[tk], q="sp")
            P.dma(ug[:, 0:n_c].rearrange("p (b c) -> p b c", c=128), sq.U[:, :, g, :].rearrange("b p c -> p b c"), writes=[uk], q="sp")
            py = PS[4 + g % 2][:, 0:n_c]
            pyk = f"bank{4 + g % 2}"
            P.op("pe", lambda e, py=py, wg=wg, ug=ug, n_c=n_c: e.matmul(py, lhsT=wg[:, 0:128], rhs=ug[:, 0:n_c], start=True, stop=False), reads=[wk, uk], writes=[pyk])
            def chain(d, i2):
                base = 128 + d * 448
                pX = PS[0 + 2 * i2][:, 0:n_c]
                pXt = PS[1 + 2 * i2][:, 0:n_c]
                kX, kXt = f"bank{0 + 2 * i2}", f"bank{1 + 2 * i2}"
                P.op("pe", lambda e, pX=pX, wg=wg, ug=ug, base=base, n_c=n_c: e.matmul(pX, lhsT=wg[:, base:base + 128], rhs=ug[:, 0:n_c], start=True, stop=True),
                     reads=[wk, uk], writes=[kX])
                yield
                P.op("pe", lambda e, pXt=pXt, wg=wg, ug=ug, base=base, n_c=n_c: e.matmul(pXt, lhsT=wg[:, base + 64:base + 192], rhs=ug[:, 0:n_c], start=True, stop=True),
                     reads=[wk, uk], writes=[kXt])
                yield
                t1, t2, bp, mfu, ql, qq = t1_r[i2], t2_r[i2], bp_r[i2], mf_r[i2], ql_r[i2], qq_r[i2]
                ec, e1, e2, hin, r16, gg_ = ec_r[i2], e1_r[i2], e2_r[i2], hin_r[i2], r16_r[i2], gg_r[i2]
                kec, ke1, ke2, khin, kr16, kgg = f"ec{i2}", f"e1{i2}", f"e2{i2}", f"hin{i2}", f"r16{i2}", f"gg{i2}"
                cq, sqq = cq_r[i2], sq_r[i2]
                k1, k2, kb, km, kl_, kq, kc, ks = f"t1{i2}", f"t2{i2}", f"bp{i2}", f"mf{i2}", f"ql{i2}", f"qq{i2}", f"cq{i2}", f"sq{i2}"
                v3 = lambda t_, n_c=n_c: t_[:, 0:n_c].rearrange("p (b j) -> p b j", j=16)
                cosb = tg[:, d, 0:16].unsqueeze(1).to_broadcast([128, n_b, 16])
                sinb = tg[:, d, 16:32].unsqueeze(1).to_broadcast([128, n_b, 16])
                mrow = tg[:, d, 32:48].unsqueeze(1).to_broadcast([128, n_b, 16])
                cos2 = tg[:, d, 48:48 + n_b]
                sin2 = tg[:, d, 80:80 + n_b]
                rho = tg[:, d, 112:113]
                rho16 = tg[:, d, 113:114]
                P.op("dve", lambda e, t1=t1, pX=pX, cosb=cosb, v3=v3: e.tensor_tensor(out=v3(t1), in0=pX.rearrange("p (b j) -> p b j", j=16), in1=cosb, op=ALU.mult),
                     reads=[kX, tk], writes=[k1])
                yield
                P.op("dve", lambda e, t2=t2, pXt=pXt, sinb=sinb, v3=v3: e.tensor_tensor(out=v3(t2), in0=pXt.rearrange("p (b j) -> p b j", j=16), in1=sinb, op=ALU.mult),
                     reads=[kXt, tk], writes=[k2])
                yield
                P.op("pool", lambda e, bp=bp, t1=t1, t2=t2, n_c=n_c: e.tensor_tensor(out=bp[:, 0:n_c], in0=t1[:, 0:n_c], in1=t2[:, 0:n_c], op=ALU.add), reads=[k1, k2], writes=[kb])
                yield
                P.op("act", lambda e, mfu=mfu, mrow=mrow, v3=v3: e.copy(out=v3(mfu), in_=mrow), reads=[tk], writes=[km])
                yield
                P.op("pool", lambda e, rho16=rho16, n_b=n_b: e.tensor_copy(out=r16[:, 0:n_b], in_=rho16.to_broadcast([128, n_b])), reads=[tk], writes=[kr16])
                yield
                rv = (lambda a_, n_c=n_c: a_[:, 0:n_c]) if d == 0 else (lambda a_, n_c=n_c: a_[:, n_c - 1::-1])
                P.op("dve", lambda e, ql=ql, mfu=mfu, bp=bp, rv=rv, n_c=n_c: e.tensor_tensor_scan(out=rv(ql), data0=mfu[:, 0:n_c], data1=rv(bp), initial=0.0, op0=ALU.mult, op1=ALU.add),
                     reads=[km, kb], writes=[kl_])
                yield
                if d == 0:
                    e_src = ql[:, 15:n_c:16]
                    b_inj = bp[:, 0:n_c:16]
                else:
                    e_src = ql[:, n_c - 16::-16]
                    b_inj = bp[:, n_c - 1::-16]
                P.op("dve", lambda e, e_src=e_src, n_b=n_b: e.tensor_copy(out=ec[:, 0:n_b], in_=e_src), reads=[kl_], writes=[kec])
                yield
                P.op("pe", lambda e, n_b=n_b: e.matmul(PS[6][:, d * 64:d * 64 + n_b], lhsT=Jm, rhs=ec[:, 0:n_b], start=True, stop=True), reads=["Jm", kec], writes=["bank6"])
                yield
                P.op("dve", lambda e, cos2=cos2, n_b=n_b: e.tensor_tensor(out=e1[:, 0:n_b], in0=ec[:, 0:n_b], in1=cos2, op=ALU.mult), reads=[kec, tk], writes=[ke1])
                yield
                P.op("dve", lambda e, sin2=sin2, n_b=n_b: e.tensor_tensor(out=e2[:, 0:n_b], in0=PS[6][:, d * 64:d * 64 + n_b], in1=sin2, op=ALU.mult), reads=["bank6", tk], writes=[ke2])
                yield
                P.op("dve", lambda e, n_b=n_b: e.tensor_tensor(out=e1[:, 0:n_b], in0=e1[:, 0:n_b], in1=e2[:, 0:n_b], op=ALU.subtract), reads=[ke1, ke2], writes=[ke1])
                yield
                P.op("dve", lambda e: e.memset(hin[:, 0:1], 0.0), writes=[khin])
                yield
                P.op("dve", lambda e, n_b=n_b: e.tensor_tensor_scan(out=hin[:, 1:n_b + 1], data0=r16[:, 0:n_b], data1=e1[:, 0:n_b], initial=0.0, op0=ALU.mult, op1=ALU.add),
                     reads=[kr16, ke1, khin], writes=[khin])
                yield
                P.op("pe", lambda e, n_b=n_b: e.matmul(PS[7][:, d * 64:d * 64 + n_b], lhsT=Jm, rhs=hin[:, 0:n_b], start=True, stop=True), reads=["Jm", khin], writes=["bank7"])
                yield
                P.op("dve", lambda e, cos2=cos2, n_b=n_b: e.tensor_tensor(out=gg_[:, 0:n_b], in0=hin[:, 0:n_b], in1=cos2, op=ALU.mult), reads=[khin, tk], writes=[kgg])
                yield
                P.op("dve", lambda e, sin2=sin2, n_b=n_b: e.tensor_tensor(out=e2[:, 0:n_b], in0=PS[7][:, d * 64:d * 64 + n_b], in1=sin2, op=ALU.mult), reads=["bank7", tk, ke1], writes=[ke2])
                yield
                P.op("dve", lambda e, n_b=n_b: e.tensor_tensor(out=gg_[:, 0:n_b], in0=gg_[:, 0:n_b], in1=e2[:, 0:n_b], op=ALU.add), reads=[kgg, ke2], writes=[kgg])
                yield
                P.op("dve", lambda e, b_inj=b_inj, rho=rho, n_b=n_b: e.scalar_tensor_tensor(out=b_inj, in0=gg_[:, 0:n_b], scalar=rho, in1=b_inj, op0=ALU.mult, op1=ALU.add),
                     reads=[kgg, tk, kb, kl_], writes=[kb])
                yield
                P.op("dve", lambda e, qq=qq, mfu=mfu, bp=bp, rv=rv, n_c=n_c: e.tensor_tensor_scan(out=rv(qq), data0=mfu[:, 0:n_c], data1=rv(bp), initial=0.0, op0=ALU.mult, op1=ALU.add),
                     reads=[km, kb], writes=[kq])
                yield
                o = 1 if d == 0 else 0
                zc = 0 if d == 0 else n_c
                P.op("pool", lambda e, cq=cq, zc=zc: e.memset(cq[:, zc:zc + 1], 0.0), reads=[kc], writes=[kc])
                yield
                P.op("pool", lambda e, sqq=sqq, zc=zc: e.memset(sqq[:, zc:zc + 1], 0.0), reads=[ks], writes=[ks])
                yield
                P.op("pool", lambda e, cq=cq, qq=qq, cosb=cosb, o=o, n_c=n_c: e.tensor_tensor(out=cq[:, o:o + n_c].rearrange("p (b j) -> p b j", j=16),
                                                                                              in0=qq[:, 0:n_c].rearrange("p (b j) -> p b j", j=16), in1=cosb, op=ALU.mult),
                     reads=[kq, tk, kc], writes=[kc])
                yield
                P.op("dve", lambda e, sqq=sqq, qq=qq, sinb=sinb, o=o, n_c=n_c: e.tensor_tensor(out=sqq[:, o:o + n_c].rearrange("p (b j) -> p b j", j=16),
                                                                                                in0=qq[:, 0:n_c].rearrange("p (b j) -> p b j", j=16), in1=sinb, op=ALU.mult),
                     reads=[kq, tk, ks], writes=[ks])
                yield
                shf = 0 if d == 0 else 1
                P.op("pe", lambda e, py=py, wg=wg, cq=cq, base=base, shf=shf, n_c=n_c: e.matmul(py, lhsT=wg[:, base + 192:base + 320], rhs=cq[:, shf:shf + n_c], start=False, stop=False),
                     reads=[wk, kc], writes=[pyk])
                yield
                P.op("pe", lambda e, py=py, wg=wg, sqq=sqq, base=base, shf=shf, n_c=n_c, d=d: e.matmul(py, lhsT=wg[:, base + 320:base + 448], rhs=sqq[:, shf:shf + n_c], start=False, stop=(d == 1)),
                     reads=[wk, ks], writes=[pyk])
                yield
            for _ in itertools.zip_longest(chain(0, 0), chain(1, 1)):
                pass
            zg = zg_r[g % 2]
            zk = f"zg{g % 2}"
            P.op("act", lambda e, zg=zg, py=py, n_c=n_c: e.activation(out=zg[:, 0:n_c], in_=py, func=AF.Gelu_apprx_tanh), reads=[pyk], writes=[zk])
            P.dma(sq.Z[:, :, g, :].rearrange("t p c -> p t c"), zg[:, 0:n_c].rearrange("p (t c) -> p t c", c=64), reads=[zk], writes=["Zscr"], q="act")
    A.release(m0)
    P.barrier()
    m0 = A.mark()
    wglu = A.bf16(8 * 2048).rearrange("p (k c) -> p k c", k=8)
    P.dma(wglu, C.wglu_s, writes=["wglu"], q="act")
    bgl_f = A.f32(2048, parts=1)
    P.dma(bgl_f, C.s5_b_glu.rearrange("(o n) -> o n", o=1), writes=["bgl_f"])
    bgl = A.bf16(2048, parts=1)
    P.op("dve", lambda e: e.tensor_copy(out=bgl, in_=bgl_f), reads=["bgl_f"], writes=["bgl"])
    ones1 = A.bf16(128, parts=1)
    P.op("pool", lambda e: e.memset(ones1, 1.0), writes=["ones1"])
    lnG = A.f32(D)
    lnB = A.f32(D)
    bc_load(C, lnG, C.ln_g[l, 0], "lnG")
    bc_load(C, lnB, C.ln_b[l, 0], "lnB")
    gm = A.f32(D)
    xt = A.f32(4 * D).rearrange("p (b c) -> p b c", b=4)
    zt = A.bf16(64 * 64).rearrange("p (g c) -> p g c", g=64)
    zc_ = A.bf16(8 * D, parts=64).rearrange("p (t c) -> p t c", t=8)
    zf = A.bf16(8 * 512).rearrange("p (k t) -> p k t", k=8)
    ga = [A.f32(512) for _ in range(2)]
    aa_ = [A.f32(512) for _ in range(2)]
    st = A.f32(12).rearrange("p (a b) -> p a b", a=2)
    mv = A.f32(2)
    rs = A.f32(1)
    nmr = A.f32(1)
    for sq in seqs:
        ci = sq.ci
        bc_load(C, gm, C.modd[l, ci, 2 * D:3 * D], "gm")
        for T in range(sq.n // 512):
            t0 = T * 512
            P.dma(zt, sq.Z[T], writes=["zt"], q="sp")
            P.dma(xt, sq.x0[t0:t0 + 512, :].rearrange("(b p) c -> p b c", p=128), writes=["xt"], q="sp")
            for b in range(4):
                P.op("act", lambda e, b=b: e.activation(out=xt[:, b, :], in_=xt[:, b, :], func=AF.Copy, scale=ALPHA), reads=["xt"], writes=["xt"])
            for g8 in range(8):
                p1 = PS[g8 % 2][0:64, :].bitcast(BF16)
                bk = f"bank{g8 % 2}"
                for gi_ in range(8):
                    g = g8 * 8 + gi_
                    P.op("pe", lambda e, p1=p1, g=g, gi_=gi_: e.transpose(p1[:, gi_ * 128:(gi_ + 1) * 128], zt[:, g, :], C.identb), reads=["zt"], writes=[bk])
                eng = "act" if g8 % 2 == 0 else "dve"
                outv = zc_[:, :, g8 * 128:(g8 + 1) * 128].rearrange("p t (g h) -> p g t h", h=16)
                inv = p1.rearrange("p (g t h) -> p g t h", g=8, t=8)
                if eng == "act":
                    P.op("act", lambda e, outv=outv, inv=inv: e.copy(out=outv, in_=inv), reads=[bk], writes=[f"zc{g8}"])
                else:
                    P.op("dve", lambda e, outv=outv, inv=inv: e.tensor_copy(out=outv, in_=inv), reads=[bk], writes=[f"zc{g8}"])
            for k in range(8):
                p2 = PS[2 + k % 2][:, 0:256].bitcast(BF16)
                bk = f"bank{2 + k % 2}"
                for t in range(8):
                    P.op("pe", lambda e, p2=p2, t=t, k=k: e.transpose(p2[:, t * 64:(t + 1) * 64], zc_[:, t, k * 128:(k + 1) * 128], C.identb[0:64, 0:64]), reads=[f"zc{k}"], writes=[bk])
                outv = zf[:, k, :].rearrange("p (c t) -> p t c", t=8)
                inv = p2.rearrange("p (t c) -> p t c", t=8)
                if k % 2 == 0:
                    P.op("act", lambda e, outv=outv, inv=inv: e.copy(out=outv, in_=inv), reads=[bk], writes=[f"zf{k}"])
                else:
                    P.op("dve", lambda e, outv=outv, inv=inv: e.tensor_copy(out=outv, in_=inv), reads=[bk], writes=[f"zf{k}"])
            zf_keys = [f"zf{k}" for k in range(8)]
            for b in range(4):
                for hh in range(2):
                    pa = PS[4 + hh]
                    pg_ = PS[6 + hh]
                    for k in range(8):
                        P.op("pe", lambda e, pa=pa, k=k, b=b, hh=hh: e.matmul(pa, lhsT=zf[:, k, b * 128:(b + 1) * 128], rhs=wglu[:, k, hh * 512:(hh + 1) * 512], start=(k == 0), stop=False),
                             reads=[zf_keys[k], "wglu"], writes=[f"bank{4 + hh}"])
                    P.op("pe", lambda e, pa=pa, hh=hh: e.matmul(pa, lhsT=ones1, rhs=bgl[:, hh * 512:(hh + 1) * 512], start=False, stop=True), reads=["ones1", "bgl"], writes=[f"bank{4 + hh}"])
                    for k in range(8):
                        P.op("pe", lambda e, pg_=pg_, k=k, b=b, hh=hh: e.matmul(pg_, lhsT=zf[:, k, b * 128:(b + 1) * 128], rhs=wglu[:, k, 1024 + hh * 512:1024 + (hh + 1) * 512], start=(k == 0), stop=False),
                             reads=[zf_keys[k], "wglu"], writes=[f"bank{6 + hh}"])
                    P.op("pe", lambda e, pg_=pg_, hh=hh: e.matmul(pg_, lhsT=ones1, rhs=bgl[:, 1024 + hh * 512:1024 + (hh + 1) * 512], start=False, stop=True), reads=["ones1", "bgl"], writes=[f"bank{6 + hh}"])
                    g_t, a_t = ga[hh], aa_[hh]
                    P.op("act", lambda e, g_t=g_t, pg_=pg_: e.activation(out=g_t, in_=pg_, func=AF.Sigmoid), reads=[f"bank{6 + hh}"], writes=[f"ga{hh}"])
                    P.op("dve", lambda e, a_t=a_t, pa=pa, g_t=g_t: e.tensor_tensor(out=a_t, in0=pa, in1=g_t, op=ALU.mult), reads=[f"bank{4 + hh}", f"ga{hh}"], writes=[f"aa{hh}"])
                    P.op("dve", lambda e, a_t=a_t, hh=hh: e.tensor_tensor(out=a_t, in0=a_t, in1=gm[:, hh * 512:(hh + 1) * 512], op=ALU.mult), reads=[f"aa{hh}", "gm"], writes=[f"aa{hh}"])
                    xs = xt[:, b, hh * 512:(hh + 1) * 512]
                    P.op("dve", lambda e, a_t=a_t, xs=xs: e.tensor_tensor(out=xs, in0=xs, in1=a_t, op=ALU.add), reads=[f"aa{hh}", "xt"], writes=["xt"])
                layer_norm_block(C, xt[:, b, :], "xt", lnG, lnB, st, mv, rs, nmr)
            P.dma(sq.x1[t0:t0 + 512, :].rearrange("(b p) c -> p b c", p=128), xt, reads=["xt"], writes=["x1"], q="sp", is_output=(sq.x1 is sq.y))
    A.release(m0)
    P.barrier()


def na_phase(C, seq_io):
    P, A, nc = C.P, C.A, C.nc
    PS = C.psum
    l = 1
    m0 = A.mark()
    wqkv = A.bf16(8 * 3072).rearrange("p (k c) -> p k c", k=8)
    P.dma(wqkv, C.wqkv_s, writes=["wqkv"], q="act")
    bqk = A.f32(16)
    load_pp(C, C.na_b_qkv[0:2048].rearrange("(m p) -> m p", p=128), bqk, "bqk")
    sc1 = A.f32(D)
    sh = A.f32(D)
    xt_r = [A.f32(4 * D).rearrange("p (b c) -> p b c", b=4) for _ in range(2)]
    tmp = A.f32(D)
    hb_r = [A.bf16(4 * D).rearrange("p (b c) -> p b c", b=4) for _ in range(2)]
    hfm_r = [A.bf16(8 * 512).rearrange("p (k t) -> p k t", k=8) for _ in range(2)]
    qk_st = [A.bf16(512) for _ in range(3)]
    v_st = [A.bf16(1040).rearrange("p (h d) -> p h d", h=16) for _ in range(2)]
    for i in range(2):
        P.op("pool", lambda e, i=i: e.memset(v_st[i][:, :, 64:65], 1.0), writes=[f"vst{i}"])
    pT = PS[0][:, 0:256].bitcast(BF16)
    cnt = 0
    gtile = 0
    for (sq, xin, xout) in seq_io:
        n = sq.n
        ci = sq.ci
        bc_load(C, sh, C.modd[l, ci, 0:D], "sh")
        bc_load(C, sc1, C.modd[l, ci, D:2 * D], "sc1")
        P.op("pool", lambda e: e.tensor_scalar(out=sc1, in0=sc1, scalar1=1.0, scalar2=None, op0=ALU.add), reads=["sc1"], writes=["sc1"])
        g0 = gtile
        gtile += n // 512

        def prologue(T):
            t0 = T * 512
            i2 = (g0 + T) % 2
            xt, hb, hfm = xt_r[i2], hb_r[i2], hfm_r[i2]
            P.dma(xt, xin[t0:t0 + 512, :].rearrange("(b p) c -> p b c", p=128), writes=[f"xt{i2}"], q="sp")
            for b in range(4):
                P.op("dve", lambda e, b=b: e.tensor_tensor(out=tmp, in0=xt[:, b, :], in1=sc1, op=ALU.mult), reads=[f"xt{i2}", "sc1"], writes=["tmp"])
                P.op("dve", lambda e, b=b: e.tensor_tensor(out=hb[:, b, :], in0=tmp, in1=sh, op=ALU.add), reads=["tmp", "sh"], writes=[f"hb{i2}{b}"])
            for k in range(8):
                for b in range(4):
                    P.op("pe", lambda e, b=b, k=k: e.transpose(pT[:, b * 128:(b + 1) * 128], hb[:, b, k * 128:(k + 1) * 128], C.identb),
                         reads=[f"hb{i2}{b}"], writes=["bank0"])
                if k % 2 == 0:
                    P.op("act", lambda e, k=k: e.copy(out=hfm[:, k, :], in_=pT), reads=["bank0"], writes=[f"hfm{i2}{k}"])
                else:
                    P.op("dve", lambda e, k=k: e.tensor_copy(out=hfm[:, k, :], in_=pT), reads=["bank0"], writes=[f"hfm{i2}{k}"])

        prologue(0)
        for T in range(n // 512):
            t0 = T * 512
            i2 = (g0 + T) % 2
            hfm = hfm_r[i2]
            if T + 1 < n // 512:
                prologue(T + 1)
            hf_keys = [f"hfm{i2}{k}" for k in range(8)]
            for mt in range(16):
                j = cnt % 3
                cnt += 1
                pb = PS[1 + j]
                pk = f"bank{1 + j}"
                for k in range(8):
                    P.op("pe", lambda e, pb=pb, k=k, mt=mt: e.matmul(pb, lhsT=wqkv[:, k, mt * 128:(mt + 1) * 128], rhs=hfm[:, k, :], start=(k == 0), stop=(k == 7)),
                         reads=["wqkv", hf_keys[k]], writes=[pk])
                stt = qk_st[j]
                sk = f"qkst{j}"
                P.op("act", lambda e, stt=stt, pb=pb, mt=mt: e.activation(out=stt, in_=pb, func=AF.Identity, bias=bqk[:, mt:mt + 1], scale=1.0),
                     reads=[pk, "bqk"], writes=[sk])
                dst = (sq.qT if mt < 8 else sq.kT)[:, mt % 8, t0:t0 + 512]
                P.dma(dst, stt, reads=[sk], writes=["qkscr"], q="sp")
            for b in range(4):
                vs = v_st[b % 2]
                vk = f"vst{b % 2}"
                for half in range(2):
                    j = cnt % 3
                    cnt += 1
                    pb = PS[1 + j]
                    pk = f"bank{1 + j}"
                    for k in range(8):
                        P.op("pe", lambda e, pb=pb, k=k, b=b, half=half: e.matmul(pb, lhsT=hfm[:, k, b * 128:(b + 1) * 128],
                                                                                   rhs=wqkv[:, k, 2048 + half * 512:2048 + (half + 1) * 512], start=(k == 0), stop=(k == 7)),
                             reads=["wqkv", hf_keys[k]], writes=[pk])
                    P.op("dve", lambda e, vs=vs, pb=pb, half=half: e.tensor_copy(out=vs[:, half * 8:(half + 1) * 8, 0:64], in_=pb.rearrange("p (h d) -> p h d", h=8)),
                         reads=[pk], writes=[vk])
                P.dma(sq.vS[t0 + b * 128:t0 + (b + 1) * 128, :], vs.rearrange("p h d -> p (h d)"), reads=[vk], writes=["vscr"], q="sp")
    A.release(m0)
    P.barrier()
    m0 = A.mark()
    wo = A.bf16(16 * D, parts=64).rearrange("p (h c) -> p h c", h=16)
    P.dma(wo, C.wo_s, writes=["wo"], q="act")
    TAB = A.bf16(16 * 17 * 64).rearrange("p (h x) -> p h x", h=16)
    m1 = A.mark()
    tstg = [A.f32(2176) for _ in range(2)]
    for hh in range(8):
        ts_ = tstg[hh % 2]
        tk = f"tstg{hh % 2}"
        P.dma(ts_, C.na_tab[:, 2 * hh:2 * hh + 2, :].rearrange("p h x -> p (h x)"), writes=[tk], q="sp")
        P.op("act", lambda e, ts_=ts_, hh=hh: e.activation(out=TAB[:, 2 * hh:2 * hh + 2, :].rearrange("p h x -> p (h x)"), in_=ts_, func=AF.Exp),
             reads=[tk], writes=["TAB"])
    P.barrier()
    A.release(m1)
    lnG = A.f32(D)
    lnB = A.f32(D)
    bc_load(C, lnG, C.ln_g[l, 0], "lnG")
    bc_load(C, lnB, C.ln_b[l, 0], "lnB")
    bvT = A.f32(16, parts=64)
    load_pp(C, C.na_b_qkv[2048:3072].rearrange("(h d) -> h d", d=64), bvT, "bvT")
    bvTb = A.bf16(16, parts=64)
    P.op("dve", lambda e: e.tensor_copy(out=bvTb, in_=bvT), reads=["bvT"], writes=["bvTb"])
    borow = A.f32(D, parts=1)
    P.dma(borow, C.na_b_o.rearrange("(o n) -> o n", o=1), writes=["borow"])
    for half in range(2):
        pb = PS[6 + half][0:1, :]
        for h in range(16):
            P.op("pe", lambda e, pb=pb, h=h, half=half: e.matmul(pb, lhsT=bvTb[:, h:h + 1], rhs=wo[:, h, half * 512:(half + 1) * 512], start=(h == 0), stop=(h == 15)),
                 reads=["bvTb", "wo"], writes=[f"bank{6 + half}"])
        P.op("dve", lambda e, pb=pb, half=half: e.tensor_tensor(out=borow[:, half * 512:(half + 1) * 512], in0=pb, in1=borow[:, half * 512:(half + 1) * 512], op=ALU.add),
             reads=[f"bank{6 + half}", "borow"], writes=["borow"])
    P.dma(C.bo2.rearrange("(o n) -> o n", o=1), borow, reads=["borow"], writes=["bo2"])
    bo = A.f32(D)
    bc_load_dep(C, bo, C.bo2, "bo", ["bo2"])
    gm = A.f32(D)
    gmb = A.f32(D)
    ones_r = A.f32(64, parts=65)
    P.op("pool", lambda e: e.memset(ones_r[64:65, :], 1.0), writes=["ones_r"])
    xt = A.f32(4 * D).rearrange("p (b c) -> p b c", b=4)
    kw = A.bf16(8 * 1536).rearrange("p (k t) -> p k t", k=8)
    qw = A.bf16(8 * 512).rearrange("p (k t) -> p k t", k=8)
    vw = A.bf16(12 * 1040).rearrange("p (b h d) -> p b h d", b=12, h=16)
    es_ring = [A.f32(512) for _ in range(3)]
    pt_ring = [A.bf16(512) for _ in range(4)]
    es4 = [A.f32(128) for _ in range(3)]
    pt4 = [A.bf16(128) for _ in range(3)]
    oT = A.bf16(16 * 512, parts=64).rearrange("p (h t) -> p h t", h=16)
    sums = A.f32(512, parts=65)
    rcp = A.f32(512, parts=64)
    rt = [A.f32(512) for _ in range(2)]
    st = A.f32(12).rearrange("p (a b) -> p a b", a=2)
    mv = A.f32(2)
    rs = A.f32(1)
    nmr = A.f32(1)
    c_es = c_pt = c_s4 = c_h = 0
    for (sq, xin, xout) in seq_io:
        n = sq.n
        ci = sq.ci
        R = n // 64
        nt = n // 512
        bc_load(C, gm, C.modd[l, ci, 2 * D:3 * D], "gm")
        P.op("pool", lambda e: e.tensor_tensor(out=gmb, in0=gm, in1=bo, op=ALU.mult), reads=["gm", "bo"], writes=["gmb"])
        for T in range(nt):
            t0 = T * 512
            wt0 = t0 - 512
            lo = max(wt0, 0)
            hi = min(t0 + 1024, n)
            P.dma(kw[:, :, lo - wt0:hi - wt0], sq.kT[:, :, lo:hi], writes=["kw"], q="sp")
            P.dma(qw, sq.qT[:, :, t0:t0 + 512], writes=["qw"], q="sp")
            P.dma(vw[:, (lo - wt0) // 128:(hi - wt0) // 128, :, :].rearrange("p b h d -> p b (h d)"),
                  sq.vS[lo:hi, :].rearrange("(b p) c -> p b c", p=128), writes=["vw"], q="sp")
            P.dma(xt, xin[t0:t0 + 512, :].rearrange("(b p) c -> p b c", p=128), writes=["xt"], q="sp")
            for b in range(4):
                P.op("dve", lambda e, b=b: e.scalar_tensor_tensor(out=xt[:, b, :], in0=xt[:, b, :], scalar=ALPHA, in1=gmb, op0=ALU.mult, op1=ALU.add), reads=["xt", "gmb"], writes=["xt"])
            wr0 = 8 * T - 8
            LA = 2
            units = [(h, qb) for h in range(16) for qb in range(4)]
            pend = {}

            def valid(kr_, r_):
                rs_ = min(max(r_ - 4, 0), R - 8)
                return kr_ < R and rs_ <= kr_ < rs_ + 8

            def issue_scores(ui):
                nonlocal c_es, c_s4
                h, qb = units[ui]
                hp, pbase = h // 2, (h % 2) * 64
                r = 8 * T + 2 * qb
                kr0 = min(max(r - 4, 0), R - 10)
                sb_i = c_es % 3
                ps_s = PS[sb_i]
                psk = f"bank{sb_i}"
                esi = c_es % 3
                c_es += 1
                for j in range(4):
                    kc0 = (kr0 + 2 * j - wr0) * 64
                    P.op("pe", lambda e, ps_s=ps_s, j=j, kc0=kc0, qb=qb, hp=hp, pbase=pbase: e.matmul(
                        ps_s[:, (3 - j) * 128:(4 - j) * 128], lhsT=kw[pbase:pbase + 64, hp, kc0:kc0 + 128],
                        rhs=qw[pbase:pbase + 64, hp, qb * 128:(qb + 1) * 128], start=True, stop=True),
                        reads=["kw", "qw"], writes=[psk])
                pend[ui] = [sb_i, esi, None]

            def issue_j4(ui):
                nonlocal c_s4
                h, qb = units[ui]
                hp, pbase = h // 2, (h % 2) * 64
                r = 8 * T + 2 * qb
                kr0 = min(max(r - 4, 0), R - 10)
                kr4 = kr0 + 8
                nv4 = sum(1 for kl in range(2) for rl in range(2) if valid(kr4 + kl, r + rl))
                if nv4:
                    i4 = c_s4 % 2
                    c_s4 += 1
                    b4 = (3, 6)[i4]
                    ps4 = PS[b4][:, 0:128]
                    kc0 = (kr4 - wr0) * 64
                    P.op("pe", lambda e, ps4=ps4, kc0=kc0, qb=qb, hp=hp, pbase=pbase: e.matmul(
                        ps4, lhsT=kw[pbase:pbase + 64, hp, kc0:kc0 + 128], rhs=qw[pbase:pbase + 64, hp, qb * 128:(qb + 1) * 128],
                        start=True, stop=True), reads=["kw", "qw"], writes=[f"bank{b4}"])
                    pend[ui][2] = i4

            def finish_unit(ui, po, pok):
                nonlocal c_pt
                h, qb = units[ui]
                sb_i, esi, i4 = pend.pop(ui)
                r = 8 * T + 2 * qb
                kr0 = min(max(r - 4, 0), R - 10)
                x03 = 1 + r - kr0
                ps_s = PS[sb_i]
                psk = f"bank{sb_i}"
                es = es_ring[esi]
                esk = f"es{esi}"
                P.op("act", lambda e, es=es, ps_s=ps_s: e.activation(out=es, in_=ps_s, func=AF.Exp, scale=0.125), reads=[psk], writes=[esk])
                pt = pt_ring[c_pt % 4]
                ptk = f"pt{c_pt % 4}"
                c_pt += 1
                P.op("dve", lambda e, pt=pt, es=es, h=h, x03=x03: e.tensor_tensor(out=pt, in0=es, in1=TAB[:, h, x03 * 64:(x03 + 8) * 64], op=ALU.mult),
                     reads=[esk, "TAB"], writes=[ptk])
                tiles = []
                zkeys = []
                for j in range(4):
                    kr = kr0 + 2 * j
                    nv = 0
                    for kl in range(2):
                        for rl in range(2):
                            if valid(kr + kl, r + rl):
                                nv += 1
                            else:
                                zk_ = ptk + f"z{len(zkeys)}"
                                zkeys.append(zk_)
                                P.op("pool", lambda e, pt=pt, j=j, kl=kl, rl=rl: e.memset(
                                    pt[kl * 64:(kl + 1) * 64, (3 - j) * 128 + rl * 64:(3 - j) * 128 + (rl + 1) * 64], 0.0),
                                    reads=[ptk], writes=[zk_])
                    if nv:
                        tiles.append((pt[:, (3 - j) * 128:(4 - j) * 128], [ptk] + zkeys, (kr - wr0) // 2))
                tiles = [(a_, [ptk] + zkeys, c_) for (a_, _, c_) in tiles]
                if i4 is not None:
                    kr = kr0 + 8
                    b4 = (3, 6)[i4]
                    ps4 = PS[b4][:, 0:128]
                    i5 = i4
                    e4, p4 = es4[i5], pt4[i5]
                    P.op("act", lambda e, e4=e4, ps4=ps4: e.activation(out=e4, in_=ps4, func=AF.Exp, scale=0.125), reads=[f"bank{b4}"], writes=[f"es4{i5}"])
                    P.op("dve", lambda e, e4=e4, p4=p4, h=h, x03=x03: e.tensor_tensor(out=p4, in0=e4, in1=TAB[:, h, (x03 - 2) * 64:x03 * 64], op=ALU.mult),
                         reads=[f"es4{i5}", "TAB"], writes=[f"pt4{i5}"])
                    z4 = []
                    for kl in range(2):
                        for rl in range(2):
                            if not valid(kr + kl, r + rl):
                                zk_ = f"pt4{i5}z{len(z4)}"
                                z4.append(zk_)
                                P.op("pool", lambda e, p4=p4, kl=kl, rl=rl: e.memset(p4[kl * 64:(kl + 1) * 64, rl * 64:(rl + 1) * 64], 0.0),
                                     reads=[f"pt4{i5}"], writes=[zk_])
                    tiles.append((p4, [f"pt4{i5}"] + z4, (kr - wr0) // 2))
                for ti, (rhs_t, rk_, blk) in enumerate(tiles):
                    P.op("pe", lambda e, po=po, rhs_t=rhs_t, blk=blk, h=h, qb=qb, ti=ti, nti=len(tiles): e.matmul(
                        po[0:65, qb * 128:(qb + 1) * 128], lhsT=vw[:, blk, h, 0:65], rhs=rhs_t, start=(ti == 0), stop=(ti == nti - 1)),
                        reads=rk_ + ["vw"], writes=[pok])

            def normalise(h, po, pok):
                P.op("act", lambda e, po=po: e.activation(out=sums[64:65, :], in_=po[64:65, :], func=AF.Ln), reads=[pok], writes=["sums"])
                P.op("act", lambda e: e.activation(out=sums[64:65, :], in_=sums[64:65, :], func=AF.Exp, scale=-1.0), reads=["sums"], writes=["sums"])
                P.op("pe", lambda e: e.matmul(PS[7][0:64, :], lhsT=ones_r[64:65, :], rhs=sums[64:65, :], start=True, stop=True),
                     reads=["sums", "ones_r"], writes=["bank7"])
                P.op("act", lambda e: e.copy(out=rcp, in_=PS[7][0:64, :]), reads=["bank7"], writes=["rcp"])
                P.op("dve", lambda e, po=po, h=h: e.tensor_tensor(out=oT[:, h, :], in0=po[0:64, :], in1=rcp, op=ALU.mult), reads=[pok, "rcp"], writes=[f"oT{h}"])

            for ui in range(min(LA, len(units))):
                issue_scores(ui)
            issue_j4(0)
            deferred = None
            for ui, (h, qb) in enumerate(units):
                if qb == 0:
                    po = PS[4 + c_h % 2]
                    pok = f"bank{4 + c_h % 2}"
                    c_h += 1
                if ui + LA < len(units):
                    issue_scores(ui + LA)
                if ui + 1 < len(units):
                    issue_j4(ui + 1)
                finish_unit(ui, po, pok)
                if qb == 1 and deferred is not None:
                    normalise(*deferred)
                    deferred = None
                if qb == 3:
                    deferred = (h, po, pok)
            normalise(*deferred)
            oT_keys = [f"oT{h}" for h in range(16)]
            for b in range(4):
                for half in range(2):
                    j = (b * 2 + half) % 2
                    pw = PS[6 + j]
                    pwk = f"bank{6 + j}"
                    for h in range(16):
                        P.op("pe", lambda e, pw=pw, h=h, b=b, half=half: e.matmul(pw, lhsT=oT[:, h, b * 128:(b + 1) * 128], rhs=wo[:, h, half * 512:(half + 1) * 512],
                                                                                  start=(h == 0), stop=(h == 15)), reads=[oT_keys[h], "wo"], writes=[pwk])
                    r_ = rt[j]
                    rk = f"rt{j}"
                    xs = xt[:, b, half * 512:(half + 1) * 512]
                    P.op("dve", lambda e, pw=pw, r_=r_, half=half: e.tensor_tensor(out=r_, in0=pw, in1=gm[:, half * 512:(half + 1) * 512], op=ALU.mult),
                         reads=[pwk, "gm"], writes=[rk])
                    P.op("dve", lambda e, r_=r_, xs=xs: e.tensor_tensor(out=xs, in0=xs, in1=r_, op=ALU.add), reads=[rk, "xt"], writes=["xt"])
                layer_norm_block(C, xt[:, b, :], "xt", lnG, lnB, st, mv, rs, nmr)
            P.dma(xout[t0:t0 + 512, :].rearrange("(b p) c -> p b c", p=128), xt, reads=["xt"], writes=["xout"], q="sp", is_output=(xout is sq.y))
    A.release(m0)
    P.barrier()


def bc_load_dep(C, dst, src_row, key, reads):
    parts = dst.shape[0]
    C.P.dma(dst, src_row.rearrange("(o n) -> o n", o=1).partition_broadcast(parts), reads=reads, writes=[key])


def rpb_layout(rpb):
    rpb = np.asarray(rpb, np.float32)
    kc = np.arange(64)[:, None]
    qc = np.arange(64)[None, :]
    cs_ = np.clip(qc - 8, 0, 48)
    valid = (kc >= cs_) & (kc < cs_ + 16)
    idx = np.clip(kc - qc + 15, 0, 30)
    out = rpb[:, :, idx]
    out = np.where(valid[None, None], out, np.float32(-30000.0)).astype(np.float32)
    return np.ascontiguousarray(out)


def tab_layout(rpb):
    t = rpb_layout(rpb)
    out = np.full((2, 64, 16, 17, 64), -30000.0, np.float32)
    for kl in range(2):
        for x in range(17):
            dr = kl + 7 - x
            if -7 <= dr <= 7:
                out[kl, :, :, x, :] = np.transpose(t[:, dr + 7, :, :], (1, 0, 2))
    return np.ascontiguousarray(out.reshape(128, 16, 17 * 64))


def s5_masks_const():
    tau = np.arange(128)[:, None] // 16
    t = np.arange(128)[None, :] // 16
    return np.ascontiguousarray(np.stack([(t >= tau), (tau >= t)]).astype(np.float32))


def shared_inputs(inp):
    f = lambda a: np.ascontiguousarray(np.asarray(a, np.float32))
    return {
        "s5_masks": s5_masks_const(),
        "w_ada": f(inp["w_ada"]), "b_ada": f(inp["b_ada"]), "ln_g": f(inp["ln_g"]), "ln_b": f(inp["ln_b"]),
        "s5_lam_re": f(inp["s5_lam_re"][0]), "s5_lam_im": f(inp["s5_lam_im"][0]), "s5_log_dt": f(inp["s5_log_dt"][0]),
        "s5_b_re": f(inp["s5_b_re"][0]), "s5_b_im": f(inp["s5_b_im"][0]), "s5_c_re": f(inp["s5_c_re"][0]), "s5_c_im": f(inp["s5_c_im"][0]),
        "s5_d": f(inp["s5_d"][0]), "s5_w_glu": f(inp["s5_w_glu"][0]), "s5_b_glu": f(inp["s5_b_glu"][0]),
        "na_w_qkv": f(inp["na_w_qkv"][0]), "na_b_qkv": f(inp["na_b_qkv"][0]), "na_tab": tab_layout(inp["na_rpb"][0]),
        "na_w_o": f(inp["na_w_o"][0]), "na_b_o": f(inp["na_b_o"][0]),
        "ffn_w_in": f(inp["ffn_w_in"]), "ffn_b_in": f(inp["ffn_b_in"]), "ffn_conv_w": f(inp["ffn_conv_w"]),
        "ffn_conv_b": f(inp["ffn_conv_b"]), "ffn_w_out": f(inp["ffn_w_out"]), "ffn_b_out": f(inp["ffn_b_out"]),
    }


_CACHE = {}


def prompt_window(i):
    r0 = min(max(32 * i - 8, 0), 256 - 48)
    return r0 * 64


def kernel(**inputs):
    inp = {k: np.asarray(v) for k, v in inputs.items()}
    cfg = dict(ns=NS_FULL, np=NP_FULL)
    if "nc" not in _CACHE:
        _CACHE["nc"] = build(cfg)
    nc, C = _CACHE["nc"]
    shared = shared_inputs(inp)
    xs = np.asarray(inp["x_sample"], np.float32)
    xp = np.asarray(inp["x_prompt"], np.float32)[0]
    in_maps = []
    for i in range(8):
        w0 = prompt_window(i)
        m = dict(shared)
        m["x_s"] = np.ascontiguousarray(xs[i])
        m["x_p"] = np.ascontiguousarray(xp[w0:w0 + NP_FULL])
        m["cs"] = np.ascontiguousarray(np.stack([inp["c_sample"][i], inp["c_prompt"][0]]).astype(np.float32))
        in_maps.append(m)
    res = run_bass_kernel_spmd(nc, in_maps, core_ids=list(range(8)))
    y_s = np.stack([np.asarray(res.results[i]["y_s"], np.float32) for i in range(8)])
    y_p = np.zeros((1, 16384, D), np.float32)
    for i in range(8):
        w0 = prompt_window(i)
        off = 2048 * i - w0
        y_p[0, 2048 * i:2048 * (i + 1)] = np.asarray(res.results[i]["y_p"], np.float32)[off:off + 2048]
    return (y_p, y_s)
```

```python
import math
import itertools
import types
import numpy as np
from contextlib import ExitStack
import concourse.bass as bass
import concourse.mybir as mybir
from concourse.bass_utils import run_bass_kernel_spmd

F32 = mybir.dt.float32
BF16 = mybir.dt.bfloat16
AF = mybir.ActivationFunctionType
ALU = mybir.AluOpType

ENGS = ("pe", "dve", "act", "pool", "sp")
D = 1024
DFF = 2816
ALPHA = 4.0 ** 0.25
LN_EPS = 1e-5
NS_FULL = 4096
NP_FULL = 3072


def _freeze(fn):
    if fn is None or fn.__closure__ is None:
        return fn
    cells = tuple(types.CellType(c.cell_contents) for c in fn.__closure__)
    return types.FunctionType(fn.__code__, fn.__globals__, fn.__name__, fn.__defaults__, cells)


class Op:
    __slots__ = ("eng", "fn", "deps", "is_dma", "idx", "needed", "semval", "dsem", "dval", "waits", "prewait")

    def __init__(self, eng, fn, is_dma):
        self.eng = eng
        self.fn = fn
        self.is_dma = is_dma
        self.deps = set()
        self.needed = False
        self.semval = None
        self.dsem = None
        self.dval = None
        self.waits = []
        self.prewait = None


class Prog:
    def __init__(self, nc, es, n_dma_sems=14):
        self.nc = nc
        self.es = es
        self.ops = {e: [] for e in ENGS}
        self.last_w = {}
        self.readers = {}
        self.K = n_dma_sems
        self.out_dmas = []
        self.last_real = {e: None for e in ENGS}
        self.dma_hist = {e: [] for e in ENGS}

    def _track(self, o, reads, writes):
        deps = o.deps
        for k in reads:
            w = self.last_w.get(k)
            if w is not None:
                deps.add(w)
        for k in writes:
            w = self.last_w.get(k)
            if w is not None:
                deps.add(w)
            for r in self.readers.get(k, ()):
                deps.add(r)
        deps.discard(o)
        for k in writes:
            self.last_w[k] = o
            self.readers[k] = []
        for k in reads:
            self.readers.setdefault(k, []).append(o)

    dead = False
    paranoid = False

    def op(self, eng, fn, reads=(), writes=()):
        if self.dead:
            return None
        o = Op(eng, _freeze(fn), False)
        o.idx = len(self.ops[eng])
        self._track(o, reads, writes)
        if self.paranoid:
            for e2 in ENGS:
                if self.last_real[e2] is not None:
                    o.deps.add(self.last_real[e2])
                for d in self.dma_hist[e2][-self.K:]:
                    o.deps.add(d)
        self.ops[eng].append(o)
        self.last_real[eng] = o
        return o

    def dma(self, out, in_, reads=(), writes=(), q="sp", is_output=False, **kw):
        if self.dead:
            return None

        def fn(e):
            return e.dma_start(out=out, in_=in_, **kw)
        o = Op(q, fn, True)
        o.idx = len(self.ops[q])
        self._track(o, reads, writes)
        if self.paranoid:
            for e2 in ENGS:
                if self.last_real[e2] is not None:
                    o.deps.add(self.last_real[e2])
                for d in self.dma_hist[e2][-self.K:]:
                    o.deps.add(d)
        self.ops[q].append(o)
        self.dma_hist[q].append(o)
        if is_output:
            self.out_dmas.append(o)
        return o

    def barrier(self):
        if self.dead:
            return
        b = Op("sp", lambda e: e.nop(), False)
        b.idx = len(self.ops["sp"])
        for e in ENGS:
            if self.last_real[e] is not None:
                b.deps.add(self.last_real[e])
            for d in self.dma_hist[e][-self.K:]:
                b.deps.add(d)
        self.ops["sp"].append(b)
        self.last_real["sp"] = b
        for e in ENGS:
            if e == "sp":
                continue
            o = Op(e, None, False)
            o.idx = len(self.ops[e])
            o.deps.add(b)
            self.ops[e].append(o)
        self.last_w.clear()
        self.readers.clear()

    def emit(self):
        nc = self.nc
        es = self.es
        esem = {e: es.enter_context(nc.semaphore(f"s_{e}")) for e in ENGS}
        dsems = {e: [es.enter_context(nc.semaphore(f"d_{e}{i}")) for i in range(self.K)] for e in ("sp", "act", "pool")}
        for e in ENGS:
            n = 0
            for o in self.ops[e]:
                if o.is_dma:
                    o.dsem = dsems[e][n % self.K]
                    o.dval = 16 * (n // self.K + 1)
                    if n >= self.K:
                        o.prewait = (o.dsem, o.dval - 16)
                    n += 1
        fin = Op("sp", None, False)
        fin.idx = len(self.ops["sp"])
        fin.deps = set(self.out_dmas)
        self.ops["sp"].append(fin)
        for f in ENGS:
            waited = {e: -1 for e in ENGS}
            dwaited = set()
            for o in self.ops[f]:
                best = {}
                for d in o.deps:
                    if d.is_dma:
                        if id(d) in dwaited:
                            continue
                        dwaited.add(id(d))
                        o.waits.append(("dma", d))
                    else:
                        if d.eng == "pe" and f == "pe":
                            continue
                        if d.idx <= waited[d.eng]:
                            continue
                        if d.eng not in best or best[d.eng].idx < d.idx:
                            best[d.eng] = d
                for en, d in best.items():
                    waited[en] = d.idx
                    d.needed = True
                    o.waits.append(("eng", d))
        for e in ENGS:
            c = 0
            for o in self.ops[e]:
                if o.needed and not o.is_dma:
                    c += 1
                    o.semval = c
        self.stats = {e: len(self.ops[e]) for e in ENGS}

        def run(ename, eng):
            for o in self.ops[ename]:
                if o.prewait is not None:
                    eng.wait_ge(o.prewait[0], o.prewait[1])
                for kind, d in o.waits:
                    if kind == "dma":
                        eng.wait_ge(d.dsem, d.dval)
                    else:
                        eng.wait_ge(esem[d.eng], d.semval)
                if o.fn is None:
                    continue
                ins = o.fn(eng)
                if o.is_dma:
                    ins.then_inc(o.dsem, 16)
                elif o.needed:
                    ins.then_inc(esem[ename], 1)

        with nc.Block() as block:
            @block.sync
            def _(e):
                run("sp", e)

            @block.tensor
            def _(e):
                run("pe", e)

            @block.vector
            def _(e):
                run("dve", e)

            @block.scalar
            def _(e):
                run("act", e)

            @block.gpsimd
            def _(e):
                run("pool", e)


class Arena:
    def __init__(self, nc, es, words):
        self.t = es.enter_context(nc.sbuf_tensor("arena", [128, words], F32))[:, :]
        self.words = words
        self.off = 0
        self.uid = 0

    def mark(self):
        return self.off

    def release(self, m):
        self.off = m

    def f32(self, n, parts=128):
        assert self.off + n <= self.words, ("arena overflow", self.off, n, self.words)
        v = self.t[0:parts, self.off:self.off + n]
        self.off += n
        return v

    def bf16(self, n, parts=128):
        w = (n + 1) // 2
        v = self.f32(w, parts).bitcast(BF16)
        return v[:, 0:n]

    def key(self, base):
        self.uid += 1
        return f"{base}#{self.uid}"


class Ctx:
    pass


def build(cfg):
    ns, npw = cfg["ns"], cfg["np"]
    phases = cfg.get("phases", ("setup", "s5", "ffn0", "na", "ffn1"))
    cfg = dict(cfg)
    cfg["phases"] = phases
    nc = bass.Bass("TRN2", target_bir_lowering=False)
    C = Ctx()
    C.nc = nc
    C.cfg = cfg

    def din(name, shape, dt=F32):
        return nc.dram_tensor(name, list(shape), dt, kind="ExternalInput").ap()

    def dout(name, shape, dt=F32):
        return nc.dram_tensor(name, list(shape), dt, kind="ExternalOutput").ap()

    def dscr(name, shape, dt=F32):
        kind = "ExternalOutput" if cfg.get("debug") and name in cfg.get("taps", ()) else "Internal"
        return nc.dram_tensor(name, list(shape), dt, kind=kind).ap()

    seqs = []
    for nm, n in (("s", ns), ("p", npw)):
        if n == 0:
            continue
        s = Ctx()
        s.name = nm
        s.n = n
        s.ci = 0 if nm == "s" else 1
        s.x0 = din(f"x_{nm}", [n, D])
        s.y = dout(f"y_{nm}", [n, D])
        s.x1 = dscr(f"x1_{nm}", [n, D])
        s.x2 = dscr(f"x2_{nm}", [n, D])
        s.x3 = dscr(f"x3_{nm}", [n, D])
        s.qT = dscr(f"qT_{nm}", [128, 8, n], BF16)
        s.kT = dscr(f"kT_{nm}", [128, 8, n], BF16)
        s.vS = dscr(f"vS_{nm}", [n, 1040], BF16)
        s.U = dscr(f"U_{nm}", [n // 1024, 128, 64, 128], BF16)
        s.Z = dscr(f"Z_{nm}", [n // 512, 128, 64, 64], BF16)
        seqs.append(s)
    C.seqs = seqs
    C.cs = din("cs", [2, D])
    C.w_ada = din("w_ada", [2, D, 6 * D])
    C.b_ada = din("b_ada", [2, 6 * D])
    C.ln_g = din("ln_g", [2, 2, D])
    C.ln_b = din("ln_b", [2, 2, D])
    C.s5_lam_re = din("s5_lam_re", [2, 64, 64])
    C.s5_lam_im = din("s5_lam_im", [2, 64, 64])
    C.s5_log_dt = din("s5_log_dt", [2, 64])
    C.s5_b_re = din("s5_b_re", [2, 64, 64, 16])
    C.s5_b_im = din("s5_b_im", [2, 64, 64, 16])
    C.s5_c_re = din("s5_c_re", [2, 64, 16, 64])
    C.s5_c_im = din("s5_c_im", [2, 64, 16, 64])
    C.s5_d = din("s5_d", [64, 16])
    C.s5_w_glu = din("s5_w_glu", [D, 2 * D])
    C.s5_b_glu = din("s5_b_glu", [2 * D])
    C.na_w_qkv = din("na_w_qkv", [D, 3 * D])
    C.na_b_qkv = din("na_b_qkv", [3 * D])
    C.na_tab = din("na_tab", [128, 16, 1088])
    C.s5_masks = din("s5_masks", [2, 128, 128])
    C.na_w_o = din("na_w_o", [D, D])
    C.na_b_o = din("na_b_o", [D])
    C.ffn_w_in = din("ffn_w_in", [2, D, 2 * DFF])
    C.ffn_b_in = din("ffn_b_in", [2, 2 * DFF])
    C.ffn_conv_w = din("ffn_conv_w", [2, 3, DFF])
    C.ffn_conv_b = din("ffn_conv_b", [2, DFF])
    C.ffn_w_out = din("ffn_w_out", [2, DFF, D])
    C.ffn_b_out = din("ffn_b_out", [2, D])
    C.modd = dscr("modd", [2, 2, 6 * D])
    C.wglu_s = dscr("wglu_s", [128, 8, 2 * D], BF16)
    C.wqkv_s = dscr("wqkv_s", [128, 8, 3 * D], BF16)
    C.wo_s = dscr("wo_s", [64, 16, D], BF16)
    C.win_s = dscr("win_s", [2, 11, 128, 8, 512], BF16)
    C.wout_s = dscr("wout_s", [2, 128, 22, D], BF16)
    C.bo2 = dscr("bo2", [D])
    C.s5w = dscr("s5w", [64, 128, 1024], BF16)
    C.s5t = dscr("s5t", [64, 128, 2, 128])
    C.s5tp = dscr("s5tp", [2, 64, 128, 128])

    with ExitStack() as es:
        P = Prog(nc, es, n_dma_sems=cfg.get('ksem', 14))
        P.paranoid = bool(cfg.get('paranoid'))
        C.P = P
        A = Arena(nc, es, cfg.get("arena_words", 53184))
        C.A = A
        C.psum = [es.enter_context(nc.psum_tensor(f"ps{i}", [128, 512], F32))[:, :] for i in range(8)]
        C.ident = A.f32(128)
        C.identb = A.bf16(128)
        C.eps = A.f32(1)
        P.op("pool", lambda e: e.memset(C.ident, 0.0), writes=["ident"])
        P.op("pool", lambda e: e.affine_select(out=C.ident, in_=C.ident, pattern=[[-1, 128]], compare_op=ALU.not_equal,
                                               fill=1.0, base=0, channel_multiplier=1), reads=["ident"], writes=["ident"])
        P.op("pool", lambda e: e.tensor_copy(out=C.identb, in_=C.ident), reads=["ident"], writes=["identb"])
        P.op("pool", lambda e: e.memset(C.eps, LN_EPS), writes=["eps"])
        P.barrier()
        if "setup" in phases:
            setup_phase(C)
        if "s5" in phases:
            if len(phases) == 2:
                for s_ in seqs:
                    s_.x1 = s_.y
            s5_phase(C)
        if "ffn0" in phases:
            ffn_phase(C, 0, [(s, (s.x1 if "s5" in phases else s.x0), (s.x2 if ("na" in phases or "ffn1" in phases) else s.y)) for s in seqs])
        if "na" in phases:
            na_phase(C, [(s, (s.x2 if "ffn0" in phases else s.x0), (s.x3 if "ffn1" in phases else s.y)) for s in seqs])
        if "ffn1" in phases:
            ffn_phase(C, 1, [(s, (s.x3 if "na" in phases else s.x0), s.y) for s in seqs])
        P.emit()
        C.stats = P.stats
    return nc, C


def cut(C, label):
    if C.cfg.get("cut") == label:
        C.P.barrier()
        C.P.dead = True


def load_pp(C, src, dst, key):
    P, A = C.P, C.A
    R, W = src.shape[0], src.shape[1]
    stg = A.f32(W, parts=R)
    sk = A.key("ppstg")
    P.dma(stg, src, writes=[sk])
    ps = C.psum[7][0:W, 0:R]
    P.op("pe", lambda e: e.transpose(ps, stg, C.ident[0:R, 0:R]), reads=[sk, "ident"], writes=["bank7"])
    P.op("dve", lambda e: e.tensor_copy(out=dst, in_=ps), reads=["bank7"], writes=[key])


def bc_load(C, dst, src_row, key, q="sp"):
    n = dst.shape[1]
    parts = dst.shape[0]
    C.P.dma(dst, src_row.rearrange("(o n) -> o n", o=1).partition_broadcast(parts), writes=[key], q=q)


def setup_phase(C):
    P, A, nc = C.P, C.A, C.nc
    m0 = A.mark()
    cT = A.f32(16).rearrange("p (k b) -> p k b", b=2)
    crow = A.f32(D, parts=2)
    P.dma(crow, C.cs, writes=["crow"])
    P.op("act", lambda e: e.activation(out=crow, in_=crow, func=AF.Silu), reads=["crow"], writes=["crow"])
    pct = C.psum[2][:, 0:16]
    for k in range(8):
        P.op("pe", lambda e, k=k: e.transpose(pct[:, 2 * k:2 * k + 2], crow[0:2, k * 128:(k + 1) * 128], C.ident[0:2, 0:2]),
             reads=["crow", "ident"], writes=["pct"])
    P.op("dve", lambda e: e.tensor_copy(out=cT, in_=pct.rearrange("p (k b) -> p k b", b=2)), reads=["pct"], writes=["cT"])
    wa = [A.f32(8 * 512).rearrange("p (k c) -> p k c", k=8) for _ in range(2)]
    bad = A.f32(6 * D, parts=2)
    modsb = A.f32(6 * D, parts=2)
    it = 0
    for l in range(2):
        P.dma(bad, C.b_ada[l].rearrange("(o n) -> o n", o=1).partition_broadcast(2), writes=["bad"])
        for cb in range(12):
            w = wa[it % 2]
            wk = f"wa{it % 2}"
            P.dma(w, C.w_ada[l][:, cb * 512:(cb + 1) * 512].rearrange("(k p) c -> p k c", p=128), writes=[wk],
                  q=("sp" if it % 2 == 0 else "act"))
            ps = C.psum[it % 2][0:2, :]
            pk = f"ps{it % 2}"
            for k in range(8):
                P.op("pe", lambda e, ps=ps, w=w, k=k: e.matmul(ps, lhsT=cT[:, k, :], rhs=w[:, k, :], start=(k == 0), stop=(k == 7)),
                     reads=[wk, "cT"], writes=[pk])
            P.op("dve", lambda e, ps=ps, cb=cb: e.tensor_tensor(out=modsb[:, cb * 512:(cb + 1) * 512], in0=ps,
                                                                in1=bad[:, cb * 512:(cb + 1) * 512], op=ALU.add),
                 reads=[pk, "bad"], writes=["modsb"])
            it += 1
        P.dma(C.modd[l], modsb, reads=["modsb"], writes=["modd"])
    A.release(m0)
    P.barrier()
    m0c = A.mark()
    CW = 5632
    stg = [A.f32(CW)] * 2
    stb = [A.bf16(CW)] * 2
    cnt = [0]
    cast_eng = ("act", "pool", "dve")

    def cast_rows(src, ncols, stores):
        i = cnt[0]
        cnt[0] += 1
        s_f, s_b = stg[i % 2], stb[i % 2]
        kf, kb = "stg0", "stb0"
        P.dma(s_f[:, 0:ncols], src, writes=[kf], q=("sp" if i % 2 == 0 else "act"))
        ce = cast_eng[i % 3]
        if ce == "act":
            P.op("act", lambda e: e.copy(out=s_b[:, 0:ncols], in_=s_f[:, 0:ncols]), reads=[kf], writes=[kb])
        else:
            P.op(ce, lambda e: e.tensor_copy(out=s_b[:, 0:ncols], in_=s_f[:, 0:ncols]), reads=[kf], writes=[kb])
        for dst, (c0, c1), (p0, p1) in stores:
            P.dma(dst, s_b[p0:p1, c0:c1], reads=[kb], writes=["wscr"], q="sp")

    ph = C.cfg.get("phases")

    def casts():
        if "s5" in ph:
            for k in range(8):
                cast_rows(C.s5_w_glu[k * 128:(k + 1) * 128, :], 2048, [(C.wglu_s[:, k, :], (0, 2048), (0, 128))])
                yield
        if "na" in ph:
            for k in range(8):
                cast_rows(C.na_w_qkv[k * 128:(k + 1) * 128, :], 3072, [(C.wqkv_s[:, k, :], (0, 3072), (0, 128))])
                yield
            for k in range(8):
                cast_rows(C.na_w_o[k * 128:(k + 1) * 128, :], 1024,
                          [(C.wo_s[:, 2 * k, :], (0, 1024), (0, 64)), (C.wo_s[:, 2 * k + 1, :], (0, 1024), (64, 128))])
                yield
        for l in range(2):
            if f"ffn{l}" not in ph and f"ffn{l}_castonly" not in ph:
                continue
            for k in range(8):
                dst_u = C.win_s[l][:, :, k, 0:256].rearrange("m p c -> p m c")
                dst_g = C.win_s[l][:, :, k, 256:512].rearrange("m p c -> p m c")
                i = cnt[0]
                cast_rows(C.ffn_w_in[l][k * 128:(k + 1) * 128, :], 5632, [])
                s_b = stb[0]
                kb = "stb0"
                P.dma(dst_u, s_b[:, 0:2816].rearrange("p (m c) -> p m c", c=256), reads=[kb], writes=["wscr"], q="sp")
                P.dma(dst_g, s_b[:, 2816:5632].rearrange("p (m c) -> p m c", c=256), reads=[kb], writes=["wscr"], q="sp")
                yield
            for k in range(22):
                cast_rows(C.ffn_w_out[l][k * 128:(k + 1) * 128, :], 1024, [(C.wout_s[l][:, k, :], (0, 1024), (0, 128))])
                yield

    bg = casts()
    C.bg = bg
    if "s5" in ph:
        s5_setup(C)
    for _ in bg:
        pass
    A.release(m0c)
    P.barrier()


def bg_step(C, n=1):
    bg = getattr(C, "bg", None)
    if bg is None:
        return
    for _ in range(n):
        next(bg, None)


def ffn_phase(C, l, seq_io):
    P, A, nc = C.P, C.A, C.nc
    m0 = A.mark()
    PS = C.psum
    wout = A.bf16(22 * D).rearrange("p (k c) -> p k c", k=22)
    P.dma(wout, C.wout_s[l], writes=["wout"], q="act")
    lnG = A.f32(D)
    lnB = A.f32(D)
    bc_load(C, lnG, C.ln_g[l, 1], "lnG")
    bc_load(C, lnB, C.ln_b[l, 1], "lnB")
    bout = A.f32(D)
    bc_load(C, bout, C.ffn_b_out[l], "bout")
    binp = A.f32(44)
    cw = A.f32(66).rearrange("p (j m) -> p j m", j=3)
    cb = A.f32(22)
    mst = A.mark()
    load_pp(C, C.ffn_b_in[l].rearrange("(m p) -> m p", p=128), binp, "binp")
    load_pp(C, C.ffn_conv_w[l].rearrange("j (m p) -> (j m) p", p=128), cw.rearrange("p j m -> p (j m)"), "cw")
    load_pp(C, C.ffn_conv_b[l].rearrange("(m p) -> m p", p=128), cb, "cb")
    sc1 = A.f32(D)
    sh = A.f32(D)
    gf = A.f32(D)
    gfb = A.f32(D)
    xt_ring = [A.f32(4 * D).rearrange("p (b c) -> p b c", b=4) for _ in range(2)]
    xh_r = [A.f32(D, parts=2) for _ in range(2)]
    tmp = A.f32(D)
    hb_r = [A.bf16(4 * D).rearrange("p (b c) -> p b c", b=4) for _ in range(2)]
    hhb_r = [A.bf16(D, parts=2) for _ in range(2)]
    hfm_r = [A.bf16(8 * 516).rearrange("p (k t) -> p k t", k=8) for _ in range(2)]
    wring = [A.bf16(8 * 512).rearrange("p (k c) -> p k c", k=8) for _ in range(2)]
    hid = A.bf16(22 * 512).rearrange("p (m t) -> p m t", m=22)
    u_sb = [A.f32(514) for _ in range(2)]
    vv = [A.f32(512) for _ in range(2)]
    g1 = [A.f32(512) for _ in range(2)]
    rt = vv
    st = A.f32(12).rearrange("p (a b) -> p a b", a=2)
    mv = A.f32(2)
    rs = A.f32(1)
    nmr = A.f32(1)
    if C.cfg.get("verbose"):
        print("ffn arena high-water", A.off, "of", A.words)
    pT = PS[0][:, 0:256].bitcast(BF16)
    pu = [PS[1], PS[2]]
    pg = [PS[3], PS[4]]
    phl = PS[5][:, 0:44]
    pTh = PS[5][:, 64:72].bitcast(BF16)
    po = [PS[6], PS[7]]
    gi = [0]
    cut(C, "A")
    for (sq, xin, xout) in seq_io:
        n = sq.n
        ci = sq.ci
        bc_load(C, sh, C.modd[l, ci, 3 * D:4 * D], "sh")
        bc_load(C, sc1, C.modd[l, ci, 4 * D:5 * D], "sc1")
        bc_load(C, gf, C.modd[l, ci, 5 * D:6 * D], "gf")
        P.op("pool", lambda e: e.tensor_scalar(out=sc1, in0=sc1, scalar1=1.0, scalar2=None, op0=ALU.add), reads=["sc1"], writes=["sc1"])
        P.op("pool", lambda e: e.tensor_tensor(out=gfb, in0=gf, in1=bout, op=ALU.mult), reads=["gf", "bout"], writes=["gfb"])
        nt = n // 512
        it0 = gi[0]
        gi[0] += nt

        def prologue(T):
            t0 = T * 512
            it = it0 + T
            i2 = it % 2
            xt, xh, hb, hhb, hfm = xt_ring[i2], xh_r[i2], hb_r[i2], hhb_r[i2], hfm_r[i2]
            xk = f"xt{i2}"
            has_l = t0 > 0
            has_r = t0 + 512 < n
            P.dma(xt, xin[t0:t0 + 512, :].rearrange("(b p) c -> p b c", p=128), writes=[xk], q="sp")
            if (has_l or has_r) and not (has_l and has_r):
                P.op("pool", lambda e: e.memset(xh, 0.0), writes=[f"xh0{i2}", f"xh1{i2}"])
            if has_l:
                P.dma(xh[0:1, :], xin[t0 - 1:t0, :], writes=[f"xh0{i2}"], q="sp")
            if has_r:
                P.dma(xh[1:2, :], xin[t0 + 512:t0 + 513, :], writes=[f"xh1{i2}"], q="sp")
            for b in range(4):
                P.op("dve", lambda e, b=b: e.tensor_tensor(out=tmp, in0=xt[:, b, :], in1=sc1, op=ALU.mult), reads=[xk, "sc1"], writes=["tmp"])
                P.op("dve", lambda e, b=b: e.tensor_tensor(out=hb[:, b, :], in0=tmp, in1=sh, op=ALU.add), reads=["tmp", "sh"], writes=[f"hb{i2}{b}"])
            if has_l or has_r:
                P.op("dve", lambda e: e.tensor_tensor(out=tmp[0:2, :], in0=xh, in1=sc1[0:2, :], op=ALU.mult), reads=[f"xh0{i2}", f"xh1{i2}", "sc1"], writes=["tmp"])
                P.op("dve", lambda e: e.tensor_tensor(out=hhb, in0=tmp[0:2, :], in1=sh[0:2, :], op=ALU.add), reads=["tmp", "sh"], writes=[f"hhb{i2}"])
            for b in range(4):
                P.op("dve", lambda e, b=b: e.scalar_tensor_tensor(out=xt[:, b, :], in0=xt[:, b, :], scalar=ALPHA, in1=gfb, op0=ALU.mult, op1=ALU.add),
                     reads=[xk, f"hb{i2}{b}", "gfb"], writes=[xk])

        def prologue_b(T):
            t0 = T * 512
            it = it0 + T
            i2 = it % 2
            xt, xh, hb, hhb, hfm = xt_ring[i2], xh_r[i2], hb_r[i2], hhb_r[i2], hfm_r[i2]
            has_l = t0 > 0
            has_r = t0 + 512 < n
            for k in range(8):
                for b in range(4):
                    P.op("pe", lambda e, b=b, k=k: e.transpose(pT[:, b * 128:(b + 1) * 128], hb[:, b, k * 128:(k + 1) * 128], C.identb),
                         reads=[f"hb{i2}{b}", "identb"], writes=["bank0"])
                if k % 2 == 0:
                    P.op("act", lambda e, k=k: e.copy(out=hfm[:, k, 2:514], in_=pT), reads=["bank0"], writes=[f"hfm{i2}{k}"])
                else:
                    P.op("dve", lambda e, k=k: e.tensor_copy(out=hfm[:, k, 2:514], in_=pT), reads=["bank0"], writes=[f"hfm{i2}{k}"])
            if has_l or has_r:
                for k in range(8):
                    P.op("pe", lambda e, k=k: e.transpose(pTh[:, 2 * k:2 * k + 2], hhb[0:2, k * 128:(k + 1) * 128], C.identb[0:2, 0:2]),
                         reads=[f"hhb{i2}", "identb"], writes=["bank5"])
                P.op("dve", lambda e: e.tensor_copy(out=hfm[:, :, 1], in_=pTh[:, 0:16:2]), reads=["bank5"], writes=[f"hfmh{i2}"])
                P.op("dve", lambda e: e.tensor_copy(out=hfm[:, :, 514], in_=pTh[:, 1:16:2]), reads=["bank5"], writes=[f"hfmh{i2}"])

        prologue(0)
        prologue_b(0)
        preloaded = set()
        for T in range(nt):
            t0 = T * 512
            it = it0 + T
            i2 = it % 2
            xt, hfm = xt_ring[i2], hfm_r[i2]
            xk = f"xt{i2}"
            has_l = t0 > 0
            has_r = t0 + 512 < n
            cut(C, "D")
            hf_keys = [f"hfm{i2}{k}" for k in range(8)]
            lvl = C.cfg.get("ffn_level", 3)
            if lvl == 1:
                P.op("dve", lambda e: e.tensor_copy(out=xt[:, 0, 0:514], in_=hfm[:, 0, 1:515]), reads=hf_keys + [f"hfmh{i2}", xk], writes=[xk])
                P.dma(xout[t0:t0 + 512, :].rearrange("(b p) c -> p b c", p=128), xt, reads=[xk], writes=["xout"], q="sp", is_output=(xout is sq.y))
                continue
            for m2 in range(11):
                wi = (it * 11 + m2) % 2
                w = wring[wi]
                wk = f"wr{wi}"
                if (it, m2) not in preloaded:
                    P.dma(w, C.win_s[l, m2], writes=[wk], q="sp")
                if m2 == 1 and T + 1 < nt:
                    prologue(T + 1)
                if m2 == 7 and T + 1 < nt:
                    prologue_b(T + 1)
                for mi in range(2):
                    m = m2 * 2 + mi
                    j = m % 2
                    puk, pgk = f"bank{1 + j}", f"bank{3 + j}"
                    for k in range(8):
                        P.op("pe", lambda e, j=j, k=k, mi=mi, w=w: e.matmul(pu[j], lhsT=w[:, k, mi * 128:(mi + 1) * 128], rhs=hfm[:, k, 2:514],
                                                                            start=(k == 0), stop=(k == 7)),
                             reads=[wk, hf_keys[k]], writes=[puk])
                    for k in range(8):
                        P.op("pe", lambda e, j=j, k=k, mi=mi, w=w: e.matmul(pg[j], lhsT=w[:, k, 256 + mi * 128:256 + (mi + 1) * 128], rhs=hfm[:, k, 2:514],
                                                                            start=(k == 0), stop=(k == 7)),
                             reads=[wk, hf_keys[k]], writes=[pgk])
                    if has_l or has_r:
                        for k in range(8):
                            P.op("pe", lambda e, k=k, mi=mi, m=m, w=w: e.matmul(phl[:, 2 * m:2 * m + 2], lhsT=w[:, k, mi * 128:(mi + 1) * 128],
                                                                                rhs=hfm[:, k, 1:515:513], start=(k == 0), stop=(k == 7)),
                                 reads=[wk, f"hfmh{i2}"], writes=["bank5"])
                    u = u_sb[j]
                    uk = f"u{j}"
                    P.op("act", lambda e, u=u, j=j, m=m: e.activation(out=u[:, 1:513], in_=pu[j], func=AF.Identity, bias=binp[:, m:m + 1], scale=1.0),
                         reads=[puk, "binp"], writes=[uk + "m"])
                    if has_l:
                        P.op("act", lambda e, u=u, m=m: e.activation(out=u[:, 0:1], in_=phl[:, 2 * m:2 * m + 1], func=AF.Identity, bias=binp[:, m:m + 1], scale=1.0),
                             reads=["bank5", "binp"], writes=[uk + "l"])
                    else:
                        P.op("pool", lambda e, u=u: e.memset(u[:, 0:1], 0.0), writes=[uk + "l"])
                    if has_r:
                        P.op("act", lambda e, u=u, m=m: e.activation(out=u[:, 513:514], in_=phl[:, 2 * m + 1:2 * m + 2], func=AF.Identity, bias=binp[:, m:m + 1], scale=1.0),
                             reads=["bank5", "binp"], writes=[uk + "r"])
                    else:
                        P.op("pool", lambda e, u=u: e.memset(u[:, 513:514], 0.0), writes=[uk + "r"])
                    v = vv[j]
                    vk = f"v{j}"
                    P.op("dve", lambda e, u=u, v=v, m=m: e.tensor_scalar(out=v, in0=u[:, 1:513], scalar1=cw[:, 1, m:m + 1], scalar2=cb[:, m:m + 1],
                                                                           op0=ALU.mult, op1=ALU.add),
                         reads=[uk + "m", "cw", "cb"], writes=[vk])
                    P.op("dve", lambda e, u=u, v=v, m=m: e.scalar_tensor_tensor(out=v, in0=u[:, 0:512], scalar=cw[:, 0, m:m + 1], in1=v, op0=ALU.mult, op1=ALU.add),
                         reads=[uk + "m", uk + "l", vk, "cw"], writes=[vk])
                    P.op("dve", lambda e, u=u, v=v, m=m: e.scalar_tensor_tensor(out=v, in0=u[:, 2:514], scalar=cw[:, 2, m:m + 1], in1=v, op0=ALU.mult, op1=ALU.add),
                         reads=[uk + "m", uk + "r", vk, "cw"], writes=[vk])
                    gg = g1[j]
                    gk = f"g1{j}"
                    P.op("act", lambda e, v=v, gg=gg: e.activation(out=gg, in_=v, func=AF.Gelu_apprx_tanh), reads=[vk], writes=[gk])
                    P.op("dve", lambda e, gg=gg, j=j, m=m: e.scalar_tensor_tensor(out=hid[:, m, :], in0=pg[j], scalar=binp[:, 22 + m:23 + m], in1=gg,
                                                                                   op0=ALU.add, op1=ALU.mult),
                         reads=[pgk, gk, "binp"], writes=[f"hid{m}"])
            hid_keys = [f"hid{m}" for m in range(22)]
            if T + 1 < nt:
                wi_n = ((it + 1) * 11) % 2
                P.dma(wring[wi_n], C.win_s[l, 0], writes=[f"wr{wi_n}"], q="sp")
                preloaded.add((it + 1, 0))
            if lvl == 2:
                P.op("dve", lambda e: e.tensor_copy(out=xt[:, 0, 0:512], in_=hid[:, 5, :]), reads=hid_keys + [xk], writes=[xk])
                P.dma(xout[t0:t0 + 512, :].rearrange("(b p) c -> p b c", p=128), xt, reads=[xk], writes=["xout"], q="sp", is_output=(xout is sq.y))
                continue
            for b in range(4):
                for half in range(2):
                    j = (b * 2 + half) % 2
                    pok = f"bank{6 + j}"
                    for k in range(22):
                        P.op("pe", lambda e, j=j, k=k, b=b, half=half: e.matmul(po[j], lhsT=hid[:, k, b * 128:(b + 1) * 128],
                                                                                  rhs=wout[:, k, half * 512:(half + 1) * 512], start=(k == 0), stop=(k == 21)),
                             reads=[hid_keys[k], "wout"], writes=[pok])
                    r = rt[j]
                    rk = f"v{j}"
                    xs = xt[:, b, half * 512:(half + 1) * 512]
                    P.op("dve", lambda e, j=j, r=r, half=half: e.tensor_tensor(out=r, in0=po[j], in1=gf[:, half * 512:(half + 1) * 512], op=ALU.mult),
                         reads=[pok, "gf"], writes=[rk])
                    P.op("dve", lambda e, r=r, xs=xs: e.tensor_tensor(out=xs, in0=xs, in1=r, op=ALU.add), reads=[rk, xk], writes=[xk])
                layer_norm_block(C, xt[:, b, :], xk, lnG, lnB, st, mv, rs, nmr)
            P.dma(xout[t0:t0 + 512, :].rearrange("(b p) c -> p b c", p=128), xt, reads=[xk], writes=["xout"], q="sp",
                  is_output=(xout is sq.y))
    A.release(m0)
    P.barrier()


def layer_norm_block(C, xb, xk, lnG, lnB, st, mv, rs, nmr):
    P = C.P
    P.op("dve", lambda e: e.bn_stats(out=st[:, 0, :], in_=xb[:, 0:512]), reads=[xk], writes=["st"])
    P.op("dve", lambda e: e.bn_stats(out=st[:, 1, :], in_=xb[:, 512:1024]), reads=[xk, "st"], writes=["st"])
    P.op("dve", lambda e: e.bn_aggr(out=mv, in_=st), reads=["st"], writes=["mv"])
    P.op("act", lambda e: e.activation(out=rs, in_=mv[:, 1:2], func=AF.Sqrt, bias=C.eps, scale=1.0), reads=["mv", "eps"], writes=["rs"])
    P.op("dve", lambda e: e.reciprocal(out=rs, in_=rs), reads=["rs"], writes=["rs"])
    P.op("dve", lambda e: e.tensor_scalar(out=nmr, in0=mv[:, 0:1], scalar1=rs, scalar2=-1.0, op0=ALU.mult, op1=ALU.mult),
         reads=["mv", "rs"], writes=["nmr"])
    P.op("act", lambda e: e.activation(out=xb, in_=xb, func=AF.Identity, bias=nmr, scale=rs), reads=[xk, "rs", "nmr"], writes=[xk])
    P.op("dve", lambda e: e.tensor_tensor(out=xb, in0=xb, in1=lnG, op=ALU.mult), reads=[xk, "lnG"], writes=[xk])
    P.op("dve", lambda e: e.tensor_tensor(out=xb, in0=xb, in1=lnB, op=ALU.add), reads=[xk, "lnB"], writes=[xk])


TWO_PI = 2.0 * math.pi
MAGIC = 12582912.0
CW1 = 6.28125
CW2 = TWO_PI - 6.28125


def sincos(C, ang, n, sin_out, cos_out, tag):
    P, A = C.P, C.A
    k = A.f32(n)
    r = A.f32(n)
    ka, kk, kr = tag + "a", tag + "k", tag + "r"
    for (shift, outp) in ((0.0, sin_out), (0.5 * math.pi, cos_out)):
        if outp is None:
            continue
        P.op("dve", lambda e, shift=shift: e.tensor_scalar(out=r, in0=ang, scalar1=shift, scalar2=None, op0=ALU.add), reads=[ka], writes=[kr])
        P.op("dve", lambda e: e.tensor_scalar(out=k, in0=r, scalar1=1.0 / TWO_PI, scalar2=MAGIC, op0=ALU.mult, op1=ALU.add), reads=[kr], writes=[kk])
        P.op("dve", lambda e: e.tensor_scalar(out=k, in0=k, scalar1=MAGIC, scalar2=None, op0=ALU.subtract), reads=[kk], writes=[kk])
        P.op("dve", lambda e: e.scalar_tensor_tensor(out=r, in0=k, scalar=-CW1, in1=r, op0=ALU.mult, op1=ALU.add), reads=[kk, kr], writes=[kr])
        P.op("dve", lambda e: e.scalar_tensor_tensor(out=r, in0=k, scalar=-CW2, in1=r, op0=ALU.mult, op1=ALU.add), reads=[kk, kr], writes=[kr])
        P.op("dve", lambda e: e.tensor_scalar(out=r, in0=r, scalar1=math.pi, scalar2=-math.pi, op0=ALU.min, op1=ALU.max), reads=[kr], writes=[kr])
        P.op("act", lambda e, outp=outp: e.activation(out=outp, in_=r, func=AF.Sin), reads=[kr], writes=[tag + "o"])


def dup_transpose(C, src, dst, key):
    P, A = C.P, C.A
    stg = A.f32(128)
    sk = A.key("dtstg")
    P.dma(stg[:, 0:64], src, writes=[sk])
    P.dma(stg[:, 64:128], src, writes=[sk + "b"])
    ps = C.psum[7][:, 0:128]
    P.op("pe", lambda e: e.transpose(ps, stg, C.ident), reads=[sk, sk + "b"], writes=["bank7"])
    P.op("dve", lambda e: e.tensor_copy(out=dst, in_=ps), reads=["bank7"], writes=[key])


def s5_setup(C):
    P, A, nc = C.P, C.A, C.nc
    PS = C.psum
    m0 = A.mark()
    NDG = 128
    LR = A.f32(NDG)
    LI = A.f32(NDG)
    DT = A.f32(NDG)
    dup_transpose(C, C.s5_lam_re.rearrange("d g p -> (d g) p"), LR, "LR")
    dup_transpose(C, C.s5_lam_im.rearrange("d g p -> (d g) p"), LI, "LI")
    ldt = A.f32(1)
    P.dma(ldt, C.s5_log_dt.rearrange("d (g o) -> (d g) o", o=1), writes=["ldt"])
    ldtb = A.f32(128)
    P.op("dve", lambda e: e.tensor_copy(out=ldtb, in_=ldt.to_broadcast([128, 128])), reads=["ldt"], writes=["ldtb"])
    P.op("pe", lambda e: e.transpose(PS[7][:, 0:128], ldtb, C.ident), reads=["ldtb"], writes=["bank7"])
    P.op("act", lambda e: e.activation(out=DT, in_=PS[7][:, 0:128], func=AF.Exp), reads=["bank7"], writes=["DT"])
    ZR = A.f32(NDG)
    ZI = A.f32(NDG)
    P.op("dve", lambda e: e.tensor_tensor(out=ZR, in0=LR, in1=DT, op=ALU.mult), reads=["LR", "DT"], writes=["ZR"])
    P.op("dve", lambda e: e.tensor_tensor(out=ZI, in0=LI, in1=DT, op=ALU.mult), reads=["LI", "DT"], writes=["ZI"])
    NP_ = 17
    PWr = A.f32(NP_ * NDG).rearrange("p (i g) -> p i g", i=NP_)
    PWi = A.f32(NP_ * NDG).rearrange("p (i g) -> p i g", i=NP_)
    Fr = A.f32(NDG)
    Fi = A.f32(NDG)
    RHO = A.f32(NDG)
    rho16 = A.f32(NDG)
    mS1 = A.mark()
    nvec = A.f32(NP_)
    for i in range(NP_):
        P.op("pool", lambda e, i=i: e.memset(nvec[:, i:i + 1], float(i - 8)), writes=["nvec"])
    ang = A.f32(NP_ * NDG).rearrange("p (i g) -> p i g", i=NP_)
    mag = A.f32(NP_ * NDG).rearrange("p (i g) -> p i g", i=NP_)
    nb3 = nvec.unsqueeze(2).to_broadcast([128, NP_, NDG])
    P.op("dve", lambda e: e.tensor_tensor(out=ang, in0=ZI.unsqueeze(1).to_broadcast([128, NP_, NDG]), in1=nb3, op=ALU.mult), reads=["ZI", "nvec"], writes=["anga"])
    P.op("dve", lambda e: e.tensor_tensor(out=mag, in0=ZR.unsqueeze(1).to_broadcast([128, NP_, NDG]), in1=nb3, op=ALU.mult), reads=["ZR", "nvec"], writes=["mag"])
    P.op("act", lambda e: e.activation(out=mag, in_=mag, func=AF.Exp), reads=["mag"], writes=["mag"])
    sn = A.f32(NP_ * NDG).rearrange("p (i g) -> p i g", i=NP_)
    cs = A.f32(NP_ * NDG).rearrange("p (i g) -> p i g", i=NP_)
    f2 = lambda t: t.rearrange("p i g -> p (i g)")
    m1 = A.mark()
    sincos(C, f2(ang), NP_ * NDG, f2(sn), f2(cs), "ang")
    P.op("dve", lambda e: e.tensor_tensor(out=PWr, in0=mag, in1=cs, op=ALU.mult), reads=["mag", "ango"], writes=["PWr"])
    P.op("dve", lambda e: e.tensor_tensor(out=PWi, in0=mag, in1=sn, op=ALU.mult), reads=["mag", "ango"], writes=["PWi"])
    den = A.f32(NDG)
    t0_ = A.f32(NDG)
    a_re, a_im = PWr[:, 9, :], PWi[:, 9, :]
    P.op("dve", lambda e: e.tensor_tensor(out=den, in0=LR, in1=LR, op=ALU.mult), reads=["LR"], writes=["den"])
    P.op("dve", lambda e: e.tensor_tensor(out=t0_, in0=LI, in1=LI, op=ALU.mult), reads=["LI"], writes=["t0_"])
    P.op("dve", lambda e: e.tensor_tensor(out=den, in0=den, in1=t0_, op=ALU.add), reads=["den", "t0_"], writes=["den"])
    P.op("dve", lambda e: e.reciprocal(out=den, in_=den), reads=["den"], writes=["den"])
    nr = A.f32(NDG)
    P.op("dve", lambda e: e.tensor_scalar(out=nr, in0=a_re, scalar1=-1.0, scalar2=None, op0=ALU.add), reads=["PWr"], writes=["nr"])
    P.op("dve", lambda e: e.tensor_tensor(out=Fr, in0=nr, in1=LR, op=ALU.mult), reads=["nr", "LR"], writes=["Fr"])
    P.op("dve", lambda e: e.tensor_tensor(out=t0_, in0=a_im, in1=LI, op=ALU.mult), reads=["PWi", "LI", "den"], writes=["t0_"])
    P.op("dve", lambda e: e.tensor_tensor(out=Fr, in0=Fr, in1=t0_, op=ALU.add), reads=["Fr", "t0_"], writes=["Fr"])
    P.op("dve", lambda e: e.tensor_tensor(out=Fr, in0=Fr, in1=den, op=ALU.mult), reads=["Fr", "den"], writes=["Fr"])
    P.op("dve", lambda e: e.tensor_tensor(out=Fi, in0=a_im, in1=LR, op=ALU.mult), reads=["PWi", "LR"], writes=["Fi"])
    P.op("dve", lambda e: e.tensor_tensor(out=t0_, in0=nr, in1=LI, op=ALU.mult), reads=["nr", "LI", "Fr"], writes=["t0_"])
    P.op("dve", lambda e: e.tensor_tensor(out=Fi, in0=Fi, in1=t0_, op=ALU.subtract), reads=["Fi", "t0_"], writes=["Fi"])
    P.op("dve", lambda e: e.tensor_tensor(out=Fi, in0=Fi, in1=den, op=ALU.mult), reads=["Fi", "den"], writes=["Fi"])
    P.op("dve", lambda e: e.tensor_copy(out=RHO, in_=mag[:, 16, :]), reads=["mag"], writes=["RHO"])
    P.op("act", lambda e: e.activation(out=rho16, in_=ZR, func=AF.Exp, scale=128.0), reads=["ZR"], writes=["rho16"])
    P.barrier()
    A.release(mS1)
    ZI2 = A.f32(NDG)
    P.op("dve", lambda e: e.tensor_tensor(out=ZI2, in0=LI, in1=DT, op=ALU.mult), writes=["ZI2"])
    mS2 = A.mark()
    jv = A.f32(48)
    for j in range(16):
        P.op("pool", lambda e, j=j: e.memset(jv[:, j:j + 1], 8.0 * j), writes=["jv"])
    for s_ in range(32):
        P.op("pool", lambda e, s_=s_: e.memset(jv[:, 16 + s_:17 + s_], 128.0 * s_), writes=["jv"])
    TT = A.f32(64 * 128).rearrange("p (g c) -> p g c", c=128)
    a48 = A.f32(64 * 48).rearrange("p (g j) -> p g j", j=48)
    s48 = A.f32(64 * 48).rearrange("p (g j) -> p g j", j=48)
    c48 = A.f32(64 * 48).rearrange("p (g j) -> p g j", j=48)
    g2 = lambda t: t.rearrange("p g j -> p (g j)")
    mS3 = A.mark()
    for d in range(2):
        ds_ = slice(d * 64, (d + 1) * 64)
        P.op("pool", lambda e: e.memset(TT, 0.0), reads=["TT"], writes=["TT"])
        P.op("dve", lambda e, ds_=ds_: e.tensor_tensor(out=a48, in0=ZI2[:, ds_].unsqueeze(2).to_broadcast([128, 64, 48]), in1=jv.unsqueeze(1).to_broadcast([128, 64, 48]), op=ALU.mult),
             reads=["ZI2", "jv", "a48o"], writes=["a48a"])
        sincos(C, g2(a48), 64 * 48, g2(s48), g2(c48), "a48")
        A.release(mS3)
        if d == 0:
            P.op("dve", lambda e: e.tensor_copy(out=TT[:, :, 0:16], in_=c48[:, :, 0:16]), reads=["a48o", "TT"], writes=["TT"])
            P.op("dve", lambda e: e.tensor_copy(out=TT[:, :, 16:32], in_=s48[:, :, 0:16]), reads=["a48o", "TT"], writes=["TT"])
        else:
            P.op("dve", lambda e: e.tensor_copy(out=TT[:, :, 0:16], in_=c48[:, :, 15::-1]), reads=["a48o", "TT"], writes=["TT"])
            P.op("dve", lambda e: e.tensor_copy(out=TT[:, :, 16:32], in_=s48[:, :, 15::-1]), reads=["a48o", "TT"], writes=["TT"])
        P.op("dve", lambda e: e.tensor_copy(out=TT[:, :, 48:80], in_=c48[:, :, 16:48]), reads=["a48o", "TT"], writes=["TT"])
        P.op("dve", lambda e: e.tensor_copy(out=TT[:, :, 80:112], in_=s48[:, :, 16:48]), reads=["a48o", "TT"], writes=["TT"])
        P.op("dve", lambda e, ds_=ds_: e.tensor_copy(out=TT[:, :, 33:48], in_=RHO[:, ds_].unsqueeze(2).to_broadcast([128, 64, 15])), reads=["RHO", "TT"], writes=["TT"])
        P.op("dve", lambda e, ds_=ds_: e.tensor_copy(out=TT[:, :, 112], in_=RHO[:, ds_]), reads=["RHO", "TT"], writes=["TT"])
        P.op("dve", lambda e, ds_=ds_: e.tensor_copy(out=TT[:, :, 113], in_=rho16[:, ds_]), reads=["rho16", "TT"], writes=["TT"])
        P.dma(C.s5t[:, :, d, :].rearrange("g p c -> p g c"), TT, reads=["TT"], writes=["s5t"], q=("sp" if d == 0 else "act"))
    P.barrier()
    A.release(mS2)
    CTr = A.f32(NDG * 16).rearrange("p (g h) -> p g h", h=16)
    CTi = A.f32(NDG * 16).rearrange("p (g h) -> p g h", h=16)
    BTr = A.f32(NDG * 16).rearrange("p (g h) -> p g h", h=16)
    BTi = A.f32(NDG * 16).rearrange("p (g h) -> p g h", h=16)
    for (src, dst, key) in ((C.s5_c_re, CTr, "CTr"), (C.s5_c_im, CTi, "CTi")):
        rows = src.rearrange("d g h p -> (d g h) p")
        for blk in range(16):
            stg = A.f32(128)
            sk = A.key("cstg")
            P.dma(stg[:, 0:64], rows[blk * 128:(blk + 1) * 128, :], writes=[sk], q=("sp" if blk % 2 == 0 else "act"))
            P.dma(stg[:, 64:128], rows[blk * 128:(blk + 1) * 128, :], writes=[sk + "b"], q=("act" if blk % 2 == 0 else "sp"))
            bk = f"bank{6 + blk % 2}"
            ps = PS[6 + blk % 2][:, 0:128]
            P.op("pe", lambda e, ps=ps, stg=stg: e.transpose(ps, stg, C.ident), reads=[sk, sk + "b"], writes=[bk])
            P.op("dve", lambda e, ps=ps, dst=dst, blk=blk: e.tensor_copy(out=dst[:, blk * 8:(blk + 1) * 8, :], in_=ps.rearrange("p (g h) -> p g h", h=16)),
                 reads=[bk], writes=[key])
    for (src, dst, key) in ((C.s5_b_re, BTr, "BTr"), (C.s5_b_im, BTi, "BTi")):
        v = src.rearrange("d g p h -> p (d g) h")
        for half in range(2):
            for q4 in range(4):
                P.dma(dst[half * 64:(half + 1) * 64, q4 * 32:(q4 + 1) * 32, :], v[:, q4 * 32:(q4 + 1) * 32, :], writes=[key + f"{half}{q4}"],
                      q=("sp" if (half + q4) % 2 == 0 else "act"))
    bt_keys = lambda k: [k + f"{h}{q}" for h in range(2) for q in range(4)]
    BBr = A.f32(NDG * 16).rearrange("p (g h) -> p g h", h=16)
    BBi = A.f32(NDG * 16).rearrange("p (g h) -> p g h", h=16)
    tq = A.f32(NDG * 16).rearrange("p (g h) -> p g h", h=16)
    frb = Fr.unsqueeze(2).to_broadcast([128, NDG, 16])
    fib = Fi.unsqueeze(2).to_broadcast([128, NDG, 16])
    P.op("dve", lambda e: e.tensor_tensor(out=BBr, in0=BTr, in1=frb, op=ALU.mult), reads=bt_keys("BTr") + ["Fr"], writes=["BBr"])
    P.op("dve", lambda e: e.tensor_tensor(out=tq, in0=BTi, in1=fib, op=ALU.mult), reads=bt_keys("BTi") + ["Fi"], writes=["tq"])
    P.op("dve", lambda e: e.tensor_tensor(out=BBr, in0=BBr, in1=tq, op=ALU.subtract), reads=["BBr", "tq"], writes=["BBr"])
    P.op("dve", lambda e: e.tensor_tensor(out=BBi, in0=BTi, in1=frb, op=ALU.mult), reads=bt_keys("BTi") + ["Fr"], writes=["BBi"])
    P.op("dve", lambda e: e.tensor_tensor(out=tq, in0=BTr, in1=fib, op=ALU.mult), reads=bt_keys("BTr") + ["Fi", "BBr"], writes=["tq"])
    P.op("dve", lambda e: e.tensor_tensor(out=BBi, in0=BBi, in1=tq, op=ALU.add), reads=["BBi", "tq"], writes=["BBi"])
    dst_ = A.f32(128, parts=64)
    for t in range(8):
        P.dma(dst_[:, t * 16:(t + 1) * 16], C.s5_d, writes=[f"dst{t}"], q=("sp" if t % 2 == 0 else "act"))
    dcol = A.f32(64)
    P.op("pe", lambda e: e.transpose(PS[7][:, 0:64], dst_, C.ident[0:64, 0:64]), reads=[f"dst{t}" for t in range(8)], writes=["bank7"])
    P.op("dve", lambda e: e.tensor_copy(out=dcol, in_=PS[7][:, 0:64]), reads=["bank7"], writes=["dcol"])
    mf = A.f32(128)
    mb = A.f32(128)
    P.dma(mf, C.s5_masks[0], writes=["mf"])
    P.dma(mb, C.s5_masks[1], writes=["mb"])
    GC = 8
    al = [A.f32(NDG * 8).rearrange("p (g t) -> p g t", t=8) for _ in range(8)]
    def gather(out_t, PW, idx_f, idx_b, sgn_top, sgn_bot, key, rk):
        for (g0, idx) in ((0, idx_f), (64, idx_b)):
            for t in range(8):
                for (p0, sg) in ((0, sgn_top), (64, sgn_bot)):
                    src_t = PW[0][p0:p0 + 64, idx[t] + 8, g0:g0 + 64] if sg[1] == "r" else PW[1][p0:p0 + 64, idx[t] + 8, g0:g0 + 64]
                    P.op("pool", lambda e, out_t=out_t, src_t=src_t, p0=p0, g0=g0, t=t, sg=sg: e.tensor_scalar(
                        out=out_t[p0:p0 + 64, g0:g0 + 64, t], in0=src_t, scalar1=float(sg[0]), scalar2=None, op0=ALU.mult), reads=rk, writes=[key])
    PW = (PWr, PWi)
    pk = ["PWr", "PWi"]
    tf = [t + 1 for t in range(8)]
    tb_ = [8 - t for t in range(8)]
    gather(al[0], PW, tf, tb_, (1, "r"), (-1, "i"), "al0", pk)
    gather(al[1], PW, tf, tb_, (-1, "i"), (-1, "r"), "al1", pk)
    gather(al[2], PW, tf, tb_, (-1, "i"), (-1, "r"), "al2", pk)
    gather(al[3], PW, tf, tb_, (-1, "r"), (1, "i"), "al3", pk)
    xin_f = [7 - t for t in range(8)]
    xin_b = [t for t in range(8)]
    gather(al[4], PW, xin_f, xin_b, (1, "r"), (1, "i"), "al4", pk)
    gather(al[5], PW, xin_f, xin_b, (-1, "i"), (1, "r"), "al5", pk)
    xtp_f = [-1 - t for t in range(8)]
    xtp_b = [t - 8 for t in range(8)]
    gather(al[6], PW, xtp_f, xtp_b, (1, "r"), (1, "i"), "al6", pk)
    gather(al[7], PW, xtp_f, xtp_b, (-1, "i"), (1, "r"), "al7", pk)
    gen = [A.bf16(GC * 128).rearrange("p (g t h) -> p g t h", g=GC, t=8) for _ in range(4)]
    tm1 = A.f32(GC * 128).rearrange("p (g t h) -> p g t h", g=GC, t=8)
    tm2 = A.f32(GC * 128).rearrange("p (g t h) -> p g t h", g=GC, t=8)
    wstage = [A.bf16(1024) for _ in range(2)]
    tpf = A.f32(128)
    tpb = A.f32(128)
    specs = ((CTr, CTi, 0, 1, ["CTr", "CTi"]), (CTr, CTi, 2, 3, ["CTr", "CTi"]), (BBr, BBi, 4, 5, ["BBr", "BBi"]), (BBr, BBi, 6, 7, ["BBr", "BBi"]))
    GEN = {}
    for d in range(2):
        for gc in range(64 // GC):
            bg_step(C, 3)
            dg0 = d * 64 + gc * GC
            for gi_, (Sr, Si, ia, ib, rk) in enumerate(specs):
                eng = "dve" if gi_ % 2 == 0 else "pool"
                srb = Sr[:, dg0:dg0 + GC, :].unsqueeze(2).to_broadcast([128, GC, 8, 16])
                sib = Si[:, dg0:dg0 + GC, :].unsqueeze(2).to_broadcast([128, GC, 8, 16])
                aa = al[ia][:, dg0:dg0 + GC, :].unsqueeze(3).to_broadcast([128, GC, 8, 16])
                bb = al[ib][:, dg0:dg0 + GC, :].unsqueeze(3).to_broadcast([128, GC, 8, 16])
                gk = f"gen{gi_}"
                P.op(eng, lambda e, srb=srb, aa=aa: e.tensor_tensor(out=tm1, in0=srb, in1=aa, op=ALU.mult), reads=rk + [f"al{ia}"], writes=["tm1"])
                P.op(eng, lambda e, sib=sib, bb=bb: e.tensor_tensor(out=tm2, in0=sib, in1=bb, op=ALU.mult), reads=rk + [f"al{ib}"], writes=["tm2"])
                P.op(eng, lambda e, gi_=gi_: e.tensor_tensor(out=gen[gi_], in0=tm1, in1=tm2, op=ALU.add), reads=["tm1", "tm2"], writes=[gk])
            for gl in range(GC):
                g = gc * GC + gl
                base = 128 + d * 448
                W1g = gen[0][:, gl].rearrange("p t h -> p (t h)")
                W2g = gen[1][:, gl].rearrange("p t h -> p (t h)")
                Xin = gen[2][:, gl].rearrange("p t h -> p (t h)")
                Xtp = gen[3][:, gl].rearrange("p t h -> p (t h)")
                ws = wstage[(d * 64 + g) % 2]
                wk = f"ws{(d * 64 + g) % 2}"
                pst = PS[0][:, 0:64].bitcast(BF16)
                P.op("pe", lambda e, pst=pst, Xin=Xin: e.transpose(pst, Xin, C.identb), reads=["gen2"], writes=["bank0"])
                P.op("act", lambda e, ws=ws, pst=pst: e.copy(out=ws[:, 0:128], in_=pst), reads=["bank0"], writes=[wk + "a"])
                P.op("act", lambda e, ws=ws, pst=pst: e.activation(out=ws[:, 128:192], in_=pst[:, 0:64], func=AF.Identity, scale=-1.0), reads=["bank0"], writes=[wk + "b"])
                P.op("dve", lambda e, ws=ws, W1g=W1g: e.tensor_copy(out=ws[:, 192:320], in_=W1g), reads=["gen0"], writes=[wk + "c"])
                P.op("dve", lambda e, ws=ws, W2g=W2g: e.tensor_copy(out=ws[:, 320:448], in_=W2g), reads=["gen1"], writes=[wk + "d"])
                P.dma(C.s5w[g, :, base:base + 448], ws[:, 0:448], reads=[wk + "a", wk + "b", wk + "c", wk + "d"], writes=["s5w"], q=("sp" if g % 2 == 0 else "act"))
                ptp = PS[1 + d][:, 0:128]
                P.op("pe", lambda e, ptp=ptp, Xtp=Xtp, W1g=W1g: e.matmul(ptp, lhsT=Xtp, rhs=W1g, start=True, stop=True), reads=["gen3", "gen0"], writes=[f"bank{1 + d}"])
                P.op("dve", lambda e, ptp=ptp, d=d: e.tensor_tensor(out=(tpf if d == 0 else tpb), in0=ptp, in1=(mf if d == 0 else mb), op=ALU.mult),
                     reads=[f"bank{1 + d}", "mf", "mb"], writes=["tpx"])
                P.dma(C.s5tp[d, g], (tpf if d == 0 else tpb), reads=["tpx"], writes=["s5tp"], q="sp")
    ta = [A.f32(128) for _ in range(2)]
    tb2 = [A.f32(128) for _ in range(2)]
    tob = [A.bf16(128) for _ in range(2)]
    for g in range(64):
        i = g % 2
        if g % 2 == 0:
            bg_step(C, 1)
        P.dma(ta[i], C.s5tp[0, g], reads=["s5tp"], writes=[f"ta{i}"], q="sp")
        P.dma(tb2[i], C.s5tp[1, g], reads=["s5tp"], writes=[f"tb{i}"], q="act")
        P.op("dve", lambda e, i=i: e.tensor_tensor(out=ta[i], in0=ta[i], in1=tb2[i], op=ALU.add), reads=[f"ta{i}", f"tb{i}"], writes=[f"ta{i}"])
        P.op("dve", lambda e, i=i, g=g: e.scalar_tensor_tensor(out=tob[i], in0=C.ident, scalar=dcol[:, g:g + 1], in1=ta[i], op0=ALU.mult, op1=ALU.add),
             reads=[f"ta{i}", "dcol"], writes=[f"tob{i}"])
        P.dma(C.s5w[g, :, 0:128], tob[i], reads=[f"tob{i}"], writes=["s5w"], q="sp")
    A.release(m0)
    P.barrier()


def s5_phase(C):
    P, A, nc = C.P, C.A, C.nc
    PS = C.psum
    l = 0
    seqs = C.seqs
    m0 = A.mark()
    sc1 = A.f32(D)
    sh = A.f32(D)
    xc = A.f32(8 * D).rearrange("p (t c) -> p t c", t=8)
    tmp = A.f32(D)
    hperm = A.bf16(8 * D).rearrange("p (g t h) -> p g t h", g=64, t=8)
    ust = [A.bf16(64 * 128).rearrange("p (g c) -> p g c", g=64) for _ in range(2)]
    cnt = 0
    for sq in seqs:
        ci = sq.ci
        bc_load(C, sh, C.modd[l, ci, 0:D], "sh")
        bc_load(C, sc1, C.modd[l, ci, D:2 * D], "sc1")
        P.op("pool", lambda e: e.tensor_scalar(out=sc1, in0=sc1, scalar1=1.0, scalar2=None, op0=ALU.add), reads=["sc1"], writes=["sc1"])
        for cb in range(sq.n // 1024):
            P.dma(xc, sq.x0[cb * 1024:(cb + 1) * 1024, :].rearrange("(c t) d -> c t d", t=8), writes=["xc"], q="sp")
            for t in range(8):
                P.op("dve", lambda e, t=t: e.tensor_tensor(out=tmp, in0=xc[:, t, :], in1=sc1, op=ALU.mult), reads=["xc", "sc1"], writes=["tmp"])
                P.op("dve", lambda e, t=t: e.tensor_tensor(out=hperm[:, :, t, :], in0=tmp.rearrange("p (g h) -> p g h", h=16),
                                                             in1=sh.rearrange("p (g h) -> p g h", h=16), op=ALU.add), reads=["tmp", "sh"], writes=["hperm"])
            us = ust[cnt % 2]
            uk = f"ust{cnt % 2}"
            cnt += 1
            for g4 in range(16):
                pT = PS[g4 % 2][:, 0:256].bitcast(BF16)
                bk = f"bank{g4 % 2}"
                for gi_ in range(4):
                    g = g4 * 4 + gi_
                    P.op("pe", lambda e, pT=pT, g=g, gi_=gi_: e.transpose(pT[:, gi_ * 128:(gi_ + 1) * 128], hperm[:, g].rearrange("p t h -> p (t h)"), C.identb),
                         reads=["hperm"], writes=[bk])
                if g4 % 2 == 0:
                    P.op("act", lambda e, pT=pT, us=us, g4=g4: e.copy(out=us[:, g4 * 4:(g4 + 1) * 4, :], in_=pT.rearrange("p (g c) -> p g c", g=4)), reads=[bk], writes=[uk + "a"])
                else:
                    P.op("dve", lambda e, pT=pT, us=us, g4=g4: e.tensor_copy(out=us[:, g4 * 4:(g4 + 1) * 4, :], in_=pT.rearrange("p (g c) -> p g c", g=4)), reads=[bk], writes=[uk + "d"])
            P.dma(sq.U[cb], us, reads=[uk + "a", uk + "d"], writes=["Uscr"], q="sp")
    A.release(m0)
    P.barrier()
    m0 = A.mark()
    Jm = A.f32(128)
    P.op("dve", lambda e: e.tensor_scalar(out=Jm[:, 0:64], in0=C.ident[:, 64:128], scalar1=-1.0, scalar2=None, op0=ALU.mult), writes=["Jm"])
    P.op("dve", lambda e: e.tensor_copy(out=Jm[:, 64:128], in_=C.ident[:, 0:64]), reads=["Jm"], writes=["Jm"])
    NCM = 512
    wg_r = [A.bf16(1024) for _ in range(2)]
    tg_r = [A.f32(256).rearrange("p (d c) -> p d c", d=2) for _ in range(2)]
    ug_r = [A.bf16(NCM) for _ in range(2)]
    t1_r = [A.f32(NCM) for _ in range(2)]
    t2_r = [A.f32(NCM) for _ in range(2)]
    bp_r = [A.f32(NCM) for _ in range(2)]
    mf_r = [A.f32(NCM) for _ in range(2)]
    ql_r = [A.f32(NCM) for _ in range(2)]
    qq_r = [A.f32(NCM) for _ in range(2)]
    cq_r = [A.bf16(NCM + 2) for _ in range(2)]
    sq_r = [A.bf16(NCM + 2) for _ in range(2)]
    zg_r = [A.bf16(NCM) for _ in range(2)]
    ec_r = [A.f32(32) for _ in range(2)]
    e1_r = [A.f32(32) for _ in range(2)]
    e2_r = [A.f32(32) for _ in range(2)]
    hin_r = [A.f32(34) for _ in range(2)]
    r16_r = [A.f32(32) for _ in range(2)]
    gg_r = [A.f32(32) for _ in range(2)]
    it = 0
    for sq in seqs:
        n_c = sq.n // 8
        n_b = n_c // 16
        ncb = sq.n // 1024
        for g in range(64):
            wg = wg_r[g % 2]
            tg = tg_r[g % 2]
            ug = ug_r[g % 2]
            wk, tk, uk = f"wg{g % 2}", f"tg{g % 2}", f"ug{g % 2}"
            P.dma(wg, C.s5w[g], writes=[wk], q="sp")
            P.dma(tg, C.s5t[g], writes=[tk], q="sp")
            P.dma(ug[:, 0:n_c].rearrange("p (b c) -> p b c", c=128), sq.U[:, :, g, :].rearrange("b p c -> p b c"), writes=[uk], q="sp")
            py = PS[4 + g % 2][:, 0:n_c]
            pyk = f"bank{4 + g % 2}"
            P.op("pe", lambda e, py=py, wg=wg, ug=ug, n_c=n_c: e.matmul(py, lhsT=wg[:, 0:128], rhs=ug[:, 0:n_c], start=True, stop=False), reads=[wk, uk], writes=[pyk])
            def chain(d, i2):
                base = 128 + d * 448
                pX = PS[0 + 2 * i2][:, 0:n_c]
                pXt = PS[1 + 2 * i2][:, 0:n_c]
                kX, kXt = f"bank{0 + 2 * i2}", f"bank{1 + 2 * i2}"
                P.op("pe", lambda e, pX=pX, wg=wg, ug=ug, base=base, n_c=n_c: e.matmul(pX, lhsT=wg[:, base:base + 128], rhs=ug[:, 0:n_c], start=True, stop=True),
                     reads=[wk, uk], writes=[kX])
                yield
                P.op("pe", lambda e, pXt=pXt, wg=wg, ug=ug, base=base, n_c=n_c: e.matmul(pXt, lhsT=wg[:, base + 64:base + 192], rhs=ug[:, 0:n_c], start=True, stop=True),
                     reads=[wk, uk], writes=[kXt])
                yield
                t1, t2, bp, mfu, ql, qq = t1_r[i2], t2_r[i2], bp_r[i2], mf_r[i2], ql_r[i2], qq_r[i2]
                ec, e1, e2, hin, r16, gg_ = ec_r[i2], e1_r[i2], e2_r[i2], hin_r[i2], r16_r[i2], gg_r[i2]
                kec, ke1, ke2, khin, kr16, kgg = f"ec{i2}", f"e1{i2}", f"e2{i2}", f"hin{i2}", f"r16{i2}", f"gg{i2}"
                cq, sqq = cq_r[i2], sq_r[i2]
                k1, k2, kb, km, kl_, kq, kc, ks = f"t1{i2}", f"t2{i2}", f"bp{i2}", f"mf{i2}", f"ql{i2}", f"qq{i2}", f"cq{i2}", f"sq{i2}"
                v3 = lambda t_, n_c=n_c: t_[:, 0:n_c].rearrange("p (b j) -> p b j", j=16)
                cosb = tg[:, d, 0:16].unsqueeze(1).to_broadcast([128, n_b, 16])
                sinb = tg[:, d, 16:32].unsqueeze(1).to_broadcast([128, n_b, 16])
                mrow = tg[:, d, 32:48].unsqueeze(1).to_broadcast([128, n_b, 16])
                cos2 = tg[:, d, 48:48 + n_b]
                sin2 = tg[:, d, 80:80 + n_b]
                rho = tg[:, d, 112:113]
                rho16 = tg[:, d, 113:114]
                P.op("dve", lambda e, t1=t1, pX=pX, cosb=cosb, v3=v3: e.tensor_tensor(out=v3(t1), in0=pX.rearrange("p (b j) -> p b j", j=16), in1=cosb, op=ALU.mult),
                     reads=[kX, tk], writes=[k1])
                yield
                P.op("dve", lambda e, t2=t2, pXt=pXt, sinb=sinb, v3=v3: e.tensor_tensor(out=v3(t2), in0=pXt.rearrange("p (b j) -> p b j", j=16), in1=sinb, op=ALU.mult),
                     reads=[kXt, tk], writes=[k2])
                yield
                P.op("dve", lambda e, bp=bp, t1=t1, t2=t2, n_c=n_c: e.tensor_tensor(out=bp[:, 0:n_c], in0=t1[:, 0:n_c], in1=t2[:, 0:n_c], op=ALU.add), reads=[k1, k2], writes=[kb])
                yield
                P.op("act", lambda e, mfu=mfu, mrow=mrow, v3=v3: e.copy(out=v3(mfu), in_=mrow), reads=[tk], writes=[km])
                yield
                P.op("pool", lambda e, rho16=rho16, n_b=n_b: e.tensor_copy(out=r16[:, 0:n_b], in_=rho16.to_broadcast([128, n_b])), reads=[tk], writes=[kr16])
                yield
                rv = (lambda a_, n_c=n_c: a_[:, 0:n_c]) if d == 0 else (lambda a_, n_c=n_c: a_[:, n_c - 1::-1])
                P.op("dve", lambda e, ql=ql, mfu=mfu, bp=bp, rv=rv, n_c=n_c: e.tensor_tensor_scan(out=rv(ql), data0=mfu[:, 0:n_c], data1=rv(bp), initial=0.0, op0=ALU.mult, op1=ALU.add),
                     reads=[km, kb], writes=[kl_])
                yield
                if d == 0:
                    e_src = ql[:, 15:n_c:16]
                    b_inj = bp[:, 0:n_c:16]
                else:
                    e_src = ql[:, n_c - 16::-16]
                    b_inj = bp[:, n_c - 1::-16]
                P.op("dve", lambda e, e_src=e_src, n_b=n_b: e.tensor_copy(out=ec[:, 0:n_b], in_=e_src), reads=[kl_], writes=[kec])
                yield
                P.op("pe", lambda e, n_b=n_b: e.matmul(PS[6][:, d * 64:d * 64 + n_b], lhsT=Jm, rhs=ec[:, 0:n_b], start=True, stop=True), reads=["Jm", kec], writes=["bank6"])
                yield
                P.op("dve", lambda e, cos2=cos2, n_b=n_b: e.tensor_tensor(out=e1[:, 0:n_b], in0=ec[:, 0:n_b], in1=cos2, op=ALU.mult), reads=[kec, tk], writes=[ke1])
                yield
                P.op("dve", lambda e, sin2=sin2, n_b=n_b: e.tensor_tensor(out=e2[:, 0:n_b], in0=PS[6][:, d * 64:d * 64 + n_b], in1=sin2, op=ALU.mult), reads=["bank6", tk], writes=[ke2])
                yield
                P.op("dve", lambda e, n_b=n_b: e.tensor_tensor(out=e1[:, 0:n_b], in0=e1[:, 0:n_b], in1=e2[:, 0:n_b], op=ALU.subtract), reads=[ke1, ke2], writes=[ke1])
                yield
                P.op("dve", lambda e: e.memset(hin[:, 0:1], 0.0), writes=[khin])
                yield
                P.op("dve", lambda e, n_b=n_b: e.tensor_tensor_scan(out=hin[:, 1:n_b + 1], data0=r16[:, 0:n_b], data1=e1[:, 0:n_b], initial=0.0, op0=ALU.mult, op1=ALU.add),
                     reads=[kr16, ke1, khin], writes=[khin])
                yield
                P.op("pe", lambda e, n_b=n_b: e.matmul(PS[7][:, d * 64:d * 64 + n_b], lhsT=Jm, rhs=hin[:, 0:n_b], start=True, stop=True), reads=["Jm", khin], writes=["bank7"])
                yield
                P.op("dve", lambda e, cos2=cos2, n_b=n_b: e.tensor_tensor(out=gg_[:, 0:n_b], in0=hin[:, 0:n_b], in1=cos2, op=ALU.mult), reads=[khin, tk], writes=[kgg])
                yield
                P.op("dve", lambda e, sin2=sin2, n_b=n_b: e.tensor_tensor(out=e2[:, 0:n_b], in0=PS[7][:, d * 64:d * 64 + n_b], in1=sin2, op=ALU.mult), reads=["bank7", tk, ke1], writes=[ke2])
                yield
                P.op("dve", lambda e, n_b=n_b: e.tensor_tensor(out=gg_[:, 0:n_b], in0=gg_[:, 0:n_b], in1=e2[:, 0:n_b], op=ALU.add), reads=[kgg, ke2], writes=[kgg])
                yield
                P.op("dve", lambda e, b_inj=b_inj, rho=rho, n_b=n_b: e.scalar_tensor_tensor(out=b_inj, in0=gg_[:, 0:n_b], scalar=rho, in1=b_inj, op0=ALU.mult, op1=ALU.add),
                     reads=[kgg, tk, kb, kl_], writes=[kb])
                yield
                P.op("dve", lambda e, qq=qq, mfu=mfu, bp=bp, rv=rv, n_c=n_c: e.tensor_tensor_scan(out=rv(qq), data0=mfu[:, 0:n_c], data1=rv(bp), initial=0.0, op0=ALU.mult, op1=ALU.add),
                     reads=[km, kb], writes=[kq])
                yield
                o = 1 if d == 0 else 0
                zc = 0 if d == 0 else n_c
                P.op("pool", lambda e, cq=cq, zc=zc: e.memset(cq[:, zc:zc + 1], 0.0), reads=[kc], writes=[kc])
                yield
                P.op("pool", lambda e, sqq=sqq, zc=zc: e.memset(sqq[:, zc:zc + 1], 0.0), reads=[ks], writes=[ks])
                yield
                P.op("dve", lambda e, cq=cq, qq=qq, cosb=cosb, o=o, n_c=n_c: e.tensor_tensor(out=cq[:, o:o + n_c].rearrange("p (b j) -> p b j", j=16),
                                                                                              in0=qq[:, 0:n_c].rearrange("p (b j) -> p b j", j=16), in1=cosb, op=ALU.mult),
                     reads=[kq, tk, kc], writes=[kc])
                yield
                P.op("dve", lambda e, sqq=sqq, qq=qq, sinb=sinb, o=o, n_c=n_c: e.tensor_tensor(out=sqq[:, o:o + n_c].rearrange("p (b j) -> p b j", j=16),
                                                                                                in0=qq[:, 0:n_c].rearrange("p (b j) -> p b j", j=16), in1=sinb, op=ALU.mult),
                     reads=[kq, tk, ks], writes=[ks])
                yield
                shf = 0 if d == 0 else 1
                P.op("pe", lambda e, py=py, wg=wg, cq=cq, base=base, shf=shf, n_c=n_c: e.matmul(py, lhsT=wg[:, base + 192:base + 320], rhs=cq[:, shf:shf + n_c], start=False, stop=False),
                     reads=[wk, kc], writes=[pyk])
                yield
                P.op("pe", lambda e, py=py, wg=wg, sqq=sqq, base=base, shf=shf, n_c=n_c, d=d: e.matmul(py, lhsT=wg[:, base + 320:base + 448], rhs=sqq[:, shf:shf + n_c], start=False, stop=(d == 1)),
                     reads=[wk, ks], writes=[pyk])
                yield
            for _ in itertools.zip_longest(chain(0, 0), chain(1, 1)):
                pass
            zg = zg_r[g % 2]
            zk = f"zg{g % 2}"
            P.op("act", lambda e, zg=zg, py=py, n_c=n_c: e.activation(out=zg[:, 0:n_c], in_=py, func=AF.Gelu_apprx_tanh), reads=[pyk], writes=[zk])
            P.dma(sq.Z[:, :, g, :].rearrange("t p c -> p t c"), zg[:, 0:n_c].rearrange("p (t c) -> p t c", c=64), reads=[zk], writes=["Zscr"], q="act")
    A.release(m0)
    P.barrier()
    m0 = A.mark()
    wglu = A.bf16(8 * 2048).rearrange("p (k c) -> p k c", k=8)
    P.dma(wglu, C.wglu_s, writes=["wglu"], q="act")
    bgl_f = A.f32(2048, parts=1)
    P.dma(bgl_f, C.s5_b_glu.rearrange("(o n) -> o n", o=1), writes=["bgl_f"])
    bgl = A.bf16(2048, parts=1)
    P.op("dve", lambda e: e.tensor_copy(out=bgl, in_=bgl_f), reads=["bgl_f"], writes=["bgl"])
    ones1 = A.bf16(128, parts=1)
    P.op("pool", lambda e: e.memset(ones1, 1.0), writes=["ones1"])
    lnG = A.f32(D)
    lnB = A.f32(D)
    bc_load(C, lnG, C.ln_g[l, 0], "lnG")
    bc_load(C, lnB, C.ln_b[l, 0], "lnB")
    gm = A.f32(D)
    xt = A.f32(4 * D).rearrange("p (b c) -> p b c", b=4)
    zt = A.bf16(64 * 64).rearrange("p (g c) -> p g c", g=64)
    zc_ = A.bf16(8 * D, parts=64).rearrange("p (t c) -> p t c", t=8)
    zf = A.bf16(8 * 512).rearrange("p (k t) -> p k t", k=8)
    ga = [A.f32(512) for _ in range(2)]
    aa_ = [A.f32(512) for _ in range(2)]
    st = A.f32(12).rearrange("p (a b) -> p a b", a=2)
    mv = A.f32(2)
    rs = A.f32(1)
    nmr = A.f32(1)
    for sq in seqs:
        ci = sq.ci
        bc_load(C, gm, C.modd[l, ci, 2 * D:3 * D], "gm")
        for T in range(sq.n // 512):
            t0 = T * 512
            P.dma(zt, sq.Z[T], writes=["zt"], q="sp")
            P.dma(xt, sq.x0[t0:t0 + 512, :].rearrange("(b p) c -> p b c", p=128), writes=["xt"], q="sp")
            for b in range(4):
                P.op("act", lambda e, b=b: e.activation(out=xt[:, b, :], in_=xt[:, b, :], func=AF.Copy, scale=ALPHA), reads=["xt"], writes=["xt"])
            for g8 in range(8):
                p1 = PS[g8 % 2][0:64, :].bitcast(BF16)
                bk = f"bank{g8 % 2}"
                for gi_ in range(8):
                    g = g8 * 8 + gi_
                    P.op("pe", lambda e, p1=p1, g=g, gi_=gi_: e.transpose(p1[:, gi_ * 128:(gi_ + 1) * 128], zt[:, g, :], C.identb), reads=["zt"], writes=[bk])
                eng = "act" if g8 % 2 == 0 else "dve"
                outv = zc_[:, :, g8 * 128:(g8 + 1) * 128].rearrange("p t (g h) -> p g t h", h=16)
                inv = p1.rearrange("p (g t h) -> p g t h", g=8, t=8)
                if eng == "act":
                    P.op("act", lambda e, outv=outv, inv=inv: e.copy(out=outv, in_=inv), reads=[bk], writes=[f"zc{g8}"])
                else:
                    P.op("dve", lambda e, outv=outv, inv=inv: e.tensor_copy(out=outv, in_=inv), reads=[bk], writes=[f"zc{g8}"])
            for k in range(8):
                p2 = PS[2 + k % 2][:, 0:256].bitcast(BF16)
                bk = f"bank{2 + k % 2}"
                for t in range(8):
                    P.op("pe", lambda e, p2=p2, t=t, k=k: e.transpose(p2[:, t * 64:(t + 1) * 64], zc_[:, t, k * 128:(k + 1) * 128], C.identb[0:64, 0:64]), reads=[f"zc{k}"], writes=[bk])
                outv = zf[:, k, :].rearrange("p (c t) -> p t c", t=8)
                inv = p2.rearrange("p (t c) -> p t c", t=8)
                if k % 2 == 0:
                    P.op("act", lambda e, outv=outv, inv=inv: e.copy(out=outv, in_=inv), reads=[bk], writes=[f"zf{k}"])
                else:
                    P.op("dve", lambda e, outv=outv, inv=inv: e.tensor_copy(out=outv, in_=inv), reads=[bk], writes=[f"zf{k}"])
            zf_keys = [f"zf{k}" for k in range(8)]
            for b in range(4):
                for hh in range(2):
                    pa = PS[4 + hh]
                    pg_ = PS[6 + hh]
                    for k in range(8):
                        P.op("pe", lambda e, pa=pa, k=k, b=b, hh=hh: e.matmul(pa, lhsT=zf[:, k, b * 128:(b + 1) * 128], rhs=wglu[:, k, hh * 512:(hh + 1) * 512], start=(k == 0), stop=False),
                             reads=[zf_keys[k], "wglu"], writes=[f"bank{4 + hh}"])
                    P.op("pe", lambda e, pa=pa, hh=hh: e.matmul(pa, lhsT=ones1, rhs=bgl[:, hh * 512:(hh + 1) * 512], start=False, stop=True), reads=["ones1", "bgl"], writes=[f"bank{4 + hh}"])
                    for k in range(8):
                        P.op("pe", lambda e, pg_=pg_, k=k, b=b, hh=hh: e.matmul(pg_, lhsT=zf[:, k, b * 128:(b + 1) * 128], rhs=wglu[:, k, 1024 + hh * 512:1024 + (hh + 1) * 512], start=(k == 0), stop=False),
                             reads=[zf_keys[k], "wglu"], writes=[f"bank{6 + hh}"])
                    P.op("pe", lambda e, pg_=pg_, hh=hh: e.matmul(pg_, lhsT=ones1, rhs=bgl[:, 1024 + hh * 512:1024 + (hh + 1) * 512], start=False, stop=True), reads=["ones1", "bgl"], writes=[f"bank{6 + hh}"])
                    g_t, a_t = ga[hh], aa_[hh]
                    P.op("act", lambda e, g_t=g_t, pg_=pg_: e.activation(out=g_t, in_=pg_, func=AF.Sigmoid), reads=[f"bank{6 + hh}"], writes=[f"ga{hh}"])
                    P.op("dve", lambda e, a_t=a_t, pa=pa, g_t=g_t: e.tensor_tensor(out=a_t, in0=pa, in1=g_t, op=ALU.mult), reads=[f"bank{4 + hh}", f"ga{hh}"], writes=[f"aa{hh}"])
                    P.op("dve", lambda e, a_t=a_t, hh=hh: e.tensor_tensor(out=a_t, in0=a_t, in1=gm[:, hh * 512:(hh + 1) * 512], op=ALU.mult), reads=[f"aa{hh}", "gm"], writes=[f"aa{hh}"])
                    xs = xt[:, b, hh * 512:(hh + 1) * 512]
                    P.op("dve", lambda e, a_t=a_t, xs=xs: e.tensor_tensor(out=xs, in0=xs, in1=a_t, op=ALU.add), reads=[f"aa{hh}", "xt"], writes=["xt"])
                layer_norm_block(C, xt[:, b, :], "xt", lnG, lnB, st, mv, rs, nmr)
            P.dma(sq.x1[t0:t0 + 512, :].rearrange("(b p) c -> p b c", p=128), xt, reads=["xt"], writes=["x1"], q="sp", is_output=(sq.x1 is sq.y))
    A.release(m0)
    P.barrier()


def na_phase(C, seq_io):
    P, A, nc = C.P, C.A, C.nc
    PS = C.psum
    l = 1
    m0 = A.mark()
    wqkv = A.bf16(8 * 3072).rearrange("p (k c) -> p k c", k=8)
    P.dma(wqkv, C.wqkv_s, writes=["wqkv"], q="act")
    bqk = A.f32(16)
    load_pp(C, C.na_b_qkv[0:2048].rearrange("(m p) -> m p", p=128), bqk, "bqk")
    sc1 = A.f32(D)
    sh = A.f32(D)
    xt_r = [A.f32(4 * D).rearrange("p (b c) -> p b c", b=4) for _ in range(2)]
    tmp = A.f32(D)
    hb_r = [A.bf16(4 * D).rearrange("p (b c) -> p b c", b=4) for _ in range(2)]
    hfm_r = [A.bf16(8 * 512).rearrange("p (k t) -> p k t", k=8) for _ in range(2)]
    qk_st = [A.bf16(512) for _ in range(3)]
    v_st = [A.bf16(1040).rearrange("p (h d) -> p h d", h=16) for _ in range(2)]
    for i in range(2):
        P.op("pool", lambda e, i=i: e.memset(v_st[i][:, :, 64:65], 1.0), writes=[f"vst{i}"])
    pT = PS[0][:, 0:256].bitcast(BF16)
    cnt = 0
    gtile = 0
    for (sq, xin, xout) in seq_io:
        n = sq.n
        ci = sq.ci
        bc_load(C, sh, C.modd[l, ci, 0:D], "sh")
        bc_load(C, sc1, C.modd[l, ci, D:2 * D], "sc1")
        P.op("pool", lambda e: e.tensor_scalar(out=sc1, in0=sc1, scalar1=1.0, scalar2=None, op0=ALU.add), reads=["sc1"], writes=["sc1"])
        g0 = gtile
        gtile += n // 512

        def prologue(T):
            t0 = T * 512
            i2 = (g0 + T) % 2
            xt, hb, hfm = xt_r[i2], hb_r[i2], hfm_r[i2]
            P.dma(xt, xin[t0:t0 + 512, :].rearrange("(b p) c -> p b c", p=128), writes=[f"xt{i2}"], q="sp")
            for b in range(4):
                P.op("dve", lambda e, b=b: e.tensor_tensor(out=tmp, in0=xt[:, b, :], in1=sc1, op=ALU.mult), reads=[f"xt{i2}", "sc1"], writes=["tmp"])
                P.op("dve", lambda e, b=b: e.tensor_tensor(out=hb[:, b, :], in0=tmp, in1=sh, op=ALU.add), reads=["tmp", "sh"], writes=[f"hb{i2}{b}"])
            for k in range(8):
                for b in range(4):
                    P.op("pe", lambda e, b=b, k=k: e.transpose(pT[:, b * 128:(b + 1) * 128], hb[:, b, k * 128:(k + 1) * 128], C.identb),
                         reads=[f"hb{i2}{b}"], writes=["bank0"])
                if k % 2 == 0:
                    P.op("act", lambda e, k=k: e.copy(out=hfm[:, k, :], in_=pT), reads=["bank0"], writes=[f"hfm{i2}{k}"])
                else:
                    P.op("dve", lambda e, k=k: e.tensor_copy(out=hfm[:, k, :], in_=pT), reads=["bank0"], writes=[f"hfm{i2}{k}"])

        prologue(0)
        for T in range(n // 512):
            t0 = T * 512
            i2 = (g0 + T) % 2
            hfm = hfm_r[i2]
            if T + 1 < n // 512:
                prologue(T + 1)
            hf_keys = [f"hfm{i2}{k}" for k in range(8)]
            for mt in range(16):
                j = cnt % 3
                cnt += 1
                pb = PS[1 + j]
                pk = f"bank{1 + j}"
                for k in range(8):
                    P.op("pe", lambda e, pb=pb, k=k, mt=mt: e.matmul(pb, lhsT=wqkv[:, k, mt * 128:(mt + 1) * 128], rhs=hfm[:, k, :], start=(k == 0), stop=(k == 7)),
                         reads=["wqkv", hf_keys[k]], writes=[pk])
                stt = qk_st[j]
                sk = f"qkst{j}"
                P.op("act", lambda e, stt=stt, pb=pb, mt=mt: e.activation(out=stt, in_=pb, func=AF.Identity, bias=bqk[:, mt:mt + 1], scale=1.0),
                     reads=[pk, "bqk"], writes=[sk])
                dst = (sq.qT if mt < 8 else sq.kT)[:, mt % 8, t0:t0 + 512]
                P.dma(dst, stt, reads=[sk], writes=["qkscr"], q="sp")
            for b in range(4):
                vs = v_st[b % 2]
                vk = f"vst{b % 2}"
                for half in range(2):
                    j = cnt % 3
                    cnt += 1
                    pb = PS[1 + j]
                    pk = f"bank{1 + j}"
                    for k in range(8):
                        P.op("pe", lambda e, pb=pb, k=k, b=b, half=half: e.matmul(pb, lhsT=hfm[:, k, b * 128:(b + 1) * 128],
                                                                                   rhs=wqkv[:, k, 2048 + half * 512:2048 + (half + 1) * 512], start=(k == 0), stop=(k == 7)),
                             reads=["wqkv", hf_keys[k]], writes=[pk])
                    P.op("dve", lambda e, vs=vs, pb=pb, half=half: e.tensor_copy(out=vs[:, half * 8:(half + 1) * 8, 0:64], in_=pb.rearrange("p (h d) -> p h d", h=8)),
                         reads=[pk], writes=[vk])
                P.dma(sq.vS[t0 + b * 128:t0 + (b + 1) * 128, :], vs.rearrange("p h d -> p (h d)"), reads=[vk], writes=["vscr"], q="sp")
    A.release(m0)
    P.barrier()
    m0 = A.mark()
    wo = A.bf16(16 * D, parts=64).rearrange("p (h c) -> p h c", h=16)
    P.dma(wo, C.wo_s, writes=["wo"], q="act")
    TAB = A.bf16(16 * 17 * 64).rearrange("p (h x) -> p h x", h=16)
    m1 = A.mark()
    tstg = [A.f32(2176) for _ in range(2)]
    for hh in range(8):
        ts_ = tstg[hh % 2]
        tk = f"tstg{hh % 2}"
        P.dma(ts_, C.na_tab[:, 2 * hh:2 * hh + 2, :].rearrange("p h x -> p (h x)"), writes=[tk], q="sp")
        P.op("act", lambda e, ts_=ts_, hh=hh: e.activation(out=TAB[:, 2 * hh:2 * hh + 2, :].rearrange("p h x -> p (h x)"), in_=ts_, func=AF.Exp),
             reads=[tk], writes=["TAB"])
    P.barrier()
    A.release(m1)
    lnG = A.f32(D)
    lnB = A.f32(D)
    bc_load(C, lnG, C.ln_g[l, 0], "lnG")
    bc_load(C, lnB, C.ln_b[l, 0], "lnB")
    bvT = A.f32(16, parts=64)
    load_pp(C, C.na_b_qkv[2048:3072].rearrange("(h d) -> h d", d=64), bvT, "bvT")
    bvTb = A.bf16(16, parts=64)
    P.op("dve", lambda e: e.tensor_copy(out=bvTb, in_=bvT), reads=["bvT"], writes=["bvTb"])
    borow = A.f32(D, parts=1)
    P.dma(borow, C.na_b_o.rearrange("(o n) -> o n", o=1), writes=["borow"])
    for half in range(2):
        pb = PS[6 + half][0:1, :]
        for h in range(16):
            P.op("pe", lambda e, pb=pb, h=h, half=half: e.matmul(pb, lhsT=bvTb[:, h:h + 1], rhs=wo[:, h, half * 512:(half + 1) * 512], start=(h == 0), stop=(h == 15)),
                 reads=["bvTb", "wo"], writes=[f"bank{6 + half}"])
        P.op("dve", lambda e, pb=pb, half=half: e.tensor_tensor(out=borow[:, half * 512:(half + 1) * 512], in0=pb, in1=borow[:, half * 512:(half + 1) * 512], op=ALU.add),
             reads=[f"bank{6 + half}", "borow"], writes=["borow"])
    P.dma(C.bo2.rearrange("(o n) -> o n", o=1), borow, reads=["borow"], writes=["bo2"])
    bo = A.f32(D)
    bc_load_dep(C, bo, C.bo2, "bo", ["bo2"])
    gm = A.f32(D)
    gmb = A.f32(D)
    ones_r = A.f32(64, parts=65)
    P.op("pool", lambda e: e.memset(ones_r[64:65, :], 1.0), writes=["ones_r"])
    xt = A.f32(4 * D).rearrange("p (b c) -> p b c", b=4)
    kw = A.bf16(8 * 1536).rearrange("p (k t) -> p k t", k=8)
    qw = A.bf16(8 * 512).rearrange("p (k t) -> p k t", k=8)
    vw = A.bf16(12 * 1040).rearrange("p (b h d) -> p b h d", b=12, h=16)
    es_ring = [A.f32(512) for _ in range(3)]
    pt_ring = [A.bf16(512) for _ in range(4)]
    es4 = [A.f32(128) for _ in range(3)]
    pt4 = [A.bf16(128) for _ in range(3)]
    oT = A.bf16(16 * 512, parts=64).rearrange("p (h t) -> p h t", h=16)
    sums = A.f32(512, parts=65)
    rcp = A.f32(512, parts=64)
    rt = [A.f32(512) for _ in range(2)]
    st = A.f32(12).rearrange("p (a b) -> p a b", a=2)
    mv = A.f32(2)
    rs = A.f32(1)
    nmr = A.f32(1)
    c_es = c_pt = c_s4 = c_h = 0
    for (sq, xin, xout) in seq_io:
        n = sq.n
        ci = sq.ci
        R = n // 64
        nt = n // 512
        bc_load(C, gm, C.modd[l, ci, 2 * D:3 * D], "gm")
        P.op("pool", lambda e: e.tensor_tensor(out=gmb, in0=gm, in1=bo, op=ALU.mult), reads=["gm", "bo"], writes=["gmb"])
        for T in range(nt):
            t0 = T * 512
            wt0 = t0 - 512
            lo = max(wt0, 0)
            hi = min(t0 + 1024, n)
            P.dma(kw[:, :, lo - wt0:hi - wt0], sq.kT[:, :, lo:hi], writes=["kw"], q="sp")
            P.dma(qw, sq.qT[:, :, t0:t0 + 512], writes=["qw"], q="sp")
            P.dma(vw[:, (lo - wt0) // 128:(hi - wt0) // 128, :, :].rearrange("p b h d -> p b (h d)"),
                  sq.vS[lo:hi, :].rearrange("(b p) c -> p b c", p=128), writes=["vw"], q="sp")
            P.dma(xt, xin[t0:t0 + 512, :].rearrange("(b p) c -> p b c", p=128), writes=["xt"], q="sp")
            for b in range(4):
                P.op("dve", lambda e, b=b: e.scalar_tensor_tensor(out=xt[:, b, :], in0=xt[:, b, :], scalar=ALPHA, in1=gmb, op0=ALU.mult, op1=ALU.add), reads=["xt", "gmb"], writes=["xt"])
            wr0 = 8 * T - 8
            LA = 2
            units = [(h, qb) for h in range(16) for qb in range(4)]
            pend = {}

            def valid(kr_, r_):
                rs_ = min(max(r_ - 4, 0), R - 8)
                return kr_ < R and rs_ <= kr_ < rs_ + 8

            def issue_scores(ui):
                nonlocal c_es, c_s4
                h, qb = units[ui]
                hp, pbase = h // 2, (h % 2) * 64
                r = 8 * T + 2 * qb
                kr0 = min(max(r - 4, 0), R - 10)
                sb_i = c_es % 3
                ps_s = PS[sb_i]
                psk = f"bank{sb_i}"
                esi = c_es % 3
                c_es += 1
                for j in range(4):
                    kc0 = (kr0 + 2 * j - wr0) * 64
                    P.op("pe", lambda e, ps_s=ps_s, j=j, kc0=kc0, qb=qb, hp=hp, pbase=pbase: e.matmul(
                        ps_s[:, (3 - j) * 128:(4 - j) * 128], lhsT=kw[pbase:pbase + 64, hp, kc0:kc0 + 128],
                        rhs=qw[pbase:pbase + 64, hp, qb * 128:(qb + 1) * 128], start=True, stop=True),
                        reads=["kw", "qw"], writes=[psk])
                pend[ui] = [sb_i, esi, None]

            def issue_j4(ui):
                nonlocal c_s4
                h, qb = units[ui]
                hp, pbase = h // 2, (h % 2) * 64
                r = 8 * T + 2 * qb
                kr0 = min(max(r - 4, 0), R - 10)
                kr4 = kr0 + 8
                nv4 = sum(1 for kl in range(2) for rl in range(2) if valid(kr4 + kl, r + rl))
                if nv4:
                    i4 = c_s4 % 2
                    c_s4 += 1
                    b4 = (3, 6)[i4]
                    ps4 = PS[b4][:, 0:128]
                    kc0 = (kr4 - wr0) * 64
                    P.op("pe", lambda e, ps4=ps4, kc0=kc0, qb=qb, hp=hp, pbase=pbase: e.matmul(
                        ps4, lhsT=kw[pbase:pbase + 64, hp, kc0:kc0 + 128], rhs=qw[pbase:pbase + 64, hp, qb * 128:(qb + 1) * 128],
                        start=True, stop=True), reads=["kw", "qw"], writes=[f"bank{b4}"])
                    pend[ui][2] = i4

            def finish_unit(ui, po, pok):
                nonlocal c_pt
                h, qb = units[ui]
                sb_i, esi, i4 = pend.pop(ui)
                r = 8 * T + 2 * qb
                kr0 = min(max(r - 4, 0), R - 10)
                x03 = 1 + r - kr0
                ps_s = PS[sb_i]
                psk = f"bank{sb_i}"
                es = es_ring[esi]
                esk = f"es{esi}"
                P.op("act", lambda e, es=es, ps_s=ps_s: e.activation(out=es, in_=ps_s, func=AF.Exp, scale=0.125), reads=[psk], writes=[esk])
                pt = pt_ring[c_pt % 4]
                ptk = f"pt{c_pt % 4}"
                c_pt += 1
                P.op("dve", lambda e, pt=pt, es=es, h=h, x03=x03: e.tensor_tensor(out=pt, in0=es, in1=TAB[:, h, x03 * 64:(x03 + 8) * 64], op=ALU.mult),
                     reads=[esk, "TAB"], writes=[ptk])
                tiles = []
                zkeys = []
                for j in range(4):
                    kr = kr0 + 2 * j
                    nv = 0
                    for kl in range(2):
                        for rl in range(2):
                            if valid(kr + kl, r + rl):
                                nv += 1
                            else:
                                zk_ = ptk + f"z{len(zkeys)}"
                                zkeys.append(zk_)
                                P.op("pool", lambda e, pt=pt, j=j, kl=kl, rl=rl: e.memset(
                                    pt[kl * 64:(kl + 1) * 64, (3 - j) * 128 + rl * 64:(3 - j) * 128 + (rl + 1) * 64], 0.0),
                                    reads=[ptk], writes=[zk_])
                    if nv:
                        tiles.append((pt[:, (3 - j) * 128:(4 - j) * 128], [ptk] + zkeys, (kr - wr0) // 2))
                tiles = [(a_, [ptk] + zkeys, c_) for (a_, _, c_) in tiles]
                if i4 is not None:
                    kr = kr0 + 8
                    b4 = (3, 6)[i4]
                    ps4 = PS[b4][:, 0:128]
                    i5 = i4
                    e4, p4 = es4[i5], pt4[i5]
                    P.op("act", lambda e, e4=e4, ps4=ps4: e.activation(out=e4, in_=ps4, func=AF.Exp, scale=0.125), reads=[f"bank{b4}"], writes=[f"es4{i5}"])
                    P.op("dve", lambda e, e4=e4, p4=p4, h=h, x03=x03: e.tensor_tensor(out=p4, in0=e4, in1=TAB[:, h, (x03 - 2) * 64:x03 * 64], op=ALU.mult),
                         reads=[f"es4{i5}", "TAB"], writes=[f"pt4{i5}"])
                    z4 = []
                    for kl in range(2):
                        for rl in range(2):
                            if not valid(kr + kl, r + rl):
                                zk_ = f"pt4{i5}z{len(z4)}"
                                z4.append(zk_)
                                P.op("pool", lambda e, p4=p4, kl=kl, rl=rl: e.memset(p4[kl * 64:(kl + 1) * 64, rl * 64:(rl + 1) * 64], 0.0),
                                     reads=[f"pt4{i5}"], writes=[zk_])
                    tiles.append((p4, [f"pt4{i5}"] + z4, (kr - wr0) // 2))
                for ti, (rhs_t, rk_, blk) in enumerate(tiles):
                    P.op("pe", lambda e, po=po, rhs_t=rhs_t, blk=blk, h=h, qb=qb, ti=ti, nti=len(tiles): e.matmul(
                        po[0:65, qb * 128:(qb + 1) * 128], lhsT=vw[:, blk, h, 0:65], rhs=rhs_t, start=(ti == 0), stop=(ti == nti - 1)),
                        reads=rk_ + ["vw"], writes=[pok])

            def normalise(h, po, pok):
                P.op("act", lambda e, po=po: e.activation(out=sums[64:65, :], in_=po[64:65, :], func=AF.Ln), reads=[pok], writes=["sums"])
                P.op("act", lambda e: e.activation(out=sums[64:65, :], in_=sums[64:65, :], func=AF.Exp, scale=-1.0), reads=["sums"], writes=["sums"])
                P.op("pe", lambda e: e.matmul(PS[7][0:64, :], lhsT=ones_r[64:65, :], rhs=sums[64:65, :], start=True, stop=True),
                     reads=["sums", "ones_r"], writes=["bank7"])
                P.op("act", lambda e: e.copy(out=rcp, in_=PS[7][0:64, :]), reads=["bank7"], writes=["rcp"])
                P.op("dve", lambda e, po=po, h=h: e.tensor_tensor(out=oT[:, h, :], in0=po[0:64, :], in1=rcp, op=ALU.mult), reads=[pok, "rcp"], writes=[f"oT{h}"])

            for ui in range(min(LA, len(units))):
                issue_scores(ui)
            issue_j4(0)
            deferred = None
            for ui, (h, qb) in enumerate(units):
                if qb == 0:
                    po = PS[4 + c_h % 2]
                    pok = f"bank{4 + c_h % 2}"
                    c_h += 1
                if ui + LA < len(units):
                    issue_scores(ui + LA)
                if ui + 1 < len(units):
                    issue_j4(ui + 1)
                finish_unit(ui, po, pok)
                if qb == 1 and deferred is not None:
                    normalise(*deferred)
                    deferred = None
                if qb == 3:
                    deferred = (h, po, pok)
            normalise(*deferred)
            oT_keys = [f"oT{h}" for h in range(16)]
            for b in range(4):
                for half in range(2):
                    j = (b * 2 + half) % 2
                    pw = PS[6 + j]
                    pwk = f"bank{6 + j}"
                    for h in range(16):
                        P.op("pe", lambda e, pw=pw, h=h, b=b, half=half: e.matmul(pw, lhsT=oT[:, h, b * 128:(b + 1) * 128], rhs=wo[:, h, half * 512:(half + 1) * 512],
                                                                                  start=(h == 0), stop=(h == 15)), reads=[oT_keys[h], "wo"], writes=[pwk])
                    r_ = rt[j]
                    rk = f"rt{j}"
                    xs = xt[:, b, half * 512:(half + 1) * 512]
                    P.op("dve", lambda e, pw=pw, r_=r_, half=half: e.tensor_tensor(out=r_, in0=pw, in1=gm[:, half * 512:(half + 1) * 512], op=ALU.mult),
                         reads=[pwk, "gm"], writes=[rk])
                    P.op("dve", lambda e, r_=r_, xs=xs: e.tensor_tensor(out=xs, in0=xs, in1=r_, op=ALU.add), reads=[rk, "xt"], writes=["xt"])
                layer_norm_block(C, xt[:, b, :], "xt", lnG, lnB, st, mv, rs, nmr)
            P.dma(xout[t0:t0 + 512, :].rearrange("(b p) c -> p b c", p=128), xt, reads=["xt"], writes=["xout"], q="sp", is_output=(xout is sq.y))
    A.release(m0)
    P.barrier()


def bc_load_dep(C, dst, src_row, key, reads):
    parts = dst.shape[0]
    C.P.dma(dst, src_row.rearrange("(o n) -> o n", o=1).partition_broadcast(parts), reads=reads, writes=[key])


def rpb_layout(rpb):
    rpb = np.asarray(rpb, np.float32)
    kc = np.arange(64)[:, None]
    qc = np.arange(64)[None, :]
    cs_ = np.clip(qc - 8, 0, 48)
    valid = (kc >= cs_) & (kc < cs_ + 16)
    idx = np.clip(kc - qc + 15, 0, 30)
    out = rpb[:, :, idx]
    out = np.where(valid[None, None], out, np.float32(-30000.0)).astype(np.float32)
    return np.ascontiguousarray(out)


def tab_layout(rpb):
    t = rpb_layout(rpb)
    out = np.full((2, 64, 16, 17, 64), -30000.0, np.float32)
    for kl in range(2):
        for x in range(17):
            dr = kl + 7 - x
            if -7 <= dr <= 7:
                out[kl, :, :, x, :] = np.transpose(t[:, dr + 7, :, :], (1, 0, 2))
    return np.ascontiguousarray(out.reshape(128, 16, 17 * 64))


def s5_masks_const():
    tau = np.arange(128)[:, None] // 16
    t = np.arange(128)[None, :] // 16
    return np.ascontiguousarray(np.stack([(t >= tau), (tau >= t)]).astype(np.float32))


def shared_inputs(inp):
    f = lambda a: np.ascontiguousarray(np.asarray(a, np.float32))
    return {
        "s5_masks": s5_masks_const(),
        "w_ada": f(inp["w_ada"]), "b_ada": f(inp["b_ada"]), "ln_g": f(inp["ln_g"]), "ln_b": f(inp["ln_b"]),
        "s5_lam_re": f(inp["s5_lam_re"][0]), "s5_lam_im": f(inp["s5_lam_im"][0]), "s5_log_dt": f(inp["s5_log_dt"][0]),
        "s5_b_re": f(inp["s5_b_re"][0]), "s5_b_im": f(inp["s5_b_im"][0]), "s5_c_re": f(inp["s5_c_re"][0]), "s5_c_im": f(inp["s5_c_im"][0]),
        "s5_d": f(inp["s5_d"][0]), "s5_w_glu": f(inp["s5_w_glu"][0]), "s5_b_glu": f(inp["s5_b_glu"][0]),
        "na_w_qkv": f(inp["na_w_qkv"][0]), "na_b_qkv": f(inp["na_b_qkv"][0]), "na_tab": tab_layout(inp["na_rpb"][0]),
        "na_w_o": f(inp["na_w_o"][0]), "na_b_o": f(inp["na_b_o"][0]),
        "ffn_w_in": f(inp["ffn_w_in"]), "ffn_b_in": f(inp["ffn_b_in"]), "ffn_conv_w": f(inp["ffn_conv_w"]),
        "ffn_conv_b": f(inp["ffn_conv_b"]), "ffn_w_out": f(inp["ffn_w_out"]), "ffn_b_out": f(inp["ffn_b_out"]),
    }


_CACHE = {}


def prompt_window(i):
    r0 = min(max(32 * i - 8, 0), 256 - 48)
    return r0 * 64


def kernel(**inputs):
    inp = {k: np.asarray(v) for k, v in inputs.items()}
    cfg = dict(ns=NS_FULL, np=NP_FULL)
    if "nc" not in _CACHE:
        _CACHE["nc"] = build(cfg)
    nc, C = _CACHE["nc"]
    shared = shared_inputs(inp)
    xs = np.asarray(inp["x_sample"], np.float32)
    xp = np.asarray(inp["x_prompt"], np.float32)[0]
    in_maps = []
    for i in range(8):
        w0 = prompt_window(i)
        m = dict(shared)
        m["x_s"] = np.ascontiguousarray(xs[i])
        m["x_p"] = np.ascontiguousarray(xp[w0:w0 + NP_FULL])
        m["cs"] = np.ascontiguousarray(np.stack([inp["c_sample"][i], inp["c_prompt"][0]]).astype(np.float32))
        in_maps.append(m)
    res = run_bass_kernel_spmd(nc, in_maps, core_ids=list(range(8)))
    y_s = np.stack([np.asarray(res.results[i]["y_s"], np.float32) for i in range(8)])
    y_p = np.zeros((1, 16384, D), np.float32)
    for i in range(8):
        w0 = prompt_window(i)
        off = 2048 * i - w0
        y_p[0, 2048 * i:2048 * (i + 1)] = np.asarray(res.results[i]["y_p"], np.float32)[off:off + 2048]
    return (y_p, y_s)
```

```python
import math
import itertools
import types
import numpy as np
from contextlib import ExitStack
import concourse.bass as bass
import concourse.mybir as mybir
from concourse.bass_utils import run_bass_kernel_spmd

F32 = mybir.dt.float32
BF16 = mybir.dt.bfloat16
AF = mybir.ActivationFunctionType
ALU = mybir.AluOpType

ENGS = ("pe", "dve", "act", "pool", "sp")
D = 1024
DFF = 2816
ALPHA = 4.0 ** 0.25
LN_EPS = 1e-5
NS_FULL = 4096
NP_FULL = 3072


def _freeze(fn):
    if fn is None or fn.__closure__ is None:
        return fn
    cells = tuple(types.CellType(c.cell_contents) for c in fn.__closure__)
    return types.FunctionType(fn.__code__, fn.__globals__, fn.__name__, fn.__defaults__, cells)


class Op:
    __slots__ = ("eng", "fn", "deps", "is_dma", "idx", "needed", "semval", "dsem", "dval", "waits", "prewait")

    def __init__(self, eng, fn, is_dma):
        self.eng = eng
        self.fn = fn
        self.is_dma = is_dma
        self.deps = set()
        self.needed = False
        self.semval = None
        self.dsem = None
        self.dval = None
        self.waits = []
        self.prewait = None


class Prog:
    def __init__(self, nc, es, n_dma_sems=14):
        self.nc = nc
        self.es = es
        self.ops = {e: [] for e in ENGS}
        self.last_w = {}
        self.readers = {}
        self.K = n_dma_sems
        self.out_dmas = []
        self.last_real = {e: None for e in ENGS}
        self.dma_hist = {e: [] for e in ENGS}

    def _track(self, o, reads, writes):
        deps = o.deps
        for k in reads:
            w = self.last_w.get(k)
            if w is not None:
                deps.add(w)
        for k in writes:
            w = self.last_w.get(k)
            if w is not None:
                deps.add(w)
            for r in self.readers.get(k, ()):
                deps.add(r)
        deps.discard(o)
        for k in writes:
            self.last_w[k] = o
            self.readers[k] = []
        for k in reads:
            self.readers.setdefault(k, []).append(o)

    dead = False
    paranoid = False

    def op(self, eng, fn, reads=(), writes=()):
        if self.dead:
            return None
        o = Op(eng, _freeze(fn), False)
        o.idx = len(self.ops[eng])
        self._track(o, reads, writes)
        if self.paranoid:
            for e2 in ENGS:
                if self.last_real[e2] is not None:
                    o.deps.add(self.last_real[e2])
                for d in self.dma_hist[e2][-self.K:]:
                    o.deps.add(d)
        self.ops[eng].append(o)
        self.last_real[eng] = o
        return o

    def dma(self, out, in_, reads=(), writes=(), q="sp", is_output=False, **kw):
        if self.dead:
            return None

        def fn(e):
            return e.dma_start(out=out, in_=in_, **kw)
        o = Op(q, fn, True)
        o.idx = len(self.ops[q])
        self._track(o, reads, writes)
        if self.paranoid:
            for e2 in ENGS:
                if self.last_real[e2] is not None:
                    o.deps.add(self.last_real[e2])
                for d in self.dma_hist[e2][-self.K:]:
                    o.deps.add(d)
        self.ops[q].append(o)
        self.dma_hist[q].append(o)
        if is_output:
            self.out_dmas.append(o)
        return o

    def barrier(self):
        if self.dead:
            return
        b = Op("sp", lambda e: e.nop(), False)
        b.idx = len(self.ops["sp"])
        for e in ENGS:
            if self.last_real[e] is not None:
                b.deps.add(self.last_real[e])
            for d in self.dma_hist[e][-self.K:]:
                b.deps.add(d)
        self.ops["sp"].append(b)
        self.last_real["sp"] = b
        for e in ENGS:
            if e == "sp":
                continue
            o = Op(e, None, False)
            o.idx = len(self.ops[e])
            o.deps.add(b)
            self.ops[e].append(o)
        self.last_w.clear()
        self.readers.clear()

    def emit(self):
        nc = self.nc
        es = self.es
        esem = {e: es.enter_context(nc.semaphore(f"s_{e}")) for e in ENGS}
        dsems = {e: [es.enter_context(nc.semaphore(f"d_{e}{i}")) for i in range(self.K)] for e in ("sp", "act", "pool")}
        for e in ENGS:
            n = 0
            for o in self.ops[e]:
                if o.is_dma:
                    o.dsem = dsems[e][n % self.K]
                    o.dval = 16 * (n // self.K + 1)
                    if n >= self.K:
                        o.prewait = (o.dsem, o.dval - 16)
                    n += 1
        fin = Op("sp", None, False)
        fin.idx = len(self.ops["sp"])
        fin.deps = set(self.out_dmas)
        self.ops["sp"].append(fin)
        for f in ENGS:
            waited = {e: -1 for e in ENGS}
            dwaited = set()
            for o in self.ops[f]:
                best = {}
                for d in o.deps:
                    if d.is_dma:
                        if id(d) in dwaited:
                            continue
                        dwaited.add(id(d))
                        o.waits.append(("dma", d))
                    else:
                        if d.eng == "pe" and f == "pe":
                            continue
                        if d.idx <= waited[d.eng]:
                            continue
                        if d.eng not in best or best[d.eng].idx < d.idx:
                            best[d.eng] = d
                for en, d in best.items():
                    waited[en] = d.idx
                    d.needed = True
                    o.waits.append(("eng", d))
        for e in ENGS:
            c = 0
            for o in self.ops[e]:
                if o.needed and not o.is_dma:
                    c += 1
                    o.semval = c
        self.stats = {e: len(self.ops[e]) for e in ENGS}

        def run(ename, eng):
            for o in self.ops[ename]:
                if o.prewait is not None:
                    eng.wait_ge(o.prewait[0], o.prewait[1])
                for kind, d in o.waits:
                    if kind == "dma":
                        eng.wait_ge(d.dsem, d.dval)
                    else:
                        eng.wait_ge(esem[d.eng], d.semval)
                if o.fn is None:
                    continue
                ins = o.fn(eng)
                if o.is_dma:
                    ins.then_inc(o.dsem, 16)
                elif o.needed:
                    ins.then_inc(esem[ename], 1)

        with nc.Block() as block:
            @block.sync
            def _(e):
                run("sp", e)

            @block.tensor
            def _(e):
                run("pe", e)

            @block.vector
            def _(e):
                run("dve", e)

            @block.scalar
            def _(e):
                run("act", e)

            @block.gpsimd
            def _(e):
                run("pool", e)


class Arena:
    def __init__(self, nc, es, words):
        self.t = es.enter_context(nc.sbuf_tensor("arena", [128, words], F32))[:, :]
        self.words = words
        self.off = 0
        self.uid = 0

    def mark(self):
        return self.off

    def release(self, m):
        self.off = m

    def f32(self, n, parts=128):
        assert self.off + n <= self.words, ("arena overflow", self.off, n, self.words)
        v = self.t[0:parts, self.off:self.off + n]
        self.off += n
        return v

    def bf16(self, n, parts=128):
        w = (n + 1) // 2
        v = self.f32(w, parts).bitcast(BF16)
        return v[:, 0:n]

    def key(self, base):
        self.uid += 1
        return f"{base}#{self.uid}"


class Ctx:
    pass


def build(cfg):
    ns, npw = cfg["ns"], cfg["np"]
    phases = cfg.get("phases", ("setup", "s5", "ffn0", "na", "ffn1"))
    cfg = dict(cfg)
    cfg["phases"] = phases
    nc = bass.Bass("TRN2", target_bir_lowering=False)
    C = Ctx()
    C.nc = nc
    C.cfg = cfg

    def din(name, shape, dt=F32):
        return nc.dram_tensor(name, list(shape), dt, kind="ExternalInput").ap()

    def dout(name, shape, dt=F32):
        return nc.dram_tensor(name, list(shape), dt, kind="ExternalOutput").ap()

    def dscr(name, shape, dt=F32):
        kind = "ExternalOutput" if cfg.get("debug") and name in cfg.get("taps", ()) else "Internal"
        return nc.dram_tensor(name, list(shape), dt, kind=kind).ap()

    seqs = []
    for nm, n in (("s", ns), ("p", npw)):
        if n == 0:
            continue
        s = Ctx()
        s.name = nm
        s.n = n
        s.ci = 0 if nm == "s" else 1
        s.x0 = din(f"x_{nm}", [n, D])
        s.y = dout(f"y_{nm}", [n, D])
        s.x1 = dscr(f"x1_{nm}", [n, D])
        s.x2 = dscr(f"x2_{nm}", [n, D])
        s.x3 = dscr(f"x3_{nm}", [n, D])
        s.qT = dscr(f"qT_{nm}", [128, 8, n], BF16)
        s.kT = dscr(f"kT_{nm}", [128, 8, n], BF16)
        s.vS = dscr(f"vS_{nm}", [n, 1040], BF16)
        s.U = dscr(f"U_{nm}", [n // 1024, 128, 64, 128], BF16)
        s.Z = dscr(f"Z_{nm}", [n // 512, 128, 64, 64], BF16)
        seqs.append(s)
    C.seqs = seqs
    C.cs = din("cs", [2, D])
    C.w_ada = din("w_ada", [2, D, 6 * D])
    C.b_ada = din("b_ada", [2, 6 * D])
    C.ln_g = din("ln_g", [2, 2, D])
    C.ln_b = din("ln_b", [2, 2, D])
    C.s5_lam_re = din("s5_lam_re", [2, 64, 64])
    C.s5_lam_im = din("s5_lam_im", [2, 64, 64])
    C.s5_log_dt = din("s5_log_dt", [2, 64])
    C.s5_b_re = din("s5_b_re", [2, 64, 64, 16])
    C.s5_b_im = din("s5_b_im", [2, 64, 64, 16])
    C.s5_c_re = din("s5_c_re", [2, 64, 16, 64])
    C.s5_c_im = din("s5_c_im", [2, 64, 16, 64])
    C.s5_d = din("s5_d", [64, 16])
    C.s5_w_glu = din("s5_w_glu", [D, 2 * D])
    C.s5_b_glu = din("s5_b_glu", [2 * D])
    C.na_w_qkv = din("na_w_qkv", [D, 3 * D])
    C.na_b_qkv = din("na_b_qkv", [3 * D])
    C.na_tab = din("na_tab", [128, 16, 1088])
    C.s5_masks = din("s5_masks", [2, 128, 128])
    C.na_w_o = din("na_w_o", [D, D])
    C.na_b_o = din("na_b_o", [D])
    C.ffn_w_in = din("ffn_w_in", [2, D, 2 * DFF])
    C.ffn_b_in = din("ffn_b_in", [2, 2 * DFF])
    C.ffn_conv_w = din("ffn_conv_w", [2, 3, DFF])
    C.ffn_conv_b = din("ffn_conv_b", [2, DFF])
    C.ffn_w_out = din("ffn_w_out", [2, DFF, D])
    C.ffn_b_out = din("ffn_b_out", [2, D])
    C.modd = dscr("modd", [2, 2, 6 * D])
    C.wglu_s = dscr("wglu_s", [128, 8, 2 * D], BF16)
    C.wqkv_s = dscr("wqkv_s", [128, 8, 3 * D], BF16)
    C.wo_s = dscr("wo_s", [64, 16, D], BF16)
    C.win_s = dscr("win_s", [2, 11, 128, 8, 512], BF16)
    C.wout_s = dscr("wout_s", [2, 128, 22, D], BF16)
    C.bo2 = dscr("bo2", [D])
    C.s5w = dscr("s5w", [64, 128, 1024], BF16)
    C.s5t = dscr("s5t", [64, 128, 2, 128])
    C.s5tp = dscr("s5tp", [2, 64, 128, 128])

    with ExitStack() as es:
        P = Prog(nc, es, n_dma_sems=cfg.get('ksem', 14))
        P.paranoid = bool(cfg.get('paranoid'))
        C.P = P
        A = Arena(nc, es, cfg.get("arena_words", 53184))
        C.A = A
        C.psum = [es.enter_context(nc.psum_tensor(f"ps{i}", [128, 512], F32))[:, :] for i in range(8)]
        C.ident = A.f32(128)
        C.identb = A.bf16(128)
        C.eps = A.f32(1)
        P.op("pool", lambda e: e.memset(C.ident, 0.0), writes=["ident"])
        P.op("pool", lambda e: e.affine_select(out=C.ident, in_=C.ident, pattern=[[-1, 128]], compare_op=ALU.not_equal,
                                               fill=1.0, base=0, channel_multiplier=1), reads=["ident"], writes=["ident"])
        P.op("pool", lambda e: e.tensor_copy(out=C.identb, in_=C.ident), reads=["ident"], writes=["identb"])
        P.op("pool", lambda e: e.memset(C.eps, LN_EPS), writes=["eps"])
        P.barrier()
        if "setup" in phases:
            setup_phase(C)
        if "s5" in phases:
            if len(phases) == 2:
                for s_ in seqs:
                    s_.x1 = s_.y
            s5_phase(C)
        if "ffn0" in phases:
            ffn_phase(C, 0, [(s, (s.x1 if "s5" in phases else s.x0), (s.x2 if ("na" in phases or "ffn1" in phases) else s.y)) for s in seqs])
        if "na" in phases:
            na_phase(C, [(s, (s.x2 if "ffn0" in phases else s.x0), (s.x3 if "ffn1" in phases else s.y)) for s in seqs])
        if "ffn1" in phases:
            ffn_phase(C, 1, [(s, (s.x3 if "na" in phases else s.x0), s.y) for s in seqs])
        P.emit()
        C.stats = P.stats
    return nc, C


def cut(C, label):
    if C.cfg.get("cut") == label:
        C.P.barrier()
        C.P.dead = True


def load_pp(C, src, dst, key):
    P, A = C.P, C.A
    R, W = src.shape[0], src.shape[1]
    stg = A.f32(W, parts=R)
    sk = A.key("ppstg")
    P.dma(stg, src, writes=[sk])
    ps = C.psum[7][0:W, 0:R]
    P.op("pe", lambda e: e.transpose(ps, stg, C.ident[0:R, 0:R]), reads=[sk, "ident"], writes=["bank7"])
    P.op("dve", lambda e: e.tensor_copy(out=dst, in_=ps), reads=["bank7"], writes=[key])


def bc_load(C, dst, src_row, key, q="sp"):
    n = dst.shape[1]
    parts = dst.shape[0]
    C.P.dma(dst, src_row.rearrange("(o n) -> o n", o=1).partition_broadcast(parts), writes=[key], q=q)


def setup_phase(C):
    P, A, nc = C.P, C.A, C.nc
    m0 = A.mark()
    cT = A.f32(16).rearrange("p (k b) -> p k b", b=2)
    crow = A.f32(D, parts=2)
    P.dma(crow, C.cs, writes=["crow"])
    P.op("act", lambda e: e.activation(out=crow, in_=crow, func=AF.Silu), reads=["crow"], writes=["crow"])
    pct = C.psum[2][:, 0:16]
    for k in range(8):
        P.op("pe", lambda e, k=k: e.transpose(pct[:, 2 * k:2 * k + 2], crow[0:2, k * 128:(k + 1) * 128], C.ident[0:2, 0:2]),
             reads=["crow", "ident"], writes=["pct"])
    P.op("dve", lambda e: e.tensor_copy(out=cT, in_=pct.rearrange("p (k b) -> p k b", b=2)), reads=["pct"], writes=["cT"])
    wa = [A.f32(8 * 512).rearrange("p (k c) -> p k c", k=8) for _ in range(2)]
    bad = A.f32(6 * D, parts=2)
    modsb = A.f32(6 * D, parts=2)
    it = 0
    for l in range(2):
        P.dma(bad, C.b_ada[l].rearrange("(o n) -> o n", o=1).partition_broadcast(2), writes=["bad"])
        for cb in range(12):
            w = wa[it % 2]
            wk = f"wa{it % 2}"
            P.dma(w, C.w_ada[l][:, cb * 512:(cb + 1) * 512].rearrange("(k p) c -> p k c", p=128), writes=[wk],
                  q=("sp" if it % 2 == 0 else "act"))
            ps = C.psum[it % 2][0:2, :]
            pk = f"ps{it % 2}"
            for k in range(8):
                P.op("pe", lambda e, ps=ps, w=w, k=k: e.matmul(ps, lhsT=cT[:, k, :], rhs=w[:, k, :], start=(k == 0), stop=(k == 7)),
                     reads=[wk, "cT"], writes=[pk])
            P.op("dve", lambda e, ps=ps, cb=cb: e.tensor_tensor(out=modsb[:, cb * 512:(cb + 1) * 512], in0=ps,
                                                                in1=bad[:, cb * 512:(cb + 1) * 512], op=ALU.add),
                 reads=[pk, "bad"], writes=["modsb"])
            it += 1
        P.dma(C.modd[l], modsb, reads=["modsb"], writes=["modd"])
    A.release(m0)
    P.barrier()
    m0c = A.mark()
    CW = 5632
    stg = [A.f32(CW)] * 2
    stb = [A.bf16(CW)] * 2
    cnt = [0]
    cast_eng = ("act", "dve", "act")

    def cast_rows(src, ncols, stores):
        i = cnt[0]
        cnt[0] += 1
        s_f, s_b = stg[i % 2], stb[i % 2]
        kf, kb = "stg0", "stb0"
        P.dma(s_f[:, 0:ncols], src, writes=[kf], q=("sp" if i % 2 == 0 else "act"))
        ce = cast_eng[i % 3]
        if ce == "act":
            P.op("act", lambda e: e.copy(out=s_b[:, 0:ncols], in_=s_f[:, 0:ncols]), reads=[kf], writes=[kb])
        else:
            P.op(ce, lambda e: e.tensor_copy(out=s_b[:, 0:ncols], in_=s_f[:, 0:ncols]), reads=[kf], writes=[kb])
        for dst, (c0, c1), (p0, p1) in stores:
            P.dma(dst, s_b[p0:p1, c0:c1], reads=[kb], writes=["wscr"], q="sp")

    ph = C.cfg.get("phases")

    def casts():
        if "s5" in ph:
            for k in range(8):
                cast_rows(C.s5_w_glu[k * 128:(k + 1) * 128, :], 2048, [(C.wglu_s[:, k, :], (0, 2048), (0, 128))])
                yield
        if "na" in ph:
            for k in range(8):
                cast_rows(C.na_w_qkv[k * 128:(k + 1) * 128, :], 3072, [(C.wqkv_s[:, k, :], (0, 3072), (0, 128))])
                yield
            for k in range(8):
                cast_rows(C.na_w_o[k * 128:(k + 1) * 128, :], 1024,
                          [(C.wo_s[:, 2 * k, :], (0, 1024), (0, 64)), (C.wo_s[:, 2 * k + 1, :], (0, 1024), (64, 128))])
                yield
        for l in range(2):
            if f"ffn{l}" not in ph and f"ffn{l}_castonly" not in ph:
                continue
            for k in range(8):
                dst_u = C.win_s[l][:, :, k, 0:256].rearrange("m p c -> p m c")
                dst_g = C.win_s[l][:, :, k, 256:512].rearrange("m p c -> p m c")
                i = cnt[0]
                cast_rows(C.ffn_w_in[l][k * 128:(k + 1) * 128, :], 5632, [])
                s_b = stb[0]
                kb = "stb0"
                P.dma(dst_u, s_b[:, 0:2816].rearrange("p (m c) -> p m c", c=256), reads=[kb], writes=["wscr"], q="sp")
                P.dma(dst_g, s_b[:, 2816:5632].rearrange("p (m c) -> p m c", c=256), reads=[kb], writes=["wscr"], q="sp")
                yield
            for k in range(22):
                cast_rows(C.ffn_w_out[l][k * 128:(k + 1) * 128, :], 1024, [(C.wout_s[l][:, k, :], (0, 1024), (0, 128))])
                yield

    bg = casts()
    C.bg = bg
    if "s5" in ph:
        s5_setup(C)
    for _ in bg:
        pass
    A.release(m0c)
    P.barrier()


def bg_step(C, n=1):
    bg = getattr(C, "bg", None)
    if bg is None:
        return
    for _ in range(n):
        next(bg, None)


def ffn_phase(C, l, seq_io):
    P, A, nc = C.P, C.A, C.nc
    m0 = A.mark()
    PS = C.psum
    wout = A.bf16(22 * D).rearrange("p (k c) -> p k c", k=22)
    P.dma(wout, C.wout_s[l], writes=["wout"], q="act")
    lnG = A.f32(D)
    lnB = A.f32(D)
    bc_load(C, lnG, C.ln_g[l, 1], "lnG")
    bc_load(C, lnB, C.ln_b[l, 1], "lnB")
    bout = A.f32(D)
    bc_load(C, bout, C.ffn_b_out[l], "bout")
    binp = A.f32(44)
    cw = A.f32(66).rearrange("p (j m) -> p j m", j=3)
    cb = A.f32(22)
    mst = A.mark()
    load_pp(C, C.ffn_b_in[l].rearrange("(m p) -> m p", p=128), binp, "binp")
    load_pp(C, C.ffn_conv_w[l].rearrange("j (m p) -> (j m) p", p=128), cw.rearrange("p j m -> p (j m)"), "cw")
    load_pp(C, C.ffn_conv_b[l].rearrange("(m p) -> m p", p=128), cb, "cb")
    sc1 = A.f32(D)
    sh = A.f32(D)
    gf = A.f32(D)
    gfb = A.f32(D)
    xt_ring = [A.f32(4 * D).rearrange("p (b c) -> p b c", b=4) for _ in range(2)]
    xh_r = [A.f32(D, parts=2) for _ in range(2)]
    tmp = A.f32(D)
    hb_r = [A.bf16(4 * D).rearrange("p (b c) -> p b c", b=4) for _ in range(2)]
    hhb_r = [A.bf16(D, parts=2) for _ in range(2)]
    hfm_r = [A.bf16(8 * 516).rearrange("p (k t) -> p k t", k=8) for _ in range(2)]
    wring = [A.bf16(8 * 512).rearrange("p (k c) -> p k c", k=8) for _ in range(2)]
    hid = A.bf16(22 * 512).rearrange("p (m t) -> p m t", m=22)
    u_sb = [A.f32(514) for _ in range(2)]
    vv = [A.f32(512) for _ in range(2)]
    g1 = [A.f32(512) for _ in range(2)]
    rt = vv
    st = A.f32(12).rearrange("p (a b) -> p a b", a=2)
    mv = A.f32(2)
    rs = A.f32(1)
    nmr = A.f32(1)
    if C.cfg.get("verbose"):
        print("ffn arena high-water", A.off, "of", A.words)
    pT = PS[0][:, 0:256].bitcast(BF16)
    pu = [PS[1], PS[2]]
    pg = [PS[3], PS[4]]
    phl = PS[5][:, 0:44]
    pTh = PS[5][:, 64:72].bitcast(BF16)
    po = [PS[6], PS[7]]
    gi = [0]
    cut(C, "A")
    for (sq, xin, xout) in seq_io:
        n = sq.n
        ci = sq.ci
        bc_load(C, sh, C.modd[l, ci, 3 * D:4 * D], "sh")
        bc_load(C, sc1, C.modd[l, ci, 4 * D:5 * D], "sc1")
        bc_load(C, gf, C.modd[l, ci, 5 * D:6 * D], "gf")
        P.op("pool", lambda e: e.tensor_scalar(out=sc1, in0=sc1, scalar1=1.0, scalar2=None, op0=ALU.add), reads=["sc1"], writes=["sc1"])
        P.op("pool", lambda e: e.tensor_tensor(out=gfb, in0=gf, in1=bout, op=ALU.mult), reads=["gf", "bout"], writes=["gfb"])
        nt = n // 512
        it0 = gi[0]
        gi[0] += nt

        def prologue(T):
            t0 = T * 512
            it = it0 + T
            i2 = it % 2
            xt, xh, hb, hhb, hfm = xt_ring[i2], xh_r[i2], hb_r[i2], hhb_r[i2], hfm_r[i2]
            xk = f"xt{i2}"
            has_l = t0 > 0
            has_r = t0 + 512 < n
            P.dma(xt, xin[t0:t0 + 512, :].rearrange("(b p) c -> p b c", p=128), writes=[xk], q="sp")
            if (has_l or has_r) and not (has_l and has_r):
                P.op("pool", lambda e: e.memset(xh, 0.0), writes=[f"xh0{i2}", f"xh1{i2}"])
            if has_l:
                P.dma(xh[0:1, :], xin[t0 - 1:t0, :], writes=[f"xh0{i2}"], q="sp")
            if has_r:
                P.dma(xh[1:2, :], xin[t0 + 512:t0 + 513, :], writes=[f"xh1{i2}"], q="sp")
            for b in range(4):
                P.op("dve", lambda e, b=b: e.tensor_tensor(out=tmp, in0=xt[:, b, :], in1=sc1, op=ALU.mult), reads=[xk, "sc1"], writes=["tmp"])
                P.op("dve", lambda e, b=b: e.tensor_tensor(out=hb[:, b, :], in0=tmp, in1=sh, op=ALU.add), reads=["tmp", "sh"], writes=[f"hb{i2}{b}"])
            if has_l or has_r:
                P.op("dve", lambda e: e.tensor_tensor(out=tmp[0:2, :], in0=xh, in1=sc1[0:2, :], op=ALU.mult), reads=[f"xh0{i2}", f"xh1{i2}", "sc1"], writes=["tmp"])
                P.op("dve", lambda e: e.tensor_tensor(out=hhb, in0=tmp[0:2, :], in1=sh[0:2, :], op=ALU.add), reads=["tmp", "sh"], writes=[f"hhb{i2}"])
            for b in range(4):
                P.op("dve", lambda e, b=b: e.scalar_tensor_tensor(out=xt[:, b, :], in0=xt[:, b, :], scalar=ALPHA, in1=gfb, op0=ALU.mult, op1=ALU.add),
                     reads=[xk, f"hb{i2}{b}", "gfb"], writes=[xk])

        def prologue_b(T):
            t0 = T * 512
            it = it0 + T
            i2 = it % 2
            xt, xh, hb, hhb, hfm = xt_ring[i2], xh_r[i2], hb_r[i2], hhb_r[i2], hfm_r[i2]
            has_l = t0 > 0
            has_r = t0 + 512 < n
            for k in range(8):
                for b in range(4):
                    P.op("pe", lambda e, b=b, k=k: e.transpose(pT[:, b * 128:(b + 1) * 128], hb[:, b, k * 128:(k + 1) * 128], C.identb),
                         reads=[f"hb{i2}{b}", "identb"], writes=["bank0"])
                if k % 2 == 0:
                    P.op("act", lambda e, k=k: e.copy(out=hfm[:, k, 2:514], in_=pT), reads=["bank0"], writes=[f"hfm{i2}{k}"])
                else:
                    P.op("dve", lambda e, k=k: e.tensor_copy(out=hfm[:, k, 2:514], in_=pT), reads=["bank0"], writes=[f"hfm{i2}{k}"])
            if has_l or has_r:
                for k in range(8):
                    P.op("pe", lambda e, k=k: e.transpose(pTh[:, 2 * k:2 * k + 2], hhb[0:2, k * 128:(k + 1) * 128], C.identb[0:2, 0:2]),
                         reads=[f"hhb{i2}", "identb"], writes=["bank5"])
                P.op("dve", lambda e: e.tensor_copy(out=hfm[:, :, 1], in_=pTh[:, 0:16:2]), reads=["bank5"], writes=[f"hfmh{i2}"])
                P.op("dve", lambda e: e.tensor_copy(out=hfm[:, :, 514], in_=pTh[:, 1:16:2]), reads=["bank5"], writes=[f"hfmh{i2}"])

        prologue(0)
        prologue_b(0)
        preloaded = set()
        for T in range(nt):
            t0 = T * 512
            it = it0 + T
            i2 = it % 2
            xt, hfm = xt_ring[i2], hfm_r[i2]
            xk = f"xt{i2}"
            has_l = t0 > 0
            has_r = t0 + 512 < n
            cut(C, "D")
            hf_keys = [f"hfm{i2}{k}" for k in range(8)]
            lvl = C.cfg.get("ffn_level", 3)
            if lvl == 1:
                P.op("dve", lambda e: e.tensor_copy(out=xt[:, 0, 0:514], in_=hfm[:, 0, 1:515]), reads=hf_keys + [f"hfmh{i2}", xk], writes=[xk])
                P.dma(xout[t0:t0 + 512, :].rearrange("(b p) c -> p b c", p=128), xt, reads=[xk], writes=["xout"], q="sp", is_output=(xout is sq.y))
                continue
            for m2 in range(11):
                wi = (it * 11 + m2) % 2
                w = wring[wi]
                wk = f"wr{wi}"
                if (it, m2) not in preloaded:
                    P.dma(w, C.win_s[l, m2], writes=[wk], q="sp")
                if m2 == 1 and T + 1 < nt:
                    prologue(T + 1)
                if m2 == 7 and T + 1 < nt:
                    prologue_b(T + 1)
                for mi in range(2):
                    m = m2 * 2 + mi
                    j = m % 2
                    puk, pgk = f"bank{1 + j}", f"bank{3 + j}"
                    for k in range(8):
                        P.op("pe", lambda e, j=j, k=k, mi=mi, w=w: e.matmul(pu[j], lhsT=w[:, k, mi * 128:(mi + 1) * 128], rhs=hfm[:, k, 2:514],
                                                                            start=(k == 0), stop=(k == 7)),
                             reads=[wk, hf_keys[k]], writes=[puk])
                    for k in range(8):
                        P.op("pe", lambda e, j=j, k=k, mi=mi, w=w: e.matmul(pg[j], lhsT=w[:, k, 256 + mi * 128:256 + (mi + 1) * 128], rhs=hfm[:, k, 2:514],
                                                                            start=(k == 0), stop=(k == 7)),
                             reads=[wk, hf_keys[k]], writes=[pgk])
                    if has_l or has_r:
                        for k in range(8):
                            P.op("pe", lambda e, k=k, mi=mi, m=m, w=w: e.matmul(phl[:, 2 * m:2 * m + 2], lhsT=w[:, k, mi * 128:(mi + 1) * 128],
                                                                                rhs=hfm[:, k, 1:515:513], start=(k == 0), stop=(k == 7)),
                                 reads=[wk, f"hfmh{i2}"], writes=["bank5"])
                    u = u_sb[j]
                    uk = f"u{j}"
                    P.op("act", lambda e, u=u, j=j, m=m: e.activation(out=u[:, 1:513], in_=pu[j], func=AF.Identity, bias=binp[:, m:m + 1], scale=1.0),
                         reads=[puk, "binp"], writes=[uk + "m"])
                    if has_l:
                        P.op("act", lambda e, u=u, m=m: e.activation(out=u[:, 0:1], in_=phl[:, 2 * m:2 * m + 1], func=AF.Identity, bias=binp[:, m:m + 1], scale=1.0),
                             reads=["bank5", "binp"], writes=[uk + "l"])
                    else:
                        P.op("pool", lambda e, u=u: e.memset(u[:, 0:1], 0.0), writes=[uk + "l"])
                    if has_r:
                        P.op("act", lambda e, u=u, m=m: e.activation(out=u[:, 513:514], in_=phl[:, 2 * m + 1:2 * m + 2], func=AF.Identity, bias=binp[:, m:m + 1], scale=1.0),
                             reads=["bank5", "binp"], writes=[uk + "r"])
                    else:
                        P.op("pool", lambda e, u=u: e.memset(u[:, 513:514], 0.0), writes=[uk + "r"])
                    v = vv[j]
                    vk = f"v{j}"
                    P.op("dve", lambda e, u=u, v=v, m=m: e.tensor_scalar(out=v, in0=u[:, 1:513], scalar1=cw[:, 1, m:m + 1], scalar2=cb[:, m:m + 1],
                                                                           op0=ALU.mult, op1=ALU.add),
                         reads=[uk + "m", "cw", "cb"], writes=[vk])
                    P.op("dve", lambda e, u=u, v=v, m=m: e.scalar_tensor_tensor(out=v, in0=u[:, 0:512], scalar=cw[:, 0, m:m + 1], in1=v, op0=ALU.mult, op1=ALU.add),
                         reads=[uk + "m", uk + "l", vk, "cw"], writes=[vk])
                    P.op("dve", lambda e, u=u, v=v, m=m: e.scalar_tensor_tensor(out=v, in0=u[:, 2:514], scalar=cw[:, 2, m:m + 1], in1=v, op0=ALU.mult, op1=ALU.add),
                         reads=[uk + "m", uk + "r", vk, "cw"], writes=[vk])
                    gg = g1[j]
                    gk = f"g1{j}"
                    P.op("act", lambda e, v=v, gg=gg: e.activation(out=gg, in_=v, func=AF.Gelu_apprx_tanh), reads=[vk], writes=[gk])
                    P.op("dve", lambda e, gg=gg, j=j, m=m: e.scalar_tensor_tensor(out=hid[:, m, :], in0=pg[j], scalar=binp[:, 22 + m:23 + m], in1=gg,
                                                                                   op0=ALU.add, op1=ALU.mult),
                         reads=[pgk, gk, "binp"], writes=[f"hid{m}"])
            hid_keys = [f"hid{m}" for m in range(22)]
            if T + 1 < nt:
                wi_n = ((it + 1) * 11) % 2
                P.dma(wring[wi_n], C.win_s[l, 0], writes=[f"wr{wi_n}"], q="sp")
                preloaded.add((it + 1, 0))
            if lvl == 2:
                P.op("dve", lambda e: e.tensor_copy(out=xt[:, 0, 0:512], in_=hid[:, 5, :]), reads=hid_keys + [xk], writes=[xk])
                P.dma(xout[t0:t0 + 512, :].rearrange("(b p) c -> p b c", p=128), xt, reads=[xk], writes=["xout"], q="sp", is_output=(xout is sq.y))
                continue
            for b in range(4):
                for half in range(2):
                    j = (b * 2 + half) % 2
                    pok = f"bank{6 + j}"
                    for k in range(22):
                        P.op("pe", lambda e, j=j, k=k, b=b, half=half: e.matmul(po[j], lhsT=hid[:, k, b * 128:(b + 1) * 128],
                                                                                  rhs=wout[:, k, half * 512:(half + 1) * 512], start=(k == 0), stop=(k == 21)),
                             reads=[hid_keys[k], "wout"], writes=[pok])
                    r = rt[j]
                    rk = f"v{j}"
                    xs = xt[:, b, half * 512:(half + 1) * 512]
                    P.op("dve", lambda e, j=j, r=r, half=half: e.tensor_tensor(out=r, in0=po[j], in1=gf[:, half * 512:(half + 1) * 512], op=ALU.mult),
                         reads=[pok, "gf"], writes=[rk])
                    P.op("dve", lambda e, r=r, xs=xs: e.tensor_tensor(out=xs, in0=xs, in1=r, op=ALU.add), reads=[rk, xk], writes=[xk])
                layer_norm_block(C, xt[:, b, :], xk, lnG, lnB, st, mv, rs, nmr)
            P.dma(xout[t0:t0 + 512, :].rearrange("(b p) c -> p b c", p=128), xt, reads=[xk], writes=["xout"], q="sp",
                  is_output=(xout is sq.y))
    A.release(m0)
    P.barrier()


def layer_norm_block(C, xb, xk, lnG, lnB, st, mv, rs, nmr):
    P = C.P
    P.op("dve", lambda e: e.bn_stats(out=st[:, 0, :], in_=xb[:, 0:512]), reads=[xk], writes=["st"])
    P.op("dve", lambda e: e.bn_stats(out=st[:, 1, :], in_=xb[:, 512:1024]), reads=[xk, "st"], writes=["st"])
    P.op("dve", lambda e: e.bn_aggr(out=mv, in_=st), reads=["st"], writes=["mv"])
    P.op("act", lambda e: e.activation(out=rs, in_=mv[:, 1:2], func=AF.Sqrt, bias=C.eps, scale=1.0), reads=["mv", "eps"], writes=["rs"])
    P.op("dve", lambda e: e.reciprocal(out=rs, in_=rs), reads=["rs"], writes=["rs"])
    P.op("dve", lambda e: e.tensor_scalar(out=nmr, in0=mv[:, 0:1], scalar1=rs, scalar2=-1.0, op0=ALU.mult, op1=ALU.mult),
         reads=["mv", "rs"], writes=["nmr"])
    P.op("act", lambda e: e.activation(out=xb, in_=xb, func=AF.Identity, bias=nmr, scale=rs), reads=[xk, "rs", "nmr"], writes=[xk])
    P.op("dve", lambda e: e.tensor_tensor(out=xb, in0=xb, in1=lnG, op=ALU.mult), reads=[xk, "lnG"], writes=[xk])
    P.op("dve", lambda e: e.tensor_tensor(out=xb, in0=xb, in1=lnB, op=ALU.add), reads=[xk, "lnB"], writes=[xk])


TWO_PI = 2.0 * math.pi
MAGIC = 12582912.0
CW1 = 6.28125
CW2 = TWO_PI - 6.28125


def sincos(C, ang, n, sin_out, cos_out, tag):
    P, A = C.P, C.A
    k = A.f32(n)
    r = A.f32(n)
    ka, kk, kr = tag + "a", tag + "k", tag + "r"
    for (shift, outp) in ((0.0, sin_out), (0.5 * math.pi, cos_out)):
        if outp is None:
            continue
        P.op("dve", lambda e, shift=shift: e.tensor_scalar(out=r, in0=ang, scalar1=shift, scalar2=None, op0=ALU.add), reads=[ka], writes=[kr])
        P.op("dve", lambda e: e.tensor_scalar(out=k, in0=r, scalar1=1.0 / TWO_PI, scalar2=MAGIC, op0=ALU.mult, op1=ALU.add), reads=[kr], writes=[kk])
        P.op("dve", lambda e: e.tensor_scalar(out=k, in0=k, scalar1=MAGIC, scalar2=None, op0=ALU.subtract), reads=[kk], writes=[kk])
        P.op("dve", lambda e: e.scalar_tensor_tensor(out=r, in0=k, scalar=-CW1, in1=r, op0=ALU.mult, op1=ALU.add), reads=[kk, kr], writes=[kr])
        P.op("dve", lambda e: e.scalar_tensor_tensor(out=r, in0=k, scalar=-CW2, in1=r, op0=ALU.mult, op1=ALU.add), reads=[kk, kr], writes=[kr])
        P.op("dve", lambda e: e.tensor_scalar(out=r, in0=r, scalar1=math.pi, scalar2=-math.pi, op0=ALU.min, op1=ALU.max), reads=[kr], writes=[kr])
        P.op("act", lambda e, outp=outp: e.activation(out=outp, in_=r, func=AF.Sin), reads=[kr], writes=[tag + "o"])


def dup_transpose(C, src, dst, key):
    P, A = C.P, C.A
    stg = A.f32(128)
    sk = A.key("dtstg")
    P.dma(stg[:, 0:64], src, writes=[sk])
    P.dma(stg[:, 64:128], src, writes=[sk + "b"])
    ps = C.psum[7][:, 0:128]
    P.op("pe", lambda e: e.transpose(ps, stg, C.ident), reads=[sk, sk + "b"], writes=["bank7"])
    P.op("dve", lambda e: e.tensor_copy(out=dst, in_=ps), reads=["bank7"], writes=[key])


def s5_setup(C):
    P, A, nc = C.P, C.A, C.nc
    PS = C.psum
    m0 = A.mark()
    NDG = 128
    LR = A.f32(NDG)
    LI = A.f32(NDG)
    DT = A.f32(NDG)
    dup_transpose(C, C.s5_lam_re.rearrange("d g p -> (d g) p"), LR, "LR")
    dup_transpose(C, C.s5_lam_im.rearrange("d g p -> (d g) p"), LI, "LI")
    ldt = A.f32(1)
    P.dma(ldt, C.s5_log_dt.rearrange("d (g o) -> (d g) o", o=1), writes=["ldt"])
    ldtb = A.f32(128)
    P.op("dve", lambda e: e.tensor_copy(out=ldtb, in_=ldt.to_broadcast([128, 128])), reads=["ldt"], writes=["ldtb"])
    P.op("pe", lambda e: e.transpose(PS[7][:, 0:128], ldtb, C.ident), reads=["ldtb"], writes=["bank7"])
    P.op("act", lambda e: e.activation(out=DT, in_=PS[7][:, 0:128], func=AF.Exp), reads=["bank7"], writes=["DT"])
    ZR = A.f32(NDG)
    ZI = A.f32(NDG)
    P.op("dve", lambda e: e.tensor_tensor(out=ZR, in0=LR, in1=DT, op=ALU.mult), reads=["LR", "DT"], writes=["ZR"])
    P.op("dve", lambda e: e.tensor_tensor(out=ZI, in0=LI, in1=DT, op=ALU.mult), reads=["LI", "DT"], writes=["ZI"])
    NP_ = 17
    PWr = A.f32(NP_ * NDG).rearrange("p (i g) -> p i g", i=NP_)
    PWi = A.f32(NP_ * NDG).rearrange("p (i g) -> p i g", i=NP_)
    Fr = A.f32(NDG)
    Fi = A.f32(NDG)
    RHO = A.f32(NDG)
    rho16 = A.f32(NDG)
    mS1 = A.mark()
    nvec = A.f32(NP_)
    for i in range(NP_):
        P.op("pool", lambda e, i=i: e.memset(nvec[:, i:i + 1], float(i - 8)), writes=["nvec"])
    ang = A.f32(NP_ * NDG).rearrange("p (i g) -> p i g", i=NP_)
    mag = A.f32(NP_ * NDG).rearrange("p (i g) -> p i g", i=NP_)
    nb3 = nvec.unsqueeze(2).to_broadcast([128, NP_, NDG])
    P.op("dve", lambda e: e.tensor_tensor(out=ang, in0=ZI.unsqueeze(1).to_broadcast([128, NP_, NDG]), in1=nb3, op=ALU.mult), reads=["ZI", "nvec"], writes=["anga"])
    P.op("dve", lambda e: e.tensor_tensor(out=mag, in0=ZR.unsqueeze(1).to_broadcast([128, NP_, NDG]), in1=nb3, op=ALU.mult), reads=["ZR", "nvec"], writes=["mag"])
    P.op("act", lambda e: e.activation(out=mag, in_=mag, func=AF.Exp), reads=["mag"], writes=["mag"])
    sn = A.f32(NP_ * NDG).rearrange("p (i g) -> p i g", i=NP_)
    cs = A.f32(NP_ * NDG).rearrange("p (i g) -> p i g", i=NP_)
    f2 = lambda t: t.rearrange("p i g -> p (i g)")
    m1 = A.mark()
    sincos(C, f2(ang), NP_ * NDG, f2(sn), f2(cs), "ang")
    P.op("dve", lambda e: e.tensor_tensor(out=PWr, in0=mag, in1=cs, op=ALU.mult), reads=["mag", "ango"], writes=["PWr"])
    P.op("dve", lambda e: e.tensor_tensor(out=PWi, in0=mag, in1=sn, op=ALU.mult), reads=["mag", "ango"], writes=["PWi"])
    den = A.f32(NDG)
    t0_ = A.f32(NDG)
    a_re, a_im = PWr[:, 9, :], PWi[:, 9, :]
    P.op("dve", lambda e: e.tensor_tensor(out=den, in0=LR, in1=LR, op=ALU.mult), reads=["LR"], writes=["den"])
    P.op("dve", lambda e: e.tensor_tensor(out=t0_, in0=LI, in1=LI, op=ALU.mult), reads=["LI"], writes=["t0_"])
    P.op("dve", lambda e: e.tensor_tensor(out=den, in0=den, in1=t0_, op=ALU.add), reads=["den", "t0_"], writes=["den"])
    P.op("dve", lambda e: e.reciprocal(out=den, in_=den), reads=["den"], writes=["den"])
    nr = A.f32(NDG)
    P.op("dve", lambda e: e.tensor_scalar(out=nr, in0=a_re, scalar1=-1.0, scalar2=None, op0=ALU.add), reads=["PWr"], writes=["nr"])
    P.op("dve", lambda e: e.tensor_tensor(out=Fr, in0=nr, in1=LR, op=ALU.mult), reads=["nr", "LR"], writes=["Fr"])
    P.op("dve", lambda e: e.tensor_tensor(out=t0_, in0=a_im, in1=LI, op=ALU.mult), reads=["PWi", "LI", "den"], writes=["t0_"])
    P.op("dve", lambda e: e.tensor_tensor(out=Fr, in0=Fr, in1=t0_, op=ALU.add), reads=["Fr", "t0_"], writes=["Fr"])
    P.op("dve", lambda e: e.tensor_tensor(out=Fr, in0=Fr, in1=den, op=ALU.mult), reads=["Fr", "den"], writes=["Fr"])
    P.op("dve", lambda e: e.tensor_tensor(out=Fi, in0=a_im, in1=LR, op=ALU.mult), reads=["PWi", "LR"], writes=["Fi"])
    P.op("dve", lambda e: e.tensor_tensor(out=t0_, in0=nr, in1=LI, op=ALU.mult), reads=["nr", "LI", "Fr"], writes=["t0_"])
    P.op("dve", lambda e: e.tensor_tensor(out=Fi, in0=Fi, in1=t0_, op=ALU.subtract), reads=["Fi", "t0_"], writes=["Fi"])
    P.op("dve", lambda e: e.tensor_tensor(out=Fi, in0=Fi, in1=den, op=ALU.mult), reads=["Fi", "den"], writes=["Fi"])
    P.op("dve", lambda e: e.tensor_copy(out=RHO, in_=mag[:, 16, :]), reads=["mag"], writes=["RHO"])
    P.op("act", lambda e: e.activation(out=rho16, in_=ZR, func=AF.Exp, scale=128.0), reads=["ZR"], writes=["rho16"])
    P.barrier()
    A.release(mS1)
    ZI2 = A.f32(NDG)
    P.op("dve", lambda e: e.tensor_tensor(out=ZI2, in0=LI, in1=DT, op=ALU.mult), writes=["ZI2"])
    mS2 = A.mark()
    jv = A.f32(48)
    for j in range(16):
        P.op("pool", lambda e, j=j: e.memset(jv[:, j:j + 1], 8.0 * j), writes=["jv"])
    for s_ in range(32):
        P.op("pool", lambda e, s_=s_: e.memset(jv[:, 16 + s_:17 + s_], 128.0 * s_), writes=["jv"])
    TT = A.f32(64 * 128).rearrange("p (g c) -> p g c", c=128)
    a48 = A.f32(64 * 48).rearrange("p (g j) -> p g j", j=48)
    s48 = A.f32(64 * 48).rearrange("p (g j) -> p g j", j=48)
    c48 = A.f32(64 * 48).rearrange("p (g j) -> p g j", j=48)
    g2 = lambda t: t.rearrange("p g j -> p (g j)")
    mS3 = A.mark()
    for d in range(2):
        ds_ = slice(d * 64, (d + 1) * 64)
        P.op("pool", lambda e: e.memset(TT, 0.0), reads=["TT"], writes=["TT"])
        P.op("dve", lambda e, ds_=ds_: e.tensor_tensor(out=a48, in0=ZI2[:, ds_].unsqueeze(2).to_broadcast([128, 64, 48]), in1=jv.unsqueeze(1).to_broadcast([128, 64, 48]), op=ALU.mult),
             reads=["ZI2", "jv", "a48o"], writes=["a48a"])
        sincos(C, g2(a48), 64 * 48, g2(s48), g2(c48), "a48")
        A.release(mS3)
        if d == 0:
            P.op("dve", lambda e: e.tensor_copy(out=TT[:, :, 0:16], in_=c48[:, :, 0:16]), reads=["a48o", "TT"], writes=["TT"])
            P.op("dve", lambda e: e.tensor_copy(out=TT[:, :, 16:32], in_=s48[:, :, 0:16]), reads=["a48o", "TT"], writes=["TT"])
        else:
            P.op("dve", lambda e: e.tensor_copy(out=TT[:, :, 0:16], in_=c48[:, :, 15::-1]), reads=["a48o", "TT"], writes=["TT"])
            P.op("dve", lambda e: e.tensor_copy(out=TT[:, :, 16:32], in_=s48[:, :, 15::-1]), reads=["a48o", "TT"], writes=["TT"])
        P.op("dve", lambda e: e.tensor_copy(out=TT[:, :, 48:80], in_=c48[:, :, 16:48]), reads=["a48o", "TT"], writes=["TT"])
        P.op("dve", lambda e: e.tensor_copy(out=TT[:, :, 80:112], in_=s48[:, :, 16:48]), reads=["a48o", "TT"], writes=["TT"])
        P.op("dve", lambda e, ds_=ds_: e.tensor_copy(out=TT[:, :, 33:48], in_=RHO[:, ds_].unsqueeze(2).to_broadcast([128, 64, 15])), reads=["RHO", "TT"], writes=["TT"])
        P.op("dve", lambda e, ds_=ds_: e.tensor_copy(out=TT[:, :, 112], in_=RHO[:, ds_]), reads=["RHO", "TT"], writes=["TT"])
        P.op("dve", lambda e, ds_=ds_: e.tensor_copy(out=TT[:, :, 113], in_=rho16[:, ds_]), reads=["rho16", "TT"], writes=["TT"])
        P.dma(C.s5t[:, :, d, :].rearrange("g p c -> p g c"), TT, reads=["TT"], writes=["s5t"], q=("sp" if d == 0 else "act"))
    P.barrier()
    A.release(mS2)
    CTr = A.f32(NDG * 16).rearrange("p (g h) -> p g h", h=16)
    CTi = A.f32(NDG * 16).rearrange("p (g h) -> p g h", h=16)
    BTr = A.f32(NDG * 16).rearrange("p (g h) -> p g h", h=16)
    BTi = A.f32(NDG * 16).rearrange("p (g h) -> p g h", h=16)
    for (src, dst, key) in ((C.s5_c_re, CTr, "CTr"), (C.s5_c_im, CTi, "CTi")):
        rows = src.rearrange("d g h p -> (d g h) p")
        for blk in range(16):
            stg = A.f32(128)
            sk = A.key("cstg")
            P.dma(stg[:, 0:64], rows[blk * 128:(blk + 1) * 128, :], writes=[sk], q=("sp" if blk % 2 == 0 else "act"))
            P.dma(stg[:, 64:128], rows[blk * 128:(blk + 1) * 128, :], writes=[sk + "b"], q=("act" if blk % 2 == 0 else "sp"))
            bk = f"bank{6 + blk % 2}"
            ps = PS[6 + blk % 2][:, 0:128]
            P.op("pe", lambda e, ps=ps, stg=stg: e.transpose(ps, stg, C.ident), reads=[sk, sk + "b"], writes=[bk])
            P.op("dve", lambda e, ps=ps, dst=dst, blk=blk: e.tensor_copy(out=dst[:, blk * 8:(blk + 1) * 8, :], in_=ps.rearrange("p (g h) -> p g h", h=16)),
                 reads=[bk], writes=[key])
    for (src, dst, key) in ((C.s5_b_re, BTr, "BTr"), (C.s5_b_im, BTi, "BTi")):
        v = src.rearrange("d g p h -> p (d g) h")
        for half in range(2):
            for q4 in range(4):
                P.dma(dst[half * 64:(half + 1) * 64, q4 * 32:(q4 + 1) * 32, :], v[:, q4 * 32:(q4 + 1) * 32, :], writes=[key + f"{half}{q4}"],
                      q=("sp" if (half + q4) % 2 == 0 else "act"))
    bt_keys = lambda k: [k + f"{h}{q}" for h in range(2) for q in range(4)]
    BBr = A.f32(NDG * 16).rearrange("p (g h) -> p g h", h=16)
    BBi = A.f32(NDG * 16).rearrange("p (g h) -> p g h", h=16)
    tq = A.f32(NDG * 16).rearrange("p (g h) -> p g h", h=16)
    frb = Fr.unsqueeze(2).to_broadcast([128, NDG, 16])
    fib = Fi.unsqueeze(2).to_broadcast([128, NDG, 16])
    P.op("dve", lambda e: e.tensor_tensor(out=BBr, in0=BTr, in1=frb, op=ALU.mult), reads=bt_keys("BTr") + ["Fr"], writes=["BBr"])
    P.op("dve", lambda e: e.tensor_tensor(out=tq, in0=BTi, in1=fib, op=ALU.mult), reads=bt_keys("BTi") + ["Fi"], writes=["tq"])
    P.op("dve", lambda e: e.tensor_tensor(out=BBr, in0=BBr, in1=tq, op=ALU.subtract), reads=["BBr", "tq"], writes=["BBr"])
    P.op("dve", lambda e: e.tensor_tensor(out=BBi, in0=BTi, in1=frb, op=ALU.mult), reads=bt_keys("BTi") + ["Fr"], writes=["BBi"])
    P.op("dve", lambda e: e.tensor_tensor(out=tq, in0=BTr, in1=fib, op=ALU.mult), reads=bt_keys("BTr") + ["Fi", "BBr"], writes=["tq"])
    P.op("dve", lambda e: e.tensor_tensor(out=BBi, in0=BBi, in1=tq, op=ALU.add), reads=["BBi", "tq"], writes=["BBi"])
    dst_ = A.f32(128, parts=64)
    for t in range(8):
        P.dma(dst_[:, t * 16:(t + 1) * 16], C.s5_d, writes=[f"dst{t}"], q=("sp" if t % 2 == 0 else "act"))
    dcol = A.f32(64)
    P.op("pe", lambda e: e.transpose(PS[7][:, 0:64], dst_, C.ident[0:64, 0:64]), reads=[f"dst{t}" for t in range(8)], writes=["bank7"])
    P.op("dve", lambda e: e.tensor_copy(out=dcol, in_=PS[7][:, 0:64]), reads=["bank7"], writes=["dcol"])
    mf = A.f32(128)
    mb = A.f32(128)
    P.dma(mf, C.s5_masks[0], writes=["mf"])
    P.dma(mb, C.s5_masks[1], writes=["mb"])
    GC = 8
    al = [A.f32(NDG * 8).rearrange("p (g t) -> p g t", t=8) for _ in range(8)]
    def gather(out_t, PW, idx_f, idx_b, sgn_top, sgn_bot, key, rk):
        for (g0, idx) in ((0, idx_f), (64, idx_b)):
            for t in range(8):
                for (p0, sg) in ((0, sgn_top), (64, sgn_bot)):
                    src_t = PW[0][p0:p0 + 64, idx[t] + 8, g0:g0 + 64] if sg[1] == "r" else PW[1][p0:p0 + 64, idx[t] + 8, g0:g0 + 64]
                    P.op("pool", lambda e, out_t=out_t, src_t=src_t, p0=p0, g0=g0, t=t, sg=sg: e.tensor_scalar(
                        out=out_t[p0:p0 + 64, g0:g0 + 64, t], in0=src_t, scalar1=float(sg[0]), scalar2=None, op0=ALU.mult), reads=rk, writes=[key])
    PW = (PWr, PWi)
    pk = ["PWr", "PWi"]
    tf = [t + 1 for t in range(8)]
    tb_ = [8 - t for t in range(8)]
    gather(al[0], PW, tf, tb_, (1, "r"), (-1, "i"), "al0", pk)
    gather(al[1], PW, tf, tb_, (-1, "i"), (-1, "r"), "al1", pk)
    gather(al[2], PW, tf, tb_, (-1, "i"), (-1, "r"), "al2", pk)
    gather(al[3], PW, tf, tb_, (-1, "r"), (1, "i"), "al3", pk)
    xin_f = [7 - t for t in range(8)]
    xin_b = [t for t in range(8)]
    gather(al[4], PW, xin_f, xin_b, (1, "r"), (1, "i"), "al4", pk)
    gather(al[5], PW, xin_f, xin_b, (-1, "i"), (1, "r"), "al5", pk)
    xtp_f = [-1 - t for t in range(8)]
    xtp_b = [t - 8 for t in range(8)]
    gather(al[6], PW, xtp_f, xtp_b, (1, "r"), (1, "i"), "al6", pk)
    gather(al[7], PW, xtp_f, xtp_b, (-1, "i"), (1, "r"), "al7", pk)
    gen = [A.bf16(GC * 128).rearrange("p (g t h) -> p g t h", g=GC, t=8) for _ in range(4)]
    tm1 = A.f32(GC * 128).rearrange("p (g t h) -> p g t h", g=GC, t=8)
    tm2 = A.f32(GC * 128).rearrange("p (g t h) -> p g t h", g=GC, t=8)
    wstage = [A.bf16(1024) for _ in range(2)]
    tpf = A.f32(128)
    tpb = A.f32(128)
    specs = ((CTr, CTi, 0, 1, ["CTr", "CTi"]), (CTr, CTi, 2, 3, ["CTr", "CTi"]), (BBr, BBi, 4, 5, ["BBr", "BBi"]), (BBr, BBi, 6, 7, ["BBr", "BBi"]))
    GEN = {}
    for d in range(2):
        for gc in range(64 // GC):
            bg_step(C, 3)
            dg0 = d * 64 + gc * GC
            for gi_, (Sr, Si, ia, ib, rk) in enumerate(specs):
                eng = "dve"
                srb = Sr[:, dg0:dg0 + GC, :].unsqueeze(2).to_broadcast([128, GC, 8, 16])
                sib = Si[:, dg0:dg0 + GC, :].unsqueeze(2).to_broadcast([128, GC, 8, 16])
                aa = al[ia][:, dg0:dg0 + GC, :].unsqueeze(3).to_broadcast([128, GC, 8, 16])
                bb = al[ib][:, dg0:dg0 + GC, :].unsqueeze(3).to_broadcast([128, GC, 8, 16])
                gk = f"gen{gi_}"
                P.op(eng, lambda e, srb=srb, aa=aa: e.tensor_tensor(out=tm1, in0=srb, in1=aa, op=ALU.mult), reads=rk + [f"al{ia}"], writes=["tm1"])
                P.op(eng, lambda e, sib=sib, bb=bb: e.tensor_tensor(out=tm2, in0=sib, in1=bb, op=ALU.mult), reads=rk + [f"al{ib}"], writes=["tm2"])
                P.op(eng, lambda e, gi_=gi_: e.tensor_tensor(out=gen[gi_], in0=tm1, in1=tm2, op=ALU.add), reads=["tm1", "tm2"], writes=[gk])
            for gl in range(GC):
                g = gc * GC + gl
                base = 128 + d * 448
                W1g = gen[0][:, gl].rearrange("p t h -> p (t h)")
                W2g = gen[1][:, gl].rearrange("p t h -> p (t h)")
                Xin = gen[2][:, gl].rearrange("p t h -> p (t h)")
                Xtp = gen[3][:, gl].rearrange("p t h -> p (t h)")
                ws = wstage[(d * 64 + g) % 2]
                wk = f"ws{(d * 64 + g) % 2}"
                pst = PS[0][:, 0:64].bitcast(BF16)
                P.op("pe", lambda e, pst=pst, Xin=Xin: e.transpose(pst, Xin, C.identb), reads=["gen2"], writes=["bank0"])
                P.op("act", lambda e, ws=ws, pst=pst: e.copy(out=ws[:, 0:128], in_=pst), reads=["bank0"], writes=[wk + "a"])
                P.op("act", lambda e, ws=ws, pst=pst: e.activation(out=ws[:, 128:192], in_=pst[:, 0:64], func=AF.Identity, scale=-1.0), reads=["bank0"], writes=[wk + "b"])
                P.op("dve", lambda e, ws=ws, W1g=W1g: e.tensor_copy(out=ws[:, 192:320], in_=W1g), reads=["gen0"], writes=[wk + "c"])
                P.op("dve", lambda e, ws=ws, W2g=W2g: e.tensor_copy(out=ws[:, 320:448], in_=W2g), reads=["gen1"], writes=[wk + "d"])
                P.dma(C.s5w[g, :, base:base + 448], ws[:, 0:448], reads=[wk + "a", wk + "b", wk + "c", wk + "d"], writes=["s5w"], q=("sp" if g % 2 == 0 else "act"))
                ptp = PS[1 + d][:, 0:128]
                P.op("pe", lambda e, ptp=ptp, Xtp=Xtp, W1g=W1g: e.matmul(ptp, lhsT=Xtp, rhs=W1g, start=True, stop=True), reads=["gen3", "gen0"], writes=[f"bank{1 + d}"])
                P.op("dve", lambda e, ptp=ptp, d=d: e.tensor_tensor(out=(tpf if d == 0 else tpb), in0=ptp, in1=(mf if d == 0 else mb), op=ALU.mult),
                     reads=[f"bank{1 + d}", "mf", "mb"], writes=["tpx"])
                P.dma(C.s5tp[d, g], (tpf if d == 0 else tpb), reads=["tpx"], writes=["s5tp"], q="sp")
    ta = [A.f32(128) for _ in range(2)]
    tb2 = [A.f32(128) for _ in range(2)]
    tob = [A.bf16(128) for _ in range(2)]
    for g in range(64):
        i = g % 2
        if g % 2 == 0:
            bg_step(C, 1)
        P.dma(ta[i], C.s5tp[0, g], reads=["s5tp"], writes=[f"ta{i}"], q="sp")
        P.dma(tb2[i], C.s5tp[1, g], reads=["s5tp"], writes=[f"tb{i}"], q="act")
        P.op("dve", lambda e, i=i: e.tensor_tensor(out=ta[i], in0=ta[i], in1=tb2[i], op=ALU.add), reads=[f"ta{i}", f"tb{i}"], writes=[f"ta{i}"])
        P.op("dve", lambda e, i=i, g=g: e.scalar_tensor_tensor(out=tob[i], in0=C.ident, scalar=dcol[:, g:g + 1], in1=ta[i], op0=ALU.mult, op1=ALU.add),
             reads=[f"ta{i}", "dcol"], writes=[f"tob{i}"])
        P.dma(C.s5w[g, :, 0:128], tob[i], reads=[f"tob{i}"], writes=["s5w"], q="sp")
    A.release(m0)
    P.barrier()


def s5_phase(C):
    P, A, nc = C.P, C.A, C.nc
    PS = C.psum
    l = 0
    seqs = C.seqs
    m0 = A.mark()
    sc1 = A.f32(D)
    sh = A.f32(D)
    xc = A.f32(8 * D).rearrange("p (t c) -> p t c", t=8)
    tmp = A.f32(D)
    hperm = A.bf16(8 * D).rearrange("p (g t h) -> p g t h", g=64, t=8)
    ust = [A.bf16(64 * 128).rearrange("p (g c) -> p g c", g=64) for _ in range(2)]
    cnt = 0
    for sq in seqs:
        ci = sq.ci
        bc_load(C, sh, C.modd[l, ci, 0:D], "sh")
        bc_load(C, sc1, C.modd[l, ci, D:2 * D], "sc1")
        P.op("pool", lambda e: e.tensor_scalar(out=sc1, in0=sc1, scalar1=1.0, scalar2=None, op0=ALU.add), reads=["sc1"], writes=["sc1"])
        for cb in range(sq.n // 1024):
            P.dma(xc, sq.x0[cb * 1024:(cb + 1) * 1024, :].rearrange("(c t) d -> c t d", t=8), writes=["xc"], q="sp")
            for t in range(8):
                P.op("dve", lambda e, t=t: e.tensor_tensor(out=tmp, in0=xc[:, t, :], in1=sc1, op=ALU.mult), reads=["xc", "sc1"], writes=["tmp"])
                P.op("dve", lambda e, t=t: e.tensor_tensor(out=hperm[:, :, t, :], in0=tmp.rearrange("p (g h) -> p g h", h=16),
                                                             in1=sh.rearrange("p (g h) -> p g h", h=16), op=ALU.add), reads=["tmp", "sh"], writes=["hperm"])
            us = ust[cnt % 2]
            uk = f"ust{cnt % 2}"
            cnt += 1
            for g4 in range(16):
                pT = PS[g4 % 2][:, 0:256].bitcast(BF16)
                bk = f"bank{g4 % 2}"
                for gi_ in range(4):
                    g = g4 * 4 + gi_
                    P.op("pe", lambda e, pT=pT, g=g, gi_=gi_: e.transpose(pT[:, gi_ * 128:(gi_ + 1) * 128], hperm[:, g].rearrange("p t h -> p (t h)"), C.identb),
                         reads=["hperm"], writes=[bk])
                if g4 % 2 == 0:
                    P.op("act", lambda e, pT=pT, us=us, g4=g4: e.copy(out=us[:, g4 * 4:(g4 + 1) * 4, :], in_=pT.rearrange("p (g c) -> p g c", g=4)), reads=[bk], writes=[uk + "a"])
                else:
                    P.op("dve", lambda e, pT=pT, us=us, g4=g4: e.tensor_copy(out=us[:, g4 * 4:(g4 + 1) * 4, :], in_=pT.rearrange("p (g c) -> p g c", g=4)), reads=[bk], writes=[uk + "d"])
            P.dma(sq.U[cb], us, reads=[uk + "a", uk + "d"], writes=["Uscr"], q="sp")
    A.release(m0)
    P.barrier()
    m0 = A.mark()
    Jm = A.f32(128)
    P.op("dve", lambda e: e.tensor_scalar(out=Jm[:, 0:64], in0=C.ident[:, 64:128], scalar1=-1.0, scalar2=None, op0=ALU.mult), writes=["Jm"])
    P.op("dve", lambda e: e.tensor_copy(out=Jm[:, 64:128], in_=C.ident[:, 0:64]), reads=["Jm"], writes=["Jm"])
    NCM = 512
    wg_r = [A.bf16(1024) for _ in range(2)]
    tg_r = [A.f32(256).rearrange("p (d c) -> p d c", d=2) for _ in range(2)]
    ug_r = [A.bf16(NCM) for _ in range(2)]
    t1_r = [A.f32(NCM) for _ in range(2)]
    t2_r = [A.f32(NCM) for _ in range(2)]
    bp_r = [A.f32(NCM) for _ in range(2)]
    mf_r = [A.f32(NCM) for _ in range(2)]
    ql_r = [A.f32(NCM) for _ in range(2)]
    qq_r = [A.f32(NCM) for _ in range(2)]
    cq_r = [A.bf16(NCM + 2) for _ in range(2)]
    sq_r = [A.bf16(NCM + 2) for _ in range(2)]
    zg_r = [A.bf16(NCM) for _ in range(2)]
    ec_r = [A.f32(32) for _ in range(2)]
    e1_r = [A.f32(32) for _ in range(2)]
    e2_r = [A.f32(32) for _ in range(2)]
    hin_r = [A.f32(34) for _ in range(2)]
    r16_r = [A.f32(32) for _ in range(2)]
    gg_r = [A.f32(32) for _ in range(2)]
    it = 0
    for sq in seqs:
        n_c = sq.n // 8
        n_b = n_c // 16
        ncb = sq.n // 1024
        for g in range(64):
            wg = wg_r[g % 2]
            tg = tg_r[g % 2]
            ug = ug_r[g % 2]
            wk, tk, uk = f"wg{g % 2}", f"tg{g % 2}", f"ug{g % 2}"
            P.dma(wg, C.s5w[g], writes=[wk], q="sp")
            P.dma(tg, C.s5t[g], writes=[tk], q="sp")
            P.dma(ug[:, 0:n_c].rearrange("p (b c) -> p b c", c=128), sq.U[:, :, g, :].rearrange("b p c -> p b c"), writes=[uk], q="sp")
            py = PS[4 + g % 2][:, 0:n_c]
            pyk = f"bank{4 + g % 2}"
            P.op("pe", lambda e, py=py, wg=wg, ug=ug, n_c=n_c: e.matmul(py, lhsT=wg[:, 0:128], rhs=ug[:, 0:n_c], start=True, stop=False), reads=[wk, uk], writes=[pyk])
            def chain(d, i2):
                base = 128 + d * 448
                pX = PS[0 + 2 * i2][:, 0:n_c]
                pXt = PS[1 + 2 * i2][:, 0:n_c]
                kX, kXt = f"bank{0 + 2 * i2}", f"bank{1 + 2 * i2}"
                P.op("pe", lambda e, pX=pX, wg=wg, ug=ug, base=base, n_c=n_c: e.matmul(pX, lhsT=wg[:, base:base + 128], rhs=ug[:, 0:n_c], start=True, stop=True),
                     reads=[wk, uk], writes=[kX])
                yield
                P.op("pe", lambda e, pXt=pXt, wg=wg, ug=ug, base=base, n_c=n_c: e.matmul(pXt, lhsT=wg[:, base + 64:base + 192], rhs=ug[:, 0:n_c], start=True, stop=True),
                     reads=[wk, uk], writes=[kXt])
                yield
                t1, t2, bp, mfu, ql, qq = t1_r[i2], t2_r[i2], bp_r[i2], mf_r[i2], ql_r[i2], qq_r[i2]
                ec, e1, e2, hin, r16, gg_ = ec_r[i2], e1_r[i2], e2_r[i2], hin_r[i2], r16_r[i2], gg_r[i2]
                kec, ke1, ke2, khin, kr16, kgg = f"ec{i2}", f"e1{i2}", f"e2{i2}", f"hin{i2}", f"r16{i2}", f"gg{i2}"
                cq, sqq = cq_r[i2], sq_r[i2]
                k1, k2, kb, km, kl_, kq, kc, ks = f"t1{i2}", f"t2{i2}", f"bp{i2}", f"mf{i2}", f"ql{i2}", f"qq{i2}", f"cq{i2}", f"sq{i2}"
                v3 = lambda t_, n_c=n_c: t_[:, 0:n_c].rearrange("p (b j) -> p b j", j=16)
                cosb = tg[:, d, 0:16].unsqueeze(1).to_broadcast([128, n_b, 16])
                sinb = tg[:, d, 16:32].unsqueeze(1).to_broadcast([128, n_b, 16])
                mrow = tg[:, d, 32:48].unsqueeze(1).to_broadcast([128, n_b, 16])
                cos2 = tg[:, d, 48:48 + n_b]
                sin2 = tg[:, d, 80:80 + n_b]
                rho = tg[:, d, 112:113]
                rho16 = tg[:, d, 113:114]
                P.op("dve", lambda e, t1=t1, pX=pX, cosb=cosb, v3=v3: e.tensor_tensor(out=v3(t1), in0=pX.rearrange("p (b j) -> p b j", j=16), in1=cosb, op=ALU.mult),
                     reads=[kX, tk], writes=[k1])
                yield
                P.op("dve", lambda e, t2=t2, pXt=pXt, sinb=sinb, v3=v3: e.tensor_tensor(out=v3(t2), in0=pXt.rearrange("p (b j) -> p b j", j=16), in1=sinb, op=ALU.mult),
                     reads=[kXt, tk], writes=[k2])
                yield
                P.op("dve", lambda e, bp=bp, t1=t1, t2=t2, n_c=n_c: e.tensor_tensor(out=bp[:, 0:n_c], in0=t1[:, 0:n_c], in1=t2[:, 0:n_c], op=ALU.add), reads=[k1, k2], writes=[kb])
                yield
                P.op("act", lambda e, mfu=mfu, mrow=mrow, v3=v3: e.copy(out=v3(mfu), in_=mrow), reads=[tk], writes=[km])
                yield
                P.op("pool", lambda e, rho16=rho16, n_b=n_b: e.tensor_copy(out=r16[:, 0:n_b], in_=rho16.to_broadcast([128, n_b])), reads=[tk], writes=[kr16])
                yield
                rv = (lambda a_, n_c=n_c: a_[:, 0:n_c]) if d == 0 else (lambda a_, n_c=n_c: a_[:, n_c - 1::-1])
                P.op("dve", lambda e, ql=ql, mfu=mfu, bp=bp, rv=rv, n_c=n_c: e.tensor_tensor_scan(out=rv(ql), data0=mfu[:, 0:n_c], data1=rv(bp), initial=0.0, op0=ALU.mult, op1=ALU.add),
                     reads=[km, kb], writes=[kl_])
                yield
                if d == 0:
                    e_src = ql[:, 15:n_c:16]
                    b_inj = bp[:, 0:n_c:16]
                else:
                    e_src = ql[:, n_c - 16::-16]
                    b_inj = bp[:, n_c - 1::-16]
                P.op("dve", lambda e, e_src=e_src, n_b=n_b: e.tensor_copy(out=ec[:, 0:n_b], in_=e_src), reads=[kl_], writes=[kec])
                yield
                P.op("pe", lambda e, n_b=n_b: e.matmul(PS[6][:, d * 64:d * 64 + n_b], lhsT=Jm, rhs=ec[:, 0:n_b], start=True, stop=True), reads=["Jm", kec], writes=["bank6"])
                yield
                P.op("dve", lambda e, cos2=cos2, n_b=n_b: e.tensor_tensor(out=e1[:, 0:n_b], in0=ec[:, 0:n_b], in1=cos2, op=ALU.mult), reads=[kec, tk], writes=[ke1])
                yield
                P.op("dve", lambda e, sin2=sin2, n_b=n_b: e.tensor_tensor(out=e2[:, 0:n_b], in0=PS[6][:, d * 64:d * 64 + n_b], in1=sin2, op=ALU.mult), reads=["bank6", tk], writes=[ke2])
                yield
                P.op("dve", lambda e, n_b=n_b: e.tensor_tensor(out=e1[:, 0:n_b], in0=e1[:, 0:n_b], in1=e2[:, 0:n_b], op=ALU.subtract), reads=[ke1, ke2], writes=[ke1])
                yield
                P.op("dve", lambda e: e.memset(hin[:, 0:1], 0.0), writes=[khin])
                yield
                P.op("dve", lambda e, n_b=n_b: e.tensor_tensor_scan(out=hin[:, 1:n_b + 1], data0=r16[:, 0:n_b], data1=e1[:, 0:n_b], initial=0.0, op0=ALU.mult, op1=ALU.add),
                     reads=[kr16, ke1, khin], writes=[khin])
                yield
                P.op("pe", lambda e, n_b=n_b: e.matmul(PS[7][:, d * 64:d * 64 + n_b], lhsT=Jm, rhs=hin[:, 0:n_b], start=True, stop=True), reads=["Jm", khin], writes=["bank7"])
                yield
                P.op("dve", lambda e, cos2=cos2, n_b=n_b: e.tensor_tensor(out=gg_[:, 0:n_b], in0=hin[:, 0:n_b], in1=cos2, op=ALU.mult), reads=[khin, tk], writes=[kgg])
                yield
                P.op("dve", lambda e, sin2=sin2, n_b=n_b: e.tensor_tensor(out=e2[:, 0:n_b], in0=PS[7][:, d * 64:d * 64 + n_b], in1=sin2, op=ALU.mult), reads=["bank7", tk, ke1], writes=[ke2])
                yield
                P.op("dve", lambda e, n_b=n_b: e.tensor_tensor(out=gg_[:, 0:n_b], in0=gg_[:, 0:n_b], in1=e2[:, 0:n_b], op=ALU.add), reads=[kgg, ke2], writes=[kgg])
                yield
                P.op("dve", lambda e, b_inj=b_inj, rho=rho, n_b=n_b: e.scalar_tensor_tensor(out=b_inj, in0=gg_[:, 0:n_b], scalar=rho, in1=b_inj, op0=ALU.mult, op1=ALU.add),
                     reads=[kgg, tk, kb, kl_], writes=[kb])
                yield
                P.op("dve", lambda e, qq=qq, mfu=mfu, bp=bp, rv=rv, n_c=n_c: e.tensor_tensor_scan(out=rv(qq), data0=mfu[:, 0:n_c], data1=rv(bp), initial=0.0, op0=ALU.mult, op1=ALU.add),
                     reads=[km, kb], writes=[kq])
                yield
                o = 1 if d == 0 else 0
                zc = 0 if d == 0 else n_c
                P.op("pool", lambda e, cq=cq, zc=zc: e.memset(cq[:, zc:zc + 1], 0.0), reads=[kc], writes=[kc])
                yield
                P.op("pool", lambda e, sqq=sqq, zc=zc: e.memset(sqq[:, zc:zc + 1], 0.0), reads=[ks], writes=[ks])
                yield
                P.op("dve", lambda e, cq=cq, qq=qq, cosb=cosb, o=o, n_c=n_c: e.tensor_tensor(out=cq[:, o:o + n_c].rearrange("p (b j) -> p b j", j=16),
                                                                                              in0=qq[:, 0:n_c].rearrange("p (b j) -> p b j", j=16), in1=cosb, op=ALU.mult),
                     reads=[kq, tk, kc], writes=[kc])
                yield
                P.op("dve", lambda e, sqq=sqq, qq=qq, sinb=sinb, o=o, n_c=n_c: e.tensor_tensor(out=sqq[:, o:o + n_c].rearrange("p (b j) -> p b j", j=16),
                                                                                                in0=qq[:, 0:n_c].rearrange("p (b j) -> p b j", j=16), in1=sinb, op=ALU.mult),
                     reads=[kq, tk, ks], writes=[ks])
                yield
                shf = 0 if d == 0 else 1
                P.op("pe", lambda e, py=py, wg=wg, cq=cq, base=base, shf=shf, n_c=n_c: e.matmul(py, lhsT=wg[:, base + 192:base + 320], rhs=cq[:, shf:shf + n_c], start=False, stop=False),
                     reads=[wk, kc], writes=[pyk])
                yield
                P.op("pe", lambda e, py=py, wg=wg, sqq=sqq, base=base, shf=shf, n_c=n_c, d=d: e.matmul(py, lhsT=wg[:, base + 320:base + 448], rhs=sqq[:, shf:shf + n_c], start=False, stop=(d == 1)),
                     reads=[wk, ks], writes=[pyk])
                yield
            for _ in itertools.zip_longest(chain(0, 0), chain(1, 1)):
                pass
            zg = zg_r[g % 2]
            zk = f"zg{g % 2}"
            P.op("act", lambda e, zg=zg, py=py, n_c=n_c: e.activation(out=zg[:, 0:n_c], in_=py, func=AF.Gelu_apprx_tanh), reads=[pyk], writes=[zk])
            P.dma(sq.Z[:, :, g, :].rearrange("t p c -> p t c"), zg[:, 0:n_c].rearrange("p (t c) -> p t c", c=64), reads=[zk], writes=["Zscr"], q="act")
    A.release(m0)
    P.barrier()
    m0 = A.mark()
    wglu = A.bf16(8 * 2048).rearrange("p (k c) -> p k c", k=8)
    P.dma(wglu, C.wglu_s, writes=["wglu"], q="act")
    bgl_f = A.f32(2048, parts=1)
    P.dma(bgl_f, C.s5_b_glu.rearrange("(o n) -> o n", o=1), writes=["bgl_f"])
    bgl = A.bf16(2048, parts=1)
    P.op("dve", lambda e: e.tensor_copy(out=bgl, in_=bgl_f), reads=["bgl_f"], writes=["bgl"])
    ones1 = A.bf16(128, parts=1)
    P.op("pool", lambda e: e.memset(ones1, 1.0), writes=["ones1"])
    lnG = A.f32(D)
    lnB = A.f32(D)
    bc_load(C, lnG, C.ln_g[l, 0], "lnG")
    bc_load(C, lnB, C.ln_b[l, 0], "lnB")
    gm = A.f32(D)
    xt = A.f32(4 * D).rearrange("p (b c) -> p b c", b=4)
    zt = A.bf16(64 * 64).rearrange("p (g c) -> p g c", g=64)
    zc_ = A.bf16(8 * D, parts=64).rearrange("p (t c) -> p t c", t=8)
    zf = A.bf16(8 * 512).rearrange("p (k t) -> p k t", k=8)
    ga = [A.f32(512) for _ in range(2)]
    aa_ = [A.f32(512) for _ in range(2)]
    st = A.f32(12).rearrange("p (a b) -> p a b", a=2)
    mv = A.f32(2)
    rs = A.f32(1)
    nmr = A.f32(1)
    for sq in seqs:
        ci = sq.ci
        bc_load(C, gm, C.modd[l, ci, 2 * D:3 * D], "gm")
        for T in range(sq.n // 512):
            t0 = T * 512
            P.dma(zt, sq.Z[T], writes=["zt"], q="sp")
            P.dma(xt, sq.x0[t0:t0 + 512, :].rearrange("(b p) c -> p b c", p=128), writes=["xt"], q="sp")
            for b in range(4):
                P.op("act", lambda e, b=b: e.activation(out=xt[:, b, :], in_=xt[:, b, :], func=AF.Copy, scale=ALPHA), reads=["xt"], writes=["xt"])
            for g8 in range(8):
                p1 = PS[g8 % 2][0:64, :].bitcast(BF16)
                bk = f"bank{g8 % 2}"
                for gi_ in range(8):
                    g = g8 * 8 + gi_
                    P.op("pe", lambda e, p1=p1, g=g, gi_=gi_: e.transpose(p1[:, gi_ * 128:(gi_ + 1) * 128], zt[:, g, :], C.identb), reads=["zt"], writes=[bk])
                eng = "act" if g8 % 2 == 0 else "dve"
                outv = zc_[:, :, g8 * 128:(g8 + 1) * 128].rearrange("p t (g h) -> p g t h", h=16)
                inv = p1.rearrange("p (g t h) -> p g t h", g=8, t=8)
                if eng == "act":
                    P.op("act", lambda e, outv=outv, inv=inv: e.copy(out=outv, in_=inv), reads=[bk], writes=[f"zc{g8}"])
                else:
                    P.op("dve", lambda e, outv=outv, inv=inv: e.tensor_copy(out=outv, in_=inv), reads=[bk], writes=[f"zc{g8}"])
            for k in range(8):
                p2 = PS[2 + k % 2][:, 0:256].bitcast(BF16)
                bk = f"bank{2 + k % 2}"
                for t in range(8):
                    P.op("pe", lambda e, p2=p2, t=t, k=k: e.transpose(p2[:, t * 64:(t + 1) * 64], zc_[:, t, k * 128:(k + 1) * 128], C.identb[0:64, 0:64]), reads=[f"zc{k}"], writes=[bk])
                outv = zf[:, k, :].rearrange("p (c t) -> p t c", t=8)
                inv = p2.rearrange("p (t c) -> p t c", t=8)
                if k % 2 == 0:
                    P.op("act", lambda e, outv=outv, inv=inv: e.copy(out=outv, in_=inv), reads=[bk], writes=[f"zf{k}"])
                else:
                    P.op("dve", lambda e, outv=outv, inv=inv: e.tensor_copy(out=outv, in_=inv), reads=[bk], writes=[f"zf{k}"])
            zf_keys = [f"zf{k}" for k in range(8)]
            for b in range(4):
                for hh in range(2):
                    pa = PS[4 + hh]
                    pg_ = PS[6 + hh]
                    for k in range(8):
                        P.op("pe", lambda e, pa=pa, k=k, b=b, hh=hh: e.matmul(pa, lhsT=zf[:, k, b * 128:(b + 1) * 128], rhs=wglu[:, k, hh * 512:(hh + 1) * 512], start=(k == 0), stop=False),
                             reads=[zf_keys[k], "wglu"], writes=[f"bank{4 + hh}"])
                    P.op("pe", lambda e, pa=pa, hh=hh: e.matmul(pa, lhsT=ones1, rhs=bgl[:, hh * 512:(hh + 1) * 512], start=False, stop=True), reads=["ones1", "bgl"], writes=[f"bank{4 + hh}"])
                    for k in range(8):
                        P.op("pe", lambda e, pg_=pg_, k=k, b=b, hh=hh: e.matmul(pg_, lhsT=zf[:, k, b * 128:(b + 1) * 128], rhs=wglu[:, k, 1024 + hh * 512:1024 + (hh + 1) * 512], start=(k == 0), stop=False),
                             reads=[zf_keys[k], "wglu"], writes=[f"bank{6 + hh}"])
                    P.op("pe", lambda e, pg_=pg_, hh=hh: e.matmul(pg_, lhsT=ones1, rhs=bgl[:, 1024 + hh * 512:1024 + (hh + 1) * 512], start=False, stop=True), reads=["ones1", "bgl"], writes=[f"bank{6 + hh}"])
                    g_t, a_t = ga[hh], aa_[hh]
                    P.op("act", lambda e, g_t=g_t, pg_=pg_: e.activation(out=g_t, in_=pg_, func=AF.Sigmoid), reads=[f"bank{6 + hh}"], writes=[f"ga{hh}"])
                    P.op("dve", lambda e, a_t=a_t, pa=pa, g_t=g_t: e.tensor_tensor(out=a_t, in0=pa, in1=g_t, op=ALU.mult), reads=[f"bank{4 + hh}", f"ga{hh}"], writes=[f"aa{hh}"])
                    P.op("dve", lambda e, a_t=a_t, hh=hh: e.tensor_tensor(out=a_t, in0=a_t, in1=gm[:, hh * 512:(hh + 1) * 512], op=ALU.mult), reads=[f"aa{hh}", "gm"], writes=[f"aa{hh}"])
                    xs = xt[:, b, hh * 512:(hh + 1) * 512]
                    P.op("dve", lambda e, a_t=a_t, xs=xs: e.tensor_tensor(out=xs, in0=xs, in1=a_t, op=ALU.add), reads=[f"aa{hh}", "xt"], writes=["xt"])
                layer_norm_block(C, xt[:, b, :], "xt", lnG, lnB, st, mv, rs, nmr)
            P.dma(sq.x1[t0:t0 + 512, :].rearrange("(b p) c -> p b c", p=128), xt, reads=["xt"], writes=["x1"], q="sp", is_output=(sq.x1 is sq.y))
    A.release(m0)
    P.barrier()


def na_phase(C, seq_io):
    P, A, nc = C.P, C.A, C.nc
    PS = C.psum
    l = 1
    m0 = A.mark()
    wqkv = A.bf16(8 * 3072).rearrange("p (k c) -> p k c", k=8)
    P.dma(wqkv, C.wqkv_s, writes=["wqkv"], q="act")
    bqk = A.f32(16)
    load_pp(C, C.na_b_qkv[0:2048].rearrange("(m p) -> m p", p=128), bqk, "bqk")
    sc1 = A.f32(D)
    sh = A.f32(D)
    xt_r = [A.f32(4 * D).rearrange("p (b c) -> p b c", b=4) for _ in range(2)]
    tmp = A.f32(D)
    hb_r = [A.bf16(4 * D).rearrange("p (b c) -> p b c", b=4) for _ in range(2)]
    hfm_r = [A.bf16(8 * 512).rearrange("p (k t) -> p k t", k=8) for _ in range(2)]
    qk_st = [A.bf16(512) for _ in range(3)]
    v_st = [A.bf16(1040).rearrange("p (h d) -> p h d", h=16) for _ in range(2)]
    for i in range(2):
        P.op("pool", lambda e, i=i: e.memset(v_st[i][:, :, 64:65], 1.0), writes=[f"vst{i}"])
    pT = PS[0][:, 0:256].bitcast(BF16)
    cnt = 0
    gtile = 0
    for (sq, xin, xout) in seq_io:
        n = sq.n
        ci = sq.ci
        bc_load(C, sh, C.modd[l, ci, 0:D], "sh")
        bc_load(C, sc1, C.modd[l, ci, D:2 * D], "sc1")
        P.op("pool", lambda e: e.tensor_scalar(out=sc1, in0=sc1, scalar1=1.0, scalar2=None, op0=ALU.add), reads=["sc1"], writes=["sc1"])
        g0 = gtile
        gtile += n // 512

        def prologue(T):
            t0 = T * 512
            i2 = (g0 + T) % 2
            xt, hb, hfm = xt_r[i2], hb_r[i2], hfm_r[i2]
            P.dma(xt, xin[t0:t0 + 512, :].rearrange("(b p) c -> p b c", p=128), writes=[f"xt{i2}"], q="sp")
            for b in range(4):
                P.op("dve", lambda e, b=b: e.tensor_tensor(out=tmp, in0=xt[:, b, :], in1=sc1, op=ALU.mult), reads=[f"xt{i2}", "sc1"], writes=["tmp"])
                P.op("dve", lambda e, b=b: e.tensor_tensor(out=hb[:, b, :], in0=tmp, in1=sh, op=ALU.add), reads=["tmp", "sh"], writes=[f"hb{i2}{b}"])
            for k in range(8):
                for b in range(4):
                    P.op("pe", lambda e, b=b, k=k: e.transpose(pT[:, b * 128:(b + 1) * 128], hb[:, b, k * 128:(k + 1) * 128], C.identb),
                         reads=[f"hb{i2}{b}"], writes=["bank0"])
                if k % 2 == 0:
                    P.op("act", lambda e, k=k: e.copy(out=hfm[:, k, :], in_=pT), reads=["bank0"], writes=[f"hfm{i2}{k}"])
                else:
                    P.op("dve", lambda e, k=k: e.tensor_copy(out=hfm[:, k, :], in_=pT), reads=["bank0"], writes=[f"hfm{i2}{k}"])

        prologue(0)
        for T in range(n // 512):
            t0 = T * 512
            i2 = (g0 + T) % 2
            hfm = hfm_r[i2]
            if T + 1 < n // 512:
                prologue(T + 1)
            hf_keys = [f"hfm{i2}{k}" for k in range(8)]
            for mt in range(16):
                j = cnt % 3
                cnt += 1
                pb = PS[1 + j]
                pk = f"bank{1 + j}"
                for k in range(8):
                    P.op("pe", lambda e, pb=pb, k=k, mt=mt: e.matmul(pb, lhsT=wqkv[:, k, mt * 128:(mt + 1) * 128], rhs=hfm[:, k, :], start=(k == 0), stop=(k == 7)),
                         reads=["wqkv", hf_keys[k]], writes=[pk])
                stt = qk_st[j]
                sk = f"qkst{j}"
                P.op("act", lambda e, stt=stt, pb=pb, mt=mt: e.activation(out=stt, in_=pb, func=AF.Identity, bias=bqk[:, mt:mt + 1], scale=1.0),
                     reads=[pk, "bqk"], writes=[sk])
                dst = (sq.qT if mt < 8 else sq.kT)[:, mt % 8, t0:t0 + 512]
                P.dma(dst, stt, reads=[sk], writes=["qkscr"], q="sp")
            for b in range(4):
                vs = v_st[b % 2]
                vk = f"vst{b % 2}"
                for half in range(2):
                    j = cnt % 3
                    cnt += 1
                    pb = PS[1 + j]
                    pk = f"bank{1 + j}"
                    for k in range(8):
                        P.op("pe", lambda e, pb=pb, k=k, b=b, half=half: e.matmul(pb, lhsT=hfm[:, k, b * 128:(b + 1) * 128],
                                                                                   rhs=wqkv[:, k, 2048 + half * 512:2048 + (half + 1) * 512], start=(k == 0), stop=(k == 7)),
                             reads=["wqkv", hf_keys[k]], writes=[pk])
                    P.op("dve", lambda e, vs=vs, pb=pb, half=half: e.tensor_copy(out=vs[:, half * 8:(half + 1) * 8, 0:64], in_=pb.rearrange("p (h d) -> p h d", h=8)),
                         reads=[pk], writes=[vk])
                P.dma(sq.vS[t0 + b * 128:t0 + (b + 1) * 128, :], vs.rearrange("p h d -> p (h d)"), reads=[vk], writes=["vscr"], q="sp")
    A.release(m0)
    P.barrier()
    m0 = A.mark()
    wo = A.bf16(16 * D, parts=64).rearrange("p (h c) -> p h c", h=16)
    P.dma(wo, C.wo_s, writes=["wo"], q="act")
    TAB = A.bf16(16 * 17 * 64).rearrange("p (h x) -> p h x", h=16)
    m1 = A.mark()
    tstg = [A.f32(2176) for _ in range(2)]
    for hh in range(8):
        ts_ = tstg[hh % 2]
        tk = f"tstg{hh % 2}"
        P.dma(ts_, C.na_tab[:, 2 * hh:2 * hh + 2, :].rearrange("p h x -> p (h x)"), writes=[tk], q="sp")
        P.op("act", lambda e, ts_=ts_, hh=hh: e.activation(out=TAB[:, 2 * hh:2 * hh + 2, :].rearrange("p h x -> p (h x)"), in_=ts_, func=AF.Exp),
             reads=[tk], writes=["TAB"])
    P.barrier()
    A.release(m1)
    lnG = A.f32(D)
    lnB = A.f32(D)
    bc_load(C, lnG, C.ln_g[l, 0], "lnG")
    bc_load(C, lnB, C.ln_b[l, 0], "lnB")
    bvT = A.f32(16, parts=64)
    load_pp(C, C.na_b_qkv[2048:3072].rearrange("(h d) -> h d", d=64), bvT, "bvT")
    bvTb = A.bf16(16, parts=64)
    P.op("dve", lambda e: e.tensor_copy(out=bvTb, in_=bvT), reads=["bvT"], writes=["bvTb"])
    borow = A.f32(D, parts=1)
    P.dma(borow, C.na_b_o.rearrange("(o n) -> o n", o=1), writes=["borow"])
    for half in range(2):
        pb = PS[6 + half][0:1, :]
        for h in range(16):
            P.op("pe", lambda e, pb=pb, h=h, half=half: e.matmul(pb, lhsT=bvTb[:, h:h + 1], rhs=wo[:, h, half * 512:(half + 1) * 512], start=(h == 0), stop=(h == 15)),
                 reads=["bvTb", "wo"], writes=[f"bank{6 + half}"])
        P.op("dve", lambda e, pb=pb, half=half: e.tensor_tensor(out=borow[:, half * 512:(half + 1) * 512], in0=pb, in1=borow[:, half * 512:(half + 1) * 512], op=ALU.add),
             reads=[f"bank{6 + half}", "borow"], writes=["borow"])
    P.dma(C.bo2.rearrange("(o n) -> o n", o=1), borow, reads=["borow"], writes=["bo2"])
    bo = A.f32(D)
    bc_load_dep(C, bo, C.bo2, "bo", ["bo2"])
    gm = A.f32(D)
    gmb = A.f32(D)
    ones_r = A.f32(64, parts=65)
    P.op("pool", lambda e: e.memset(ones_r[64:65, :], 1.0), writes=["ones_r"])
    xt = A.f32(4 * D).rearrange("p (b c) -> p b c", b=4)
    kw = A.bf16(8 * 1536).rearrange("p (k t) -> p k t", k=8)
    qw = A.bf16(8 * 512).rearrange("p (k t) -> p k t", k=8)
    vw = A.bf16(12 * 1040).rearrange("p (b h d) -> p b h d", b=12, h=16)
    es_ring = [A.f32(512) for _ in range(3)]
    pt_ring = [A.bf16(512) for _ in range(4)]
    es4 = [A.f32(128) for _ in range(3)]
    pt4 = [A.bf16(128) for _ in range(3)]
    oT = A.bf16(16 * 512, parts=64).rearrange("p (h t) -> p h t", h=16)
    sums = A.f32(512, parts=65)
    rcp = A.f32(512, parts=64)
    rt = [A.f32(512) for _ in range(2)]
    st = A.f32(12).rearrange("p (a b) -> p a b", a=2)
    mv = A.f32(2)
    rs = A.f32(1)
    nmr = A.f32(1)
    c_es = c_pt = c_s4 = c_h = 0
    for (sq, xin, xout) in seq_io:
        n = sq.n
        ci = sq.ci
        R = n // 64
        nt = n // 512
        bc_load(C, gm, C.modd[l, ci, 2 * D:3 * D], "gm")
        P.op("pool", lambda e: e.tensor_tensor(out=gmb, in0=gm, in1=bo, op=ALU.mult), reads=["gm", "bo"], writes=["gmb"])
        for T in range(nt):
            t0 = T * 512
            def kv_loads(T_):
                t0_ = T_ * 512
                wt0 = t0_ - 512
                lo = max(wt0, 0)
                hi = min(t0_ + 1024, n)
                P.dma(kw[:, :, lo - wt0:hi - wt0], sq.kT[:, :, lo:hi], writes=["kw"], q="sp")
                P.dma(qw, sq.qT[:, :, t0_:t0_ + 512], writes=["qw"], q="sp")
                P.dma(vw[:, (lo - wt0) // 128:(hi - wt0) // 128, :, :].rearrange("p b h d -> p b (h d)"),
                      sq.vS[lo:hi, :].rearrange("(b p) c -> p b c", p=128), writes=["vw"], q="sp")

            if T == 0:
                kv_loads(0)
            wr0 = 8 * T - 8
            LA = 2
            units = [(h, qb) for h in range(16) for qb in range(4)]
            pend = {}

            def valid(kr_, r_):
                rs_ = min(max(r_ - 4, 0), R - 8)
                return kr_ < R and rs_ <= kr_ < rs_ + 8

            def issue_scores(ui):
                nonlocal c_es, c_s4
                h, qb = units[ui]
                hp, pbase = h // 2, (h % 2) * 64
                r = 8 * T + 2 * qb
                kr0 = min(max(r - 4, 0), R - 10)
                sb_i = c_es % 3
                ps_s = PS[sb_i]
                psk = f"bank{sb_i}"
                esi = c_es % 3
                c_es += 1
                for j in range(4):
                    kc0 = (kr0 + 2 * j - wr0) * 64
                    P.op("pe", lambda e, ps_s=ps_s, j=j, kc0=kc0, qb=qb, hp=hp, pbase=pbase: e.matmul(
                        ps_s[:, (3 - j) * 128:(4 - j) * 128], lhsT=kw[pbase:pbase + 64, hp, kc0:kc0 + 128],
                        rhs=qw[pbase:pbase + 64, hp, qb * 128:(qb + 1) * 128], start=True, stop=True),
                        reads=["kw", "qw"], writes=[psk])
                pend[ui] = [sb_i, esi, None]

            def issue_j4(ui):
                nonlocal c_s4
                h, qb = units[ui]
                hp, pbase = h // 2, (h % 2) * 64
                r = 8 * T + 2 * qb
                kr0 = min(max(r - 4, 0), R - 10)
                kr4 = kr0 + 8
                nv4 = sum(1 for kl in range(2) for rl in range(2) if valid(kr4 + kl, r + rl))
                if nv4:
                    i4 = c_s4 % 2
                    c_s4 += 1
                    b4 = (3, 6)[i4]
                    ps4 = PS[b4][:, 0:128]
                    kc0 = (kr4 - wr0) * 64
                    P.op("pe", lambda e, ps4=ps4, kc0=kc0, qb=qb, hp=hp, pbase=pbase: e.matmul(
                        ps4, lhsT=kw[pbase:pbase + 64, hp, kc0:kc0 + 128], rhs=qw[pbase:pbase + 64, hp, qb * 128:(qb + 1) * 128],
                        start=True, stop=True), reads=["kw", "qw"], writes=[f"bank{b4}"])
                    pend[ui][2] = i4

            def finish_unit(ui, po, pok):
                nonlocal c_pt
                h, qb = units[ui]
                sb_i, esi, i4 = pend.pop(ui)
                r = 8 * T + 2 * qb
                kr0 = min(max(r - 4, 0), R - 10)
                x03 = 1 + r - kr0
                ps_s = PS[sb_i]
                psk = f"bank{sb_i}"
                es = es_ring[esi]
                esk = f"es{esi}"
                P.op("act", lambda e, es=es, ps_s=ps_s: e.activation(out=es, in_=ps_s, func=AF.Exp, scale=0.125), reads=[psk], writes=[esk])
                pt = pt_ring[c_pt % 4]
                ptk = f"pt{c_pt % 4}"
                c_pt += 1
                P.op("dve", lambda e, pt=pt, es=es, h=h, x03=x03: e.tensor_tensor(out=pt, in0=es, in1=TAB[:, h, x03 * 64:(x03 + 8) * 64], op=ALU.mult),
                     reads=[esk, "TAB"], writes=[ptk])
                tiles = []
                zkeys = []
                for j in range(4):
                    kr = kr0 + 2 * j
                    nv = 0
                    for kl in range(2):
                        for rl in range(2):
                            if valid(kr + kl, r + rl):
                                nv += 1
                            else:
                                zk_ = ptk + f"z{len(zkeys)}"
                                zkeys.append(zk_)
                                P.op("pool", lambda e, pt=pt, j=j, kl=kl, rl=rl: e.memset(
                                    pt[kl * 64:(kl + 1) * 64, (3 - j) * 128 + rl * 64:(3 - j) * 128 + (rl + 1) * 64], 0.0),
                                    reads=[ptk], writes=[zk_])
                    if nv:
                        tiles.append((pt[:, (3 - j) * 128:(4 - j) * 128], [ptk] + zkeys, (kr - wr0) // 2))
                tiles = [(a_, [ptk] + zkeys, c_) for (a_, _, c_) in tiles]
                if i4 is not None:
                    kr = kr0 + 8
                    b4 = (3, 6)[i4]
                    ps4 = PS[b4][:, 0:128]
                    i5 = i4
                    e4, p4 = es4[i5], pt4[i5]
                    P.op("act", lambda e, e4=e4, ps4=ps4: e.activation(out=e4, in_=ps4, func=AF.Exp, scale=0.125), reads=[f"bank{b4}"], writes=[f"es4{i5}"])
                    P.op("dve", lambda e, e4=e4, p4=p4, h=h, x03=x03: e.tensor_tensor(out=p4, in0=e4, in1=TAB[:, h, (x03 - 2) * 64:x03 * 64], op=ALU.mult),
                         reads=[f"es4{i5}", "TAB"], writes=[f"pt4{i5}"])
                    z4 = []
                    for kl in range(2):
                        for rl in range(2):
                            if not valid(kr + kl, r + rl):
                                zk_ = f"pt4{i5}z{len(z4)}"
                                z4.append(zk_)
                                P.op("pool", lambda e, p4=p4, kl=kl, rl=rl: e.memset(p4[kl * 64:(kl + 1) * 64, rl * 64:(rl + 1) * 64], 0.0),
                                     reads=[f"pt4{i5}"], writes=[zk_])
                    tiles.append((p4, [f"pt4{i5}"] + z4, (kr - wr0) // 2))
                for ti, (rhs_t, rk_, blk) in enumerate(tiles):
                    P.op("pe", lambda e, po=po, rhs_t=rhs_t, blk=blk, h=h, qb=qb, ti=ti, nti=len(tiles): e.matmul(
                        po[0:65, qb * 128:(qb + 1) * 128], lhsT=vw[:, blk, h, 0:65], rhs=rhs_t, start=(ti == 0), stop=(ti == nti - 1)),
                        reads=rk_ + ["vw"], writes=[pok])

            def normalise(h, po, pok):
                P.op("act", lambda e, po=po: e.activation(out=sums[64:65, :], in_=po[64:65, :], func=AF.Ln), reads=[pok], writes=["sums"])
                P.op("act", lambda e: e.activation(out=sums[64:65, :], in_=sums[64:65, :], func=AF.Exp, scale=-1.0), reads=["sums"], writes=["sums"])
                P.op("pe", lambda e: e.matmul(PS[7][0:64, :], lhsT=ones_r[64:65, :], rhs=sums[64:65, :], start=True, stop=True),
                     reads=["sums", "ones_r"], writes=["bank7"])
                P.op("act", lambda e: e.copy(out=rcp, in_=PS[7][0:64, :]), reads=["bank7"], writes=["rcp"])
                P.op("dve", lambda e, po=po, h=h: e.tensor_tensor(out=oT[:, h, :], in0=po[0:64, :], in1=rcp, op=ALU.mult), reads=[pok, "rcp"], writes=[f"oT{h}"])

            for ui in range(min(LA, len(units))):
                issue_scores(ui)
            issue_j4(0)
            deferred = None
            for ui, (h, qb) in enumerate(units):
                if qb == 0:
                    po = PS[4 + c_h % 2]
                    pok = f"bank{4 + c_h % 2}"
                    c_h += 1
                if ui + LA < len(units):
                    issue_scores(ui + LA)
                if ui + 1 < len(units):
                    issue_j4(ui + 1)
                finish_unit(ui, po, pok)
                if qb == 1 and deferred is not None:
                    normalise(*deferred)
                    deferred = None
                if qb == 3:
                    deferred = (h, po, pok)
            normalise(*deferred)
            if T + 1 < nt:
                kv_loads(T + 1)
            P.dma(xt, xin[t0:t0 + 512, :].rearrange("(b p) c -> p b c", p=128), writes=["xt"], q="sp")
            for b in range(4):
                P.op("dve", lambda e, b=b: e.scalar_tensor_tensor(out=xt[:, b, :], in0=xt[:, b, :], scalar=ALPHA, in1=gmb, op0=ALU.mult, op1=ALU.add), reads=["xt", "gmb"], writes=["xt"])
            oT_keys = [f"oT{h}" for h in range(16)]
            for b in range(4):
                for half in range(2):
                    j = (b * 2 + half) % 2
                    pw = PS[6 + j]
                    pwk = f"bank{6 + j}"
                    for h in range(16):
                        P.op("pe", lambda e, pw=pw, h=h, b=b, half=half: e.matmul(pw, lhsT=oT[:, h, b * 128:(b + 1) * 128], rhs=wo[:, h, half * 512:(half + 1) * 512],
                                                                                  start=(h == 0), stop=(h == 15)), reads=[oT_keys[h], "wo"], writes=[pwk])
                    r_ = rt[j]
                    rk = f"rt{j}"
                    xs = xt[:, b, half * 512:(half + 1) * 512]
                    P.op("dve", lambda e, pw=pw, r_=r_, half=half: e.tensor_tensor(out=r_, in0=pw, in1=gm[:, half * 512:(half + 1) * 512], op=ALU.mult),
                         reads=[pwk, "gm"], writes=[rk])
                    P.op("dve", lambda e, r_=r_, xs=xs: e.tensor_tensor(out=xs, in0=xs, in1=r_, op=ALU.add), reads=[rk, "xt"], writes=["xt"])
                layer_norm_block(C, xt[:, b, :], "xt", lnG, lnB, st, mv, rs, nmr)
            P.dma(xout[t0:t0 + 512, :].rearrange("(b p) c -> p b c", p=128), xt, reads=["xt"], writes=["xout"], q="sp", is_output=(xout is sq.y))
    A.release(m0)
    P.barrier()


def bc_load_dep(C, dst, src_row, key, reads):
    parts = dst.shape[0]
    C.P.dma(dst, src_row.rearrange("(o n) -> o n", o=1).partition_broadcast(parts), reads=reads, writes=[key])


def rpb_layout(rpb):
    rpb = np.asarray(rpb, np.float32)
    kc = np.arange(64)[:, None]
    qc = np.arange(64)[None, :]
    cs_ = np.clip(qc - 8, 0, 48)
    valid = (kc >= cs_) & (kc < cs_ + 16)
    idx = np.clip(kc - qc + 15, 0, 30)
    out = rpb[:, :, idx]
    out = np.where(valid[None, None], out, np.float32(-30000.0)).astype(np.float32)
    return np.ascontiguousarray(out)


def tab_layout(rpb):
    t = rpb_layout(rpb)
    out = np.full((2, 64, 16, 17, 64), -30000.0, np.float32)
    for kl in range(2):
        for x in range(17):
            dr = kl + 7 - x
            if -7 <= dr <= 7:
                out[kl, :, :, x, :] = np.transpose(t[:, dr + 7, :, :], (1, 0, 2))
    return np.ascontiguousarray(out.reshape(128, 16, 17 * 64))


def s5_masks_const():
    tau = np.arange(128)[:, None] // 16
    t = np.arange(128)[None, :] // 16
    return np.ascontiguousarray(np.stack([(t >= tau), (tau >= t)]).astype(np.float32))


def shared_inputs(inp):
    f = lambda a: np.ascontiguousarray(np.asarray(a, np.float32))
    return {
        "s5_masks": s5_masks_const(),
        "w_ada": f(inp["w_ada"]), "b_ada": f(inp["b_ada"]), "ln_g": f(inp["ln_g"]), "ln_b": f(inp["ln_b"]),
        "s5_lam_re": f(inp["s5_lam_re"][0]), "s5_lam_im": f(inp["s5_lam_im"][0]), "s5_log_dt": f(inp["s5_log_dt"][0]),
        "s5_b_re": f(inp["s5_b_re"][0]), "s5_b_im": f(inp["s5_b_im"][0]), "s5_c_re": f(inp["s5_c_re"][0]), "s5_c_im": f(inp["s5_c_im"][0]),
        "s5_d": f(inp["s5_d"][0]), "s5_w_glu": f(inp["s5_w_glu"][0]), "s5_b_glu": f(inp["s5_b_glu"][0]),
        "na_w_qkv": f(inp["na_w_qkv"][0]), "na_b_qkv": f(inp["na_b_qkv"][0]), "na_tab": tab_layout(inp["na_rpb"][0]),
        "na_w_o": f(inp["na_w_o"][0]), "na_b_o": f(inp["na_b_o"][0]),
        "ffn_w_in": f(inp["ffn_w_in"]), "ffn_b_in": f(inp["ffn_b_in"]), "ffn_conv_w": f(inp["ffn_conv_w"]),
        "ffn_conv_b": f(inp["ffn_conv_b"]), "ffn_w_out": f(inp["ffn_w_out"]), "ffn_b_out": f(inp["ffn_b_out"]),
    }


_CACHE = {}


def prompt_window(i):
    r0 = min(max(32 * i - 8, 0), 256 - 48)
    return r0 * 64


def kernel(**inputs):
    inp = {k: np.asarray(v) for k, v in inputs.items()}
    cfg = dict(ns=NS_FULL, np=NP_FULL)
    if "nc" not in _CACHE:
        _CACHE["nc"] = build(cfg)
    nc, C = _CACHE["nc"]
    shared = shared_inputs(inp)
    xs = np.asarray(inp["x_sample"], np.float32)
    xp = np.asarray(inp["x_prompt"], np.float32)[0]
    in_maps = []
    for i in range(8):
        w0 = prompt_window(i)
        m = dict(shared)
        m["x_s"] = np.ascontiguousarray(xs[i])
        m["x_p"] = np.ascontiguousarray(xp[w0:w0 + NP_FULL])
        m["cs"] = np.ascontiguousarray(np.stack([inp["c_sample"][i], inp["c_prompt"][0]]).astype(np.float32))
        in_maps.append(m)
    res = run_bass_kernel_spmd(nc, in_maps, core_ids=list(range(8)))
    y_s = np.stack([np.asarray(res.results[i]["y_s"], np.float32) for i in range(8)])
    y_p = np.zeros((1, 16384, D), np.float32)
    for i in range(8):
        w0 = prompt_window(i)
        off = 2048 * i - w0
        y_p[0, 2048 * i:2048 * (i + 1)] = np.asarray(res.results[i]["y_p"], np.float32)[off:off + 2048]
    return (y_p, y_s)
```

```python
import math
import itertools
import types
import numpy as np
from contextlib import ExitStack
import concourse.bass as bass
import concourse.mybir as mybir
from concourse.bass_utils import run_bass_kernel_spmd

F32 = mybir.dt.float32
BF16 = mybir.dt.bfloat16
AF = mybir.ActivationFunctionType
ALU = mybir.AluOpType

ENGS = ("pe", "dve", "act", "pool", "sp")
D = 1024
DFF = 2816
ALPHA = 4.0 ** 0.25
LN_EPS = 1e-5
NS_FULL = 4096
NP_FULL = 3072


def _freeze(fn):
    if fn is None or fn.__closure__ is None:
        return fn
    cells = tuple(types.CellType(c.cell_contents) for c in fn.__closure__)
    return types.FunctionType(fn.__code__, fn.__globals__, fn.__name__, fn.__defaults__, cells)


class Op:
    __slots__ = ("eng", "fn", "deps", "is_dma", "idx", "needed", "semval", "dsem", "dval", "waits", "prewait")

    def __init__(self, eng, fn, is_dma):
        self.eng = eng
        self.fn = fn
        self.is_dma = is_dma
        self.deps = set()
        self.needed = False
        self.semval = None
        self.dsem = None
        self.dval = None
        self.waits = []
        self.prewait = None


class Prog:
    def __init__(self, nc, es, n_dma_sems=14):
        self.nc = nc
        self.es = es
        self.ops = {e: [] for e in ENGS}
        self.last_w = {}
        self.readers = {}
        self.K = n_dma_sems
        self.out_dmas = []
        self.last_real = {e: None for e in ENGS}
        self.dma_hist = {e: [] for e in ENGS}

    def _track(self, o, reads, writes):
        deps = o.deps
        for k in reads:
            w = self.last_w.get(k)
            if w is not None:
                deps.add(w)
        for k in writes:
            w = self.last_w.get(k)
            if w is not None:
                deps.add(w)
            for r in self.readers.get(k, ()):
                deps.add(r)
        deps.discard(o)
        for k in writes:
            self.last_w[k] = o
            self.readers[k] = []
        for k in reads:
            self.readers.setdefault(k, []).append(o)

    dead = False
    paranoid = False

    def op(self, eng, fn, reads=(), writes=()):
        if self.dead:
            return None
        o = Op(eng, _freeze(fn), False)
        o.idx = len(self.ops[eng])
        self._track(o, reads, writes)
        if self.paranoid:
            for e2 in ENGS:
                if self.last_real[e2] is not None:
                    o.deps.add(self.last_real[e2])
                for d in self.dma_hist[e2][-self.K:]:
                    o.deps.add(d)
        self.ops[eng].append(o)
        self.last_real[eng] = o
        return o

    def dma(self, out, in_, reads=(), writes=(), q="sp", is_output=False, **kw):
        if self.dead:
            return None

        def fn(e):
            return e.dma_start(out=out, in_=in_, **kw)
        o = Op(q, fn, True)
        o.idx = len(self.ops[q])
        self._track(o, reads, writes)
        if self.paranoid:
            for e2 in ENGS:
                if self.last_real[e2] is not None:
                    o.deps.add(self.last_real[e2])
                for d in self.dma_hist[e2][-self.K:]:
                    o.deps.add(d)
        self.ops[q].append(o)
        self.dma_hist[q].append(o)
        if is_output:
            self.out_dmas.append(o)
        return o

    def barrier(self):
        if self.dead:
            return
        b = Op("sp", lambda e: e.nop(), False)
        b.idx = len(self.ops["sp"])
        for e in ENGS:
            if self.last_real[e] is not None:
                b.deps.add(self.last_real[e])
            for d in self.dma_hist[e][-self.K:]:
                b.deps.add(d)
        self.ops["sp"].append(b)
        self.last_real["sp"] = b
        for e in ENGS:
            if e == "sp":
                continue
            o = Op(e, None, False)
            o.idx = len(self.ops[e])
            o.deps.add(b)
            self.ops[e].append(o)
        self.last_w.clear()
        self.readers.clear()

    def emit(self):
        nc = self.nc
        es = self.es
        esem = {e: es.enter_context(nc.semaphore(f"s_{e}")) for e in ENGS}
        dsems = {e: [es.enter_context(nc.semaphore(f"d_{e}{i}")) for i in range(self.K)] for e in ("sp", "act", "pool")}
        for e in ENGS:
            n = 0
            for o in self.ops[e]:
                if o.is_dma:
                    o.dsem = dsems[e][n % self.K]
                    o.dval = 16 * (n // self.K + 1)
                    if n >= self.K:
                        o.prewait = (o.dsem, o.dval - 16)
                    n += 1
        fin = Op("sp", None, False)
        fin.idx = len(self.ops["sp"])
        fin.deps = set(self.out_dmas)
        self.ops["sp"].append(fin)
        for f in ENGS:
            waited = {e: -1 for e in ENGS}
            dwaited = set()
            for o in self.ops[f]:
                best = {}
                for d in o.deps:
                    if d.is_dma:
                        if id(d) in dwaited:
                            continue
                        dwaited.add(id(d))
                        o.waits.append(("dma", d))
                    else:
                        if d.eng == "pe" and f == "pe":
                            continue
                        if d.idx <= waited[d.eng]:
                            continue
                        if d.eng not in best or best[d.eng].idx < d.idx:
                            best[d.eng] = d
                for en, d in best.items():
                    waited[en] = d.idx
                    d.needed = True
                    o.waits.append(("eng", d))
        for e in ENGS:
            c = 0
            for o in self.ops[e]:
                if o.needed and not o.is_dma:
                    c += 1
                    o.semval = c
        self.stats = {e: len(self.ops[e]) for e in ENGS}

        def run(ename, eng):
            for o in self.ops[ename]:
                if o.prewait is not None:
                    eng.wait_ge(o.prewait[0], o.prewait[1])
                for kind, d in o.waits:
                    if kind == "dma":
                        eng.wait_ge(d.dsem, d.dval)
                    else:
                        eng.wait_ge(esem[d.eng], d.semval)
                if o.fn is None:
                    continue
                ins = o.fn(eng)
                if o.is_dma:
                    ins.then_inc(o.dsem, 16)
                elif o.needed:
                    ins.then_inc(esem[ename], 1)

        with nc.Block() as block:
            @block.sync
            def _(e):
                run("sp", e)

            @block.tensor
            def _(e):
                run("pe", e)

            @block.vector
            def _(e):
                run("dve", e)

            @block.scalar
            def _(e):
                run("act", e)

            @block.gpsimd
            def _(e):
                run("pool", e)


class Arena:
    def __init__(self, nc, es, words):
        self.t = es.enter_context(nc.sbuf_tensor("arena", [128, words], F32))[:, :]
        self.words = words
        self.off = 0
        self.uid = 0

    def mark(self):
        return self.off

    def release(self, m):
        self.off = m

    def f32(self, n, parts=128):
        assert self.off + n <= self.words, ("arena overflow", self.off, n, self.words)
        v = self.t[0:parts, self.off:self.off + n]
        self.off += n
        return v

    def bf16(self, n, parts=128):
        w = (n + 1) // 2
        v = self.f32(w, parts).bitcast(BF16)
        return v[:, 0:n]

    def key(self, base):
        self.uid += 1
        return f"{base}#{self.uid}"


class Ctx:
    pass


def build(cfg):
    ns, npw = cfg["ns"], cfg["np"]
    phases = cfg.get("phases", ("setup", "s5", "ffn0", "na", "ffn1"))
    cfg = dict(cfg)
    cfg["phases"] = phases
    nc = bass.Bass("TRN2", target_bir_lowering=False)
    C = Ctx()
    C.nc = nc
    C.cfg = cfg

    def din(name, shape, dt=F32):
        return nc.dram_tensor(name, list(shape), dt, kind="ExternalInput").ap()

    def dout(name, shape, dt=F32):
        return nc.dram_tensor(name, list(shape), dt, kind="ExternalOutput").ap()

    def dscr(name, shape, dt=F32):
        kind = "ExternalOutput" if cfg.get("debug") and name in cfg.get("taps", ()) else "Internal"
        return nc.dram_tensor(name, list(shape), dt, kind=kind).ap()

    seqs = []
    for nm, n in (("s", ns), ("p", npw)):
        if n == 0:
            continue
        s = Ctx()
        s.name = nm
        s.n = n
        s.ci = 0 if nm == "s" else 1
        s.x0 = din(f"x_{nm}", [n, D])
        s.y = dout(f"y_{nm}", [n, D])
        s.x1 = dscr(f"x1_{nm}", [n, D])
        s.x2 = dscr(f"x2_{nm}", [n, D])
        s.x3 = dscr(f"x3_{nm}", [n, D])
        s.qT = dscr(f"qT_{nm}", [128, 8, n], BF16)
        s.kT = dscr(f"kT_{nm}", [128, 8, n], BF16)
        s.vS = dscr(f"vS_{nm}", [n, 1040], BF16)
        s.U = dscr(f"U_{nm}", [n // 1024, 128, 64, 128], BF16)
        s.Z = dscr(f"Z_{nm}", [n // 512, 128, 64, 64], BF16)
        seqs.append(s)
    C.seqs = seqs
    C.cs = din("cs", [2, D])
    C.w_ada = din("w_ada", [2, D, 6 * D])
    C.b_ada = din("b_ada", [2, 6 * D])
    C.ln_g = din("ln_g", [2, 2, D])
    C.ln_b = din("ln_b", [2, 2, D])
    C.s5_lam_re = din("s5_lam_re", [2, 64, 64])
    C.s5_lam_im = din("s5_lam_im", [2, 64, 64])
    C.s5_log_dt = din("s5_log_dt", [2, 64])
    C.s5_b_re = din("s5_b_re", [2, 64, 64, 16])
    C.s5_b_im = din("s5_b_im", [2, 64, 64, 16])
    C.s5_c_re = din("s5_c_re", [2, 64, 16, 64])
    C.s5_c_im = din("s5_c_im", [2, 64, 16, 64])
    C.s5_d = din("s5_d", [64, 16])
    C.s5_w_glu = din("s5_w_glu", [D, 2 * D])
    C.s5_b_glu = din("s5_b_glu", [2 * D])
    C.na_w_qkv = din("na_w_qkv", [D, 3 * D])
    C.na_b_qkv = din("na_b_qkv", [3 * D])
    C.na_tab = din("na_tab", [128, 16, 1088])
    C.s5_masks = din("s5_masks", [2, 128, 128])
    C.na_w_o = din("na_w_o", [D, D])
    C.na_b_o = din("na_b_o", [D])
    C.ffn_w_in = din("ffn_w_in", [2, D, 2 * DFF])
    C.ffn_b_in = din("ffn_b_in", [2, 2 * DFF])
    C.ffn_conv_w = din("ffn_conv_w", [2, 3, DFF])
    C.ffn_conv_b = din("ffn_conv_b", [2, DFF])
    C.ffn_w_out = din("ffn_w_out", [2, DFF, D])
    C.ffn_b_out = din("ffn_b_out", [2, D])
    C.modd = dscr("modd", [2, 2, 6 * D])
    C.wglu_s = dscr("wglu_s", [128, 8, 2 * D], BF16)
    C.wqkv_s = dscr("wqkv_s", [128, 8, 3 * D], BF16)
    C.wo_s = dscr("wo_s", [64, 16, D], BF16)
    C.win_s = dscr("win_s", [2, 11, 128, 8, 512], BF16)
    C.wout_s = dscr("wout_s", [2, 128, 22, D], BF16)
    C.bo2 = dscr("bo2", [D])
    C.s5w = dscr("s5w", [64, 128, 1024], BF16)
    C.s5t = dscr("s5t", [64, 128, 2, 128])
    C.s5tp = dscr("s5tp", [2, 64, 128, 128])

    with ExitStack() as es:
        P = Prog(nc, es, n_dma_sems=cfg.get('ksem', 14))
        P.paranoid = bool(cfg.get('paranoid'))
        C.P = P
        A = Arena(nc, es, cfg.get("arena_words", 53184))
        C.A = A
        C.psum = [es.enter_context(nc.psum_tensor(f"ps{i}", [128, 512], F32))[:, :] for i in range(8)]
        C.ident = A.f32(128)
        C.identb = A.bf16(128)
        C.eps = A.f32(1)
        P.op("pool", lambda e: e.memset(C.ident, 0.0), writes=["ident"])
        P.op("pool", lambda e: e.affine_select(out=C.ident, in_=C.ident, pattern=[[-1, 128]], compare_op=ALU.not_equal,
                                               fill=1.0, base=0, channel_multiplier=1), reads=["ident"], writes=["ident"])
        P.op("pool", lambda e: e.tensor_copy(out=C.identb, in_=C.ident), reads=["ident"], writes=["identb"])
        P.op("pool", lambda e: e.memset(C.eps, LN_EPS), writes=["eps"])
        P.barrier()
        if "setup" in phases:
            setup_phase(C)
        if "s5" in phases:
            if len(phases) == 2:
                for s_ in seqs:
                    s_.x1 = s_.y
            s5_phase(C)
        if "ffn0" in phases:
            ffn_phase(C, 0, [(s, (s.x1 if "s5" in phases else s.x0), (s.x2 if ("na" in phases or "ffn1" in phases) else s.y)) for s in seqs])
        if "na" in phases:
            na_phase(C, [(s, (s.x2 if "ffn0" in phases else s.x0), (s.x3 if "ffn1" in phases else s.y)) for s in seqs])
        if "ffn1" in phases:
            ffn_phase(C, 1, [(s, (s.x3 if "na" in phases else s.x0), s.y) for s in seqs])
        P.emit()
        C.stats = P.stats
    return nc, C


def cut(C, label):
    if C.cfg.get("cut") == label:
        C.P.barrier()
        C.P.dead = True


def load_pp(C, src, dst, key):
    P, A = C.P, C.A
    R, W = src.shape[0], src.shape[1]
    stg = A.f32(W, parts=R)
    sk = A.key("ppstg")
    P.dma(stg, src, writes=[sk])
    ps = C.psum[7][0:W, 0:R]
    P.op("pe", lambda e: e.transpose(ps, stg, C.ident[0:R, 0:R]), reads=[sk, "ident"], writes=["bank7"])
    P.op("dve", lambda e: e.tensor_copy(out=dst, in_=ps), reads=["bank7"], writes=[key])


def bc_load(C, dst, src_row, key, q="sp"):
    n = dst.shape[1]
    parts = dst.shape[0]
    C.P.dma(dst, src_row.rearrange("(o n) -> o n", o=1).partition_broadcast(parts), writes=[key], q=q)


def setup_phase(C):
    P, A, nc = C.P, C.A, C.nc
    m0 = A.mark()
    cT = A.f32(16).rearrange("p (k b) -> p k b", b=2)
    crow = A.f32(D, parts=2)
    P.dma(crow, C.cs, writes=["crow"])
    P.op("act", lambda e: e.activation(out=crow, in_=crow, func=AF.Silu), reads=["crow"], writes=["crow"])
    pct = C.psum[2][:, 0:16]
    for k in range(8):
        P.op("pe", lambda e, k=k: e.transpose(pct[:, 2 * k:2 * k + 2], crow[0:2, k * 128:(k + 1) * 128], C.ident[0:2, 0:2]),
             reads=["crow", "ident"], writes=["pct"])
    P.op("dve", lambda e: e.tensor_copy(out=cT, in_=pct.rearrange("p (k b) -> p k b", b=2)), reads=["pct"], writes=["cT"])
    wa = [A.f32(8 * 512).rearrange("p (k c) -> p k c", k=8) for _ in range(2)]
    bad = A.f32(6 * D, parts=2)
    modsb = A.f32(6 * D, parts=2)
    it = 0
    for l in range(2):
        P.dma(bad, C.b_ada[l].rearrange("(o n) -> o n", o=1).partition_broadcast(2), writes=["bad"])
        for cb in range(12):
            w = wa[it % 2]
            wk = f"wa{it % 2}"
            P.dma(w, C.w_ada[l][:, cb * 512:(cb + 1) * 512].rearrange("(k p) c -> p k c", p=128), writes=[wk],
                  q=("sp" if it % 2 == 0 else "act"))
            ps = C.psum[it % 2][0:2, :]
            pk = f"ps{it % 2}"
            for k in range(8):
                P.op("pe", lambda e, ps=ps, w=w, k=k: e.matmul(ps, lhsT=cT[:, k, :], rhs=w[:, k, :], start=(k == 0), stop=(k == 7)),
                     reads=[wk, "cT"], writes=[pk])
            P.op("dve", lambda e, ps=ps, cb=cb: e.tensor_tensor(out=modsb[:, cb * 512:(cb + 1) * 512], in0=ps,
                                                                in1=bad[:, cb * 512:(cb + 1) * 512], op=ALU.add),
                 reads=[pk, "bad"], writes=["modsb"])
            it += 1
        P.dma(C.modd[l], modsb, reads=["modsb"], writes=["modd"])
    A.release(m0)
    P.barrier()
    m0c = A.mark()
    CW = 5632
    stg = [A.f32(CW)] * 2
    stb = [A.bf16(CW)] * 2
    cnt = [0]
    cast_eng = ("act", "dve", "act")

    def cast_rows(src, ncols, stores):
        i = cnt[0]
        cnt[0] += 1
        s_f, s_b = stg[i % 2], stb[i % 2]
        kf, kb = "stg0", "stb0"
        P.dma(s_f[:, 0:ncols], src, writes=[kf], q=("sp" if i % 2 == 0 else "act"))
        ce = cast_eng[i % 3]
        if ce == "act":
            P.op("act", lambda e: e.copy(out=s_b[:, 0:ncols], in_=s_f[:, 0:ncols]), reads=[kf], writes=[kb])
        else:
            P.op(ce, lambda e: e.tensor_copy(out=s_b[:, 0:ncols], in_=s_f[:, 0:ncols]), reads=[kf], writes=[kb])
        for dst, (c0, c1), (p0, p1) in stores:
            P.dma(dst, s_b[p0:p1, c0:c1], reads=[kb], writes=["wscr"], q="sp")

    ph = C.cfg.get("phases")

    def casts():
        if "s5" in ph:
            for k in range(8):
                cast_rows(C.s5_w_glu[k * 128:(k + 1) * 128, :], 2048, [(C.wglu_s[:, k, :], (0, 2048), (0, 128))])
                yield
        if "na" in ph:
            for k in range(8):
                cast_rows(C.na_w_qkv[k * 128:(k + 1) * 128, :], 3072, [(C.wqkv_s[:, k, :], (0, 3072), (0, 128))])
                yield
            for k in range(8):
                cast_rows(C.na_w_o[k * 128:(k + 1) * 128, :], 1024,
                          [(C.wo_s[:, 2 * k, :], (0, 1024), (0, 64)), (C.wo_s[:, 2 * k + 1, :], (0, 1024), (64, 128))])
                yield
        for l in range(2):
            if f"ffn{l}" not in ph and f"ffn{l}_castonly" not in ph:
                continue
            for k in range(8):
                dst_u = C.win_s[l][:, :, k, 0:256].rearrange("m p c -> p m c")
                dst_g = C.win_s[l][:, :, k, 256:512].rearrange("m p c -> p m c")
                i = cnt[0]
                cast_rows(C.ffn_w_in[l][k * 128:(k + 1) * 128, :], 5632, [])
                s_b = stb[0]
                kb = "stb0"
                P.dma(dst_u, s_b[:, 0:2816].rearrange("p (m c) -> p m c", c=256), reads=[kb], writes=["wscr"], q="sp")
                P.dma(dst_g, s_b[:, 2816:5632].rearrange("p (m c) -> p m c", c=256), reads=[kb], writes=["wscr"], q="sp")
                yield
            for k in range(22):
                cast_rows(C.ffn_w_out[l][k * 128:(k + 1) * 128, :], 1024, [(C.wout_s[l][:, k, :], (0, 1024), (0, 128))])
                yield

    bg = casts()
    C.bg = bg
    if "s5" in ph:
        s5_setup(C)
    for _ in bg:
        pass
    A.release(m0c)
    P.barrier()


def bg_step(C, n=1):
    bg = getattr(C, "bg", None)
    if bg is None:
        return
    for _ in range(n):
        next(bg, None)


def ffn_phase(C, l, seq_io):
    P, A, nc = C.P, C.A, C.nc
    m0 = A.mark()
    PS = C.psum
    wout = A.bf16(22 * D).rearrange("p (k c) -> p k c", k=22)
    P.dma(wout, C.wout_s[l], writes=["wout"], q="act")
    lnG = A.f32(D)
    lnB = A.f32(D)
    bc_load(C, lnG, C.ln_g[l, 1], "lnG")
    bc_load(C, lnB, C.ln_b[l, 1], "lnB")
    bout = A.f32(D)
    bc_load(C, bout, C.ffn_b_out[l], "bout")
    binp = A.f32(44)
    cw = A.f32(66).rearrange("p (j m) -> p j m", j=3)
    cb = A.f32(22)
    mst = A.mark()
    load_pp(C, C.ffn_b_in[l].rearrange("(m p) -> m p", p=128), binp, "binp")
    load_pp(C, C.ffn_conv_w[l].rearrange("j (m p) -> (j m) p", p=128), cw.rearrange("p j m -> p (j m)"), "cw")
    load_pp(C, C.ffn_conv_b[l].rearrange("(m p) -> m p", p=128), cb, "cb")
    sc1 = A.f32(D)
    sh = A.f32(D)
    gf = A.f32(D)
    gfb = A.f32(D)
    xt_ring = [A.f32(4 * D).rearrange("p (b c) -> p b c", b=4) for _ in range(2)]
    xh_r = [A.f32(D, parts=2) for _ in range(2)]
    tmp = A.f32(D)
    hb_r = [A.bf16(4 * D).rearrange("p (b c) -> p b c", b=4) for _ in range(2)]
    hhb_r = [A.bf16(D, parts=2) for _ in range(2)]
    hfm_r = [A.bf16(8 * 516).rearrange("p (k t) -> p k t", k=8) for _ in range(2)]
    wring = [A.bf16(8 * 512).rearrange("p (k c) -> p k c", k=8) for _ in range(2)]
    hid = A.bf16(22 * 512).rearrange("p (m t) -> p m t", m=22)
    u_sb = [A.f32(514) for _ in range(2)]
    vv = [A.f32(512) for _ in range(2)]
    g1 = [A.f32(512) for _ in range(2)]
    rt = vv
    st = A.f32(12).rearrange("p (a b) -> p a b", a=2)
    mv = A.f32(2)
    rs = A.f32(1)
    nmr = A.f32(1)
    if C.cfg.get("verbose"):
        print("ffn arena high-water", A.off, "of", A.words)
    pT = PS[0][:, 0:256].bitcast(BF16)
    pu = [PS[1], PS[2]]
    pg = [PS[3], PS[4]]
    phl = PS[5][:, 0:44]
    pTh = PS[5][:, 64:72].bitcast(BF16)
    po = [PS[6], PS[7]]
    gi = [0]
    cut(C, "A")
    for (sq, xin, xout) in seq_io:
        n = sq.n
        ci = sq.ci
        bc_load(C, sh, C.modd[l, ci, 3 * D:4 * D], "sh")
        bc_load(C, sc1, C.modd[l, ci, 4 * D:5 * D], "sc1")
        bc_load(C, gf, C.modd[l, ci, 5 * D:6 * D], "gf")
        P.op("pool", lambda e: e.tensor_scalar(out=sc1, in0=sc1, scalar1=1.0, scalar2=None, op0=ALU.add), reads=["sc1"], writes=["sc1"])
        P.op("pool", lambda e: e.tensor_tensor(out=gfb, in0=gf, in1=bout, op=ALU.mult), reads=["gf", "bout"], writes=["gfb"])
        nt = n // 512
        it0 = gi[0]
        gi[0] += nt

        def prologue(T):
            t0 = T * 512
            it = it0 + T
            i2 = it % 2
            xt, xh, hb, hhb, hfm = xt_ring[i2], xh_r[i2], hb_r[i2], hhb_r[i2], hfm_r[i2]
            xk = f"xt{i2}"
            has_l = t0 > 0
            has_r = t0 + 512 < n
            P.dma(xt, xin[t0:t0 + 512, :].rearrange("(b p) c -> p b c", p=128), writes=[xk], q="sp")
            if (has_l or has_r) and not (has_l and has_r):
                P.op("pool", lambda e: e.memset(xh, 0.0), writes=[f"xh0{i2}", f"xh1{i2}"])
            if has_l:
                P.dma(xh[0:1, :], xin[t0 - 1:t0, :], writes=[f"xh0{i2}"], q="sp")
            if has_r:
                P.dma(xh[1:2, :], xin[t0 + 512:t0 + 513, :], writes=[f"xh1{i2}"], q="sp")
            for b in range(4):
                P.op("dve", lambda e, b=b: e.tensor_tensor(out=tmp, in0=xt[:, b, :], in1=sc1, op=ALU.mult), reads=[xk, "sc1"], writes=["tmp"])
                P.op("dve", lambda e, b=b: e.tensor_tensor(out=hb[:, b, :], in0=tmp, in1=sh, op=ALU.add), reads=["tmp", "sh"], writes=[f"hb{i2}{b}"])
            if has_l or has_r:
                P.op("dve", lambda e: e.tensor_tensor(out=tmp[0:2, :], in0=xh, in1=sc1[0:2, :], op=ALU.mult), reads=[f"xh0{i2}", f"xh1{i2}", "sc1"], writes=["tmp"])
                P.op("dve", lambda e: e.tensor_tensor(out=hhb, in0=tmp[0:2, :], in1=sh[0:2, :], op=ALU.add), reads=["tmp", "sh"], writes=[f"hhb{i2}"])
            for b in range(4):
                P.op("dve", lambda e, b=b: e.scalar_tensor_tensor(out=xt[:, b, :], in0=xt[:, b, :], scalar=ALPHA, in1=gfb, op0=ALU.mult, op1=ALU.add),
                     reads=[xk, f"hb{i2}{b}", "gfb"], writes=[xk])

        def prologue_b(T):
            t0 = T * 512
            it = it0 + T
            i2 = it % 2
            xt, xh, hb, hhb, hfm = xt_ring[i2], xh_r[i2], hb_r[i2], hhb_r[i2], hfm_r[i2]
            has_l = t0 > 0
            has_r = t0 + 512 < n
            for k in range(8):
                for b in range(4):
                    P.op("pe", lambda e, b=b, k=k: e.transpose(pT[:, b * 128:(b + 1) * 128], hb[:, b, k * 128:(k + 1) * 128], C.identb),
                         reads=[f"hb{i2}{b}", "identb"], writes=["bank0"])
                if k % 2 == 0:
                    P.op("act", lambda e, k=k: e.copy(out=hfm[:, k, 2:514], in_=pT), reads=["bank0"], writes=[f"hfm{i2}{k}"])
                else:
                    P.op("dve", lambda e, k=k: e.tensor_copy(out=hfm[:, k, 2:514], in_=pT), reads=["bank0"], writes=[f"hfm{i2}{k}"])
            if has_l or has_r:
                for k in range(8):
                    P.op("pe", lambda e, k=k: e.transpose(pTh[:, 2 * k:2 * k + 2], hhb[0:2, k * 128:(k + 1) * 128], C.identb[0:2, 0:2]),
                         reads=[f"hhb{i2}", "identb"], writes=["bank5"])
                P.op("dve", lambda e: e.tensor_copy(out=hfm[:, :, 1], in_=pTh[:, 0:16:2]), reads=["bank5"], writes=[f"hfmh{i2}"])
                P.op("dve", lambda e: e.tensor_copy(out=hfm[:, :, 514], in_=pTh[:, 1:16:2]), reads=["bank5"], writes=[f"hfmh{i2}"])

        prologue(0)
        prologue_b(0)
        preloaded = set()
        for T in range(nt):
            t0 = T * 512
            it = it0 + T
            i2 = it % 2
            xt, hfm = xt_ring[i2], hfm_r[i2]
            xk = f"xt{i2}"
            has_l = t0 > 0
            has_r = t0 + 512 < n
            cut(C, "D")
            hf_keys = [f"hfm{i2}{k}" for k in range(8)]
            lvl = C.cfg.get("ffn_level", 3)
            if lvl == 1:
                P.op("dve", lambda e: e.tensor_copy(out=xt[:, 0, 0:514], in_=hfm[:, 0, 1:515]), reads=hf_keys + [f"hfmh{i2}", xk], writes=[xk])
                P.dma(xout[t0:t0 + 512, :].rearrange("(b p) c -> p b c", p=128), xt, reads=[xk], writes=["xout"], q="sp", is_output=(xout is sq.y))
                continue
            for m2 in range(11):
                wi = (it * 11 + m2) % 2
                w = wring[wi]
                wk = f"wr{wi}"
                if (it, m2) not in preloaded:
                    P.dma(w, C.win_s[l, m2], writes=[wk], q="sp")
                if m2 == 1 and T + 1 < nt:
                    prologue(T + 1)
                if m2 == 7 and T + 1 < nt:
                    prologue_b(T + 1)
                for mi in range(2):
                    m = m2 * 2 + mi
                    j = m % 2
                    puk, pgk = f"bank{1 + j}", f"bank{3 + j}"
                    for k in range(8):
                        P.op("pe", lambda e, j=j, k=k, mi=mi, w=w: e.matmul(pu[j], lhsT=w[:, k, mi * 128:(mi + 1) * 128], rhs=hfm[:, k, 2:514],
                                                                            start=(k == 0), stop=(k == 7)),
                             reads=[wk, hf_keys[k]], writes=[puk])
                    for k in range(8):
                        P.op("pe", lambda e, j=j, k=k, mi=mi, w=w: e.matmul(pg[j], lhsT=w[:, k, 256 + mi * 128:256 + (mi + 1) * 128], rhs=hfm[:, k, 2:514],
                                                                            start=(k == 0), stop=(k == 7)),
                             reads=[wk, hf_keys[k]], writes=[pgk])
                    if has_l or has_r:
                        for k in range(8):
                            P.op("pe", lambda e, k=k, mi=mi, m=m, w=w: e.matmul(phl[:, 2 * m:2 * m + 2], lhsT=w[:, k, mi * 128:(mi + 1) * 128],
                                                                                rhs=hfm[:, k, 1:515:513], start=(k == 0), stop=(k == 7)),
                                 reads=[wk, f"hfmh{i2}"], writes=["bank5"])
                    u = u_sb[j]
                    uk = f"u{j}"
                    P.op("act", lambda e, u=u, j=j, m=m: e.activation(out=u[:, 1:513], in_=pu[j], func=AF.Identity, bias=binp[:, m:m + 1], scale=1.0),
                         reads=[puk, "binp"], writes=[uk + "m"])
                    if has_l:
                        P.op("act", lambda e, u=u, m=m: e.activation(out=u[:, 0:1], in_=phl[:, 2 * m:2 * m + 1], func=AF.Identity, bias=binp[:, m:m + 1], scale=1.0),
                             reads=["bank5", "binp"], writes=[uk + "l"])
                    else:
                        P.op("pool", lambda e, u=u: e.memset(u[:, 0:1], 0.0), writes=[uk + "l"])
                    if has_r:
                        P.op("act", lambda e, u=u, m=m: e.activation(out=u[:, 513:514], in_=phl[:, 2 * m + 1:2 * m + 2], func=AF.Identity, bias=binp[:, m:m + 1], scale=1.0),
                             reads=["bank5", "binp"], writes=[uk + "r"])
                    else:
                        P.op("pool", lambda e, u=u: e.memset(u[:, 513:514], 0.0), writes=[uk + "r"])
                    v = vv[j]
                    vk = f"v{j}"
                    P.op("dve", lambda e, u=u, v=v, m=m: e.tensor_scalar(out=v, in0=u[:, 1:513], scalar1=cw[:, 1, m:m + 1], scalar2=cb[:, m:m + 1],
                                                                           op0=ALU.mult, op1=ALU.add),
                         reads=[uk + "m", "cw", "cb"], writes=[vk])
                    P.op("dve", lambda e, u=u, v=v, m=m: e.scalar_tensor_tensor(out=v, in0=u[:, 0:512], scalar=cw[:, 0, m:m + 1], in1=v, op0=ALU.mult, op1=ALU.add),
                         reads=[uk + "m", uk + "l", vk, "cw"], writes=[vk])
                    P.op("dve", lambda e, u=u, v=v, m=m: e.scalar_tensor_tensor(out=v, in0=u[:, 2:514], scalar=cw[:, 2, m:m + 1], in1=v, op0=ALU.mult, op1=ALU.add),
                         reads=[uk + "m", uk + "r", vk, "cw"], writes=[vk])
                    gg = g1[j]
                    gk = f"g1{j}"
                    P.op("act", lambda e, v=v, gg=gg: e.activation(out=gg, in_=v, func=AF.Gelu_apprx_tanh), reads=[vk], writes=[gk])
                    P.op("dve", lambda e, gg=gg, j=j, m=m: e.scalar_tensor_tensor(out=hid[:, m, :], in0=pg[j], scalar=binp[:, 22 + m:23 + m], in1=gg,
                                                                                   op0=ALU.add, op1=ALU.mult),
                         reads=[pgk, gk, "binp"], writes=[f"hid{m}"])
            hid_keys = [f"hid{m}" for m in range(22)]
            if T + 1 < nt:
                wi_n = ((it + 1) * 11) % 2
                P.dma(wring[wi_n], C.win_s[l, 0], writes=[f"wr{wi_n}"], q="sp")
                preloaded.add((it + 1, 0))
            if lvl == 2:
                P.op("dve", lambda e: e.tensor_copy(out=xt[:, 0, 0:512], in_=hid[:, 5, :]), reads=hid_keys + [xk], writes=[xk])
                P.dma(xout[t0:t0 + 512, :].rearrange("(b p) c -> p b c", p=128), xt, reads=[xk], writes=["xout"], q="sp", is_output=(xout is sq.y))
                continue
            for b in range(4):
                for half in range(2):
                    j = (b * 2 + half) % 2
                    pok = f"bank{6 + j}"
                    for k in range(22):
                        P.op("pe", lambda e, j=j, k=k, b=b, half=half: e.matmul(po[j], lhsT=hid[:, k, b * 128:(b + 1) * 128],
                                                                                  rhs=wout[:, k, half * 512:(half + 1) * 512], start=(k == 0), stop=(k == 21)),
                             reads=[hid_keys[k], "wout"], writes=[pok])
                    r = rt[j]
                    rk = f"v{j}"
                    xs = xt[:, b, half * 512:(half + 1) * 512]
                    P.op("dve", lambda e, j=j, r=r, half=half: e.tensor_tensor(out=r, in0=po[j], in1=gf[:, half * 512:(half + 1) * 512], op=ALU.mult),
                         reads=[pok, "gf"], writes=[rk])
                    P.op("dve", lambda e, r=r, xs=xs: e.tensor_tensor(out=xs, in0=xs, in1=r, op=ALU.add), reads=[rk, xk], writes=[xk])
                layer_norm_block(C, xt[:, b, :], xk, lnG, lnB, st, mv, rs, nmr)
            P.dma(xout[t0:t0 + 512, :].rearrange("(b p) c -> p b c", p=128), xt, reads=[xk], writes=["xout"], q="sp",
                  is_output=(xout is sq.y))
    A.release(m0)
    P.barrier()


def layer_norm_block(C, xb, xk, lnG, lnB, st, mv, rs, nmr):
    P = C.P
    P.op("dve", lambda e: e.bn_stats(out=st[:, 0, :], in_=xb[:, 0:512]), reads=[xk], writes=["st"])
    P.op("dve", lambda e: e.bn_stats(out=st[:, 1, :], in_=xb[:, 512:1024]), reads=[xk, "st"], writes=["st"])
    P.op("dve", lambda e: e.bn_aggr(out=mv, in_=st), reads=["st"], writes=["mv"])
    P.op("act", lambda e: e.activation(out=rs, in_=mv[:, 1:2], func=AF.Sqrt, bias=C.eps, scale=1.0), reads=["mv", "eps"], writes=["rs"])
    P.op("dve", lambda e: e.reciprocal(out=rs, in_=rs), reads=["rs"], writes=["rs"])
    P.op("dve", lambda e: e.tensor_scalar(out=nmr, in0=mv[:, 0:1], scalar1=rs, scalar2=-1.0, op0=ALU.mult, op1=ALU.mult),
         reads=["mv", "rs"], writes=["nmr"])
    P.op("act", lambda e: e.activation(out=xb, in_=xb, func=AF.Identity, bias=nmr, scale=rs), reads=[xk, "rs", "nmr"], writes=[xk])
    P.op("dve", lambda e: e.tensor_tensor(out=xb, in0=xb, in1=lnG, op=ALU.mult), reads=[xk, "lnG"], writes=[xk])
    P.op("dve", lambda e: e.tensor_tensor(out=xb, in0=xb, in1=lnB, op=ALU.add), reads=[xk, "lnB"], writes=[xk])


TWO_PI = 2.0 * math.pi
MAGIC = 12582912.0
CW1 = 6.28125
CW2 = TWO_PI - 6.28125


def sincos(C, ang, n, sin_out, cos_out, tag):
    P, A = C.P, C.A
    k = A.f32(n)
    r = A.f32(n)
    ka, kk, kr = tag + "a", tag + "k", tag + "r"
    for (shift, outp) in ((0.0, sin_out), (0.5 * math.pi, cos_out)):
        if outp is None:
            continue
        P.op("dve", lambda e, shift=shift: e.tensor_scalar(out=r, in0=ang, scalar1=shift, scalar2=None, op0=ALU.add), reads=[ka], writes=[kr])
        P.op("dve", lambda e: e.tensor_scalar(out=k, in0=r, scalar1=1.0 / TWO_PI, scalar2=MAGIC, op0=ALU.mult, op1=ALU.add), reads=[kr], writes=[kk])
        P.op("dve", lambda e: e.tensor_scalar(out=k, in0=k, scalar1=MAGIC, scalar2=None, op0=ALU.subtract), reads=[kk], writes=[kk])
        P.op("dve", lambda e: e.scalar_tensor_tensor(out=r, in0=k, scalar=-CW1, in1=r, op0=ALU.mult, op1=ALU.add), reads=[kk, kr], writes=[kr])
        P.op("dve", lambda e: e.scalar_tensor_tensor(out=r, in0=k, scalar=-CW2, in1=r, op0=ALU.mult, op1=ALU.add), reads=[kk, kr], writes=[kr])
        P.op("dve", lambda e: e.tensor_scalar(out=r, in0=r, scalar1=math.pi, scalar2=-math.pi, op0=ALU.min, op1=ALU.max), reads=[kr], writes=[kr])
        P.op("act", lambda e, outp=outp: e.activation(out=outp, in_=r, func=AF.Sin), reads=[kr], writes=[tag + "o"])


def dup_transpose(C, src, dst, key):
    P, A = C.P, C.A
    stg = A.f32(128)
    sk = A.key("dtstg")
    P.dma(stg[:, 0:64], src, writes=[sk])
    P.dma(stg[:, 64:128], src, writes=[sk + "b"])
    ps = C.psum[7][:, 0:128]
    P.op("pe", lambda e: e.transpose(ps, stg, C.ident), reads=[sk, sk + "b"], writes=["bank7"])
    P.op("dve", lambda e: e.tensor_copy(out=dst, in_=ps), reads=["bank7"], writes=[key])


def s5_setup(C):
    P, A, nc = C.P, C.A, C.nc
    PS = C.psum
    m0 = A.mark()
    NDG = 128
    LR = A.f32(NDG)
    LI = A.f32(NDG)
    DT = A.f32(NDG)
    dup_transpose(C, C.s5_lam_re.rearrange("d g p -> (d g) p"), LR, "LR")
    dup_transpose(C, C.s5_lam_im.rearrange("d g p -> (d g) p"), LI, "LI")
    ldt = A.f32(1)
    P.dma(ldt, C.s5_log_dt.rearrange("d (g o) -> (d g) o", o=1), writes=["ldt"])
    ldtb = A.f32(128)
    P.op("dve", lambda e: e.tensor_copy(out=ldtb, in_=ldt.to_broadcast([128, 128])), reads=["ldt"], writes=["ldtb"])
    P.op("pe", lambda e: e.transpose(PS[7][:, 0:128], ldtb, C.ident), reads=["ldtb"], writes=["bank7"])
    P.op("act", lambda e: e.activation(out=DT, in_=PS[7][:, 0:128], func=AF.Exp), reads=["bank7"], writes=["DT"])
    ZR = A.f32(NDG)
    ZI = A.f32(NDG)
    P.op("dve", lambda e: e.tensor_tensor(out=ZR, in0=LR, in1=DT, op=ALU.mult), reads=["LR", "DT"], writes=["ZR"])
    P.op("dve", lambda e: e.tensor_tensor(out=ZI, in0=LI, in1=DT, op=ALU.mult), reads=["LI", "DT"], writes=["ZI"])
    NP_ = 17
    PWr = A.f32(NP_ * NDG).rearrange("p (i g) -> p i g", i=NP_)
    PWi = A.f32(NP_ * NDG).rearrange("p (i g) -> p i g", i=NP_)
    Fr = A.f32(NDG)
    Fi = A.f32(NDG)
    RHO = A.f32(NDG)
    rho16 = A.f32(NDG)
    mS1 = A.mark()
    nvec = A.f32(NP_)
    for i in range(NP_):
        P.op("pool", lambda e, i=i: e.memset(nvec[:, i:i + 1], float(i - 8)), writes=["nvec"])
    ang = A.f32(NP_ * NDG).rearrange("p (i g) -> p i g", i=NP_)
    mag = A.f32(NP_ * NDG).rearrange("p (i g) -> p i g", i=NP_)
    nb3 = nvec.unsqueeze(2).to_broadcast([128, NP_, NDG])
    P.op("dve", lambda e: e.tensor_tensor(out=ang, in0=ZI.unsqueeze(1).to_broadcast([128, NP_, NDG]), in1=nb3, op=ALU.mult), reads=["ZI", "nvec"], writes=["anga"])
    P.op("dve", lambda e: e.tensor_tensor(out=mag, in0=ZR.unsqueeze(1).to_broadcast([128, NP_, NDG]), in1=nb3, op=ALU.mult), reads=["ZR", "nvec"], writes=["mag"])
    P.op("act", lambda e: e.activation(out=mag, in_=mag, func=AF.Exp), reads=["mag"], writes=["mag"])
    sn = A.f32(NP_ * NDG).rearrange("p (i g) -> p i g", i=NP_)
    cs = A.f32(NP_ * NDG).rearrange("p (i g) -> p i g", i=NP_)
    f2 = lambda t: t.rearrange("p i g -> p (i g)")
    m1 = A.mark()
    sincos(C, f2(ang), NP_ * NDG, f2(sn), f2(cs), "ang")
    P.op("dve", lambda e: e.tensor_tensor(out=PWr, in0=mag, in1=cs, op=ALU.mult), reads=["mag", "ango"], writes=["PWr"])
    P.op("dve", lambda e: e.tensor_tensor(out=PWi, in0=mag, in1=sn, op=ALU.mult), reads=["mag", "ango"], writes=["PWi"])
    den = A.f32(NDG)
    t0_ = A.f32(NDG)
    a_re, a_im = PWr[:, 9, :], PWi[:, 9, :]
    P.op("dve", lambda e: e.tensor_tensor(out=den, in0=LR, in1=LR, op=ALU.mult), reads=["LR"], writes=["den"])
    P.op("dve", lambda e: e.tensor_tensor(out=t0_, in0=LI, in1=LI, op=ALU.mult), reads=["LI"], writes=["t0_"])
    P.op("dve", lambda e: e.tensor_tensor(out=den, in0=den, in1=t0_, op=ALU.add), reads=["den", "t0_"], writes=["den"])
    P.op("dve", lambda e: e.reciprocal(out=den, in_=den), reads=["den"], writes=["den"])
    nr = A.f32(NDG)
    P.op("dve", lambda e: e.tensor_scalar(out=nr, in0=a_re, scalar1=-1.0, scalar2=None, op0=ALU.add), reads=["PWr"], writes=["nr"])
    P.op("dve", lambda e: e.tensor_tensor(out=Fr, in0=nr, in1=LR, op=ALU.mult), reads=["nr", "LR"], writes=["Fr"])
    P.op("dve", lambda e: e.tensor_tensor(out=t0_, in0=a_im, in1=LI, op=ALU.mult), reads=["PWi", "LI", "den"], writes=["t0_"])
    P.op("dve", lambda e: e.tensor_tensor(out=Fr, in0=Fr, in1=t0_, op=ALU.add), reads=["Fr", "t0_"], writes=["Fr"])
    P.op("dve", lambda e: e.tensor_tensor(out=Fr, in0=Fr, in1=den, op=ALU.mult), reads=["Fr", "den"], writes=["Fr"])
    P.op("dve", lambda e: e.tensor_tensor(out=Fi, in0=a_im, in1=LR, op=ALU.mult), reads=["PWi", "LR"], writes=["Fi"])
    P.op("dve", lambda e: e.tensor_tensor(out=t0_, in0=nr, in1=LI, op=ALU.mult), reads=["nr", "LI", "Fr"], writes=["t0_"])
    P.op("dve", lambda e: e.tensor_tensor(out=Fi, in0=Fi, in1=t0_, op=ALU.subtract), reads=["Fi", "t0_"], writes=["Fi"])
    P.op("dve", lambda e: e.tensor_tensor(out=Fi, in0=Fi, in1=den, op=ALU.mult), reads=["Fi", "den"], writes=["Fi"])
    P.op("dve", lambda e: e.tensor_copy(out=RHO, in_=mag[:, 16, :]), reads=["mag"], writes=["RHO"])
    P.op("act", lambda e: e.activation(out=rho16, in_=ZR, func=AF.Exp, scale=128.0), reads=["ZR"], writes=["rho16"])
    P.barrier()
    A.release(mS1)
    ZI2 = A.f32(NDG)
    P.op("dve", lambda e: e.tensor_tensor(out=ZI2, in0=LI, in1=DT, op=ALU.mult), writes=["ZI2"])
    mS2 = A.mark()
    jv = A.f32(48)
    for j in range(16):
        P.op("pool", lambda e, j=j: e.memset(jv[:, j:j + 1], 8.0 * j), writes=["jv"])
    for s_ in range(32):
        P.op("pool", lambda e, s_=s_: e.memset(jv[:, 16 + s_:17 + s_], 128.0 * s_), writes=["jv"])
    TT = A.f32(64 * 128).rearrange("p (g c) -> p g c", c=128)
    a48 = A.f32(64 * 48).rearrange("p (g j) -> p g j", j=48)
    s48 = A.f32(64 * 48).rearrange("p (g j) -> p g j", j=48)
    c48 = A.f32(64 * 48).rearrange("p (g j) -> p g j", j=48)
    g2 = lambda t: t.rearrange("p g j -> p (g j)")
    mS3 = A.mark()
    for d in range(2):
        ds_ = slice(d * 64, (d + 1) * 64)
        P.op("pool", lambda e: e.memset(TT, 0.0), reads=["TT"], writes=["TT"])
        P.op("dve", lambda e, ds_=ds_: e.tensor_tensor(out=a48, in0=ZI2[:, ds_].unsqueeze(2).to_broadcast([128, 64, 48]), in1=jv.unsqueeze(1).to_broadcast([128, 64, 48]), op=ALU.mult),
             reads=["ZI2", "jv", "a48o"], writes=["a48a"])
        sincos(C, g2(a48), 64 * 48, g2(s48), g2(c48), "a48")
        A.release(mS3)
        if d == 0:
            P.op("dve", lambda e: e.tensor_copy(out=TT[:, :, 0:16], in_=c48[:, :, 0:16]), reads=["a48o", "TT"], writes=["TT"])
            P.op("dve", lambda e: e.tensor_copy(out=TT[:, :, 16:32], in_=s48[:, :, 0:16]), reads=["a48o", "TT"], writes=["TT"])
        else:
            P.op("dve", lambda e: e.tensor_copy(out=TT[:, :, 0:16], in_=c48[:, :, 15::-1]), reads=["a48o", "TT"], writes=["TT"])
            P.op("dve", lambda e: e.tensor_copy(out=TT[:, :, 16:32], in_=s48[:, :, 15::-1]), reads=["a48o", "TT"], writes=["TT"])
        P.op("dve", lambda e: e.tensor_copy(out=TT[:, :, 48:80], in_=c48[:, :, 16:48]), reads=["a48o", "TT"], writes=["TT"])
        P.op("dve", lambda e: e.tensor_copy(out=TT[:, :, 80:112], in_=s48[:, :, 16:48]), reads=["a48o", "TT"], writes=["TT"])
        P.op("dve", lambda e, ds_=ds_: e.tensor_copy(out=TT[:, :, 33:48], in_=RHO[:, ds_].unsqueeze(2).to_broadcast([128, 64, 15])), reads=["RHO", "TT"], writes=["TT"])
        P.op("dve", lambda e, ds_=ds_: e.tensor_copy(out=TT[:, :, 112], in_=RHO[:, ds_]), reads=["RHO", "TT"], writes=["TT"])
        P.op("dve", lambda e, ds_=ds_: e.tensor_copy(out=TT[:, :, 113], in_=rho16[:, ds_]), reads=["rho16", "TT"], writes=["TT"])
        P.dma(C.s5t[:, :, d, :].rearrange("g p c -> p g c"), TT, reads=["TT"], writes=["s5t"], q=("sp" if d == 0 else "act"))
    P.barrier()
    A.release(mS2)
    CTr = A.f32(NDG * 16).rearrange("p (g h) -> p g h", h=16)
    CTi = A.f32(NDG * 16).rearrange("p (g h) -> p g h", h=16)
    BTr = A.f32(NDG * 16).rearrange("p (g h) -> p g h", h=16)
    BTi = A.f32(NDG * 16).rearrange("p (g h) -> p g h", h=16)
    for (src, dst, key) in ((C.s5_c_re, CTr, "CTr"), (C.s5_c_im, CTi, "CTi")):
        rows = src.rearrange("d g h p -> (d g h) p")
        for blk in range(16):
            stg = A.f32(128)
            sk = A.key("cstg")
            P.dma(stg[:, 0:64], rows[blk * 128:(blk + 1) * 128, :], writes=[sk], q=("sp" if blk % 2 == 0 else "act"))
            P.dma(stg[:, 64:128], rows[blk * 128:(blk + 1) * 128, :], writes=[sk + "b"], q=("act" if blk % 2 == 0 else "sp"))
            bk = f"bank{6 + blk % 2}"
            ps = PS[6 + blk % 2][:, 0:128]
            P.op("pe", lambda e, ps=ps, stg=stg: e.transpose(ps, stg, C.ident), reads=[sk, sk + "b"], writes=[bk])
            P.op("dve", lambda e, ps=ps, dst=dst, blk=blk: e.tensor_copy(out=dst[:, blk * 8:(blk + 1) * 8, :], in_=ps.rearrange("p (g h) -> p g h", h=16)),
                 reads=[bk], writes=[key])
    for (src, dst, key) in ((C.s5_b_re, BTr, "BTr"), (C.s5_b_im, BTi, "BTi")):
        v = src.rearrange("d g p h -> p (d g) h")
        for half in range(2):
            for q4 in range(4):
                P.dma(dst[half * 64:(half + 1) * 64, q4 * 32:(q4 + 1) * 32, :], v[:, q4 * 32:(q4 + 1) * 32, :], writes=[key + f"{half}{q4}"],
                      q=("sp" if (half + q4) % 2 == 0 else "act"))
    bt_keys = lambda k: [k + f"{h}{q}" for h in range(2) for q in range(4)]
    BBr = A.f32(NDG * 16).rearrange("p (g h) -> p g h", h=16)
    BBi = A.f32(NDG * 16).rearrange("p (g h) -> p g h", h=16)
    tq = A.f32(NDG * 16).rearrange("p (g h) -> p g h", h=16)
    frb = Fr.unsqueeze(2).to_broadcast([128, NDG, 16])
    fib = Fi.unsqueeze(2).to_broadcast([128, NDG, 16])
    P.op("dve", lambda e: e.tensor_tensor(out=BBr, in0=BTr, in1=frb, op=ALU.mult), reads=bt_keys("BTr") + ["Fr"], writes=["BBr"])
    P.op("dve", lambda e: e.tensor_tensor(out=tq, in0=BTi, in1=fib, op=ALU.mult), reads=bt_keys("BTi") + ["Fi"], writes=["tq"])
    P.op("dve", lambda e: e.tensor_tensor(out=BBr, in0=BBr, in1=tq, op=ALU.subtract), reads=["BBr", "tq"], writes=["BBr"])
    P.op("dve", lambda e: e.tensor_tensor(out=BBi, in0=BTi, in1=frb, op=ALU.mult), reads=bt_keys("BTi") + ["Fr"], writes=["BBi"])
    P.op("dve", lambda e: e.tensor_tensor(out=tq, in0=BTr, in1=fib, op=ALU.mult), reads=bt_keys("BTr") + ["Fi", "BBr"], writes=["tq"])
    P.op("dve", lambda e: e.tensor_tensor(out=BBi, in0=BBi, in1=tq, op=ALU.add), reads=["BBi", "tq"], writes=["BBi"])
    dst_ = A.f32(128, parts=64)
    for t in range(8):
        P.dma(dst_[:, t * 16:(t + 1) * 16], C.s5_d, writes=[f"dst{t}"], q=("sp" if t % 2 == 0 else "act"))
    dcol = A.f32(64)
    P.op("pe", lambda e: e.transpose(PS[7][:, 0:64], dst_, C.ident[0:64, 0:64]), reads=[f"dst{t}" for t in range(8)], writes=["bank7"])
    P.op("dve", lambda e: e.tensor_copy(out=dcol, in_=PS[7][:, 0:64]), reads=["bank7"], writes=["dcol"])
    mf = A.f32(128)
    mb = A.f32(128)
    P.dma(mf, C.s5_masks[0], writes=["mf"])
    P.dma(mb, C.s5_masks[1], writes=["mb"])
    GC = 8
    al = [A.f32(NDG * 8).rearrange("p (g t) -> p g t", t=8) for _ in range(8)]
    def gather(out_t, PW, idx_f, idx_b, sgn_top, sgn_bot, key, rk):
        for (g0, idx) in ((0, idx_f), (64, idx_b)):
            for t in range(8):
                for (p0, sg) in ((0, sgn_top), (64, sgn_bot)):
                    src_t = PW[0][p0:p0 + 64, idx[t] + 8, g0:g0 + 64] if sg[1] == "r" else PW[1][p0:p0 + 64, idx[t] + 8, g0:g0 + 64]
                    P.op("pool", lambda e, out_t=out_t, src_t=src_t, p0=p0, g0=g0, t=t, sg=sg: e.tensor_scalar(
                        out=out_t[p0:p0 + 64, g0:g0 + 64, t], in0=src_t, scalar1=float(sg[0]), scalar2=None, op0=ALU.mult), reads=rk, writes=[key])
    PW = (PWr, PWi)
    pk = ["PWr", "PWi"]
    tf = [t + 1 for t in range(8)]
    tb_ = [8 - t for t in range(8)]
    gather(al[0], PW, tf, tb_, (1, "r"), (-1, "i"), "al0", pk)
    gather(al[1], PW, tf, tb_, (-1, "i"), (-1, "r"), "al1", pk)
    gather(al[2], PW, tf, tb_, (-1, "i"), (-1, "r"), "al2", pk)
    gather(al[3], PW, tf, tb_, (-1, "r"), (1, "i"), "al3", pk)
    xin_f = [7 - t for t in range(8)]
    xin_b = [t for t in range(8)]
    gather(al[4], PW, xin_f, xin_b, (1, "r"), (1, "i"), "al4", pk)
    gather(al[5], PW, xin_f, xin_b, (-1, "i"), (1, "r"), "al5", pk)
    xtp_f = [-1 - t for t in range(8)]
    xtp_b = [t - 8 for t in range(8)]
    gather(al[6], PW, xtp_f, xtp_b, (1, "r"), (1, "i"), "al6", pk)
    gather(al[7], PW, xtp_f, xtp_b, (-1, "i"), (1, "r"), "al7", pk)
    gen = [A.bf16(GC * 128).rearrange("p (g t h) -> p g t h", g=GC, t=8) for _ in range(4)]
    tm1 = A.f32(GC * 128).rearrange("p (g t h) -> p g t h", g=GC, t=8)
    tm2 = A.f32(GC * 128).rearrange("p (g t h) -> p g t h", g=GC, t=8)
    wstage = [A.bf16(1024) for _ in range(2)]
    tpf = A.f32(128)
    tpb = A.f32(128)
    specs = ((CTr, CTi, 0, 1, ["CTr", "CTi"]), (CTr, CTi, 2, 3, ["CTr", "CTi"]), (BBr, BBi, 4, 5, ["BBr", "BBi"]), (BBr, BBi, 6, 7, ["BBr", "BBi"]))
    GEN = {}
    for d in range(2):
        for gc in range(64 // GC):
            bg_step(C, 3)
            dg0 = d * 64 + gc * GC
            for gi_, (Sr, Si, ia, ib, rk) in enumerate(specs):
                eng = "dve"
                srb = Sr[:, dg0:dg0 + GC, :].unsqueeze(2).to_broadcast([128, GC, 8, 16])
                sib = Si[:, dg0:dg0 + GC, :].unsqueeze(2).to_broadcast([128, GC, 8, 16])
                aa = al[ia][:, dg0:dg0 + GC, :].unsqueeze(3).to_broadcast([128, GC, 8, 16])
                bb = al[ib][:, dg0:dg0 + GC, :].unsqueeze(3).to_broadcast([128, GC, 8, 16])
                gk = f"gen{gi_}"
                P.op(eng, lambda e, srb=srb, aa=aa: e.tensor_tensor(out=tm1, in0=srb, in1=aa, op=ALU.mult), reads=rk + [f"al{ia}"], writes=["tm1"])
                P.op(eng, lambda e, sib=sib, bb=bb: e.tensor_tensor(out=tm2, in0=sib, in1=bb, op=ALU.mult), reads=rk + [f"al{ib}"], writes=["tm2"])
                P.op(eng, lambda e, gi_=gi_: e.tensor_tensor(out=gen[gi_], in0=tm1, in1=tm2, op=ALU.add), reads=["tm1", "tm2"], writes=[gk])
            for gl in range(GC):
                g = gc * GC + gl
                base = 128 + d * 448
                W1g = gen[0][:, gl].rearrange("p t h -> p (t h)")
                W2g = gen[1][:, gl].rearrange("p t h -> p (t h)")
                Xin = gen[2][:, gl].rearrange("p t h -> p (t h)")
                Xtp = gen[3][:, gl].rearrange("p t h -> p (t h)")
                ws = wstage[(d * 64 + g) % 2]
                wk = f"ws{(d * 64 + g) % 2}"
                pst = PS[0][:, 0:64].bitcast(BF16)
                P.op("pe", lambda e, pst=pst, Xin=Xin: e.transpose(pst, Xin, C.identb), reads=["gen2"], writes=["bank0"])
                P.op("act", lambda e, ws=ws, pst=pst: e.copy(out=ws[:, 0:128], in_=pst), reads=["bank0"], writes=[wk + "a"])
                P.op("act", lambda e, ws=ws, pst=pst: e.activation(out=ws[:, 128:192], in_=pst[:, 0:64], func=AF.Identity, scale=-1.0), reads=["bank0"], writes=[wk + "b"])
                P.op("dve", lambda e, ws=ws, W1g=W1g: e.tensor_copy(out=ws[:, 192:320], in_=W1g), reads=["gen0"], writes=[wk + "c"])
                P.op("dve", lambda e, ws=ws, W2g=W2g: e.tensor_copy(out=ws[:, 320:448], in_=W2g), reads=["gen1"], writes=[wk + "d"])
                P.dma(C.s5w[g, :, base:base + 448], ws[:, 0:448], reads=[wk + "a", wk + "b", wk + "c", wk + "d"], writes=["s5w"], q=("sp" if g % 2 == 0 else "act"))
                ptp = PS[1 + d][:, 0:128]
                P.op("pe", lambda e, ptp=ptp, Xtp=Xtp, W1g=W1g: e.matmul(ptp, lhsT=Xtp, rhs=W1g, start=True, stop=True), reads=["gen3", "gen0"], writes=[f"bank{1 + d}"])
                P.op("dve", lambda e, ptp=ptp, d=d: e.tensor_tensor(out=(tpf if d == 0 else tpb), in0=ptp, in1=(mf if d == 0 else mb), op=ALU.mult),
                     reads=[f"bank{1 + d}", "mf", "mb"], writes=["tpx"])
                P.dma(C.s5tp[d, g], (tpf if d == 0 else tpb), reads=["tpx"], writes=["s5tp"], q="sp")
    ta = [A.f32(128) for _ in range(2)]
    tb2 = [A.f32(128) for _ in range(2)]
    tob = [A.bf16(128) for _ in range(2)]
    for g in range(64):
        i = g % 2
        if g % 2 == 0:
            bg_step(C, 1)
        P.dma(ta[i], C.s5tp[0, g], reads=["s5tp"], writes=[f"ta{i}"], q="sp")
        P.dma(tb2[i], C.s5tp[1, g], reads=["s5tp"], writes=[f"tb{i}"], q="act")
        P.op("dve", lambda e, i=i: e.tensor_tensor(out=ta[i], in0=ta[i], in1=tb2[i], op=ALU.add), reads=[f"ta{i}", f"tb{i}"], writes=[f"ta{i}"])
        P.op("dve", lambda e, i=i, g=g: e.scalar_tensor_tensor(out=tob[i], in0=C.ident, scalar=dcol[:, g:g + 1], in1=ta[i], op0=ALU.mult, op1=ALU.add),
             reads=[f"ta{i}", "dcol"], writes=[f"tob{i}"])
        P.dma(C.s5w[g, :, 0:128], tob[i], reads=[f"tob{i}"], writes=["s5w"], q="sp")
    A.release(m0)
    P.barrier()


def s5_phase(C):
    P, A, nc = C.P, C.A, C.nc
    PS = C.psum
    l = 0
    seqs = C.seqs
    m0 = A.mark()
    sc1 = A.f32(D)
    sh = A.f32(D)
    xc = A.f32(8 * D).rearrange("p (t c) -> p t c", t=8)
    tmp = A.f32(D)
    hperm = A.bf16(8 * D).rearrange("p (g t h) -> p g t h", g=64, t=8)
    ust = [A.bf16(64 * 128).rearrange("p (g c) -> p g c", g=64) for _ in range(2)]
    cnt = 0
    for sq in seqs:
        ci = sq.ci
        bc_load(C, sh, C.modd[l, ci, 0:D], "sh")
        bc_load(C, sc1, C.modd[l, ci, D:2 * D], "sc1")
        P.op("pool", lambda e: e.tensor_scalar(out=sc1, in0=sc1, scalar1=1.0, scalar2=None, op0=ALU.add), reads=["sc1"], writes=["sc1"])
        for cb in range(sq.n // 1024):
            P.dma(xc, sq.x0[cb * 1024:(cb + 1) * 1024, :].rearrange("(c t) d -> c t d", t=8), writes=["xc"], q="sp")
            for t in range(8):
                P.op("dve", lambda e, t=t: e.tensor_tensor(out=tmp, in0=xc[:, t, :], in1=sc1, op=ALU.mult), reads=["xc", "sc1"], writes=["tmp"])
                P.op("dve", lambda e, t=t: e.tensor_tensor(out=hperm[:, :, t, :], in0=tmp.rearrange("p (g h) -> p g h", h=16),
                                                             in1=sh.rearrange("p (g h) -> p g h", h=16), op=ALU.add), reads=["tmp", "sh"], writes=["hperm"])
            us = ust[cnt % 2]
            uk = f"ust{cnt % 2}"
            cnt += 1
            for g4 in range(16):
                pT = PS[g4 % 2][:, 0:256].bitcast(BF16)
                bk = f"bank{g4 % 2}"
                for gi_ in range(4):
                    g = g4 * 4 + gi_
                    P.op("pe", lambda e, pT=pT, g=g, gi_=gi_: e.transpose(pT[:, gi_ * 128:(gi_ + 1) * 128], hperm[:, g].rearrange("p t h -> p (t h)"), C.identb),
                         reads=["hperm"], writes=[bk])
                if g4 % 2 == 0:
                    P.op("act", lambda e, pT=pT, us=us, g4=g4: e.copy(out=us[:, g4 * 4:(g4 + 1) * 4, :], in_=pT.rearrange("p (g c) -> p g c", g=4)), reads=[bk], writes=[uk + "a"])
                else:
                    P.op("dve", lambda e, pT=pT, us=us, g4=g4: e.tensor_copy(out=us[:, g4 * 4:(g4 + 1) * 4, :], in_=pT.rearrange("p (g c) -> p g c", g=4)), reads=[bk], writes=[uk + "d"])
            P.dma(sq.U[cb], us, reads=[uk + "a", uk + "d"], writes=["Uscr"], q="sp")
    A.release(m0)
    P.barrier()
    m0 = A.mark()
    Jm = A.f32(128)
    P.op("dve", lambda e: e.tensor_scalar(out=Jm[:, 0:64], in0=C.ident[:, 64:128], scalar1=-1.0, scalar2=None, op0=ALU.mult), writes=["Jm"])
    P.op("dve", lambda e: e.tensor_copy(out=Jm[:, 64:128], in_=C.ident[:, 0:64]), reads=["Jm"], writes=["Jm"])
    NCM = 512
    wg_r = [A.bf16(1024) for _ in range(2)]
    tg_r = [A.f32(256).rearrange("p (d c) -> p d c", d=2) for _ in range(2)]
    ug_r = [A.bf16(NCM) for _ in range(2)]
    t1_r = [A.f32(NCM) for _ in range(2)]
    t2_r = [A.f32(NCM) for _ in range(2)]
    bp_r = [A.f32(NCM) for _ in range(2)]
    mf_r = [A.f32(NCM) for _ in range(2)]
    ql_r = [A.f32(NCM) for _ in range(2)]
    qq_r = [A.f32(NCM) for _ in range(2)]
    cq_r = [A.bf16(NCM + 2) for _ in range(2)]
    sq_r = [A.bf16(NCM + 2) for _ in range(2)]
    zg_r = [A.bf16(NCM) for _ in range(2)]
    ec_r = [A.f32(32) for _ in range(2)]
    e1_r = [A.f32(32) for _ in range(2)]
    e2_r = [A.f32(32) for _ in range(2)]
    hin_r = [A.f32(34) for _ in range(2)]
    r16_r = [A.f32(32) for _ in range(2)]
    gg_r = [A.f32(32) for _ in range(2)]
    it = 0
    for sq in seqs:
        n_c = sq.n // 8
        n_b = n_c // 16
        ncb = sq.n // 1024
        for g in range(64):
            wg = wg_r[g % 2]
            tg = tg_r[g % 2]
            ug = ug_r[g % 2]
            wk, tk, uk = f"wg{g % 2}", f"tg{g % 2}", f"ug{g % 2}"
            P.dma(wg, C.s5w[g], writes=[wk], q="sp")
            P.dma(tg, C.s5t[g], writes=[tk], q="sp")
            P.dma(ug[:, 0:n_c].rearrange("p (b c) -> p b c", c=128), sq.U[:, :, g, :].rearrange("b p c -> p b c"), writes=[uk], q="sp")
            py = PS[4 + g % 2][:, 0:n_c]
            pyk = f"bank{4 + g % 2}"
            P.op("pe", lambda e, py=py, wg=wg, ug=ug, n_c=n_c: e.matmul(py, lhsT=wg[:, 0:128], rhs=ug[:, 0:n_c], start=True, stop=False), reads=[wk, uk], writes=[pyk])
            def chain(d, i2):
                base = 128 + d * 448
                pX = PS[0 + 2 * i2][:, 0:n_c]
                pXt = PS[1 + 2 * i2][:, 0:n_c]
                kX, kXt = f"bank{0 + 2 * i2}", f"bank{1 + 2 * i2}"
                P.op("pe", lambda e, pX=pX, wg=wg, ug=ug, base=base, n_c=n_c: e.matmul(pX, lhsT=wg[:, base:base + 128], rhs=ug[:, 0:n_c], start=True, stop=True),
                     reads=[wk, uk], writes=[kX])
                yield
                P.op("pe", lambda e, pXt=pXt, wg=wg, ug=ug, base=base, n_c=n_c: e.matmul(pXt, lhsT=wg[:, base + 64:base + 192], rhs=ug[:, 0:n_c], start=True, stop=True),
                     reads=[wk, uk], writes=[kXt])
                yield
                t1, t2, bp, mfu, ql, qq = t1_r[i2], t2_r[i2], bp_r[i2], mf_r[i2], ql_r[i2], qq_r[i2]
                ec, e1, e2, hin, r16, gg_ = ec_r[i2], e1_r[i2], e2_r[i2], hin_r[i2], r16_r[i2], gg_r[i2]
                kec, ke1, ke2, khin, kr16, kgg = f"ec{i2}", f"e1{i2}", f"e2{i2}", f"hin{i2}", f"r16{i2}", f"gg{i2}"
                cq, sqq = cq_r[i2], sq_r[i2]
                k1, k2, kb, km, kl_, kq, kc, ks = f"t1{i2}", f"t2{i2}", f"bp{i2}", f"mf{i2}", f"ql{i2}", f"qq{i2}", f"cq{i2}", f"sq{i2}"
                v3 = lambda t_, n_c=n_c: t_[:, 0:n_c].rearrange("p (b j) -> p b j", j=16)
                cosb = tg[:, d, 0:16].unsqueeze(1).to_broadcast([128, n_b, 16])
                sinb = tg[:, d, 16:32].unsqueeze(1).to_broadcast([128, n_b, 16])
                mrow = tg[:, d, 32:48].unsqueeze(1).to_broadcast([128, n_b, 16])
                cos2 = tg[:, d, 48:48 + n_b]
                sin2 = tg[:, d, 80:80 + n_b]
                rho = tg[:, d, 112:113]
                rho16 = tg[:, d, 113:114]
                P.op("dve", lambda e, t1=t1, pX=pX, cosb=cosb, v3=v3: e.tensor_tensor(out=v3(t1), in0=pX.rearrange("p (b j) -> p b j", j=16), in1=cosb, op=ALU.mult),
                     reads=[kX, tk], writes=[k1])
                yield
                P.op("dve", lambda e, t2=t2, pXt=pXt, sinb=sinb, v3=v3: e.tensor_tensor(out=v3(t2), in0=pXt.rearrange("p (b j) -> p b j", j=16), in1=sinb, op=ALU.mult),
                     reads=[kXt, tk], writes=[k2])
                yield
                P.op("dve", lambda e, bp=bp, t1=t1, t2=t2, n_c=n_c: e.tensor_tensor(out=bp[:, 0:n_c], in0=t1[:, 0:n_c], in1=t2[:, 0:n_c], op=ALU.add), reads=[k1, k2], writes=[kb])
                yield
                P.op("act", lambda e, mfu=mfu, mrow=mrow, v3=v3: e.copy(out=v3(mfu), in_=mrow), reads=[tk], writes=[km])
                yield
                P.op("pool", lambda e, rho16=rho16, n_b=n_b: e.tensor_copy(out=r16[:, 0:n_b], in_=rho16.to_broadcast([128, n_b])), reads=[tk], writes=[kr16])
                yield
                rv = (lambda a_, n_c=n_c: a_[:, 0:n_c]) if d == 0 else (lambda a_, n_c=n_c: a_[:, n_c - 1::-1])
                P.op("dve", lambda e, ql=ql, mfu=mfu, bp=bp, rv=rv, n_c=n_c: e.tensor_tensor_scan(out=rv(ql), data0=mfu[:, 0:n_c], data1=rv(bp), initial=0.0, op0=ALU.mult, op1=ALU.add),
                     reads=[km, kb], writes=[kl_])
                yield
                if d == 0:
                    e_src = ql[:, 15:n_c:16]
                    b_inj = bp[:, 0:n_c:16]
                else:
                    e_src = ql[:, n_c - 16::-16]
                    b_inj = bp[:, n_c - 1::-16]
                P.op("dve", lambda e, e_src=e_src, n_b=n_b: e.tensor_copy(out=ec[:, 0:n_b], in_=e_src), reads=[kl_], writes=[kec])
                yield
                P.op("pe", lambda e, n_b=n_b: e.matmul(PS[6][:, d * 64:d * 64 + n_b], lhsT=Jm, rhs=ec[:, 0:n_b], start=True, stop=True), reads=["Jm", kec], writes=["bank6"])
                yield
                P.op("dve", lambda e, cos2=cos2, n_b=n_b: e.tensor_tensor(out=e1[:, 0:n_b], in0=ec[:, 0:n_b], in1=cos2, op=ALU.mult), reads=[kec, tk], writes=[ke1])
                yield
                P.op("dve", lambda e, sin2=sin2, n_b=n_b: e.tensor_tensor(out=e2[:, 0:n_b], in0=PS[6][:, d * 64:d * 64 + n_b], in1=sin2, op=ALU.mult), reads=["bank6", tk], writes=[ke2])
                yield
                P.op("dve", lambda e, n_b=n_b: e.tensor_tensor(out=e1[:, 0:n_b], in0=e1[:, 0:n_b], in1=e2[:, 0:n_b], op=ALU.subtract), reads=[ke1, ke2], writes=[ke1])
                yield
                P.op("dve", lambda e: e.memset(hin[:, 0:1], 0.0), writes=[khin])
                yield
                P.op("dve", lambda e, n_b=n_b: e.tensor_tensor_scan(out=hin[:, 1:n_b + 1], data0=r16[:, 0:n_b], data1=e1[:, 0:n_b], initial=0.0, op0=ALU.mult, op1=ALU.add),
                     reads=[kr16, ke1, khin], writes=[khin])
                yield
                P.op("pe", lambda e, n_b=n_b: e.matmul(PS[7][:, d * 64:d * 64 + n_b], lhsT=Jm, rhs=hin[:, 0:n_b], start=True, stop=True), reads=["Jm", khin], writes=["bank7"])
                yield
                P.op("dve", lambda e, cos2=cos2, n_b=n_b: e.tensor_tensor(out=gg_[:, 0:n_b], in0=hin[:, 0:n_b], in1=cos2, op=ALU.mult), reads=[khin, tk], writes=[kgg])
                yield
                P.op("dve", lambda e, sin2=sin2, n_b=n_b: e.tensor_tensor(out=e2[:, 0:n_b], in0=PS[7][:, d * 64:d * 64 + n_b], in1=sin2, op=ALU.mult), reads=["bank7", tk, ke1], writes=[ke2])
                yield
                P.op("dve", lambda e, n_b=n_b: e.tensor_tensor(out=gg_[:, 0:n_b], in0=gg_[:, 0:n_b], in1=e2[:, 0:n_b], op=ALU.add), reads=[kgg, ke2], writes=[kgg])
                yield
                P.op("dve", lambda e, b_inj=b_inj, rho=rho, n_b=n_b: e.scalar_tensor_tensor(out=b_inj, in0=gg_[:, 0:n_b], scalar=rho, in1=b_inj, op0=ALU.mult, op1=ALU.add),
                     reads=[kgg, tk, kb, kl_], writes=[kb])
                yield
                P.op("dve", lambda e, qq=qq, mfu=mfu, bp=bp, rv=rv, n_c=n_c: e.tensor_tensor_scan(out=rv(qq), data0=mfu[:, 0:n_c], data1=rv(bp), initial=0.0, op0=ALU.mult, op1=ALU.add),
                     reads=[km, kb], writes=[kq])
                yield
                o = 1 if d == 0 else 0
                zc = 0 if d == 0 else n_c
                P.op("pool", lambda e, cq=cq, zc=zc: e.memset(cq[:, zc:zc + 1], 0.0), reads=[kc], writes=[kc])
                yield
                P.op("pool", lambda e, sqq=sqq, zc=zc: e.memset(sqq[:, zc:zc + 1], 0.0), reads=[ks], writes=[ks])
                yield
                P.op("dve", lambda e, cq=cq, qq=qq, cosb=cosb, o=o, n_c=n_c: e.tensor_tensor(out=cq[:, o:o + n_c].rearrange("p (b j) -> p b j", j=16),
                                                                                              in0=qq[:, 0:n_c].rearrange("p (b j) -> p b j", j=16), in1=cosb, op=ALU.mult),
                     reads=[kq, tk, kc], writes=[kc])
                yield
                P.op("dve", lambda e, sqq=sqq, qq=qq, sinb=sinb, o=o, n_c=n_c: e.tensor_tensor(out=sqq[:, o:o + n_c].rearrange("p (b j) -> p b j", j=16),
                                                                                                in0=qq[:, 0:n_c].rearrange("p (b j) -> p b j", j=16), in1=sinb, op=ALU.mult),
                     reads=[kq, tk, ks], writes=[ks])
                yield
                shf = 0 if d == 0 else 1
                P.op("pe", lambda e, py=py, wg=wg, cq=cq, base=base, shf=shf, n_c=n_c: e.matmul(py, lhsT=wg[:, base + 192:base + 320], rhs=cq[:, shf:shf + n_c], start=False, stop=False),
                     reads=[wk, kc], writes=[pyk])
                yield
                P.op("pe", lambda e, py=py, wg=wg, sqq=sqq, base=base, shf=shf, n_c=n_c, d=d: e.matmul(py, lhsT=wg[:, base + 320:base + 448], rhs=sqq[:, shf:shf + n_c], start=False, stop=(d == 1)),
                     reads=[wk, ks], writes=[pyk])
                yield
            for _ in itertools.zip_longest(chain(0, 0), chain(1, 1)):
                pass
            zg = zg_r[g % 2]
            zk = f"zg{g % 2}"
            P.op("act", lambda e, zg=zg, py=py, n_c=n_c: e.activation(out=zg[:, 0:n_c], in_=py, func=AF.Gelu_apprx_tanh), reads=[pyk], writes=[zk])
            P.dma(sq.Z[:, :, g, :].rearrange("t p c -> p t c"), zg[:, 0:n_c].rearrange("p (t c) -> p t c", c=64), reads=[zk], writes=["Zscr"], q="act")
    A.release(m0)
    P.barrier()
    m0 = A.mark()
    wglu = A.bf16(8 * 2048).rearrange("p (k c) -> p k c", k=8)
    P.dma(wglu, C.wglu_s, writes=["wglu"], q="act")
    bgl_f = A.f32(2048, parts=1)
    P.dma(bgl_f, C.s5_b_glu.rearrange("(o n) -> o n", o=1), writes=["bgl_f"])
    bgl = A.bf16(2048, parts=1)
    P.op("dve", lambda e: e.tensor_copy(out=bgl, in_=bgl_f), reads=["bgl_f"], writes=["bgl"])
    ones1 = A.bf16(128, parts=1)
    P.op("pool", lambda e: e.memset(ones1, 1.0), writes=["ones1"])
    lnG = A.f32(D)
    lnB = A.f32(D)
    bc_load(C, lnG, C.ln_g[l, 0], "lnG")
    bc_load(C, lnB, C.ln_b[l, 0], "lnB")
    gm = A.f32(D)
    xt = A.f32(4 * D).rearrange("p (b c) -> p b c", b=4)
    zt = A.bf16(64 * 64).rearrange("p (g c) -> p g c", g=64)
    zc_ = A.bf16(8 * D, parts=64).rearrange("p (t c) -> p t c", t=8)
    zf = A.bf16(8 * 512).rearrange("p (k t) -> p k t", k=8)
    ga = [A.f32(512) for _ in range(2)]
    aa_ = [A.f32(512) for _ in range(2)]
    st = A.f32(12).rearrange("p (a b) -> p a b", a=2)
    mv = A.f32(2)
    rs = A.f32(1)
    nmr = A.f32(1)
    for sq in seqs:
        ci = sq.ci
        bc_load(C, gm, C.modd[l, ci, 2 * D:3 * D], "gm")
        for T in range(sq.n // 512):
            t0 = T * 512
            P.dma(zt, sq.Z[T], writes=["zt"], q="sp")
            P.dma(xt, sq.x0[t0:t0 + 512, :].rearrange("(b p) c -> p b c", p=128), writes=["xt"], q="sp")
            for b in range(4):
                P.op("act", lambda e, b=b: e.activation(out=xt[:, b, :], in_=xt[:, b, :], func=AF.Copy, scale=ALPHA), reads=["xt"], writes=["xt"])
            for g8 in range(8):
                p1 = PS[g8 % 2][0:64, :].bitcast(BF16)
                bk = f"bank{g8 % 2}"
                for gi_ in range(8):
                    g = g8 * 8 + gi_
                    P.op("pe", lambda e, p1=p1, g=g, gi_=gi_: e.transpose(p1[:, gi_ * 128:(gi_ + 1) * 128], zt[:, g, :], C.identb), reads=["zt"], writes=[bk])
                eng = "act" if g8 % 2 == 0 else "dve"
                outv = zc_[:, :, g8 * 128:(g8 + 1) * 128].rearrange("p t (g h) -> p g t h", h=16)
                inv = p1.rearrange("p (g t h) -> p g t h", g=8, t=8)
                if eng == "act":
                    P.op("act", lambda e, outv=outv, inv=inv: e.copy(out=outv, in_=inv), reads=[bk], writes=[f"zc{g8}"])
                else:
                    P.op("dve", lambda e, outv=outv, inv=inv: e.tensor_copy(out=outv, in_=inv), reads=[bk], writes=[f"zc{g8}"])
            for k in range(8):
                p2 = PS[2 + k % 2][:, 0:256].bitcast(BF16)
                bk = f"bank{2 + k % 2}"
                for t in range(8):
                    P.op("pe", lambda e, p2=p2, t=t, k=k: e.transpose(p2[:, t * 64:(t + 1) * 64], zc_[:, t, k * 128:(k + 1) * 128], C.identb[0:64, 0:64]), reads=[f"zc{k}"], writes=[bk])
                outv = zf[:, k, :].rearrange("p (c t) -> p t c", t=8)
                inv = p2.rearrange("p (t c) -> p t c", t=8)
                if k % 2 == 0:
                    P.op("act", lambda e, outv=outv, inv=inv: e.copy(out=outv, in_=inv), reads=[bk], writes=[f"zf{k}"])
                else:
                    P.op("dve", lambda e, outv=outv, inv=inv: e.tensor_copy(out=outv, in_=inv), reads=[bk], writes=[f"zf{k}"])
            zf_keys = [f"zf{k}" for k in range(8)]
            for b in range(4):
                for hh in range(2):
                    pa = PS[4 + hh]
                    pg_ = PS[6 + hh]
                    for k in range(8):
                        P.op("pe", lambda e, pa=pa, k=k, b=b, hh=hh: e.matmul(pa, lhsT=zf[:, k, b * 128:(b + 1) * 128], rhs=wglu[:, k, hh * 512:(hh + 1) * 512], start=(k == 0), stop=False),
                             reads=[zf_keys[k], "wglu"], writes=[f"bank{4 + hh}"])
                    P.op("pe", lambda e, pa=pa, hh=hh: e.matmul(pa, lhsT=ones1, rhs=bgl[:, hh * 512:(hh + 1) * 512], start=False, stop=True), reads=["ones1", "bgl"], writes=[f"bank{4 + hh}"])
                    for k in range(8):
                        P.op("pe", lambda e, pg_=pg_, k=k, b=b, hh=hh: e.matmul(pg_, lhsT=zf[:, k, b * 128:(b + 1) * 128], rhs=wglu[:, k, 1024 + hh * 512:1024 + (hh + 1) * 512], start=(k == 0), stop=False),
                             reads=[zf_keys[k], "wglu"], writes=[f"bank{6 + hh}"])
                    P.op("pe", lambda e, pg_=pg_, hh=hh: e.matmul(pg_, lhsT=ones1, rhs=bgl[:, 1024 + hh * 512:1024 + (hh + 1) * 512], start=False, stop=True), reads=["ones1", "bgl"], writes=[f"bank{6 + hh}"])
                    g_t, a_t = ga[hh], aa_[hh]
                    P.op("act", lambda e, g_t=g_t, pg_=pg_: e.activation(out=g_t, in_=pg_, func=AF.Sigmoid), reads=[f"bank{6 + hh}"], writes=[f"ga{hh}"])
                    P.op("dve", lambda e, a_t=a_t, pa=pa, g_t=g_t: e.tensor_tensor(out=a_t, in0=pa, in1=g_t, op=ALU.mult), reads=[f"bank{4 + hh}", f"ga{hh}"], writes=[f"aa{hh}"])
                    P.op("dve", lambda e, a_t=a_t, hh=hh: e.tensor_tensor(out=a_t, in0=a_t, in1=gm[:, hh * 512:(hh + 1) * 512], op=ALU.mult), reads=[f"aa{hh}", "gm"], writes=[f"aa{hh}"])
                    xs = xt[:, b, hh * 512:(hh + 1) * 512]
                    P.op("dve", lambda e, a_t=a_t, xs=xs: e.tensor_tensor(out=xs, in0=xs, in1=a_t, op=ALU.add), reads=[f"aa{hh}", "xt"], writes=["xt"])
                layer_norm_block(C, xt[:, b, :], "xt", lnG, lnB, st, mv, rs, nmr)
            P.dma(sq.x1[t0:t0 + 512, :].rearrange("(b p) c -> p b c", p=128), xt, reads=["xt"], writes=["x1"], q="sp", is_output=(sq.x1 is sq.y))
    A.release(m0)
    P.barrier()


def na_phase(C, seq_io):
    P, A, nc = C.P, C.A, C.nc
    PS = C.psum
    l = 1
    m0 = A.mark()
    wqkv = A.bf16(8 * 3072).rearrange("p (k c) -> p k c", k=8)
    P.dma(wqkv, C.wqkv_s, writes=["wqkv"], q="act")
    bqk = A.f32(16)
    load_pp(C, C.na_b_qkv[0:2048].rearrange("(m p) -> m p", p=128), bqk, "bqk")
    sc1 = A.f32(D)
    sh = A.f32(D)
    xt_r = [A.f32(4 * D).rearrange("p (b c) -> p b c", b=4) for _ in range(2)]
    tmp = A.f32(D)
    hb_r = [A.bf16(4 * D).rearrange("p (b c) -> p b c", b=4) for _ in range(2)]
    hfm_r = [A.bf16(8 * 512).rearrange("p (k t) -> p k t", k=8) for _ in range(2)]
    qk_st = [A.bf16(512) for _ in range(3)]
    v_st = [A.bf16(1040).rearrange("p (h d) -> p h d", h=16) for _ in range(2)]
    for i in range(2):
        P.op("pool", lambda e, i=i: e.memset(v_st[i][:, :, 64:65], 1.0), writes=[f"vst{i}"])
    pT = PS[0][:, 0:256].bitcast(BF16)
    cnt = 0
    gtile = 0
    for (sq, xin, xout) in seq_io:
        n = sq.n
        ci = sq.ci
        bc_load(C, sh, C.modd[l, ci, 0:D], "sh")
        bc_load(C, sc1, C.modd[l, ci, D:2 * D], "sc1")
        P.op("pool", lambda e: e.tensor_scalar(out=sc1, in0=sc1, scalar1=1.0, scalar2=None, op0=ALU.add), reads=["sc1"], writes=["sc1"])
        g0 = gtile
        gtile += n // 512

        def prologue(T):
            t0 = T * 512
            i2 = (g0 + T) % 2
            xt, hb, hfm = xt_r[i2], hb_r[i2], hfm_r[i2]
            P.dma(xt, xin[t0:t0 + 512, :].rearrange("(b p) c -> p b c", p=128), writes=[f"xt{i2}"], q="sp")
            for b in range(4):
                P.op("dve", lambda e, b=b: e.tensor_tensor(out=tmp, in0=xt[:, b, :], in1=sc1, op=ALU.mult), reads=[f"xt{i2}", "sc1"], writes=["tmp"])
                P.op("dve", lambda e, b=b: e.tensor_tensor(out=hb[:, b, :], in0=tmp, in1=sh, op=ALU.add), reads=["tmp", "sh"], writes=[f"hb{i2}{b}"])

        def prologue_b(T):
            i2 = (g0 + T) % 2
            hb, hfm = hb_r[i2], hfm_r[i2]
            for k in range(8):
                for b in range(4):
                    P.op("pe", lambda e, b=b, k=k: e.transpose(pT[:, b * 128:(b + 1) * 128], hb[:, b, k * 128:(k + 1) * 128], C.identb),
                         reads=[f"hb{i2}{b}"], writes=["bank0"])
                if k % 2 == 0:
                    P.op("act", lambda e, k=k: e.copy(out=hfm[:, k, :], in_=pT), reads=["bank0"], writes=[f"hfm{i2}{k}"])
                else:
                    P.op("dve", lambda e, k=k: e.tensor_copy(out=hfm[:, k, :], in_=pT), reads=["bank0"], writes=[f"hfm{i2}{k}"])

        prologue(0)
        prologue_b(0)
        for T in range(n // 512):
            t0 = T * 512
            i2 = (g0 + T) % 2
            hfm = hfm_r[i2]
            if T + 1 < n // 512:
                prologue(T + 1)
            hf_keys = [f"hfm{i2}{k}" for k in range(8)]
            for mt in range(16):
                j = cnt % 3
                cnt += 1
                pb = PS[1 + j]
                pk = f"bank{1 + j}"
                for k in range(8):
                    P.op("pe", lambda e, pb=pb, k=k, mt=mt: e.matmul(pb, lhsT=wqkv[:, k, mt * 128:(mt + 1) * 128], rhs=hfm[:, k, :], start=(k == 0), stop=(k == 7)),
                         reads=["wqkv", hf_keys[k]], writes=[pk])
                stt = qk_st[j]
                sk = f"qkst{j}"
                P.op("act", lambda e, stt=stt, pb=pb, mt=mt: e.activation(out=stt, in_=pb, func=AF.Identity, bias=bqk[:, mt:mt + 1], scale=1.0),
                     reads=[pk, "bqk"], writes=[sk])
                dst = (sq.qT if mt < 8 else sq.kT)[:, mt % 8, t0:t0 + 512]
                P.dma(dst, stt, reads=[sk], writes=["qkscr"], q="sp")
            if T + 1 < n // 512:
                prologue_b(T + 1)
            for b in range(4):
                vs = v_st[b % 2]
                vk = f"vst{b % 2}"
                for half in range(2):
                    j = cnt % 3
                    cnt += 1
                    pb = PS[1 + j]
                    pk = f"bank{1 + j}"
                    for k in range(8):
                        P.op("pe", lambda e, pb=pb, k=k, b=b, half=half: e.matmul(pb, lhsT=hfm[:, k, b * 128:(b + 1) * 128],
                                                                                   rhs=wqkv[:, k, 2048 + half * 512:2048 + (half + 1) * 512], start=(k == 0), stop=(k == 7)),
                             reads=["wqkv", hf_keys[k]], writes=[pk])
                    P.op("dve", lambda e, vs=vs, pb=pb, half=half: e.tensor_copy(out=vs[:, half * 8:(half + 1) * 8, 0:64], in_=pb.rearrange("p (h d) -> p h d", h=8)),
                         reads=[pk], writes=[vk])
                P.dma(sq.vS[t0 + b * 128:t0 + (b + 1) * 128, :], vs.rearrange("p h d -> p (h d)"), reads=[vk], writes=["vscr"], q="sp")
    A.release(m0)
    P.barrier()
    m0 = A.mark()
    wo = A.bf16(16 * D, parts=64).rearrange("p (h c) -> p h c", h=16)
    P.dma(wo, C.wo_s, writes=["wo"], q="act")
    TAB = A.bf16(16 * 17 * 64).rearrange("p (h x) -> p h x", h=16)
    m1 = A.mark()
    tstg = [A.f32(2176) for _ in range(2)]
    for hh in range(8):
        ts_ = tstg[hh % 2]
        tk = f"tstg{hh % 2}"
        P.dma(ts_, C.na_tab[:, 2 * hh:2 * hh + 2, :].rearrange("p h x -> p (h x)"), writes=[tk], q="sp")
        P.op("act", lambda e, ts_=ts_, hh=hh: e.activation(out=TAB[:, 2 * hh:2 * hh + 2, :].rearrange("p h x -> p (h x)"), in_=ts_, func=AF.Exp),
             reads=[tk], writes=["TAB"])
    P.barrier()
    A.release(m1)
    lnG = A.f32(D)
    lnB = A.f32(D)
    bc_load(C, lnG, C.ln_g[l, 0], "lnG")
    bc_load(C, lnB, C.ln_b[l, 0], "lnB")
    bvT = A.f32(16, parts=64)
    load_pp(C, C.na_b_qkv[2048:3072].rearrange("(h d) -> h d", d=64), bvT, "bvT")
    bvTb = A.bf16(16, parts=64)
    P.op("dve", lambda e: e.tensor_copy(out=bvTb, in_=bvT), reads=["bvT"], writes=["bvTb"])
    borow = A.f32(D, parts=1)
    P.dma(borow, C.na_b_o.rearrange("(o n) -> o n", o=1), writes=["borow"])
    for half in range(2):
        pb = PS[6 + half][0:1, :]
        for h in range(16):
            P.op("pe", lambda e, pb=pb, h=h, half=half: e.matmul(pb, lhsT=bvTb[:, h:h + 1], rhs=wo[:, h, half * 512:(half + 1) * 512], start=(h == 0), stop=(h == 15)),
                 reads=["bvTb", "wo"], writes=[f"bank{6 + half}"])
        P.op("dve", lambda e, pb=pb, half=half: e.tensor_tensor(out=borow[:, half * 512:(half + 1) * 512], in0=pb, in1=borow[:, half * 512:(half + 1) * 512], op=ALU.add),
             reads=[f"bank{6 + half}", "borow"], writes=["borow"])
    P.dma(C.bo2.rearrange("(o n) -> o n", o=1), borow, reads=["borow"], writes=["bo2"])
    bo = A.f32(D)
    bc_load_dep(C, bo, C.bo2, "bo", ["bo2"])
    gm = A.f32(D)
    gmb = A.f32(D)
    ones_r = A.f32(64, parts=65)
    P.op("pool", lambda e: e.memset(ones_r[64:65, :], 1.0), writes=["ones_r"])
    xt = A.f32(4 * D).rearrange("p (b c) -> p b c", b=4)
    kw = A.bf16(8 * 1536).rearrange("p (k t) -> p k t", k=8)
    qw = A.bf16(8 * 512).rearrange("p (k t) -> p k t", k=8)
    vw = A.bf16(12 * 1040).rearrange("p (b h d) -> p b h d", b=12, h=16)
    es_ring = [A.f32(512) for _ in range(3)]
    pt_ring = [A.bf16(512) for _ in range(4)]
    es4 = [A.f32(128) for _ in range(3)]
    pt4 = [A.bf16(128) for _ in range(3)]
    oT = A.bf16(16 * 512, parts=64).rearrange("p (h t) -> p h t", h=16)
    sums = A.f32(512, parts=65)
    rcp = A.f32(512, parts=64)
    rt = [A.f32(512) for _ in range(2)]
    st = A.f32(12).rearrange("p (a b) -> p a b", a=2)
    mv = A.f32(2)
    rs = A.f32(1)
    nmr = A.f32(1)
    c_es = c_pt = c_s4 = c_h = 0
    for (sq, xin, xout) in seq_io:
        n = sq.n
        ci = sq.ci
        R = n // 64
        nt = n // 512
        bc_load(C, gm, C.modd[l, ci, 2 * D:3 * D], "gm")
        P.op("pool", lambda e: e.tensor_tensor(out=gmb, in0=gm, in1=bo, op=ALU.mult), reads=["gm", "bo"], writes=["gmb"])
        for T in range(nt):
            t0 = T * 512
            def kv_loads(T_):
                t0_ = T_ * 512
                wt0 = t0_ - 512
                lo = max(wt0, 0)
                hi = min(t0_ + 1024, n)
                P.dma(kw[:, :, lo - wt0:hi - wt0], sq.kT[:, :, lo:hi], writes=["kw"], q="sp")
                P.dma(qw, sq.qT[:, :, t0_:t0_ + 512], writes=["qw"], q="sp")
                P.dma(vw[:, (lo - wt0) // 128:(hi - wt0) // 128, :, :].rearrange("p b h d -> p b (h d)"),
                      sq.vS[lo:hi, :].rearrange("(b p) c -> p b c", p=128), writes=["vw"], q="sp")

            if T == 0:
                kv_loads(0)
            wr0 = 8 * T - 8
            LA = 2
            units = [(h, qb) for h in range(16) for qb in range(4)]
            pend = {}

            def valid(kr_, r_):
                rs_ = min(max(r_ - 4, 0), R - 8)
                return kr_ < R and rs_ <= kr_ < rs_ + 8

            def issue_scores(ui):
                nonlocal c_es, c_s4
                h, qb = units[ui]
                hp, pbase = h // 2, (h % 2) * 64
                r = 8 * T + 2 * qb
                kr0 = min(max(r - 4, 0), R - 10)
                sb_i = c_es % 3
                ps_s = PS[sb_i]
                psk = f"bank{sb_i}"
                esi = c_es % 3
                c_es += 1
                for j in range(4):
                    kc0 = (kr0 + 2 * j - wr0) * 64
                    P.op("pe", lambda e, ps_s=ps_s, j=j, kc0=kc0, qb=qb, hp=hp, pbase=pbase: e.matmul(
                        ps_s[:, (3 - j) * 128:(4 - j) * 128], lhsT=kw[pbase:pbase + 64, hp, kc0:kc0 + 128],
                        rhs=qw[pbase:pbase + 64, hp, qb * 128:(qb + 1) * 128], start=True, stop=True),
                        reads=["kw", "qw"], writes=[psk])
                pend[ui] = [sb_i, esi, None]

            def issue_j4(ui):
                nonlocal c_s4
                h, qb = units[ui]
                hp, pbase = h // 2, (h % 2) * 64
                r = 8 * T + 2 * qb
                kr0 = min(max(r - 4, 0), R - 10)
                kr4 = kr0 + 8
                nv4 = sum(1 for kl in range(2) for rl in range(2) if valid(kr4 + kl, r + rl))
                if nv4:
                    i4 = c_s4 % 2
                    c_s4 += 1
                    b4 = (3, 6)[i4]
                    ps4 = PS[b4][:, 0:128]
                    kc0 = (kr4 - wr0) * 64
                    P.op("pe", lambda e, ps4=ps4, kc0=kc0, qb=qb, hp=hp, pbase=pbase: e.matmul(
                        ps4, lhsT=kw[pbase:pbase + 64, hp, kc0:kc0 + 128], rhs=qw[pbase:pbase + 64, hp, qb * 128:(qb + 1) * 128],
                        start=True, stop=True), reads=["kw", "qw"], writes=[f"bank{b4}"])
                    pend[ui][2] = i4

            def finish_unit(ui, po, pok):
                nonlocal c_pt
                h, qb = units[ui]
                sb_i, esi, i4 = pend.pop(ui)
                r = 8 * T + 2 * qb
                kr0 = min(max(r - 4, 0), R - 10)
                x03 = 1 + r - kr0
                ps_s = PS[sb_i]
                psk = f"bank{sb_i}"
                es = es_ring[esi]
                esk = f"es{esi}"
                P.op("act", lambda e, es=es, ps_s=ps_s: e.activation(out=es, in_=ps_s, func=AF.Exp, scale=0.125), reads=[psk], writes=[esk])
                pt = pt_ring[c_pt % 4]
                ptk = f"pt{c_pt % 4}"
                c_pt += 1
                P.op("dve", lambda e, pt=pt, es=es, h=h, x03=x03: e.tensor_tensor(out=pt, in0=es, in1=TAB[:, h, x03 * 64:(x03 + 8) * 64], op=ALU.mult),
                     reads=[esk, "TAB"], writes=[ptk])
                tiles = []
                zkeys = []
                for j in range(4):
                    kr = kr0 + 2 * j
                    nv = 0
                    for kl in range(2):
                        for rl in range(2):
                            if valid(kr + kl, r + rl):
                                nv += 1
                            else:
                                zk_ = ptk + f"z{len(zkeys)}"
                                zkeys.append(zk_)
                                P.op("pool", lambda e, pt=pt, j=j, kl=kl, rl=rl: e.memset(
                                    pt[kl * 64:(kl + 1) * 64, (3 - j) * 128 + rl * 64:(3 - j) * 128 + (rl + 1) * 64], 0.0),
                                    reads=[ptk], writes=[zk_])
                    if nv:
                        tiles.append((pt[:, (3 - j) * 128:(4 - j) * 128], [ptk] + zkeys, (kr - wr0) // 2))
                tiles = [(a_, [ptk] + zkeys, c_) for (a_, _, c_) in tiles]
                if i4 is not None:
                    kr = kr0 + 8
                    b4 = (3, 6)[i4]
                    ps4 = PS[b4][:, 0:128]
                    i5 = i4
                    e4, p4 = es4[i5], pt4[i5]
                    P.op("act", lambda e, e4=e4, ps4=ps4: e.activation(out=e4, in_=ps4, func=AF.Exp, scale=0.125), reads=[f"bank{b4}"], writes=[f"es4{i5}"])
                    P.op("dve", lambda e, e4=e4, p4=p4, h=h, x03=x03: e.tensor_tensor(out=p4, in0=e4, in1=TAB[:, h, (x03 - 2) * 64:x03 * 64], op=ALU.mult),
                         reads=[f"es4{i5}", "TAB"], writes=[f"pt4{i5}"])
                    z4 = []
                    for kl in range(2):
                        for rl in range(2):
                            if not valid(kr + kl, r + rl):
                                zk_ = f"pt4{i5}z{len(z4)}"
                                z4.append(zk_)
                                P.op("pool", lambda e, p4=p4, kl=kl, rl=rl: e.memset(p4[kl * 64:(kl + 1) * 64, rl * 64:(rl + 1) * 64], 0.0),
                                     reads=[f"pt4{i5}"], writes=[zk_])
                    tiles.append((p4, [f"pt4{i5}"] + z4, (kr - wr0) // 2))
                for ti, (rhs_t, rk_, blk) in enumerate(tiles):
                    P.op("pe", lambda e, po=po, rhs_t=rhs_t, blk=blk, h=h, qb=qb, ti=ti, nti=len(tiles): e.matmul(
                        po[0:65, qb * 128:(qb + 1) * 128], lhsT=vw[:, blk, h, 0:65], rhs=rhs_t, start=(ti == 0), stop=(ti == nti - 1)),
                        reads=rk_ + ["vw"], writes=[pok])

            def normalise(h, po, pok):
                P.op("act", lambda e, po=po: e.activation(out=sums[64:65, :], in_=po[64:65, :], func=AF.Ln), reads=[pok], writes=["sums"])
                P.op("act", lambda e: e.activation(out=sums[64:65, :], in_=sums[64:65, :], func=AF.Exp, scale=-1.0), reads=["sums"], writes=["sums"])
                P.op("pe", lambda e: e.matmul(PS[7][0:64, :], lhsT=ones_r[64:65, :], rhs=sums[64:65, :], start=True, stop=True),
                     reads=["sums", "ones_r"], writes=["bank7"])
                P.op("act", lambda e: e.copy(out=rcp, in_=PS[7][0:64, :]), reads=["bank7"], writes=["rcp"])
                P.op("dve", lambda e, po=po, h=h: e.tensor_tensor(out=oT[:, h, :], in0=po[0:64, :], in1=rcp, op=ALU.mult), reads=[pok, "rcp"], writes=[f"oT{h}"])

            for ui in range(min(LA, len(units))):
                issue_scores(ui)
            issue_j4(0)
            deferred = None
            for ui, (h, qb) in enumerate(units):
                if qb == 0:
                    po = PS[4 + c_h % 2]
                    pok = f"bank{4 + c_h % 2}"
                    c_h += 1
                if ui + LA < len(units):
                    issue_scores(ui + LA)
                if ui + 1 < len(units):
                    issue_j4(ui + 1)
                finish_unit(ui, po, pok)
                if qb == 1 and deferred is not None:
                    normalise(*deferred)
                    deferred = None
                if qb == 3:
                    deferred = (h, po, pok)
            normalise(*deferred)
            if T + 1 < nt:
                kv_loads(T + 1)
            P.dma(xt, xin[t0:t0 + 512, :].rearrange("(b p) c -> p b c", p=128), writes=["xt"], q="sp")
            for b in range(4):
                P.op("dve", lambda e, b=b: e.scalar_tensor_tensor(out=xt[:, b, :], in0=xt[:, b, :], scalar=ALPHA, in1=gmb, op0=ALU.mult, op1=ALU.add), reads=["xt", "gmb"], writes=["xt"])
            oT_keys = [f"oT{h}" for h in range(16)]
            for b in range(4):
                for half in range(2):
                    j = (b * 2 + half) % 2
                    pw = PS[6 + j]
                    pwk = f"bank{6 + j}"
                    for h in range(16):
                        P.op("pe", lambda e, pw=pw, h=h, b=b, half=half: e.matmul(pw, lhsT=oT[:, h, b * 128:(b + 1) * 128], rhs=wo[:, h, half * 512:(half + 1) * 512],
                                                                                  start=(h == 0), stop=(h == 15)), reads=[oT_keys[h], "wo"], writes=[pwk])
                    r_ = rt[j]
                    rk = f"rt{j}"
                    xs = xt[:, b, half * 512:(half + 1) * 512]
                    P.op("dve", lambda e, pw=pw, r_=r_, half=half: e.tensor_tensor(out=r_, in0=pw, in1=gm[:, half * 512:(half + 1) * 512], op=ALU.mult),
                         reads=[pwk, "gm"], writes=[rk])
                    P.op("dve", lambda e, r_=r_, xs=xs: e.tensor_tensor(out=xs, in0=xs, in1=r_, op=ALU.add), reads=[rk, "xt"], writes=["xt"])
                layer_norm_block(C, xt[:, b, :], "xt", lnG, lnB, st, mv, rs, nmr)
            P.dma(xout[t0:t0 + 512, :].rearrange("(b p) c -> p b c", p=128), xt, reads=["xt"], writes=["xout"], q="sp", is_output=(xout is sq.y))
    A.release(m0)
    P.barrier()


def bc_load_dep(C, dst, src_row, key, reads):
    parts = dst.shape[0]
    C.P.dma(dst, src_row.rearrange("(o n) -> o n", o=1).partition_broadcast(parts), reads=reads, writes=[key])


def rpb_layout(rpb):
    rpb = np.asarray(rpb, np.float32)
    kc = np.arange(64)[:, None]
    qc = np.arange(64)[None, :]
    cs_ = np.clip(qc - 8, 0, 48)
    valid = (kc >= cs_) & (kc < cs_ + 16)
    idx = np.clip(kc - qc + 15, 0, 30)
    out = rpb[:, :, idx]
    out = np.where(valid[None, None], out, np.float32(-30000.0)).astype(np.float32)
    return np.ascontiguousarray(out)


def tab_layout(rpb):
    t = rpb_layout(rpb)
    out = np.full((2, 64, 16, 17, 64), -30000.0, np.float32)
    for kl in range(2):
        for x in range(17):
            dr = kl + 7 - x
            if -7 <= dr <= 7:
                out[kl, :, :, x, :] = np.transpose(t[:, dr + 7, :, :], (1, 0, 2))
    return np.ascontiguousarray(out.reshape(128, 16, 17 * 64))


def s5_masks_const():
    tau = np.arange(128)[:, None] // 16
    t = np.arange(128)[None, :] // 16
    return np.ascontiguousarray(np.stack([(t >= tau), (tau >= t)]).astype(np.float32))


def shared_inputs(inp):
    f = lambda a: np.ascontiguousarray(np.asarray(a, np.float32))
    return {
        "s5_masks": s5_masks_const(),
        "w_ada": f(inp["w_ada"]), "b_ada": f(inp["b_ada"]), "ln_g": f(inp["ln_g"]), "ln_b": f(inp["ln_b"]),
        "s5_lam_re": f(inp["s5_lam_re"][0]), "s5_lam_im": f(inp["s5_lam_im"][0]), "s5_log_dt": f(inp["s5_log_dt"][0]),
        "s5_b_re": f(inp["s5_b_re"][0]), "s5_b_im": f(inp["s5_b_im"][0]), "s5_c_re": f(inp["s5_c_re"][0]), "s5_c_im": f(inp["s5_c_im"][0]),
        "s5_d": f(inp["s5_d"][0]), "s5_w_glu": f(inp["s5_w_glu"][0]), "s5_b_glu": f(inp["s5_b_glu"][0]),
        "na_w_qkv": f(inp["na_w_qkv"][0]), "na_b_qkv": f(inp["na_b_qkv"][0]), "na_tab": tab_layout(inp["na_rpb"][0]),
        "na_w_o": f(inp["na_w_o"][0]), "na_b_o": f(inp["na_b_o"][0]),
        "ffn_w_in": f(inp["ffn_w_in"]), "ffn_b_in": f(inp["ffn_b_in"]), "ffn_conv_w": f(inp["ffn_conv_w"]),
        "ffn_conv_b": f(inp["ffn_conv_b"]), "ffn_w_out": f(inp["ffn_w_out"]), "ffn_b_out": f(inp["ffn_b_out"]),
    }


_CACHE = {}


def prompt_window(i):
    r0 = min(max(32 * i - 8, 0), 256 - 48)
    return r0 * 64


def kernel(**inputs):
    inp = {k: np.asarray(v) for k, v in inputs.items()}
    cfg = dict(ns=NS_FULL, np=NP_FULL)
    if "nc" not in _CACHE:
        _CACHE["nc"] = build(cfg)
    nc, C = _CACHE["nc"]
    shared = shared_inputs(inp)
    xs = np.asarray(inp["x_sample"], np.float32)
    xp = np.asarray(inp["x_prompt"], np.float32)[0]
    in_maps = []
    for i in range(8):
        w0 = prompt_window(i)
        m = dict(shared)
        m["x_s"] = np.ascontiguousarray(xs[i])
        m["x_p"] = np.ascontiguousarray(xp[w0:w0 + NP_FULL])
        m["cs"] = np.ascontiguousarray(np.stack([inp["c_sample"][i], inp["c_prompt"][0]]).astype(np.float32))
        in_maps.append(m)
    res = run_bass_kernel_spmd(nc, in_maps, core_ids=list(range(8)))
    y_s = np.stack([np.asarray(res.results[i]["y_s"], np.float32) for i in range(8)])
    y_p = np.zeros((1, 16384, D), np.float32)
    for i in range(8):
        w0 = prompt_window(i)
        off = 2048 * i - w0
        y_p[0, 2048 * i:2048 * (i + 1)] = np.asarray(res.results[i]["y_p"], np.float32)[off:off + 2048]
    return (y_p, y_s)
```
